# Optimizing a Trainium2 kernel written in Bass

```python
import math
import jax, jax.numpy as jnp
from jax import lax
import numpy as np

D_MODEL = 1024
BATCH = 1
SEQ = 16384
DEPTH = 1
DEC_BATCH = 8
DEC_SEQ = 8192
PAST_LEN = 128

MIX_WIDTH = D_MODEL
ATTN_WIDTH = D_MODEL // 2
ATTN_HEAD_DIM = 64
ATTN_HEADS = ATTN_WIDTH // ATTN_HEAD_DIM
ATTN_KV_HEADS = 2
ATTN_GROUP = ATTN_HEADS // ATTN_KV_HEADS
KV_DIM = ATTN_KV_HEADS * ATTN_HEAD_DIM
WINDOW = 128
BLOCK = 128
DN_WIDTH = MIX_WIDTH - ATTN_WIDTH
DN_HEAD_DIM = 128
DN_HEADS = DN_WIDTH // DN_HEAD_DIM
CONV_WIDTH = 5
CONV_PAD = CONV_WIDTH // 2
CHUNK = 64
D_FF = 2816
ALPHA = (2.0 * DEPTH) ** 0.25
BETA_INIT = (8.0 * DEPTH) ** -0.25
LN_EPS = 1e-5
RMS_EPS = 1e-6

OFF_AK = ATTN_WIDTH
OFF_AV = OFF_AK + KV_DIM
OFF_DQ = OFF_AV + KV_DIM
OFF_DK = OFF_DQ + DN_WIDTH
OFF_DV = OFF_DK + DN_WIDTH
OFF_Z = OFF_DV + DN_WIDTH
OFF_B = OFF_Z + DN_WIDTH
OFF_A = OFF_B + 2 * DN_HEADS
PROJ_DIM = OFF_A + 2 * DN_HEADS
SPLIT_POINTS = (OFF_AK, OFF_AV, OFF_DQ, OFF_DK, OFF_DV, OFF_Z, OFF_B, OFF_A)

kernel_name = "hymba_swa_gdn_macaron_deepnorm_encoder"


def layer_norm(x, gain, bias):
    xf = x.astype(jnp.float32)
    mu = jnp.mean(xf, axis=-1, keepdims=True)
    var = jnp.mean(jnp.square(xf - mu), axis=-1, keepdims=True)
    y = (xf - mu) * lax.rsqrt(var + LN_EPS) * gain.astype(jnp.float32) + bias.astype(jnp.float32)
    return y.astype(x.dtype)


def swiglu(x, w_in, w_out):
    gate, up = jnp.split(x @ w_in, 2, axis=-1)
    return (jax.nn.silu(gate) * up) @ w_out


def l2norm(t):
    return t * lax.rsqrt(jnp.sum(t * t, axis=-1, keepdims=True) + RMS_EPS)


def windowed_gqa_attention(q, k, v, sink):
    b, l, _ = q.shape
    nb = l // BLOCK
    qb = q.reshape(b, nb, BLOCK, ATTN_KV_HEADS, ATTN_GROUP, ATTN_HEAD_DIM)

    def band(t):
        tp = jnp.pad(t, ((0, 0), (BLOCK, BLOCK), (0, 0)))
        tp = tp.reshape(b, nb + 2, BLOCK, ATTN_KV_HEADS, ATTN_HEAD_DIM)
        return jnp.concatenate([tp[:, :-2], tp[:, 1:-1], tp[:, 2:]], axis=2)

    kb, vb = band(k), band(v)
    scores = jnp.einsum('bnqhgd,bnkhd->bnhgqk', qb, kb,
                        preferred_element_type=jnp.float32) * (ATTN_HEAD_DIM ** -0.5)
    qi = jnp.arange(BLOCK)[:, None]
    kj = jnp.arange(3 * BLOCK)[None, :]
    dist = jnp.abs(qi - kj + BLOCK)
    in_win = dist <= WINDOW
    s_abs = (jnp.arange(nb)[:, None] - 1) * BLOCK + jnp.arange(3 * BLOCK)[None, :]
    in_seq = (s_abs >= 0) & (s_abs < l)
    mask = in_win[None, :, :] & in_seq[:, None, :]
    slopes = 2.0 ** (-8.0 * jnp.arange(1, ATTN_HEADS + 1, dtype=jnp.float32) / ATTN_HEADS)
    alibi = (-slopes[:, None, None] * dist.astype(jnp.float32)[None]).reshape(
        ATTN_KV_HEADS, ATTN_GROUP, BLOCK, 3 * BLOCK)
    logits = jnp.where(mask[None, :, None, None], scores + alibi[None, None], -jnp.inf)
    sink_l = sink.astype(jnp.float32).reshape(ATTN_KV_HEADS, ATTN_GROUP)[None, None, :, :, None, None]
    m = jnp.maximum(jnp.max(logits, axis=-1, keepdims=True), sink_l)
    p = jnp.exp(logits - m)
    probs = p / (jnp.sum(p, axis=-1, keepdims=True) + jnp.exp(sink_l - m))
    out = jnp.einsum('bnhgqk,bnkhd->bnqhgd', probs.astype(vb.dtype), vb)
    return out.reshape(b, l, ATTN_WIDTH)


def short_conv(x, w):
    c = x.shape[-1]
    return lax.conv_general_dilated(x, w[:, None, :], window_strides=(1,),
                                    padding=[(CONV_PAD, CONV_PAD)],
                                    dimension_numbers=('NWC', 'WIO', 'NWC'),
                                    feature_group_count=c)


def gated_delta_chunked(q, k, v, g, beta):
    b, h, l, dk = q.shape
    dv = v.shape[-1]
    nc = l // CHUNK
    q = q.reshape(b, h, nc, CHUNK, dk)
    k = k.reshape(b, h, nc, CHUNK, dk)
    v = v.reshape(b, h, nc, CHUNK, dv)
    gc = jnp.cumsum(g.reshape(b, h, nc, CHUNK), axis=-1)
    beta = beta.reshape(b, h, nc, CHUNK)
    idx = jnp.arange(CHUNK)
    incl = idx[:, None] >= idx[None, :]
    strict = idx[:, None] > idx[None, :]
    diff = gc[..., :, None] - gc[..., None, :]
    decay = jnp.where(incl, jnp.exp(jnp.where(incl, diff, 0.0)), 0.0)
    k_beta = k * beta[..., None]
    m = jnp.where(strict, jnp.einsum('bhnid,bhnjd->bhnij', k_beta, k) * decay, 0.0)
    a = m + jnp.eye(CHUNK, dtype=m.dtype)
    rhs = jnp.concatenate([v * beta[..., None], k_beta * jnp.exp(gc)[..., None]], axis=-1)
    sol = lax.linalg.triangular_solve(a, rhs, left_side=True, lower=True, unit_diagonal=True)
    u, w = sol[..., :dv], sol[..., dv:]
    attn = jnp.where(incl, jnp.einsum('bhnid,bhnjd->bhnij', q, k) * decay, 0.0)
    q_dec = q * jnp.exp(gc)[..., None]
    k_dec = k * jnp.exp(gc[..., -1:] - gc)[..., None]
    g_last = jnp.exp(gc[..., -1])

    def step(state, xs):
        q_c, k_c, u_c, w_c, attn_c, gl_c = xs
        v_new = u_c - jnp.einsum('bhcd,bhde->bhce', w_c, state)
        o_c = (jnp.einsum('bhcd,bhde->bhce', q_c, state)
               + jnp.einsum('bhij,bhje->bhie', attn_c, v_new))
        state = state * gl_c[..., None, None] + jnp.einsum('bhcd,bhce->bhde', k_c, v_new)
        return state, o_c

    xs = tuple(jnp.moveaxis(t, 2, 0) for t in (q_dec, k_dec, u, w, attn, g_last))
    s0 = jnp.zeros((b, h, dk, dv), jnp.float32)
    _, o = lax.scan(step, s0, xs)
    return jnp.moveaxis(o, 0, 2).reshape(b, h, l, dv)


def hybrid_mixer(x, w_in, conv_w, sink, a_log, dt_bias, norm_gain, w_out):
    b, l, _ = x.shape
    proj = x @ w_in
    aq, ak, av, dq, dk_, dv_, z, bb, aa = jnp.split(proj, SPLIT_POINTS, axis=-1)
    o_attn = windowed_gqa_attention(aq, ak, av, sink)
    qkv = jax.nn.silu(short_conv(jnp.concatenate([dq, dk_, dv_], axis=-1), conv_w))
    dq, dk_, dv_ = jnp.split(qkv, 3, axis=-1)

    def heads(t):
        return t.reshape(b, l, DN_HEADS, DN_HEAD_DIM).transpose(0, 2, 1, 3).astype(jnp.float32)

    q = l2norm(heads(dq)) * (DN_HEAD_DIM ** -0.5)
    k = l2norm(heads(dk_))
    v = heads(dv_)
    beta = jax.nn.sigmoid(bb.astype(jnp.float32)).reshape(b, l, 2, DN_HEADS).transpose(2, 0, 3, 1)
    a_in = aa.astype(jnp.float32).reshape(b, l, 2, DN_HEADS).transpose(2, 0, 3, 1)
    g = -jnp.exp(a_log.astype(jnp.float32))[:, None, :, None] * jax.nn.softplus(
        a_in + dt_bias.astype(jnp.float32)[:, None, :, None])
    o_fwd = gated_delta_chunked(q, k, v, g[0], beta[0])
    o_bwd = jnp.flip(gated_delta_chunked(jnp.flip(q, 2), jnp.flip(k, 2), jnp.flip(v, 2),
                                         jnp.flip(g[1], 2), jnp.flip(beta[1], 2)), 2)
    o = (o_fwd + o_bwd).transpose(0, 2, 1, 3)
    zg = z.astype(jnp.float32).reshape(b, l, DN_HEADS, DN_HEAD_DIM)
    o = (o * lax.rsqrt(jnp.mean(o * o, axis=-1, keepdims=True) + RMS_EPS)
         * norm_gain.astype(jnp.float32) * jax.nn.silu(zg))
    o_dn = o.reshape(b, l, DN_WIDTH).astype(x.dtype)
    return jnp.concatenate([o_attn, o_dn], axis=-1) @ w_out


def setup_inputs(seed: int = 0) -> dict:
    key = jax.random.key(seed)
    ks = jax.random.split(key, 16)
    f32 = jnp.float32
    x_prompt = jax.random.normal(ks[0], (BATCH, SEQ, D_MODEL), f32)
    x_sample = jax.random.normal(ks[1], (DEC_BATCH, DEC_SEQ, D_MODEL), f32)
    ffn1_w_in = jax.random.normal(ks[2], (DEPTH, D_MODEL, 2 * D_FF), f32) * (D_MODEL ** -0.5) * BETA_INIT
    ffn1_w_out = jax.random.normal(ks[3], (DEPTH, D_FF, D_MODEL), f32) * (D_FF ** -0.5) * BETA_INIT
    col_scale = jnp.concatenate([
        jnp.ones((OFF_AV,), f32), jnp.full((KV_DIM,), BETA_INIT, f32),
        jnp.ones((OFF_DV - OFF_DQ,), f32), jnp.full((DN_WIDTH,), BETA_INIT, f32),
        jnp.ones((PROJ_DIM - OFF_Z,), f32)])
    w_in = jax.random.normal(ks[4], (DEPTH, D_MODEL, PROJ_DIM), f32) * (D_MODEL ** -0.5) * col_scale
    conv_w = jax.random.normal(ks[5], (DEPTH, CONV_WIDTH, 3 * DN_WIDTH), f32) * (CONV_WIDTH ** -0.5)
    attn_sink = jax.random.normal(ks[6], (DEPTH, ATTN_HEADS), f32) * 0.5
    dn_a_log = jnp.log(jax.random.uniform(ks[7], (DEPTH, 2, DN_HEADS), f32, 1.0, 16.0))
    dt = jnp.exp(jax.random.uniform(ks[8], (DEPTH, 2, DN_HEADS), f32, math.log(1e-3), math.log(1e-1)))
    dn_dt_bias = dt + jnp.log(-jnp.expm1(-dt))
    dn_norm_gain = 1.0 + 0.02 * jax.random.normal(ks[9], (DEPTH, DN_HEAD_DIM), f32)
    w_out = jax.random.normal(ks[10], (DEPTH, MIX_WIDTH, D_MODEL), f32) * (MIX_WIDTH ** -0.5) * BETA_INIT
    ffn2_w_in = jax.random.normal(ks[11], (DEPTH, D_MODEL, 2 * D_FF), f32) * (D_MODEL ** -0.5) * BETA_INIT
    ffn2_w_out = jax.random.normal(ks[12], (DEPTH, D_FF, D_MODEL), f32) * (D_FF ** -0.5) * BETA_INIT
    ln_gain = 1.0 + 0.02 * jax.random.normal(ks[13], (DEPTH, 3, D_MODEL), f32)
    ln_bias = 0.02 * jax.random.normal(ks[14], (DEPTH, 3, D_MODEL), f32)
    return {"x_prompt": x_prompt, "x_sample": x_sample,
            "ffn1_w_in": ffn1_w_in, "ffn1_w_out": ffn1_w_out,
            "w_in": w_in, "conv_w": conv_w, "attn_sink": attn_sink,
            "dn_a_log": dn_a_log, "dn_dt_bias": dn_dt_bias, "dn_norm_gain": dn_norm_gain,
            "w_out": w_out, "ffn2_w_in": ffn2_w_in, "ffn2_w_out": ffn2_w_out,
            "ln_gain": ln_gain, "ln_bias": ln_bias}


def reference(x_prompt, x_sample, ffn1_w_in, ffn1_w_out, w_in, conv_w, attn_sink,
              dn_a_log, dn_dt_bias, dn_norm_gain, w_out, ffn2_w_in, ffn2_w_out,
              ln_gain, ln_bias):
    def trunk(x):
        for i in range(DEPTH):
            x = layer_norm(ALPHA * x + 0.5 * swiglu(x, ffn1_w_in[i], ffn1_w_out[i]),
                           ln_gain[i, 0], ln_bias[i, 0])
            x = layer_norm(ALPHA * x + hybrid_mixer(x, w_in[i], conv_w[i], attn_sink[i],
                                                   dn_a_log[i], dn_dt_bias[i],
                                                   dn_norm_gain[i], w_out[i]),
                           ln_gain[i, 1], ln_bias[i, 1])
            x = layer_norm(ALPHA * x + 0.5 * swiglu(x, ffn2_w_in[i], ffn2_w_out[i]),
                           ln_gain[i, 2], ln_bias[i, 2])
        return x

    y_prompt = trunk(x_prompt)
    y_sample = trunk(x_sample)
    return (y_prompt, y_sample)
```

```python
from contextlib import ExitStack
import numpy as np
import ml_dtypes
import concourse.bass as bass
import concourse.mybir as mybir
from concourse.bass_utils import run_bass_kernel_spmd

F32 = mybir.dt.float32
BF16 = mybir.dt.bfloat16
AF = mybir.ActivationFunctionType
ALU = mybir.AluOpType
AX = mybir.AxisListType

D = 1024
DFF = 2816
NCH = DFF // 128
PROJ = 2832
ALPHA = 2.0 ** 0.25
LN_EPS = 1e-5
RMS_EPS = 1e-6
NEG = -1.0e6
N_CORES = 8
L_S = 8192
L_P = 16384


class Buf:
    __slots__ = ("lw", "rd", "excl", "swt", "srt")

    def __init__(self):
        self.lw = None
        self.rd = []
        self.excl = False
        self.swt = 0.0
        self.srt = 0.0


DMA_K = {"sp": 8, "pool": 4, "act": 4}
COMPUTE = ("pe", "act", "dve", "pool")


class Sched:
    def __init__(self, nc, ctx):
        self.nc = nc
        self.ops = []
        self.cur = None
        self.eng_t = {}
        self.csem = {e: ctx.enter_context(nc.semaphore("c_" + e)) for e in COMPUTE}
        self.ccnt = {e: 0 for e in COMPUTE}
        self.dsem = {q: [ctx.enter_context(nc.semaphore(f"d_{q}{i}")) for i in range(k)]
                     for q, k in DMA_K.items()}
        self.dcnt = {q: 0 for q in DMA_K}
        self.bufs = []

    def buf(self):
        b = Buf()
        self.bufs.append(b)
        return b

    COST = {"pe": 230.0, "act": 450.0, "dve": 350.0, "pool": 2500.0, "sp": 100.0}

    def op(self, eng, fn, r=(), w=(), dma=False, c=None):
        o = (eng, fn, tuple(r), tuple(w), dma, c)
        if self.cur is None:
            self._place(o)
        else:
            self.cur.append(o)

    def _est(self, o):
        eng, fn, r, w, dma, c = o
        t = self.eng_t.get(eng, 0.0)
        for b in r:
            if b.swt + 200.0 > t:
                t = b.swt + 200.0
        for b in w:
            m = max(b.swt, b.srt) + 200.0
            if m > t:
                t = m
        return t

    def _place(self, o):
        eng, fn, r, w, dma, c = o
        t = self._est(o)
        if dma:
            self.eng_t[eng] = t + 100.0
            fin = t + (c if c is not None else 4000.0)
        else:
            fin = t + (c if c is not None else self.COST[eng])
            self.eng_t[eng] = fin
        for b in r:
            if fin > b.srt:
                b.srt = fin
        for b in w:
            b.swt = fin
            b.srt = 0.0
        self.ops.append((eng, fn, r, w, dma))

    def begin(self):
        self.cur = []

    def end(self):
        c = self.cur
        self.cur = None
        return c

    def merge(self, lists):
        lists = [l for l in lists if l]
        ptr = [0] * len(lists)
        while True:
            best = None
            for k, l in enumerate(lists):
                if ptr[k] < len(l):
                    t = self._est(l[ptr[k]])
                    key = (t, ptr[k] / len(l))
                    if best is None or key < best[0]:
                        best = (key, k)
            if best is None:
                break
            k = best[1]
            self._place(lists[k][ptr[k]])
            ptr[k] += 1

    def flush(self):
        nc = self.nc
        ops = self.ops
        n = len(ops)
        deps = [None] * n
        for i, (eng, fn, r, w, dma) in enumerate(ops):
            d = set()
            for b in r:
                if b.lw is not None:
                    d.add(b.lw)
                if b.excl:
                    for q in b.rd:
                        if ops[q][0] != eng:
                            d.add(q)
            for b in w:
                if b.lw is not None:
                    d.add(b.lw)
                d.update(b.rd)
            d.discard(i)
            for b in r:
                b.rd.append(i)
            for b in w:
                b.lw = i
                b.rd = []
            deps[i] = d
        need_inc = [False] * n
        for i in range(n):
            eng, _, _, _, dma = ops[i]
            keep = []
            for p in deps[i]:
                pe, _, _, _, pdma = ops[p]
                if (not dma) and (not pdma) and pe == eng == "pe":
                    continue
                keep.append(p)
                if not pdma:
                    need_inc[p] = True
            deps[i] = keep
        target = [None] * n
        dma_prev = [None] * n
        per_eng = {e: [] for e in ("pe", "act", "dve", "pool", "sp")}
        for i in range(n):
            eng, _, _, _, dma = ops[i]
            per_eng[eng].append(i)
            if dma:
                j = self.dcnt[eng]
                k = DMA_K[eng]
                self.dcnt[eng] = j + 1
                target[i] = (self.dsem[eng][j % k], 16 * (j // k + 1))
                if j >= k:
                    dma_prev[i] = (self.dsem[eng][j % k], 16 * (j // k))
            elif need_inc[i]:
                self.ccnt[eng] += 1
                target[i] = (self.csem[eng], self.ccnt[eng])
        final = {}
        for i in range(n):
            if target[i] is not None:
                s, v = target[i]
                final[id(s)] = (s, max(v, final.get(id(s), (s, 0))[1]))

        def emit(ename, e):
            waited = {}

            def wait(s, v):
                if waited.get(id(s), 0) >= v:
                    return
                waited[id(s)] = v
                e.wait_ge(s, v)

            for i in per_eng[ename]:
                eng, fn, _, _, dma = ops[i]
                if dma_prev[i] is not None:
                    wait(*dma_prev[i])
                for p in deps[i]:
                    wait(*target[p])
                ins = fn(e)
                if target[i] is not None:
                    s, v = target[i]
                    ins.then_inc(s, 16 if dma else 1)
            for s, v in final.values():
                wait(s, v)

        with nc.Block() as block:
            @block.sync
            def _(e):
                emit("sp", e)

            @block.tensor
            def _(e):
                emit("pe", e)

            @block.scalar
            def _(e):
                emit("act", e)

            @block.vector
            def _(e):
                emit("dve", e)

            @block.gpsimd
            def _(e):
                emit("pool", e)
        self.ops = []
        self.eng_t = {}
        for b in self.bufs:
            b.lw = None
            b.rd = []
            b.swt = 0.0
            b.srt = 0.0


class Pool:
    uid = 0

    def __init__(self, S, ctx, name, shape, dtype, n=1, psum=False):
        nc = S.nc
        self.t = []
        for i in range(n):
            alloc = nc.psum_tensor if psum else nc.sbuf_tensor
            Pool.uid += 1
            h = ctx.enter_context(alloc(f"{name}_{i}_{Pool.uid}", list(shape), dtype))
            bb = S.buf()
            bb.excl = psum
            self.t.append((h, bb))
        self.i = 0
        self.S = S

    def get(self):
        r = self.t[self.i % len(self.t)]
        self.i += 1
        return r


class SubPool:
    def __init__(self, items):
        self.t = list(items)
        self.i = 0

    def get(self):
        r = self.t[self.i % len(self.t)]
        self.i += 1
        return r


def _consts():
    c = {}
    c["ident"] = np.eye(128, dtype=np.float32)
    c["ones"] = np.ones((128, 128), np.float32)
    tk = np.arange(128)[:, None]
    tq = np.arange(128)[None, :]
    ab = np.zeros((128, 3, 2, 4, 128), np.float32)
    for kb in range(3):
        dist = np.abs(tq - tk - (kb - 1) * 128)
        for kvh in range(2):
            for g in range(4):
                h = kvh * 4 + g
                slope = 2.0 ** (-8.0 * (h + 1) / 8.0)
                ab[:, kb, kvh, g, :] = np.where(dist <= 128, -slope * dist, NEG)
    c["abias"] = ab.reshape(128, 3 * 2 * 512)
    a = np.arange(128)
    same = (a[:, None] // 64) == (a[None, :] // 64)
    dm = np.zeros((128, 2, 5, 128), np.float32)
    for d in range(2):
        if d == 0:
            le = a[:, None] <= a[None, :]
            lt = a[:, None] < a[None, :]
        else:
            le = a[:, None] >= a[None, :]
            lt = a[:, None] > a[None, :]
        dm[:, d, 0, :] = (same & le)
        dm[:, d, 1, :] = same
        dm[:, d, 2, :] = np.where(same & le, 0.0, NEG)
        dm[:, d, 3, :] = (same & lt)
        dm[:, d, 4, :] = np.eye(128)
    c["dmask"] = dm.reshape(128, 2 * 5 * 128)
    return c


import os
STOP = int(os.environ.get("KSTOP", "99"))
KSUB = int(os.environ.get("KSUB", "0"))
KDBG = int(os.environ.get("KDBG", "0"))


class _Stop(Exception):
    pass


def build_nc(seqs):
    nc = bass.Bass("TRN2", target_bir_lowering=False)
    try:
        _build(nc, seqs)
    except _Stop:
        pass
    return nc


def _build(nc, seqs):
    ctx = ExitStack()
    with ctx:
        S = Sched(nc, ctx)

        def chk(n):
            if KSUB == n:
                S.flush()
                raise _Stop()

        def dram(name, shape, dt, kind):
            return nc.dram_tensor(name, list(shape), dt, kind=kind)

        xin = {nm: dram("x_" + nm, [L, D], F32, "ExternalInput") for nm, L, rot in seqs}
        yout = {nm: dram("y_" + nm, [(L // 8) if rot else L, D], F32, "ExternalOutput") for nm, L, rot in seqs}
        w_flag = dram("wflag", [1, 8], F32, "ExternalInput")
        w_f1i = dram("ffn1_w_in", [D, 2 * DFF], F32, "ExternalInput")
        w_f1o = dram("ffn1_w_out", [DFF, D], F32, "ExternalInput")
        w_in = dram("w_in", [D, PROJ], F32, "ExternalInput")
        w_cv = dram("conv_w", [5, 1536], F32, "ExternalInput")
        w_sink = dram("attn_sink", [1, 8], F32, "ExternalInput")
        w_alog = dram("dn_a_log", [1, 8], F32, "ExternalInput")
        w_dtb = dram("dn_dt_bias", [1, 8], F32, "ExternalInput")
        w_ng = dram("dn_norm_gain", [1, 128], F32, "ExternalInput")
        w_out = dram("w_out", [D, D], F32, "ExternalInput")
        w_f2i = dram("ffn2_w_in", [D, 2 * DFF], F32, "ExternalInput")
        w_f2o = dram("ffn2_w_out", [DFF, D], F32, "ExternalInput")
        w_lng = dram("ln_gain", [1, 3 * D], F32, "ExternalInput")
        w_lnb = dram("ln_bias", [1, 3 * D], F32, "ExternalInput")
        c_ident = dram("ident", [128, 128], F32, "ExternalInput")
        c_ones = dram("ones", [128, 128], F32, "ExternalInput")
        c_abias = dram("abias", [128, 3072], F32, "ExternalInput")
        c_dmask = dram("dmask", [128, 1280], F32, "ExternalInput")

        LM = max(L for _, L, _r in seqs)
        s_f1i = dram("s_f1i", [11, 128, 4096], BF16, "Internal")
        s_f1o = dram("s_f1o", [DFF, D], BF16, "Internal")
        s_win = dram("s_win", [D, PROJ], BF16, "Internal")
        s_wout = dram("s_wout", [D, D], BF16, "Internal")
        s_f2i = dram("s_f2i", [11, 128, 4096], BF16, "Internal")
        s_f2o = dram("s_f2o", [DFF, D], BF16, "Internal")
        DK = "ExternalOutput" if KDBG else "Internal"
        X1 = dram("X1", [LM, D], F32, DK)
        QT = dram("QT", [128, 4, LM], BF16, "Internal")
        KT = dram("KT", [128, LM], BF16, "Internal")
        VX = dram("VX", [LM, 130], BF16, "Internal")
        DT = dram("DT", [128, 12, LM], F32, "Internal")
        ZZ = dram("ZZ", [LM, 512], F32, "Internal")
        BA = dram("BA", [LM, 16], F32, "Internal")
        OA = dram("OA", [LM, 512], F32, DK)
        ODN = [dram(f"ODN{d}", [LM, 512], F32, DK) for d in range(2)]

        def bc(t, off, n):
            return bass.AP(t, off, [[0, 128], [1, n]])

        PS = Pool(S, ctx, "ps", [128, 512], F32, n=8, psum=True)
        cps = [PS]
        ident, identb = Pool(S, ctx, "ident", [128, 128], F32).get()
        ones, onesb = Pool(S, ctx, "ones", [128, 128], F32).get()
        wf, wfb = Pool(S, ctx, "wf", [128, 8], F32).get()
        S.op("sp", lambda e: e.dma_start(out=wf[:], in_=bc(w_flag, 0, 8)), w=[wfb], dma=True)
        lnc = {}

        def load_ln(cx, lis):
            for li in lis:
                g, gb = Pool(S, cx, "lng", [128, D], F32).get()
                b, bb = Pool(S, cx, "lnb", [128, D], F32).get()
                S.op("sp", lambda e, g=g, li=li: e.dma_start(out=g[:], in_=bc(w_lng, li * D, D)), w=[gb], dma=True)
                S.op("sp", lambda e, b=b, li=li: e.dma_start(out=b[:], in_=bc(w_lnb, li * D, D)), w=[bb], dma=True)
                lnc[li] = (g, gb, b, bb)
        S.op("sp", lambda e: e.dma_start(out=ident[:], in_=c_ident.ap()), w=[identb], dma=True)
        S.op("sp", lambda e: e.dma_start(out=ones[:], in_=c_ones.ap()), w=[onesb], dma=True)

        with ExitStack() as c0:
            STG = Pool(S, c0, "stg", [128, 2048], F32, n=3)
            STB = Pool(S, c0, "stb", [128, 2048], BF16, n=3)
            rr = [0]
            def cast_op(a, ab_, b, bb_, cw):
                k = rr[0] % 3
                rr[0] += 1
                if k == 0:
                    S.op("dve", lambda e: e.tensor_copy(out=b[:, 0:cw], in_=a[:, 0:cw]), r=[ab_], w=[bb_])
                elif k == 1:
                    S.op("act", lambda e: e.copy(out=b[:, 0:cw], in_=a[:, 0:cw]), r=[ab_], w=[bb_])
                else:
                    S.op("pool", lambda e: e.tensor_copy(out=b[:, 0:cw], in_=a[:, 0:cw]), r=[ab_], w=[bb_])

            def conv_ffn_in(src, dst):
                d5 = dst.ap().rearrange("j p (k u c) -> j p k u c", k=8, u=2, c=256)
                for k in range(8):
                    for u in range(2):
                        for jj in range(0, 11, 4):
                            ng = min(4, 11 - jj)
                            a, ab_ = STG.get()
                            b, bb_ = STB.get()
                            S.op("sp", lambda e, a=a, k=k, u=u, jj=jj, ng=ng: e.dma_start(
                                out=a[:, 0:ng * 256], in_=src.ap()[k * 128:(k + 1) * 128, u * DFF + jj * 256:u * DFF + (jj + ng) * 256]),
                                w=[ab_], dma=True)
                            cast_op(a, ab_, b, bb_, ng * 256)
                            S.op("pool", lambda e, b=b, k=k, u=u, jj=jj, ng=ng: e.dma_start(
                                out=d5[jj:jj + ng, :, k, u, :].rearrange("j p c -> p j c"),
                                in_=b[:, 0:ng * 256].rearrange("p (j c) -> p j c", c=256)), r=[bb_], dma=True)

            conv_ffn_in(w_f1i, s_f1i)
            conv_ffn_in(w_f2i, s_f2i)
            for src, dst, R, C in ((w_f1o, s_f1o, DFF, D),
                                   (w_in, s_win, D, PROJ), (w_out, s_wout, D, D),
                                   (w_f2o, s_f2o, DFF, D)):
                for r0 in range(0, R, 128):
                    for c0_ in range(0, C, 2048):
                        cw = min(2048, C - c0_)
                        a, ab_ = STG.get()
                        b, bb_ = STB.get()
                        S.op("sp", lambda e, a=a, r0=r0, c0_=c0_, cw=cw, src=src:
                             e.dma_start(out=a[:, 0:cw], in_=src.ap()[r0:r0 + 128, c0_:c0_ + cw]),
                             w=[ab_], dma=True)
                        k = rr[0] % 3
                        rr[0] += 1
                        if k == 0:
                            S.op("dve", lambda e, a=a, b=b, cw=cw: e.tensor_copy(out=b[:, 0:cw], in_=a[:, 0:cw]),
                                 r=[ab_], w=[bb_])
                        elif k == 1:
                            S.op("act", lambda e, a=a, b=b, cw=cw: e.copy(out=b[:, 0:cw], in_=a[:, 0:cw]),
                                 r=[ab_], w=[bb_])
                        else:
                            S.op("pool", lambda e, a=a, b=b, cw=cw: e.tensor_copy(out=b[:, 0:cw], in_=a[:, 0:cw]),
                                 r=[ab_], w=[bb_])
                        S.op("pool", lambda e, b=b, r0=r0, c0_=c0_, cw=cw, dst=dst:
                             e.dma_start(out=dst.ap()[r0:r0 + 128, c0_:c0_ + cw], in_=b[:, 0:cw]),
                             r=[bb_], dma=True)
            S.flush()
        if STOP == 0:
            return nc

        def transpose_tok(src, srcb, dst, dstb, nsub, rot=[0]):
            for k in range(8):
                p, pb = cps[0].get()
                for s in range(nsub):
                    S.op("pe", lambda e, p=p, s=s, k=k: e.transpose(
                        out=p[:, s * 128:(s + 1) * 128], in_=src[:, s, k * 128:(k + 1) * 128], identity=ident[:]),
                        r=[srcb, identb], w=[pb])
                rot[0] += 1
                if rot[0] % 2:
                    S.op("dve", lambda e, p=p, k=k: e.tensor_copy(out=dst[:, k, 0:nsub * 128], in_=p[:, 0:nsub * 128]),
                         r=[pb], w=[dstb])
                else:
                    S.op("act", lambda e, p=p, k=k: e.copy(out=dst[:, k, 0:nsub * 128], in_=p[:, 0:nsub * 128]),
                         r=[pb], w=[dstb])

        def layer_norm(y, yb, li, pools, nsub=4):
            ST, MV = pools
            for s in range(nsub):
                st, stb = ST.get()
                mv, mvb = MV.get()
                S.op("dve", lambda e, st=st, s=s: e.bn_stats(out=st[:, 0:6], in_=y[:, s, 0:512]), r=[yb], w=[stb])
                S.op("dve", lambda e, st=st, s=s: e.bn_stats(out=st[:, 6:12], in_=y[:, s, 512:1024]), r=[yb], w=[stb])
                S.op("dve", lambda e, st=st, mv=mv: e.bn_aggr(out=mv[:, 0:2], in_=st[:, 0:12]), r=[stb], w=[mvb])
                S.op("act", lambda e, mv=mv: e.activation(out=mv[:, 2:3], in_=mv[:, 1:2], func=AF.Ln, bias=LN_EPS, scale=1.0),
                     r=[mvb], w=[mvb])
                S.op("act", lambda e, mv=mv: e.activation(out=mv[:, 3:4], in_=mv[:, 2:3], func=AF.Exp, scale=-0.5),
                     r=[mvb], w=[mvb])
                S.op("dve", lambda e, mv=mv: e.scalar_tensor_tensor(out=mv[:, 4:5], in0=mv[:, 0:1], scalar=-1.0, in1=mv[:, 3:4],
                                                                    op0=ALU.mult, op1=ALU.mult), r=[mvb], w=[mvb])
                S.op("act", lambda e, mv=mv, s=s: e.activation(out=y[:, s, :], in_=y[:, s, :], func=AF.Identity,
                                                              bias=mv[:, 4:5], scale=mv[:, 3:4]), r=[mvb, yb], w=[yb], c=1500.0)
                lg, lgb, lb_, lbb = lnc[li]
                S.op("pool", lambda e, s=s, lg=lg: e.tensor_tensor(out=y[:, s, :], in0=y[:, s, :], in1=lg[:], op=ALU.mult), r=[yb, lgb], w=[yb], c=9400.0)
                S.op("pool", lambda e, s=s, lb_=lb_: e.tensor_tensor(out=y[:, s, :], in0=y[:, s, :], in1=lb_[:], op=ALU.add), r=[yb, lbb], w=[yb], c=9400.0)

        def ffn(xT, xTb, xa, xab, wsc, wo, wob, GT, WG, SG):
            gT, gTb = GT.get()
            for j in range(11):
                wg, wgb = WG.get()
                S.op("sp" if j % 2 == 0 else "act", lambda e, wg=wg, j=j: e.dma_start(
                    out=wg[:].rearrange("p k u c -> p (k u c)"), in_=wsc.ap()[j]), w=[wgb], dma=True)
                for hf in range(2):
                    c = 2 * j + hf
                    pg, pgb = cps[0].get()
                    pu, pub = cps[0].get()
                    for k in range(8):
                        S.op("pe", lambda e, pg=pg, wg=wg, k=k, hf=hf: e.matmul(
                            pg[:, :], lhsT=wg[:, k, 0, hf * 128:(hf + 1) * 128], rhs=xT[:, k, :], start=(k == 0), stop=(k == 7)),
                            r=[wgb, xTb], w=[pgb])
                    for k in range(8):
                        S.op("pe", lambda e, pu=pu, wg=wg, k=k, hf=hf: e.matmul(
                            pu[:, :], lhsT=wg[:, k, 1, hf * 128:(hf + 1) * 128], rhs=xT[:, k, :], start=(k == 0), stop=(k == 7)),
                            r=[wgb, xTb], w=[pub])
                    sg, sgb = SG.get()
                    S.op("act", lambda e, sg=sg, pg=pg: e.activation(out=sg[:, :], in_=pg[:, :], func=AF.Silu), r=[pgb], w=[sgb])
                    S.op("dve", lambda e, sg=sg, pu=pu, c=c: e.tensor_tensor(out=gT[:, c, :], in0=sg[:, :], in1=pu[:, :], op=ALU.mult),
                         r=[sgb, pub], w=[gTb])
            for s in range(4):
                for nh in range(2):
                    po, pob = cps[0].get()
                    for c in range(NCH):
                        S.op("pe", lambda e, po=po, c=c, s=s, nh=nh: e.matmul(
                            po[:, :], lhsT=gT[:, c, s * 128:(s + 1) * 128], rhs=wo[:, c, nh * 512:(nh + 1) * 512],
                            start=(c == 0), stop=(c == NCH - 1)), r=[gTb, wob], w=[pob])
                    S.op("dve", lambda e, po=po, s=s, nh=nh: e.scalar_tensor_tensor(
                        out=xa[:, s, nh * 512:(nh + 1) * 512], in0=po[:, :], scalar=0.5, in1=xa[:, s, nh * 512:(nh + 1) * 512],
                        op0=ALU.mult, op1=ALU.add), r=[pob, xab], w=[xab])

        evr = [0]

        def evac(dst_fn, p, pb, wbufs, rbufs=()):
            evr[0] += 1
            if evr[0] % 2:
                S.op("dve", lambda e: e.tensor_copy(out=dst_fn(), in_=p()), r=[pb, *rbufs], w=wbufs)
            else:
                S.op("act", lambda e: e.copy(out=dst_fn(), in_=p()), r=[pb, *rbufs], w=wbufs)

        for nm, L, rot in seqs:
            OWN = (L // 8) if rot else L
            OWNB = OWN // 128
            x_d, y_d = xin[nm], yout[nm]
            NT = L // 512
            NB = L // 128
            with ExitStack() as c1:
                wo, wob = Pool(S, c1, "wo1", [128, NCH, D], BF16).get()
                wi, wib = Pool(S, c1, "wi", [128, 8, PROJ], BF16).get()
                S.op("sp", lambda e: e.dma_start(out=wo[:], in_=s_f1o.ap().rearrange("(c p) d -> p c d", p=128)), w=[wob], dma=True)
                S.op("sp", lambda e: e.dma_start(out=wi[:], in_=s_win.ap().rearrange("(k p) c -> p k c", p=128)), w=[wib], dma=True)
                load_ln(c1, [0])
                XT = Pool(S, c1, "xt", [128, 4, D], F32, n=1)
                XTT = Pool(S, c1, "xtt", [128, 8, 512], BF16, n=1)
                X1TT = Pool(S, c1, "x1tt", [128, 8, 512], BF16, n=2)
                PSA1 = SubPool(PS.t[0:4])
                PSB1 = SubPool(PS.t[4:8])
                pend1 = [None]
                GT = Pool(S, c1, "gt", [128, NCH, 512], BF16, n=1)
                WG = Pool(S, c1, "wg", [128, 8, 2, 256], BF16, n=2)
                SG = Pool(S, c1, "sg", [128, 512], BF16, n=2)
                ST = Pool(S, c1, "st", [128, 12], F32, n=2)
                MV = Pool(S, c1, "mv", [128, 8], F32, n=2)
                QTT = Pool(S, c1, "qtt", [128, 4, 512], BF16, n=1)
                KTT = Pool(S, c1, "ktt", [128, 512], BF16, n=1)
                DTT = Pool(S, c1, "dtt", [128, 512], F32, n=3)
                VXT = Pool(S, c1, "vxt", [128, 4, 130], BF16, n=1)
                ZT = Pool(S, c1, "zt", [128, 4, 512], F32, n=1)
                BAT = Pool(S, c1, "bat", [128, 4, 16], F32, n=1)
                for ti in range(NT):
                    t0 = ti * 512
                    S.begin()
                    cps[0] = PSA1
                    xt, xtb = XT.get()
                    S.op("sp", lambda e, xt=xt, t0=t0: e.dma_start(
                        out=xt[:], in_=x_d.ap()[t0:t0 + 512, :].rearrange("(s p) d -> p s d", p=128)), w=[xtb], dma=True)
                    xT, xTb = XTT.get()
                    transpose_tok(xt, xtb, xT, xTb, 4)
                    S.op("act", lambda e, xt=xt: e.mul(out=xt[:], in_=xt[:], mul=ALPHA), r=[xtb], w=[xtb])
                    ffn(xT, xTb, xt, xtb, s_f1i, wo, wob, GT, WG, SG)
                    layer_norm(xt, xtb, 0, (ST, MV))
                    S.op("pool", lambda e, xt=xt, t0=t0: e.dma_start(
                        out=X1.ap()[t0:t0 + 512, :].rearrange("(s p) d -> p s d", p=128), in_=xt[:]), r=[xtb], dma=True)
                    x1T, x1Tb = X1TT.get()
                    transpose_tok(xt, xtb, x1T, x1Tb, 4)
                    la = S.end()
                    S.begin()
                    cps[0] = PSB1
                    qtt, qttb = QTT.get()
                    for c in range(4):
                        p, pb = cps[0].get()
                        for k in range(8):
                            S.op("pe", lambda e, p=p, k=k, c=c, x1T=x1T: e.matmul(
                                p[:, :], lhsT=wi[:, k, c * 128:(c + 1) * 128],
                                rhs=x1T[:, k, :], start=(k == 0), stop=(k == 7)), r=[wib, x1Tb], w=[pb])
                        evac(lambda qtt=qtt, c=c: qtt[:, c, :], lambda p=p: p[:, :], pb, [qttb])
                    S.op("pool", lambda e, qtt=qtt, t0=t0: e.dma_start(out=QT.ap()[:, :, t0:t0 + 512], in_=qtt[:]), r=[qttb], dma=True)
                    ktt, kttb = KTT.get()
                    p, pb = cps[0].get()
                    for k in range(8):
                        S.op("pe", lambda e, p=p, k=k, x1T=x1T: e.matmul(
                            p[:, :], lhsT=wi[:, k, 512:640], rhs=x1T[:, k, :], start=(k == 0), stop=(k == 7)), r=[wib, x1Tb], w=[pb])
                    evac(lambda ktt=ktt: ktt[:, :], lambda p=p: p[:, :], pb, [kttb])
                    S.op("pool", lambda e, ktt=ktt, t0=t0: e.dma_start(out=KT.ap()[:, t0:t0 + 512], in_=ktt[:]), r=[kttb], dma=True)
                    for c in range(12):
                        p, pb = cps[0].get()
                        for k in range(8):
                            S.op("pe", lambda e, p=p, k=k, c=c, x1T=x1T: e.matmul(
                                p[:, :], lhsT=wi[:, k, 768 + c * 128:768 + (c + 1) * 128], rhs=x1T[:, k, :],
                                start=(k == 0), stop=(k == 7)), r=[wib, x1Tb], w=[pb])
                        dtt, dttb = DTT.get()
                        evac(lambda dtt=dtt: dtt[:, :], lambda p=p: p[:, :], pb, [dttb])
                        S.op("sp", lambda e, dtt=dtt, c=c, t0=t0: e.dma_start(out=DT.ap()[:, c, t0:t0 + 512], in_=dtt[:]),
                             r=[dttb], dma=True)
                    vxt, vxtb = VXT.get()
                    zt, ztb = ZT.get()
                    bat, batb = BAT.get()
                    S.op("pool", lambda e, vxt=vxt: e.memset(vxt[:], 1.0), w=[vxtb])
                    for s in range(4):
                        p, pb = cps[0].get()
                        for k in range(8):
                            S.op("pe", lambda e, p=p, k=k, s=s, x1T=x1T: e.matmul(
                                p[:, 0:128], lhsT=x1T[:, k, s * 128:(s + 1) * 128], rhs=wi[:, k, 640:768],
                                start=(k == 0), stop=(k == 7)), r=[wib, x1Tb], w=[pb])
                        evac(lambda vxt=vxt, s=s: vxt[:, s, :].rearrange("p (h c) -> p h c", h=2)[:, :, 0:64],
                             lambda p=p: p[:, 0:128].rearrange("p (h c) -> p h c", h=2), pb, [vxtb])
                        p, pb = cps[0].get()
                        for k in range(8):
                            S.op("pe", lambda e, p=p, k=k, s=s, x1T=x1T: e.matmul(
                                p[:, :], lhsT=x1T[:, k, s * 128:(s + 1) * 128], rhs=wi[:, k, 2304:2816],
                                start=(k == 0), stop=(k == 7)), r=[wib, x1Tb], w=[pb])
                        evac(lambda zt=zt, s=s: zt[:, s, :], lambda p=p: p[:, :], pb, [ztb])
                        p, pb = cps[0].get()
                        for k in range(8):
                            S.op("pe", lambda e, p=p, k=k, s=s, x1T=x1T: e.matmul(
                                p[:, 0:16], lhsT=x1T[:, k, s * 128:(s + 1) * 128], rhs=wi[:, k, 2816:2832],
                                start=(k == 0), stop=(k == 7)), r=[wib, x1Tb], w=[pb])
                        evac(lambda bat=bat, s=s: bat[:, s, :], lambda p=p: p[:, 0:16], pb, [batb])
                    S.op("pool", lambda e, vxt=vxt, t0=t0: e.dma_start(
                        out=VX.ap()[t0:t0 + 512, :].rearrange("(s p) c -> p s c", p=128), in_=vxt[:]), r=[vxtb], dma=True)
                    S.op("pool", lambda e, zt=zt, t0=t0: e.dma_start(
                        out=ZZ.ap()[t0:t0 + 512, :].rearrange("(s p) c -> p s c", p=128), in_=zt[:]), r=[ztb], dma=True)
                    S.op("pool", lambda e, bat=bat, t0=t0: e.dma_start(
                        out=BA.ap()[t0:t0 + 512, :].rearrange("(s p) c -> p s c", p=128), in_=bat[:]), r=[batb], dma=True)
                    lb = S.end()
                    S.merge([la] + ([pend1[0]] if pend1[0] else []))
                    pend1[0] = lb
                S.merge([pend1[0]])
                cps[0] = PS
                S.flush()
            if STOP == 1:
                return nc

            with ExitStack() as c2:
                abias, abiasb = Pool(S, c2, "abias", [128, 3072], F32).get()
                esink, esinkb = Pool(S, c2, "esink", [128, 8], F32).get()
                S.op("sp", lambda e: e.dma_start(out=abias[:], in_=c_abias.ap()), w=[abiasb], dma=True)
                S.op("sp", lambda e: e.dma_start(out=esink[:], in_=bc(w_sink, 0, 8)), w=[esinkb], dma=True)
                S.op("act", lambda e: e.activation(out=esink[:], in_=esink[:], func=AF.Exp), r=[esinkb], w=[esinkb])
                QB = Pool(S, c2, "qb", [128, 4, 128], BF16, n=2)
                KB = Pool(S, c2, "kb", [128, 3, 128], BF16, n=2)
                VB = Pool(S, c2, "vb", [128, 3, 130], BF16, n=2)
                TB = Pool(S, c2, "tb", [128, 512], F32, n=2)
                PT = Pool(S, c2, "pt", [128, 512], BF16, n=6)
                DEN = Pool(S, c2, "den", [128, 16], F32, n=2)
                OB = Pool(S, c2, "ob", [128, 512], F32, n=2)
                for i in range(OWNB):
                    kbs = [kb for kb in range(3) if (rot or 0 <= i + kb - 1 < NB)]
                    lo, hi = kbs[0], kbs[-1] + 1
                    qb, qbb = QB.get()
                    kbt, kbb = KB.get()
                    vb, vbb = VB.get()
                    S.op("sp", lambda e, qb=qb, i=i: e.dma_start(out=qb[:], in_=QT.ap()[:, :, i * 128:(i + 1) * 128]), w=[qbb], dma=True)
                    if rot and (i == 0 or i == OWNB - 1):
                        for kb in range(3):
                            bi = (i + kb - 1) % NB
                            S.op("sp", lambda e, kbt=kbt, kb=kb, bi=bi: e.dma_start(
                                out=kbt[:, kb, :], in_=KT.ap()[:, bi * 128:(bi + 1) * 128]), w=[kbb], dma=True)
                            S.op("sp", lambda e, vb=vb, kb=kb, bi=bi: e.dma_start(
                                out=vb[:, kb, :], in_=VX.ap()[bi * 128:(bi + 1) * 128, :]), w=[vbb], dma=True)
                        hk, fi = (0, 7) if i == 0 else (2, 0)
                        S.op("act", lambda e, vb=vb, hk=hk, fi=fi: e.activation(
                            out=vb[:, hk, :], in_=vb[:, hk, :], func=AF.Copy, scale=wf[:, fi:fi + 1]), r=[vbb, wfb], w=[vbb])
                        lo, hi = 0, 0
                    if hi > lo:
                        S.op("sp", lambda e, kbt=kbt, i=i, lo=lo, hi=hi: e.dma_start(
                            out=kbt[:, lo:hi, :],
                            in_=KT.ap()[:, (i + lo - 1) * 128:(i + hi - 1) * 128].rearrange("p (b t) -> p b t", t=128)), w=[kbb], dma=True)
                        S.op("sp", lambda e, vb=vb, i=i, lo=lo, hi=hi: e.dma_start(
                            out=vb[:, lo:hi, :],
                            in_=VX.ap()[(i + lo - 1) * 128:(i + hi - 1) * 128, :].rearrange("(b p) c -> p b c", p=128)), w=[vbb], dma=True)
                    pos = []
                    for kvh in range(2):
                        po, pob = PS.get()
                        pos.append((po, pob))
                        b0 = kvh * 64
                        pts = []
                        for kb in kbs:
                            ps_, psb = PS.get()
                            S.op("pe", lambda e, ps_=ps_, kbt=kbt, qb=qb, kb=kb, b0=b0: e.matmul(
                                ps_[:, :], lhsT=kbt[b0:b0 + 64, kb, :], rhs=qb[b0:b0 + 64, :, :].rearrange("p c t -> p (c t)"), start=True, stop=True),
                                r=[kbb, qbb], w=[psb])
                            tb, tbb = TB.get()
                            off = (kb * 2 + kvh) * 512
                            S.op("dve", lambda e, tb=tb, ps_=ps_, off=off: e.scalar_tensor_tensor(
                                out=tb[:, :], in0=ps_[:, :], scalar=0.125, in1=abias[:, off:off + 512], op0=ALU.mult, op1=ALU.add),
                                r=[psb, abiasb], w=[tbb])
                            pt, ptb = PT.get()
                            S.op("act", lambda e, tb=tb, pt=pt: e.activation(out=pt[:, :], in_=tb[:, :], func=AF.Exp), r=[tbb], w=[ptb])
                            pts.append((kb, pt, ptb))
                        for g in range(4):
                            for kb, pt, ptb in pts:
                                S.op("pe", lambda e, po=po, pt=pt, vb=vb, g=g, kb=kb, kvh=kvh, st_=(kb == kbs[0]), sp_=(kb == kbs[-1]): e.matmul(
                                    po[:, g * 65:(g + 1) * 65], lhsT=pt[:, g * 128:(g + 1) * 128], rhs=vb[:, kb, kvh * 65:(kvh + 1) * 65],
                                    start=st_, stop=sp_), r=[ptb, vbb], w=[pob])
                    den, denb = DEN.get()
                    ob, obb = OB.get()
                    for kvh in range(2):
                        po, pob = pos[kvh]
                        S.op("dve", lambda e, den=den, po=po, kvh=kvh: e.tensor_tensor(
                            out=den[:, kvh * 4:(kvh + 1) * 4], in0=po[:, 0:260].rearrange("p (g c) -> p g c", c=65)[:, :, 64],
                            in1=esink[:, kvh * 4:(kvh + 1) * 4], op=ALU.add), r=[pob, esinkb], w=[denb])
                    S.op("dve", lambda e, den=den: e.reciprocal(out=den[:, 8:16], in_=den[:, 0:8]), r=[denb], w=[denb])
                    for kvh in range(2):
                        po, pob = pos[kvh]
                        for g in range(4):
                            h = kvh * 4 + g
                            S.op("dve", lambda e, ob=ob, po=po, den=den, g=g, h=h: e.tensor_scalar(
                                out=ob[:, h * 64:(h + 1) * 64], in0=po[:, g * 65:g * 65 + 64], scalar1=den[:, 8 + h:9 + h], scalar2=None,
                                op0=ALU.mult), r=[pob, denb], w=[obb])
                    S.op("pool", lambda e, ob=ob, i=i: e.dma_start(out=OA.ap()[i * 128:(i + 1) * 128, :], in_=ob[:]), r=[obb], dma=True)
                S.flush()
            if STOP == 2:
                return nc

            for dr in range(2):
                with ExitStack() as c3:
                    dmk, dmkb = Pool(S, c3, "dmk", [128, 5, 128], F32).get()
                    strict4, strict4b = Pool(S, c3, "strict4", [128, 4, 128], F32).get()
                    eye4, eye4b = Pool(S, c3, "eye4", [128, 4, 128], F32).get()
                    cw, cwb = Pool(S, c3, "cw", [128, 12, 5], F32).get()
                    gpar, gparb = Pool(S, c3, "gpar", [128, 8], F32).get()
                    S.op("sp", lambda e: e.dma_start(out=dmk[:], in_=c_dmask.ap()[:, dr * 640:(dr + 1) * 640].rearrange(
                        "p (m i) -> p m i", i=128)), w=[dmkb], dma=True)
                    for h in range(4):
                        S.op("sp", lambda e, h=h: e.dma_start(out=strict4[:, h, :], in_=c_dmask.ap()[:, dr * 640 + 384:dr * 640 + 512]),
                             w=[strict4b], dma=True)
                        S.op("sp", lambda e, h=h: e.dma_start(out=eye4[:, h, :], in_=c_ident.ap()), w=[eye4b], dma=True)
                    for c in range(12):
                        S.op("sp", lambda e, c=c: e.dma_start(out=cw[:, c, :], in_=w_cv.ap()[:, c * 128:(c + 1) * 128].rearrange("j p -> p j"),
                                                              allow_slow_non_contiguous=True), w=[cwb], dma=True)
                    S.op("sp", lambda e: e.dma_start(out=gpar[:, 0:4], in_=bc(w_alog, dr * 4, 4)), w=[gparb], dma=True)
                    S.op("sp", lambda e: e.dma_start(out=gpar[:, 4:8], in_=bc(w_dtb, dr * 4, 4)), w=[gparb], dma=True)
                    S.op("act", lambda e: e.activation(out=gpar[:, 0:4], in_=gpar[:, 0:4], func=AF.Exp), r=[gparb], w=[gparb])
                    S.op("dve", lambda e: e.tensor_scalar(out=gpar[:, 0:4], in0=gpar[:, 0:4], scalar1=-1.0, scalar2=None, op0=ALU.mult),
                         r=[gparb], w=[gparb])
                    U = lambda: dmk[:, 0, :]
                    BONES = lambda: dmk[:, 1, :]
                    MINC = lambda: dmk[:, 2, :]
                    PSA = SubPool(PS.t[0:3])
                    PSB = SubPool(PS.t[3:5])
                    PSC = SubPool(PS.t[5:8])
                    XIN = Pool(S, c3, "xin", [128, 12, 132], F32, n=2)
                    BAI = Pool(S, c3, "bai", [128, 16], F32, n=2)
                    CA = Pool(S, c3, "ca", [128, 12, 128], F32, n=1)
                    SL = Pool(S, c3, "sl", [128, 12, 128], F32, n=1)
                    TM = Pool(S, c3, "tm", [128, 12, 128], F32, n=1)
                    SS = Pool(S, c3, "ss", [128, 16], F32, n=2)
                    JK = Pool(S, c3, "jk", [128, 8, 128], F32, n=1)
                    NRM = Pool(S, c3, "nrm", [128, 8, 128], F32, n=1)
                    QKT = Pool(S, c3, "qkt", [128, 8, 128], BF16, n=1)
                    GU = Pool(S, c3, "gu", [128, 4, 128], F32, n=1)
                    DTMP = Pool(S, c3, "dtmp", [128, 4, 128], F32, n=1)
                    DCY = Pool(S, c3, "dcy", [128, 4, 128], F32, n=1)
                    GS = Pool(S, c3, "gs", [128, 32], F32, n=3)
                    EROW = Pool(S, c3, "erow", [128, 4, 128], F32, n=3)
                    ATT = Pool(S, c3, "att", [128, 4, 128], BF16, n=3)
                    KD = Pool(S, c3, "kd", [128, 4, 128], BF16, n=3)
                    QD = Pool(S, c3, "qd", [128, 4, 128], BF16, n=3)
                    XM = Pool(S, c3, "xm", [128, 4, 128], F32, n=2)
                    VBF = Pool(S, c3, "vbf", [128, 4, 128], BF16, n=2)
                    KG = Pool(S, c3, "kg", [128, 4, 128], BF16, n=2)
                    PK = Pool(S, c3, "pk", [128, 4, 128], BF16, n=2)
                    PKT = Pool(S, c3, "pkt", [128, 4, 128], BF16, n=2)
                    RF = Pool(S, c3, "rf", [128, 4, 128], F32, n=2)
                    RB = Pool(S, c3, "rbb", [128, 4, 128], BF16, n=2)
                    UB = Pool(S, c3, "ub", [128, 4, 128], F32, n=2)
                    WT = Pool(S, c3, "wt", [128, 4, 128], BF16, n=2)
                    VN = Pool(S, c3, "vn", [128, 4, 128], BF16, n=2)
                    OT = Pool(S, c3, "ot", [128, 4, 128], F32, n=2)
                    SF = Pool(S, c3, "sf", [128, 4, 128], F32, n=2)
                    SB_ = Pool(S, c3, "sbs", [128, 4, 128], BF16, n=2)
                    st8 = {}
                    st8["sf"], st8["sfb"] = SF.get()
                    st8["sb"], st8["sbb"] = SB_.get()
                    S.op("pool", lambda e: e.memset(st8["sf"][:], 0.0), w=[st8["sfb"]])
                    S.op("pool", lambda e: e.memset(st8["sb"][:], 0.0), w=[st8["sbb"]])
                    S.flush()

                    def v4(p):
                        return p[:, :].rearrange("p (h d) -> p h d", d=128)

                    def stage_fe(ti):
                        T = {}
                        t0 = ti * 128
                        xi, xib = XIN.get()
                        own = (not rot) or ti < OWNB
                        a0 = max(t0 - 2, 0)
                        a1 = min(t0 + 130, L)
                        if (a0 != t0 - 2 or a1 != t0 + 130) and not rot:
                            S.op("pool", lambda e: e.memset(xi[:], 0.0), w=[xib])
                        S.op("sp", lambda e: e.dma_start(out=xi[:, :, a0 - (t0 - 2):a1 - (t0 - 2)], in_=DT.ap()[:, :, a0:a1]), w=[xib], dma=True)
                        if rot:
                            if t0 == 0:
                                S.op("sp", lambda e: e.dma_start(out=xi[:, :, 0:2], in_=DT.ap()[:, :, L - 2:L]), w=[xib], dma=True)
                            if t0 + 128 == L:
                                S.op("sp", lambda e: e.dma_start(out=xi[:, :, 130:132], in_=DT.ap()[:, :, 0:2]), w=[xib], dma=True)
                            if ti % OWNB == 0:
                                fi = (ti // OWNB - 1) % 8
                                S.op("act", lambda e: e.activation(out=xi[:, :, 0:2], in_=xi[:, :, 0:2], func=AF.Copy, scale=wf[:, fi:fi + 1]),
                                     r=[xib, wfb], w=[xib])
                            if ti % OWNB == OWNB - 1:
                                fi2 = ti // OWNB
                                S.op("act", lambda e: e.activation(out=xi[:, :, 130:132], in_=xi[:, :, 130:132], func=AF.Copy, scale=wf[:, fi2:fi2 + 1]),
                                     r=[xib, wfb], w=[xib])
                        bai, baib = BAI.get()
                        S.op("sp", lambda e: e.dma_start(out=bai[:], in_=BA.ap()[t0:t0 + 128, :]), w=[baib], dma=True)
                        ca, cab = CA.get()
                        for j in range(5):
                            for c in range(12):
                                if j == 0:
                                    S.op("act", lambda e, c=c: e.activation(
                                        out=ca[:, c, :], in_=xi[:, c, 0:128], func=AF.Copy, scale=cw[:, c, 0:1]),
                                        r=[xib, cwb], w=[cab])
                                else:
                                    S.op("dve", lambda e, c=c, j=j: e.scalar_tensor_tensor(
                                        out=ca[:, c, :], in0=xi[:, c, j:j + 128], scalar=cw[:, c, j:j + 1], in1=ca[:, c, :],
                                        op0=ALU.mult, op1=ALU.add), r=[xib, cwb, cab], w=[cab])
                        sl, slb = SL.get()
                        S.op("act", lambda e: e.activation(out=sl[:], in_=ca[:], func=AF.Silu), r=[cab], w=[slb])
                        tm, tmb = TM.get()
                        for q4 in range(3):
                            p, pb = PSA.get()
                            for h in range(4):
                                S.op("pe", lambda e, p=p, q4=q4, h=h: e.transpose(
                                    out=p[:, h * 128:(h + 1) * 128], in_=sl[:, q4 * 4 + h, :], identity=ident[:]), r=[slb, identb], w=[pb])
                            evac(lambda q4=q4: tm[:, q4 * 4:(q4 + 1) * 4, :], lambda p=p: v4(p), pb, [tmb])
                        ss, ssb = SS.get()
                        jk, jkb = JK.get()
                        S.op("act", lambda e: e.activation(out=jk[:], in_=tm[:, 0:8, :], func=AF.Square), r=[tmb], w=[jkb])
                        S.op("dve", lambda e: e.tensor_reduce(out=ss[:, 0:8], in_=jk[:], axis=AX.X, op=ALU.add), r=[jkb], w=[ssb])
                        S.op("act", lambda e: e.activation(out=ss[:, 8:16], in_=ss[:, 0:8], func=AF.Ln, bias=RMS_EPS, scale=1.0), r=[ssb], w=[ssb])
                        S.op("act", lambda e: e.activation(out=ss[:, 8:16], in_=ss[:, 8:16], func=AF.Exp, scale=-0.5), r=[ssb], w=[ssb])
                        S.op("dve", lambda e: e.tensor_scalar(out=ss[:, 8:12], in0=ss[:, 8:12], scalar1=128.0 ** -0.5, scalar2=None, op0=ALU.mult),
                             r=[ssb], w=[ssb])
                        nrm, nrmb = NRM.get()
                        for idx in range(8):
                            if idx % 2:
                                S.op("dve", lambda e, idx=idx: e.tensor_scalar(
                                    out=nrm[:, idx, :], in0=tm[:, idx, :], scalar1=ss[:, 8 + idx:9 + idx], scalar2=None, op0=ALU.mult),
                                    r=[tmb, ssb], w=[nrmb])
                            else:
                                S.op("act", lambda e, idx=idx: e.activation(
                                    out=nrm[:, idx, :], in_=tm[:, idx, :], func=AF.Copy, scale=ss[:, 8 + idx:9 + idx]),
                                    r=[tmb, ssb], w=[nrmb])
                        vbf, vbfb = VBF.get()
                        S.op("act", lambda e: e.copy(out=vbf[:], in_=tm[:, 8:12, :]), r=[tmb], w=[vbfb])
                        qkt, qktb = QKT.get()
                        for q4 in ((0, 1) if own else (1,)):
                            p, pb = PSA.get()
                            for h in range(4):
                                S.op("pe", lambda e, p=p, q4=q4, h=h: e.transpose(
                                    out=p[:, h * 128:(h + 1) * 128], in_=nrm[:, q4 * 4 + h, :], identity=ident[:]), r=[nrmb, identb], w=[pb])
                            evac(lambda q4=q4: qkt[:, q4 * 4:(q4 + 1) * 4, :], lambda p=p: v4(p), pb, [qktb])
                        gs, gsb = GS.get()
                        S.op("act", lambda e: e.activation(out=gs[:, 0:4], in_=bai[:, dr * 4:dr * 4 + 4], func=AF.Sigmoid), r=[baib], w=[gsb])
                        S.op("dve", lambda e: e.tensor_scalar(out=gs[:, 4:8], in0=gs[:, 0:4], scalar1=-1.0, scalar2=None, op0=ALU.mult), r=[gsb], w=[gsb])
                        S.op("dve", lambda e: e.tensor_tensor(out=gs[:, 8:12], in0=bai[:, 8 + dr * 4:12 + dr * 4], in1=gpar[:, 4:8], op=ALU.add),
                             r=[baib, gparb], w=[gsb])
                        S.op("act", lambda e: e.activation(out=gs[:, 8:12], in_=gs[:, 8:12], func=AF.Exp), r=[gsb], w=[gsb])
                        S.op("act", lambda e: e.activation(out=gs[:, 8:12], in_=gs[:, 8:12], func=AF.Ln, bias=1.0, scale=1.0), r=[gsb], w=[gsb])
                        S.op("dve", lambda e: e.tensor_tensor(out=gs[:, 8:12], in0=gs[:, 8:12], in1=gpar[:, 0:4], op=ALU.mult), r=[gsb, gparb], w=[gsb])
                        p, pb = PSA.get()
                        S.op("pe", lambda e, p=p: e.matmul(p[:, 0:4], lhsT=U(), rhs=gs[:, 8:12], start=True, stop=True), r=[dmkb, gsb], w=[pb])
                        S.op("pe", lambda e, p=p: e.matmul(p[:, 4:8], lhsT=BONES(), rhs=gs[:, 8:12], start=True, stop=True), r=[dmkb, gsb], w=[pb])
                        S.op("dve", lambda e, p=p: e.tensor_copy(out=gs[:, 12:16], in_=p[:, 0:4]), r=[pb], w=[gsb])
                        S.op("dve", lambda e, p=p: e.tensor_scalar(out=gs[:, 16:20], in0=p[:, 0:4], scalar1=-1.0, scalar2=None, op0=ALU.mult), r=[pb], w=[gsb])
                        S.op("dve", lambda e, p=p: e.tensor_tensor(out=gs[:, 24:28], in0=p[:, 4:8], in1=gs[:, 12:16], op=ALU.subtract), r=[pb, gsb], w=[gsb])
                        S.op("act", lambda e: e.activation(out=gs[:, 20:24], in_=gs[:, 12:16], func=AF.Exp), r=[gsb], w=[gsb])
                        S.op("act", lambda e: e.activation(out=gs[:, 24:28], in_=gs[:, 24:28], func=AF.Exp), r=[gsb], w=[gsb])
                        gu, gub = GU.get()
                        for h in range(4):
                            S.op("act", lambda e, h=h: e.activation(
                                out=gu[:, h, :], in_=U(), func=AF.Copy, scale=gs[:, 8 + h:9 + h]), r=[dmkb, gsb], w=[gub])
                        pg, pgb = PSA.get()
                        for h in range(4):
                            S.op("pe", lambda e, h=h: e.matmul(pg[:, h * 128:(h + 1) * 128], lhsT=ones[:], rhs=gu[:, h, :], start=True, stop=True),
                                 r=[onesb, gub], w=[pgb])
                        erow, erowb = EROW.get()
                        S.op("act", lambda e: e.activation(out=erow[:], in_=v4(pg), func=AF.Exp), r=[pgb], w=[erowb])
                        dtmp, dtmpb = DTMP.get()
                        for h in range(4):
                            S.op("dve", lambda e, h=h: e.scalar_tensor_tensor(
                                out=dtmp[:, h, :], in0=pg[:, h * 128:(h + 1) * 128], scalar=gs[:, 16 + h:17 + h], in1=MINC(),
                                op0=ALU.add, op1=ALU.add), r=[pgb, gsb, dmkb], w=[dtmpb])
                        dcy, dcyb = DCY.get()
                        S.op("act", lambda e: e.activation(out=dcy[:], in_=dtmp[:], func=AF.Exp), r=[dtmpb], w=[dcyb])
                        pkk, pkkb = PSA.get()
                        for h in range(4):
                            S.op("pe", lambda e, h=h: e.matmul(pkk[:, h * 128:(h + 1) * 128], lhsT=qkt[:, 4 + h, :], rhs=qkt[:, 4 + h, :],
                                                               start=True, stop=True), r=[qktb], w=[pkkb])
                        if own:
                            pqk, pqkb = PSA.get()
                            for h in range(4):
                                S.op("pe", lambda e, h=h: e.matmul(pqk[:, h * 128:(h + 1) * 128], lhsT=qkt[:, 4 + h, :], rhs=qkt[:, h, :],
                                                                   start=True, stop=True), r=[qktb], w=[pqkb])
                        xm, xmb = XM.get()
                        for h in range(4):
                            S.op("dve", lambda e, h=h: e.scalar_tensor_tensor(
                                out=xm[:, h, :], in0=pkk[:, h * 128:(h + 1) * 128], scalar=gs[:, 4 + h:5 + h], in1=dcy[:, h, :],
                                op0=ALU.mult, op1=ALU.mult), r=[pkkb, gsb, dcyb], w=[xmb])
                        S.op("dve", lambda e: e.tensor_tensor(out=xm[:], in0=xm[:], in1=strict4[:], op=ALU.mult), r=[xmb, strict4b], w=[xmb])
                        att, attb = ATT.get()
                        if own:
                            S.op("dve", lambda e: e.tensor_tensor(out=att[:], in0=v4(pqk), in1=dcy[:], op=ALU.mult), r=[pqkb, dcyb], w=[attb])
                        kg, kgb = KG.get()
                        kd, kdb = KD.get()
                        for h in range(4):
                            S.op("act", lambda e, h=h: e.activation(
                                out=kg[:, h, :], in_=nrm[:, 4 + h, :], func=AF.Copy, scale=gs[:, 20 + h:21 + h]),
                                r=[nrmb, gsb], w=[kgb])
                            S.op("act", lambda e, h=h: e.activation(
                                out=kd[:, h, :], in_=nrm[:, 4 + h, :], func=AF.Copy, scale=gs[:, 24 + h:25 + h]),
                                r=[nrmb, gsb], w=[kdb])
                        qd, qdb = QD.get()
                        if own:
                            S.op("dve", lambda e: e.tensor_tensor(out=qd[:], in0=qkt[:, 0:4, :], in1=erow[:], op=ALU.mult), r=[qktb, erowb], w=[qdb])
                        T.update(own=own, ti=ti, t0=t0, xm=xm, xmb=xmb, vbf=vbf, vbfb=vbfb, kg=kg, kgb=kgb, gs=gs, gsb=gsb, att=att, attb=attb,
                                 kd=kd, kdb=kdb, qd=qd, qdb=qdb, erow=erow, erowb=erowb)
                        return T

                    def stage_inv(T):
                        xm, xmb, gs, gsb = T["xm"], T["xmb"], T["gs"], T["gsb"]
                        vbf, vbfb, kg, kgb = T["vbf"], T["vbfb"], T["kg"], T["kgb"]
                        pk, pkb_ = PK.get()
                        S.op("act", lambda e, pk=pk: e.copy(out=pk[:], in_=xm[:]), r=[xmb], w=[pkb_])
                        ptp, ptpb = PSB.get()
                        for h in range(4):
                            S.op("pe", lambda e, h=h: e.transpose(out=ptp[:, h * 128:(h + 1) * 128], in_=xm[:, h, :], identity=ident[:]),
                                 r=[xmb, identb], w=[ptpb])
                        pkt, pktb = PKT.get()
                        evac(lambda pkt=pkt: pkt[:], lambda: v4(ptp), ptpb, [pktb])
                        rf, rfb = RF.get()
                        rb, rbb = RB.get()
                        S.op("dve", lambda e, rf=rf: e.tensor_tensor(out=rf[:], in0=xm[:], in1=eye4[:], op=ALU.add), r=[xmb, eye4b], w=[rfb])
                        S.op("act", lambda e, rb=rb, rf=rf: e.copy(out=rb[:], in_=rf[:]), r=[rfb], w=[rbb])
                        for rnd in range(5):
                            pa, pab = PSB.get()
                            for h in range(4):
                                S.op("pe", lambda e, pa=pa, pk=pk, pkt=pkt, h=h: e.matmul(pa[:, h * 128:(h + 1) * 128], lhsT=pk[:, h, :], rhs=pkt[:, h, :],
                                                                                         start=True, stop=True), r=[pkb_, pktb], w=[pab])
                            if rnd < 4:
                                pb2, pb2b = PSB.get()
                                for h in range(4):
                                    S.op("pe", lambda e, pb2=pb2, pk=pk, pkt=pkt, h=h: e.matmul(pb2[:, h * 128:(h + 1) * 128], lhsT=pkt[:, h, :],
                                                                                               rhs=pk[:, h, :], start=True, stop=True),
                                         r=[pkb_, pktb], w=[pb2b])
                            pktn, pktnb = PKT.get()
                            S.op("act", lambda e, pktn=pktn, pa=pa: e.copy(out=pktn[:], in_=v4(pa)), r=[pab], w=[pktnb])
                            if rnd < 4:
                                pkn, pknb = PK.get()
                                S.op("dve", lambda e, pkn=pkn, pb2=pb2: e.tensor_copy(out=pkn[:], in_=v4(pb2)), r=[pb2b], w=[pknb])
                                pk, pkb_ = pkn, pknb
                            pkt, pktb = pktn, pktnb
                            pr, prb = PSB.get()
                            for h in range(4):
                                S.op("pe", lambda e, pr=pr, pkt=pkt, rb=rb, h=h: e.matmul(pr[:, h * 128:(h + 1) * 128], lhsT=pkt[:, h, :], rhs=rb[:, h, :],
                                                                                         start=True, stop=True), r=[pktb, rbb], w=[prb])
                            rfn, rfnb = RF.get()
                            rbn, rbnb = RB.get()
                            S.op("dve", lambda e, rbn=rbn, rf=rf, pr=pr: e.tensor_tensor(out=rbn[:], in0=v4(pr), in1=rf[:], op=ALU.add),
                                 r=[prb, rfb], w=[rbnb])
                            if rnd < 4:
                                S.op("dve", lambda e, rfn=rfn, rf=rf, pr=pr: e.tensor_tensor(out=rfn[:], in0=v4(pr), in1=rf[:], op=ALU.add),
                                     r=[prb, rfb], w=[rfnb])
                            rf, rfb, rb, rbb = rfn, rfnb, rbn, rbnb
                        pu, pub = PSB.get()
                        pw, pwb = PSB.get()
                        for h in range(4):
                            S.op("pe", lambda e, h=h, rb=rb: e.matmul(pu[:, h * 128:(h + 1) * 128], lhsT=rb[:, h, :], rhs=vbf[:, h, :], start=True, stop=True),
                                 r=[rbb, vbfb], w=[pub])
                        for h in range(4):
                            S.op("pe", lambda e, h=h, rb=rb: e.matmul(pw[:, h * 128:(h + 1) * 128], lhsT=kg[:, h, :], rhs=rb[:, h, :], start=True, stop=True),
                                 r=[rbb, kgb], w=[pwb])
                        ub, ubb = UB.get()
                        for h in range(4):
                            S.op("dve", lambda e, h=h: e.tensor_scalar(
                                out=ub[:, h, :], in0=pu[:, h * 128:(h + 1) * 128], scalar1=gs[:, h:h + 1], scalar2=None, op0=ALU.mult),
                                r=[pub, gsb], w=[ubb])
                        wt, wtb = WT.get()
                        S.op("act", lambda e: e.copy(out=wt[:], in_=v4(pw)), r=[pwb], w=[wtb])
                        T.update(ub=ub, ubb=ubb, wt=wt, wtb=wtb)

                    def stage_scan(T):
                        gs, gsb, att, attb, kd, kdb, qd, qdb = T["gs"], T["gsb"], T["att"], T["attb"], T["kd"], T["kdb"], T["qd"], T["qdb"]
                        erow, erowb, ub, ubb, wt, wtb, t0 = T["erow"], T["erowb"], T["ub"], T["ubb"], T["wt"], T["wtb"], T["t0"]
                        own, ti = T["own"], T["ti"]
                        if own:
                            ot, otb = OT.get()
                        if rot:
                            fi = None
                            if dr == 0 and ti % OWNB == 0 and ti != OWNB:
                                fi = (ti // OWNB - 1) % 8
                            if dr == 1 and ti % OWNB == OWNB - 1 and ti != NB - 1:
                                fi = ti // OWNB
                            if fi is not None:
                                sf0, sf0b, sb0, sb0b = st8["sf"], st8["sfb"], st8["sb"], st8["sbb"]
                                S.op("dve", lambda e, sf0=sf0, fi=fi: e.tensor_scalar(out=sf0[:], in0=sf0[:], scalar1=wf[:, fi:fi + 1], scalar2=None,
                                                                                      op0=ALU.mult), r=[sf0b, wfb], w=[sf0b])
                                S.op("dve", lambda e, sb0=sb0, fi=fi: e.tensor_scalar(out=sb0[:], in0=sb0[:], scalar1=wf[:, fi:fi + 1], scalar2=None,
                                                                                      op0=ALU.mult), r=[sb0b, wfb], w=[sb0b])
                        for ck in ((0, 1) if dr == 0 else (1, 0)):
                            po_ = ck * 64
                            lastcol = (po_ + 63) if dr == 0 else po_
                            sf, sfb, sb, sbb = st8["sf"], st8["sfb"], st8["sb"], st8["sbb"]
                            pws, pwsb = PSC.get()
                            for h in range(4):
                                S.op("pe", lambda e, pws=pws, sb=sb, h=h: e.matmul(pws[:, h * 128:(h + 1) * 128], lhsT=wt[:, h, :], rhs=sb[:, h, :],
                                                                                   start=True, stop=True), r=[wtb, sbb], w=[pwsb])
                            vn, vnb = VN.get()
                            for h in range(4):
                                S.op("dve", lambda e, vn=vn, pws=pws, h=h, po_=po_: e.scalar_tensor_tensor(
                                    out=vn[po_:po_ + 64, h, :], in0=pws[po_:po_ + 64, h * 128:(h + 1) * 128], scalar=gs[po_:po_ + 64, 4 + h:5 + h],
                                    in1=ub[po_:po_ + 64, h, :], op0=ALU.mult, op1=ALU.add), r=[pwsb, gsb, ubb], w=[vnb])
                            pD, pDb = PSC.get()
                            for h in range(4):
                                S.op("pe", lambda e, pD=pD, vn=vn, h=h, po_=po_: e.matmul(
                                    pD[:, h * 128:(h + 1) * 128], lhsT=kd[po_:po_ + 64, h, :], rhs=vn[po_:po_ + 64, h, :], start=True, stop=True),
                                    r=[kdb, vnb], w=[pDb])
                            if own:
                                pO, pOb = PSC.get()
                                for h in range(4):
                                    S.op("pe", lambda e, pO=pO, sb=sb, h=h: e.matmul(pO[:, h * 128:(h + 1) * 128], lhsT=qd[:, h, :], rhs=sb[:, h, :],
                                                                                     start=True, stop=False), r=[qdb, sbb], w=[pOb])
                                    S.op("pe", lambda e, pO=pO, vn=vn, h=h, po_=po_: e.matmul(
                                        pO[:, h * 128:(h + 1) * 128], lhsT=att[po_:po_ + 64, h, :], rhs=vn[po_:po_ + 64, h, :], start=False, stop=True),
                                        r=[attb, vnb], w=[pOb])
                            sfn, sfnb = SF.get()
                            sbn, sbnb = SB_.get()
                            for h in range(4):
                                S.op("dve", lambda e, sbn=sbn, sf=sf, pD=pD, h=h, lastcol=lastcol: e.scalar_tensor_tensor(
                                    out=sbn[:, h, :], in0=sf[:, h, :], scalar=erow[:, h, lastcol:lastcol + 1], in1=pD[:, h * 128:(h + 1) * 128],
                                    op0=ALU.mult, op1=ALU.add), r=[sfb, erowb, pDb], w=[sbnb])
                            for h in range(4):
                                S.op("dve", lambda e, sfn=sfn, sf=sf, pD=pD, h=h, lastcol=lastcol: e.scalar_tensor_tensor(
                                    out=sfn[:, h, :], in0=sf[:, h, :], scalar=erow[:, h, lastcol:lastcol + 1], in1=pD[:, h * 128:(h + 1) * 128],
                                    op0=ALU.mult, op1=ALU.add), r=[sfb, erowb, pDb], w=[sfnb])
                            if own:
                                S.op("act", lambda e, pO=pO, po_=po_: e.copy(out=ot[po_:po_ + 64, :, :], in_=v4(pO)[po_:po_ + 64]), r=[pOb], w=[otb])
                            st8["sf"], st8["sfb"], st8["sb"], st8["sbb"] = sfn, sfnb, sbn, sbnb
                        if own:
                            S.op("pool", lambda e: e.dma_start(out=ODN[dr].ap()[t0:t0 + 128, :], in_=ot[:].rearrange("p h d -> p (h d)")),
                                 r=[otb], dma=True)

                    if rot and dr == 0:
                        order = list(range(OWNB, NB)) + list(range(OWNB))
                    else:
                        order = list(range(NB)) if dr == 0 else list(range(NB - 1, -1, -1))
                    ctxs = {}
                    for step in range(NB + 2):
                        lists = []
                        if step < NB:
                            S.begin()
                            ctxs[step] = stage_fe(order[step])
                            lists.append(S.end())
                        if 1 <= step < NB + 1:
                            S.begin()
                            stage_inv(ctxs[step - 1])
                            lists.append(S.end())
                        if step >= 2:
                            S.begin()
                            stage_scan(ctxs.pop(step - 2))
                            lists.append(S.end())
                        S.merge(lists)
                    S.flush()
                if STOP == 3:
                    return nc

            with ExitStack() as c5:
                wo, wob = Pool(S, c5, "wo2", [128, NCH, D], BF16).get()
                wm, wmb = Pool(S, c5, "wm", [128, 8, D], BF16).get()
                ng4, ng4b = Pool(S, c5, "ng4", [128, 4, 128], F32).get()
                S.op("sp", lambda e: e.dma_start(out=wo[:], in_=s_f2o.ap().rearrange("(c p) d -> p c d", p=128)), w=[wob], dma=True)
                S.op("sp", lambda e: e.dma_start(out=wm[:], in_=s_wout.ap().rearrange("(k p) c -> p k c", p=128)), w=[wmb], dma=True)
                for h in range(4):
                    S.op("sp", lambda e, h=h: e.dma_start(out=ng4[:, h, :], in_=bc(w_ng, 0, 128)), w=[ng4b], dma=True)
                load_ln(c5, [1, 2])
                B0 = Pool(S, c5, "b0", [128, 4, D], F32, n=1)
                B1 = Pool(S, c5, "b1", [128, 4, D], F32, n=2)
                XTT = Pool(S, c5, "xtt5", [128, 8, 512], BF16, n=1)
                X2TT = Pool(S, c5, "x2tt5", [128, 8, 512], BF16, n=2)
                STB = Pool(S, c5, "st5b", [128, 12], F32, n=2)
                MVB = Pool(S, c5, "mv5b", [128, 8], F32, n=2)
                PSA5 = SubPool(PS.t[0:3])
                PSB5 = SubPool(PS.t[3:8])
                pend5 = [None]
                GT = Pool(S, c5, "gt5", [128, NCH, 512], BF16, n=1)
                WG = Pool(S, c5, "wg5", [128, 8, 2, 256], BF16, n=2)
                SG = Pool(S, c5, "sg5", [128, 512], BF16, n=2)
                ST = Pool(S, c5, "st5", [128, 12], F32, n=2)
                MV = Pool(S, c5, "mv5", [128, 8], F32, n=2)
                OF = Pool(S, c5, "of", [128, 4, 128], F32, n=2)
                OBk = Pool(S, c5, "obk", [128, 4, 128], F32, n=2)
                ZI = Pool(S, c5, "zi", [128, 4, 128], F32, n=2)
                SQ = Pool(S, c5, "sq", [128, 4, 128], F32, n=2)
                RS = Pool(S, c5, "rs", [128, 8], F32, n=2)
                for ti in range(OWN // 512):
                    t0 = ti * 512
                    S.begin()
                    cps[0] = PSA5
                    b0, b0b = B0.get()
                    b1, b1b = B1.get()
                    S.op("sp", lambda e, b0=b0, t0=t0: e.dma_start(
                        out=b0[:], in_=X1.ap()[t0:t0 + 512, :].rearrange("(s p) d -> p s d", p=128)), w=[b0b], dma=True)
                    S.op("sp", lambda e, b1=b1, t0=t0: e.dma_start(
                        out=b1[:, :, 0:512], in_=OA.ap()[t0:t0 + 512, :].rearrange("(s p) d -> p s d", p=128)), w=[b1b], dma=True)
                    for s in range(4):
                        r0 = t0 + s * 128
                        of_, ofb = OF.get()
                        obk, obkb = OBk.get()
                        zi, zib = ZI.get()
                        S.op("sp", lambda e, of_=of_, r0=r0: e.dma_start(out=of_[:].rearrange("p h d -> p (h d)"), in_=ODN[0].ap()[r0:r0 + 128, :]),
                             w=[ofb], dma=True)
                        S.op("sp", lambda e, obk=obk, r0=r0: e.dma_start(out=obk[:].rearrange("p h d -> p (h d)"), in_=ODN[1].ap()[r0:r0 + 128, :]),
                             w=[obkb], dma=True)
                        S.op("sp", lambda e, zi=zi, r0=r0: e.dma_start(out=zi[:].rearrange("p h d -> p (h d)"), in_=ZZ.ap()[r0:r0 + 128, :]),
                             w=[zib], dma=True)
                        S.op("dve", lambda e, of_=of_, obk=obk: e.tensor_tensor(out=of_[:], in0=of_[:], in1=obk[:], op=ALU.add), r=[ofb, obkb], w=[ofb])
                        sq, sqb = SQ.get()
                        rs, rsb = RS.get()
                        S.op("pool", lambda e, sq=sq, of_=of_: e.tensor_tensor(out=sq[:], in0=of_[:], in1=of_[:], op=ALU.mult), r=[ofb], w=[sqb])
                        S.op("dve", lambda e, sq=sq, rs=rs: e.tensor_reduce(out=rs[:, 0:4], in_=sq[:], axis=AX.X, op=ALU.add), r=[sqb], w=[rsb])
                        S.op("act", lambda e, rs=rs: e.activation(out=rs[:, 4:8], in_=rs[:, 0:4], func=AF.Ln, bias=RMS_EPS, scale=1.0 / 128.0),
                             r=[rsb], w=[rsb])
                        S.op("act", lambda e, rs=rs: e.activation(out=rs[:, 4:8], in_=rs[:, 4:8], func=AF.Exp, scale=-0.5), r=[rsb], w=[rsb])
                        S.op("act", lambda e, zi=zi: e.activation(out=zi[:], in_=zi[:], func=AF.Silu), r=[zib], w=[zib])
                        S.op("pool", lambda e, zi=zi: e.tensor_tensor(out=zi[:], in0=zi[:], in1=ng4[:], op=ALU.mult), r=[zib, ng4b], w=[zib])
                        for h in range(4):
                            S.op("dve", lambda e, b1=b1, of_=of_, rs=rs, zi=zi, s=s, h=h: e.scalar_tensor_tensor(
                                out=b1[:, s, 512 + h * 128:512 + (h + 1) * 128], in0=of_[:, h, :], scalar=rs[:, 4 + h:5 + h], in1=zi[:, h, :],
                                op0=ALU.mult, op1=ALU.mult), r=[ofb, rsb, zib], w=[b1b])
                    mT, mTb = XTT.get()
                    transpose_tok(b1, b1b, mT, mTb, 4)
                    S.op("act", lambda e, b0=b0: e.mul(out=b0[:], in_=b0[:], mul=ALPHA), r=[b0b], w=[b0b])
                    for s in range(4):
                        for nh in range(2):
                            po, pob = cps[0].get()
                            for k in range(8):
                                S.op("pe", lambda e, po=po, mT=mT, k=k, s=s, nh=nh: e.matmul(
                                    po[:, :], lhsT=mT[:, k, s * 128:(s + 1) * 128], rhs=wm[:, k, nh * 512:(nh + 1) * 512],
                                    start=(k == 0), stop=(k == 7)), r=[mTb, wmb], w=[pob])
                            S.op("dve", lambda e, po=po, b1=b1, b0=b0, s=s, nh=nh: e.tensor_tensor(
                                out=b1[:, s, nh * 512:(nh + 1) * 512], in0=po[:, :], in1=b0[:, s, nh * 512:(nh + 1) * 512], op=ALU.add),
                                r=[pob, b0b], w=[b1b])
                    layer_norm(b1, b1b, 1, (ST, MV))
                    x2T, x2Tb = X2TT.get()
                    transpose_tok(b1, b1b, x2T, x2Tb, 4)
                    S.op("act", lambda e, b1=b1: e.mul(out=b1[:], in_=b1[:], mul=ALPHA), r=[b1b], w=[b1b])
                    la = S.end()
                    S.begin()
                    cps[0] = PSB5
                    ffn(x2T, x2Tb, b1, b1b, s_f2i, wo, wob, GT, WG, SG)
                    layer_norm(b1, b1b, 2, (STB, MVB))
                    S.op("pool", lambda e, b1=b1, t0=t0: e.dma_start(
                        out=y_d.ap()[t0:t0 + 512, :].rearrange("(s p) d -> p s d", p=128), in_=b1[:]), r=[b1b], dma=True)
                    lb = S.end()
                    S.merge([la] + ([pend5[0]] if pend5[0] else []))
                    pend5[0] = lb
                S.merge([pend5[0]])
                cps[0] = PS
                S.flush()
    return nc


_NC_CACHE = {}


def _run(seqs, in_maps, n_cores):
    key = tuple(seqs)
    if key not in _NC_CACHE:
        _NC_CACHE[key] = build_nc(seqs)
    nc = _NC_CACHE[key]
    return run_bass_kernel_spmd(nc, in_maps, core_ids=list(range(n_cores)))


def _common_inputs(inp):
    f = lambda a: np.ascontiguousarray(np.asarray(a, dtype=np.float32))
    m = {
        "ffn1_w_in": f(inp["ffn1_w_in"][0]), "ffn1_w_out": f(inp["ffn1_w_out"][0]),
        "w_in": f(inp["w_in"][0]), "conv_w": f(inp["conv_w"][0]),
        "attn_sink": f(inp["attn_sink"]).reshape(1, 8),
        "dn_a_log": f(inp["dn_a_log"]).reshape(1, 8), "dn_dt_bias": f(inp["dn_dt_bias"]).reshape(1, 8),
        "dn_norm_gain": f(inp["dn_norm_gain"]).reshape(1, 128),
        "w_out": f(inp["w_out"][0]), "ffn2_w_in": f(inp["ffn2_w_in"][0]), "ffn2_w_out": f(inp["ffn2_w_out"][0]),
        "ln_gain": f(inp["ln_gain"]).reshape(1, 3 * D), "ln_bias": f(inp["ln_bias"]).reshape(1, 3 * D),
    }
    wq = m["w_in"][:, 0:512].reshape(D, 2, 4, 64).transpose(0, 2, 1, 3).reshape(D, 512)
    m["w_in"] = np.ascontiguousarray(np.concatenate([wq, m["w_in"][:, 512:]], axis=1))
    m.update(_consts())
    return m


def kernel(**inputs):
    xp = np.asarray(inputs["x_prompt"], dtype=np.float32)
    xs = np.asarray(inputs["x_sample"], dtype=np.float32)
    common = _common_inputs(inputs)
    seqs = (("s", xs.shape[1], False), ("p", xp.shape[1], True))
    Lp = xp.shape[1]
    sl = Lp // N_CORES
    in_maps = []
    for c in range(N_CORES):
        m = dict(common)
        m["x_s"] = np.ascontiguousarray(xs[c])
        m["x_p"] = np.ascontiguousarray(np.concatenate([xp[0, c * sl:], xp[0, :c * sl]], axis=0))
        m["wflag"] = np.array([[0.0 if (c + s_) % 8 == 7 else 1.0 for s_ in range(8)]], np.float32)
        in_maps.append(m)
    res = _run(seqs, in_maps, N_CORES)
    y_s = np.stack([np.asarray(res.results[c]["y_s"], dtype=np.float32) for c in range(N_CORES)], 0)
    y_p = np.concatenate([np.asarray(res.results[c]["y_p"], dtype=np.float32) for c in range(N_CORES)], 0)[None]
    return (y_p, y_s)
```

```python
from contextlib import ExitStack
import numpy as np
import ml_dtypes
import concourse.bass as bass
import concourse.mybir as mybir
from concourse.bass_utils import run_bass_kernel_spmd

F32 = mybir.dt.float32
BF16 = mybir.dt.bfloat16
AF = mybir.ActivationFunctionType
ALU = mybir.AluOpType
AX = mybir.AxisListType

D = 1024
DFF = 2816
NCH = DFF // 128
PROJ = 2832
ALPHA = 2.0 ** 0.25
LN_EPS = 1e-5
RMS_EPS = 1e-6
NEG = -1.0e6
N_CORES = 8
L_S = 8192
L_P = 16384


class Buf:
    __slots__ = ("lw", "rd", "excl", "swt", "srt")

    def __init__(self):
        self.lw = None
        self.rd = []
        self.excl = False
        self.swt = 0.0
        self.srt = 0.0


DMA_K = {"sp": 8, "pool": 4, "act": 4}
COMPUTE = ("pe", "act", "dve", "pool")


class Sched:
    def __init__(self, nc, ctx):
        self.nc = nc
        self.ops = []
        self.cur = None
        self.eng_t = {}
        self.csem = {e: ctx.enter_context(nc.semaphore("c_" + e)) for e in COMPUTE}
        self.ccnt = {e: 0 for e in COMPUTE}
        self.dsem = {q: [ctx.enter_context(nc.semaphore(f"d_{q}{i}")) for i in range(k)]
                     for q, k in DMA_K.items()}
        self.dcnt = {q: 0 for q in DMA_K}
        self.bufs = []

    def buf(self):
        b = Buf()
        self.bufs.append(b)
        return b

    COST = {"pe": 230.0, "act": 450.0, "dve": 350.0, "pool": 2500.0, "sp": 100.0}

    def op(self, eng, fn, r=(), w=(), dma=False, c=None):
        o = (eng, fn, tuple(r), tuple(w), dma, c)
        if self.cur is None:
            self._place(o)
        else:
            self.cur.append(o)

    def _est(self, o):
        eng, fn, r, w, dma, c = o
        t = self.eng_t.get(eng, 0.0)
        for b in r:
            if b.swt + 200.0 > t:
                t = b.swt + 200.0
        for b in w:
            m = max(b.swt, b.srt) + 200.0
            if m > t:
                t = m
        return t

    def _place(self, o):
        eng, fn, r, w, dma, c = o
        t = self._est(o)
        if dma:
            self.eng_t[eng] = t + 100.0
            fin = t + (c if c is not None else 4000.0)
        else:
            fin = t + (c if c is not None else self.COST[eng])
            self.eng_t[eng] = fin
        for b in r:
            if fin > b.srt:
                b.srt = fin
        for b in w:
            b.swt = fin
            b.srt = 0.0
        self.ops.append((eng, fn, r, w, dma))

    def begin(self):
        self.cur = []

    def end(self):
        c = self.cur
        self.cur = None
        return c

    def merge(self, lists):
        lists = [l for l in lists if l]
        ptr = [0] * len(lists)
        while True:
            best = None
            for k, l in enumerate(lists):
                if ptr[k] < len(l):
                    t = self._est(l[ptr[k]])
                    key = (t, ptr[k] / len(l))
                    if best is None or key < best[0]:
                        best = (key, k)
            if best is None:
                break
            k = best[1]
            self._place(lists[k][ptr[k]])
            ptr[k] += 1

    def flush(self):
        nc = self.nc
        ops = self.ops
        n = len(ops)
        deps = [None] * n
        for i, (eng, fn, r, w, dma) in enumerate(ops):
            d = set()
            for b in r:
                if b.lw is not None:
                    d.add(b.lw)
                if b.excl:
                    for q in b.rd:
                        if ops[q][0] != eng:
                            d.add(q)
            for b in w:
                if b.lw is not None:
                    d.add(b.lw)
                d.update(b.rd)
            d.discard(i)
            for b in r:
                b.rd.append(i)
            for b in w:
                b.lw = i
                b.rd = []
            deps[i] = d
        need_inc = [False] * n
        for i in range(n):
            eng, _, _, _, dma = ops[i]
            keep = []
            for p in deps[i]:
                pe, _, _, _, pdma = ops[p]
                if (not dma) and (not pdma) and pe == eng == "pe":
                    continue
                keep.append(p)
                if not pdma:
                    need_inc[p] = True
            deps[i] = keep
        target = [None] * n
        dma_prev = [None] * n
        per_eng = {e: [] for e in ("pe", "act", "dve", "pool", "sp")}
        for i in range(n):
            eng, _, _, _, dma = ops[i]
            per_eng[eng].append(i)
            if dma:
                j = self.dcnt[eng]
                k = DMA_K[eng]
                self.dcnt[eng] = j + 1
                target[i] = (self.dsem[eng][j % k], 16 * (j // k + 1))
                if j >= k:
                    dma_prev[i] = (self.dsem[eng][j % k], 16 * (j // k))
            elif need_inc[i]:
                self.ccnt[eng] += 1
                target[i] = (self.csem[eng], self.ccnt[eng])
        final = {}
        for i in range(n):
            if target[i] is not None:
                s, v = target[i]
                final[id(s)] = (s, max(v, final.get(id(s), (s, 0))[1]))

        def emit(ename, e):
            waited = {}

            def wait(s, v):
                if waited.get(id(s), 0) >= v:
                    return
                waited[id(s)] = v
                e.wait_ge(s, v)

            for i in per_eng[ename]:
                eng, fn, _, _, dma = ops[i]
                if dma_prev[i] is not None:
                    wait(*dma_prev[i])
                for p in deps[i]:
                    wait(*target[p])
                ins = fn(e)
                if target[i] is not None:
                    s, v = target[i]
                    ins.then_inc(s, 16 if dma else 1)
            for s, v in final.values():
                wait(s, v)

        with nc.Block() as block:
            @block.sync
            def _(e):
                emit("sp", e)

            @block.tensor
            def _(e):
                emit("pe", e)

            @block.scalar
            def _(e):
                emit("act", e)

            @block.vector
            def _(e):
                emit("dve", e)

            @block.gpsimd
            def _(e):
                emit("pool", e)
        self.ops = []
        self.eng_t = {}
        for b in self.bufs:
            b.lw = None
            b.rd = []
            b.swt = 0.0
            b.srt = 0.0


class Pool:
    uid = 0

    def __init__(self, S, ctx, name, shape, dtype, n=1, psum=False):
        nc = S.nc
        self.t = []
        for i in range(n):
            alloc = nc.psum_tensor if psum else nc.sbuf_tensor
            Pool.uid += 1
            h = ctx.enter_context(alloc(f"{name}_{i}_{Pool.uid}", list(shape), dtype))
            bb = S.buf()
            bb.excl = psum
            self.t.append((h, bb))
        self.i = 0
        self.S = S

    def get(self):
        r = self.t[self.i % len(self.t)]
        self.i += 1
        return r


class SubPool:
    def __init__(self, items):
        self.t = list(items)
        self.i = 0

    def get(self):
        r = self.t[self.i % len(self.t)]
        self.i += 1
        return r


def _consts():
    c = {}
    c["ident"] = np.eye(128, dtype=np.float32)
    c["ones"] = np.ones((128, 128), np.float32)
    tk = np.arange(128)[:, None]
    tq = np.arange(128)[None, :]
    ab = np.zeros((128, 3, 2, 4, 128), np.float32)
    for kb in range(3):
        dist = np.abs(tq - tk - (kb - 1) * 128)
        for kvh in range(2):
            for g in range(4):
                h = kvh * 4 + g
                slope = 2.0 ** (-8.0 * (h + 1) / 8.0)
                ab[:, kb, kvh, g, :] = np.where(dist <= 128, -slope * dist, NEG)
    c["abias"] = ab.reshape(128, 3 * 2 * 512)
    a = np.arange(128)
    same = (a[:, None] // 64) == (a[None, :] // 64)
    dm = np.zeros((128, 2, 5, 128), np.float32)
    for d in range(2):
        if d == 0:
            le = a[:, None] <= a[None, :]
            lt = a[:, None] < a[None, :]
        else:
            le = a[:, None] >= a[None, :]
            lt = a[:, None] > a[None, :]
        dm[:, d, 0, :] = (same & le)
        dm[:, d, 1, :] = same
        dm[:, d, 2, :] = np.where(same & le, 0.0, NEG)
        dm[:, d, 3, :] = (same & lt)
        dm[:, d, 4, :] = np.eye(128)
    c["dmask"] = dm.reshape(128, 2 * 5 * 128)
    return c


import os
STOP = int(os.environ.get("KSTOP", "99"))
KSUB = int(os.environ.get("KSUB", "0"))
KDBG = int(os.environ.get("KDBG", "0"))


class _Stop(Exception):
    pass


def build_nc(seqs):
    nc = bass.Bass("TRN2", target_bir_lowering=False)
    try:
        _build(nc, seqs)
    except _Stop:
        pass
    return nc


def _build(nc, seqs):
    ctx = ExitStack()
    with ctx:
        S = Sched(nc, ctx)

        def chk(n):
            if KSUB == n:
                S.flush()
                raise _Stop()

        def dram(name, shape, dt, kind):
            return nc.dram_tensor(name, list(shape), dt, kind=kind)

        xin = {nm: dram("x_" + nm, [L, D], F32, "ExternalInput") for nm, L, rot in seqs}
        yout = {nm: dram("y_" + nm, [(L // 8) if rot else L, D], F32, "ExternalOutput") for nm, L, rot in seqs}
        w_flag = dram("wflag", [1, 8], F32, "ExternalInput")
        w_f1i = dram("ffn1_w_in", [D, 2 * DFF], F32, "ExternalInput")
        w_f1o = dram("ffn1_w_out", [DFF, D], F32, "ExternalInput")
        w_in = dram("w_in", [D, PROJ], F32, "ExternalInput")
        w_cv = dram("conv_w", [5, 1536], F32, "ExternalInput")
        w_sink = dram("attn_sink", [1, 8], F32, "ExternalInput")
        w_alog = dram("dn_a_log", [1, 8], F32, "ExternalInput")
        w_dtb = dram("dn_dt_bias", [1, 8], F32, "ExternalInput")
        w_ng = dram("dn_norm_gain", [1, 128], F32, "ExternalInput")
        w_out = dram("w_out", [D, D], F32, "ExternalInput")
        w_f2i = dram("ffn2_w_in", [D, 2 * DFF], F32, "ExternalInput")
        w_f2o = dram("ffn2_w_out", [DFF, D], F32, "ExternalInput")
        w_lng = dram("ln_gain", [1, 3 * D], F32, "ExternalInput")
        w_lnb = dram("ln_bias", [1, 3 * D], F32, "ExternalInput")
        c_ident = dram("ident", [128, 128], F32, "ExternalInput")
        c_ones = dram("ones", [128, 128], F32, "ExternalInput")
        c_abias = dram("abias", [128, 3072], F32, "ExternalInput")
        c_dmask = dram("dmask", [128, 1280], F32, "ExternalInput")

        LM = max(L for _, L, _r in seqs)
        s_f1i = dram("s_f1i", [11, 128, 4096], BF16, "Internal")
        s_f1o = dram("s_f1o", [DFF, D], BF16, "Internal")
        s_win = dram("s_win", [D, PROJ], BF16, "Internal")
        s_wout = dram("s_wout", [D, D], BF16, "Internal")
        s_f2i = dram("s_f2i", [11, 128, 4096], BF16, "Internal")
        s_f2o = dram("s_f2o", [DFF, D], BF16, "Internal")
        DK = "ExternalOutput" if KDBG else "Internal"
        X1 = dram("X1", [LM, D], F32, DK)
        QT = dram("QT", [128, 4, LM], BF16, "Internal")
        KT = dram("KT", [128, LM], BF16, "Internal")
        VX = dram("VX", [LM, 130], BF16, "Internal")
        DT = dram("DT", [128, 12, LM], F32, "Internal")
        ZZ = dram("ZZ", [LM, 512], F32, "Internal")
        BA = dram("BA", [LM, 16], F32, "Internal")
        OA = dram("OA", [LM, 512], F32, DK)
        ODN = [dram(f"ODN{d}", [LM, 512], F32, DK) for d in range(2)]
        FEK = dram("FEK", [LM // 128, 128, 512], F32, "Internal")
        FEV = dram("FEV", [LM // 128, 128, 512], BF16, "Internal")
        FEQ = dram("FEQ", [LM // 128, 128, 1024], BF16, "Internal")

        def bc(t, off, n):
            return bass.AP(t, off, [[0, 128], [1, n]])

        PS = Pool(S, ctx, "ps", [128, 512], F32, n=8, psum=True)
        cps = [PS]
        ident, identb = Pool(S, ctx, "ident", [128, 128], F32).get()
        ones, onesb = Pool(S, ctx, "ones", [128, 128], F32).get()
        wf, wfb = Pool(S, ctx, "wf", [128, 8], F32).get()
        S.op("sp", lambda e: e.dma_start(out=wf[:], in_=bc(w_flag, 0, 8)), w=[wfb], dma=True)
        lnc = {}

        def load_ln(cx, lis):
            for li in lis:
                g, gb = Pool(S, cx, "lng", [128, D], F32).get()
                b, bb = Pool(S, cx, "lnb", [128, D], F32).get()
                S.op("sp", lambda e, g=g, li=li: e.dma_start(out=g[:], in_=bc(w_lng, li * D, D)), w=[gb], dma=True)
                S.op("sp", lambda e, b=b, li=li: e.dma_start(out=b[:], in_=bc(w_lnb, li * D, D)), w=[bb], dma=True)
                lnc[li] = (g, gb, b, bb)
        S.op("sp", lambda e: e.dma_start(out=ident[:], in_=c_ident.ap()), w=[identb], dma=True)
        S.op("sp", lambda e: e.dma_start(out=ones[:], in_=c_ones.ap()), w=[onesb], dma=True)

        with ExitStack() as c0:
            STG = Pool(S, c0, "stg", [128, 2048], F32, n=3)
            STB = Pool(S, c0, "stb", [128, 2048], BF16, n=3)
            rr = [0]
            def cast_op(a, ab_, b, bb_, cw):
                k = rr[0] % 3
                rr[0] += 1
                if k == 0:
                    S.op("dve", lambda e: e.tensor_copy(out=b[:, 0:cw], in_=a[:, 0:cw]), r=[ab_], w=[bb_])
                elif k == 1:
                    S.op("act", lambda e: e.copy(out=b[:, 0:cw], in_=a[:, 0:cw]), r=[ab_], w=[bb_])
                else:
                    S.op("pool", lambda e: e.tensor_copy(out=b[:, 0:cw], in_=a[:, 0:cw]), r=[ab_], w=[bb_])

            def conv_ffn_in(src, dst):
                d5 = dst.ap().rearrange("j p (k u c) -> j p k u c", k=8, u=2, c=256)
                for k in range(8):
                    for u in range(2):
                        for jj in range(0, 11, 4):
                            ng = min(4, 11 - jj)
                            a, ab_ = STG.get()
                            b, bb_ = STB.get()
                            S.op("sp", lambda e, a=a, k=k, u=u, jj=jj, ng=ng: e.dma_start(
                                out=a[:, 0:ng * 256], in_=src.ap()[k * 128:(k + 1) * 128, u * DFF + jj * 256:u * DFF + (jj + ng) * 256]),
                                w=[ab_], dma=True)
                            cast_op(a, ab_, b, bb_, ng * 256)
                            S.op("pool", lambda e, b=b, k=k, u=u, jj=jj, ng=ng: e.dma_start(
                                out=d5[jj:jj + ng, :, k, u, :].rearrange("j p c -> p j c"),
                                in_=b[:, 0:ng * 256].rearrange("p (j c) -> p j c", c=256)), r=[bb_], dma=True)

            conv_ffn_in(w_f1i, s_f1i)
            conv_ffn_in(w_f2i, s_f2i)
            for src, dst, R, C in ((w_f1o, s_f1o, DFF, D),
                                   (w_in, s_win, D, PROJ), (w_out, s_wout, D, D),
                                   (w_f2o, s_f2o, DFF, D)):
                for r0 in range(0, R, 128):
                    for c0_ in range(0, C, 2048):
                        cw = min(2048, C - c0_)
                        a, ab_ = STG.get()
                        b, bb_ = STB.get()
                        S.op("sp", lambda e, a=a, r0=r0, c0_=c0_, cw=cw, src=src:
                             e.dma_start(out=a[:, 0:cw], in_=src.ap()[r0:r0 + 128, c0_:c0_ + cw]),
                             w=[ab_], dma=True)
                        k = rr[0] % 3
                        rr[0] += 1
                        if k == 0:
                            S.op("dve", lambda e, a=a, b=b, cw=cw: e.tensor_copy(out=b[:, 0:cw], in_=a[:, 0:cw]),
                                 r=[ab_], w=[bb_])
                        elif k == 1:
                            S.op("act", lambda e, a=a, b=b, cw=cw: e.copy(out=b[:, 0:cw], in_=a[:, 0:cw]),
                                 r=[ab_], w=[bb_])
                        else:
                            S.op("pool", lambda e, a=a, b=b, cw=cw: e.tensor_copy(out=b[:, 0:cw], in_=a[:, 0:cw]),
                                 r=[ab_], w=[bb_])
                        S.op("pool", lambda e, b=b, r0=r0, c0_=c0_, cw=cw, dst=dst:
                             e.dma_start(out=dst.ap()[r0:r0 + 128, c0_:c0_ + cw], in_=b[:, 0:cw]),
                             r=[bb_], dma=True)
            S.flush()
        if STOP == 0:
            return nc

        def transpose_tok(src, srcb, dst, dstb, nsub, rot=[0]):
            for k in range(8):
                p, pb = cps[0].get()
                for s in range(nsub):
                    S.op("pe", lambda e, p=p, s=s, k=k: e.transpose(
                        out=p[:, s * 128:(s + 1) * 128], in_=src[:, s, k * 128:(k + 1) * 128], identity=ident[:]),
                        r=[srcb, identb], w=[pb])
                rot[0] += 1
                if rot[0] % 2:
                    S.op("dve", lambda e, p=p, k=k: e.tensor_copy(out=dst[:, k, 0:nsub * 128], in_=p[:, 0:nsub * 128]),
                         r=[pb], w=[dstb])
                else:
                    S.op("act", lambda e, p=p, k=k: e.copy(out=dst[:, k, 0:nsub * 128], in_=p[:, 0:nsub * 128]),
                         r=[pb], w=[dstb])

        def layer_norm(y, yb, li, pools, nsub=4):
            ST, MV = pools
            for s in range(nsub):
                st, stb = ST.get()
                mv, mvb = MV.get()
                S.op("dve", lambda e, st=st, s=s: e.bn_stats(out=st[:, 0:6], in_=y[:, s, 0:512]), r=[yb], w=[stb])
                S.op("dve", lambda e, st=st, s=s: e.bn_stats(out=st[:, 6:12], in_=y[:, s, 512:1024]), r=[yb], w=[stb])
                S.op("dve", lambda e, st=st, mv=mv: e.bn_aggr(out=mv[:, 0:2], in_=st[:, 0:12]), r=[stb], w=[mvb])
                S.op("act", lambda e, mv=mv: e.activation(out=mv[:, 2:3], in_=mv[:, 1:2], func=AF.Ln, bias=LN_EPS, scale=1.0),
                     r=[mvb], w=[mvb])
                S.op("act", lambda e, mv=mv: e.activation(out=mv[:, 3:4], in_=mv[:, 2:3], func=AF.Exp, scale=-0.5),
                     r=[mvb], w=[mvb])
                S.op("dve", lambda e, mv=mv: e.scalar_tensor_tensor(out=mv[:, 4:5], in0=mv[:, 0:1], scalar=-1.0, in1=mv[:, 3:4],
                                                                    op0=ALU.mult, op1=ALU.mult), r=[mvb], w=[mvb])
                S.op("act", lambda e, mv=mv, s=s: e.activation(out=y[:, s, :], in_=y[:, s, :], func=AF.Identity,
                                                              bias=mv[:, 4:5], scale=mv[:, 3:4]), r=[mvb, yb], w=[yb], c=1500.0)
                lg, lgb, lb_, lbb = lnc[li]
                S.op("pool", lambda e, s=s, lg=lg: e.tensor_tensor(out=y[:, s, :], in0=y[:, s, :], in1=lg[:], op=ALU.mult), r=[yb, lgb], w=[yb], c=9400.0)
                S.op("pool", lambda e, s=s, lb_=lb_: e.tensor_tensor(out=y[:, s, :], in0=y[:, s, :], in1=lb_[:], op=ALU.add), r=[yb, lbb], w=[yb], c=9400.0)

        def ffn(xT, xTb, xa, xab, wsc, wo, wob, GT, WG, SG):
            gT, gTb = GT.get()
            for j in range(11):
                wg, wgb = WG.get()
                S.op("sp" if j % 2 == 0 else "act", lambda e, wg=wg, j=j: e.dma_start(
                    out=wg[:].rearrange("p k u c -> p (k u c)"), in_=wsc.ap()[j]), w=[wgb], dma=True)
                for hf in range(2):
                    c = 2 * j + hf
                    pg, pgb = cps[0].get()
                    pu, pub = cps[0].get()
                    for k in range(8):
                        S.op("pe", lambda e, pg=pg, wg=wg, k=k, hf=hf: e.matmul(
                            pg[:, :], lhsT=wg[:, k, 0, hf * 128:(hf + 1) * 128], rhs=xT[:, k, :], start=(k == 0), stop=(k == 7)),
                            r=[wgb, xTb], w=[pgb])
                    for k in range(8):
                        S.op("pe", lambda e, pu=pu, wg=wg, k=k, hf=hf: e.matmul(
                            pu[:, :], lhsT=wg[:, k, 1, hf * 128:(hf + 1) * 128], rhs=xT[:, k, :], start=(k == 0), stop=(k == 7)),
                            r=[wgb, xTb], w=[pub])
                    sg, sgb = SG.get()
                    S.op("act", lambda e, sg=sg, pg=pg: e.activation(out=sg[:, :], in_=pg[:, :], func=AF.Silu), r=[pgb], w=[sgb])
                    S.op("dve", lambda e, sg=sg, pu=pu, c=c: e.tensor_tensor(out=gT[:, c, :], in0=sg[:, :], in1=pu[:, :], op=ALU.mult),
                         r=[sgb, pub], w=[gTb])
            for s in range(4):
                for nh in range(2):
                    po, pob = cps[0].get()
                    for c in range(NCH):
                        S.op("pe", lambda e, po=po, c=c, s=s, nh=nh: e.matmul(
                            po[:, :], lhsT=gT[:, c, s * 128:(s + 1) * 128], rhs=wo[:, c, nh * 512:(nh + 1) * 512],
                            start=(c == 0), stop=(c == NCH - 1)), r=[gTb, wob], w=[pob])
                    S.op("dve", lambda e, po=po, s=s, nh=nh: e.scalar_tensor_tensor(
                        out=xa[:, s, nh * 512:(nh + 1) * 512], in0=po[:, :], scalar=0.5, in1=xa[:, s, nh * 512:(nh + 1) * 512],
                        op0=ALU.mult, op1=ALU.add), r=[pob, xab], w=[xab])

        evr = [0]

        def evac(dst_fn, p, pb, wbufs, rbufs=()):
            evr[0] += 1
            if evr[0] % 2:
                S.op("dve", lambda e: e.tensor_copy(out=dst_fn(), in_=p()), r=[pb, *rbufs], w=wbufs)
            else:
                S.op("act", lambda e: e.copy(out=dst_fn(), in_=p()), r=[pb, *rbufs], w=wbufs)

        for nm, L, rot in seqs:
            OWN = (L // 8) if rot else L
            OWNB = OWN // 128
            x_d, y_d = xin[nm], yout[nm]
            NT = L // 512
            NB = L // 128
            with ExitStack() as c1:
                wo, wob = Pool(S, c1, "wo1", [128, NCH, D], BF16).get()
                wi, wib = Pool(S, c1, "wi", [128, 8, PROJ], BF16).get()
                S.op("sp", lambda e: e.dma_start(out=wo[:], in_=s_f1o.ap().rearrange("(c p) d -> p c d", p=128)), w=[wob], dma=True)
                S.op("sp", lambda e: e.dma_start(out=wi[:], in_=s_win.ap().rearrange("(k p) c -> p k c", p=128)), w=[wib], dma=True)
                load_ln(c1, [0])
                XT = Pool(S, c1, "xt", [128, 4, D], F32, n=1)
                XTT = Pool(S, c1, "xtt", [128, 8, 512], BF16, n=1)
                X1TT = Pool(S, c1, "x1tt", [128, 8, 512], BF16, n=2)
                PSA1 = SubPool(PS.t[0:4])
                PSB1 = SubPool(PS.t[4:8])
                pend1 = [None]
                GT = Pool(S, c1, "gt", [128, NCH, 512], BF16, n=1)
                WG = Pool(S, c1, "wg", [128, 8, 2, 256], BF16, n=2)
                SG = Pool(S, c1, "sg", [128, 512], BF16, n=2)
                ST = Pool(S, c1, "st", [128, 12], F32, n=2)
                MV = Pool(S, c1, "mv", [128, 8], F32, n=2)
                QTT = Pool(S, c1, "qtt", [128, 4, 512], BF16, n=1)
                KTT = Pool(S, c1, "ktt", [128, 512], BF16, n=1)
                DTT = Pool(S, c1, "dtt", [128, 512], F32, n=3)
                VXT = Pool(S, c1, "vxt", [128, 4, 130], BF16, n=1)
                ZT = Pool(S, c1, "zt", [128, 4, 512], F32, n=1)
                BAT = Pool(S, c1, "bat", [128, 4, 16], F32, n=1)
                for ti in range(NT):
                    t0 = ti * 512
                    S.begin()
                    cps[0] = PSA1
                    xt, xtb = XT.get()
                    S.op("sp", lambda e, xt=xt, t0=t0: e.dma_start(
                        out=xt[:], in_=x_d.ap()[t0:t0 + 512, :].rearrange("(s p) d -> p s d", p=128)), w=[xtb], dma=True)
                    xT, xTb = XTT.get()
                    transpose_tok(xt, xtb, xT, xTb, 4)
                    S.op("act", lambda e, xt=xt: e.mul(out=xt[:], in_=xt[:], mul=ALPHA), r=[xtb], w=[xtb])
                    ffn(xT, xTb, xt, xtb, s_f1i, wo, wob, GT, WG, SG)
                    layer_norm(xt, xtb, 0, (ST, MV))
                    S.op("pool", lambda e, xt=xt, t0=t0: e.dma_start(
                        out=X1.ap()[t0:t0 + 512, :].rearrange("(s p) d -> p s d", p=128), in_=xt[:]), r=[xtb], dma=True)
                    x1T, x1Tb = X1TT.get()
                    transpose_tok(xt, xtb, x1T, x1Tb, 4)
                    la = S.end()
                    S.begin()
                    cps[0] = PSB1
                    qtt, qttb = QTT.get()
                    for c in range(4):
                        p, pb = cps[0].get()
                        for k in range(8):
                            S.op("pe", lambda e, p=p, k=k, c=c, x1T=x1T: e.matmul(
                                p[:, :], lhsT=wi[:, k, c * 128:(c + 1) * 128],
                                rhs=x1T[:, k, :], start=(k == 0), stop=(k == 7)), r=[wib, x1Tb], w=[pb])
                        evac(lambda qtt=qtt, c=c: qtt[:, c, :], lambda p=p: p[:, :], pb, [qttb])
                    S.op("pool", lambda e, qtt=qtt, t0=t0: e.dma_start(out=QT.ap()[:, :, t0:t0 + 512], in_=qtt[:]), r=[qttb], dma=True)
                    ktt, kttb = KTT.get()
                    p, pb = cps[0].get()
                    for k in range(8):
                        S.op("pe", lambda e, p=p, k=k, x1T=x1T: e.matmul(
                            p[:, :], lhsT=wi[:, k, 512:640], rhs=x1T[:, k, :], start=(k == 0), stop=(k == 7)), r=[wib, x1Tb], w=[pb])
                    evac(lambda ktt=ktt: ktt[:, :], lambda p=p: p[:, :], pb, [kttb])
                    S.op("pool", lambda e, ktt=ktt, t0=t0: e.dma_start(out=KT.ap()[:, t0:t0 + 512], in_=ktt[:]), r=[kttb], dma=True)
                    for c in range(12):
                        p, pb = cps[0].get()
                        for k in range(8):
                            S.op("pe", lambda e, p=p, k=k, c=c, x1T=x1T: e.matmul(
                                p[:, :], lhsT=wi[:, k, 768 + c * 128:768 + (c + 1) * 128], rhs=x1T[:, k, :],
                                start=(k == 0), stop=(k == 7)), r=[wib, x1Tb], w=[pb])
                        dtt, dttb = DTT.get()
                        evac(lambda dtt=dtt: dtt[:, :], lambda p=p: p[:, :], pb, [dttb])
                        S.op("sp", lambda e, dtt=dtt, c=c, t0=t0: e.dma_start(out=DT.ap()[:, c, t0:t0 + 512], in_=dtt[:]),
                             r=[dttb], dma=True)
                    vxt, vxtb = VXT.get()
                    zt, ztb = ZT.get()
                    bat, batb = BAT.get()
                    S.op("pool", lambda e, vxt=vxt: e.memset(vxt[:], 1.0), w=[vxtb])
                    for s in range(4):
                        p, pb = cps[0].get()
                        for k in range(8):
                            S.op("pe", lambda e, p=p, k=k, s=s, x1T=x1T: e.matmul(
                                p[:, 0:128], lhsT=x1T[:, k, s * 128:(s + 1) * 128], rhs=wi[:, k, 640:768],
                                start=(k == 0), stop=(k == 7)), r=[wib, x1Tb], w=[pb])
                        evac(lambda vxt=vxt, s=s: vxt[:, s, :].rearrange("p (h c) -> p h c", h=2)[:, :, 0:64],
                             lambda p=p: p[:, 0:128].rearrange("p (h c) -> p h c", h=2), pb, [vxtb])
                        p, pb = cps[0].get()
                        for k in range(8):
                            S.op("pe", lambda e, p=p, k=k, s=s, x1T=x1T: e.matmul(
                                p[:, :], lhsT=x1T[:, k, s * 128:(s + 1) * 128], rhs=wi[:, k, 2304:2816],
                                start=(k == 0), stop=(k == 7)), r=[wib, x1Tb], w=[pb])
                        evac(lambda zt=zt, s=s: zt[:, s, :], lambda p=p: p[:, :], pb, [ztb])
                        p, pb = cps[0].get()
                        for k in range(8):
                            S.op("pe", lambda e, p=p, k=k, s=s, x1T=x1T: e.matmul(
                                p[:, 0:16], lhsT=x1T[:, k, s * 128:(s + 1) * 128], rhs=wi[:, k, 2816:2832],
                                start=(k == 0), stop=(k == 7)), r=[wib, x1Tb], w=[pb])
                        evac(lambda bat=bat, s=s: bat[:, s, :], lambda p=p: p[:, 0:16], pb, [batb])
                    S.op("pool", lambda e, vxt=vxt, t0=t0: e.dma_start(
                        out=VX.ap()[t0:t0 + 512, :].rearrange("(s p) c -> p s c", p=128), in_=vxt[:]), r=[vxtb], dma=True)
                    S.op("pool", lambda e, zt=zt, t0=t0: e.dma_start(
                        out=ZZ.ap()[t0:t0 + 512, :].rearrange("(s p) c -> p s c", p=128), in_=zt[:]), r=[ztb], dma=True)
                    S.op("pool", lambda e, bat=bat, t0=t0: e.dma_start(
                        out=BA.ap()[t0:t0 + 512, :].rearrange("(s p) c -> p s c", p=128), in_=bat[:]), r=[batb], dma=True)
                    lb = S.end()
                    S.merge([la] + ([pend1[0]] if pend1[0] else []))
                    pend1[0] = lb
                S.merge([pend1[0]])
                cps[0] = PS
                S.flush()
            if STOP == 1:
                return nc

            with ExitStack() as c2:
                abias, abiasb = Pool(S, c2, "abias", [128, 3072], F32).get()
                esink, esinkb = Pool(S, c2, "esink", [128, 8], F32).get()
                S.op("sp", lambda e: e.dma_start(out=abias[:], in_=c_abias.ap()), w=[abiasb], dma=True)
                S.op("sp", lambda e: e.dma_start(out=esink[:], in_=bc(w_sink, 0, 8)), w=[esinkb], dma=True)
                S.op("act", lambda e: e.activation(out=esink[:], in_=esink[:], func=AF.Exp), r=[esinkb], w=[esinkb])
                QB = Pool(S, c2, "qb", [128, 4, 128], BF16, n=2)
                KB = Pool(S, c2, "kb", [128, 3, 128], BF16, n=2)
                VB = Pool(S, c2, "vb", [128, 3, 130], BF16, n=2)
                TB = Pool(S, c2, "tb", [128, 512], F32, n=2)
                PT = Pool(S, c2, "pt", [128, 512], BF16, n=6)
                DEN = Pool(S, c2, "den", [128, 16], F32, n=2)
                OB = Pool(S, c2, "ob", [128, 512], F32, n=2)
                for i in range(OWNB):
                    kbs = [kb for kb in range(3) if (rot or 0 <= i + kb - 1 < NB)]
                    lo, hi = kbs[0], kbs[-1] + 1
                    qb, qbb = QB.get()
                    kbt, kbb = KB.get()
                    vb, vbb = VB.get()
                    S.op("sp", lambda e, qb=qb, i=i: e.dma_start(out=qb[:], in_=QT.ap()[:, :, i * 128:(i + 1) * 128]), w=[qbb], dma=True)
                    if rot and (i == 0 or i == OWNB - 1):
                        for kb in range(3):
                            bi = (i + kb - 1) % NB
                            S.op("sp", lambda e, kbt=kbt, kb=kb, bi=bi: e.dma_start(
                                out=kbt[:, kb, :], in_=KT.ap()[:, bi * 128:(bi + 1) * 128]), w=[kbb], dma=True)
                            S.op("sp", lambda e, vb=vb, kb=kb, bi=bi: e.dma_start(
                                out=vb[:, kb, :], in_=VX.ap()[bi * 128:(bi + 1) * 128, :]), w=[vbb], dma=True)
                        hk, fi = (0, 7) if i == 0 else (2, 0)
                        S.op("act", lambda e, vb=vb, hk=hk, fi=fi: e.activation(
                            out=vb[:, hk, :], in_=vb[:, hk, :], func=AF.Copy, scale=wf[:, fi:fi + 1]), r=[vbb, wfb], w=[vbb])
                        lo, hi = 0, 0
                    if hi > lo:
                        S.op("sp", lambda e, kbt=kbt, i=i, lo=lo, hi=hi: e.dma_start(
                            out=kbt[:, lo:hi, :],
                            in_=KT.ap()[:, (i + lo - 1) * 128:(i + hi - 1) * 128].rearrange("p (b t) -> p b t", t=128)), w=[kbb], dma=True)
                        S.op("sp", lambda e, vb=vb, i=i, lo=lo, hi=hi: e.dma_start(
                            out=vb[:, lo:hi, :],
                            in_=VX.ap()[(i + lo - 1) * 128:(i + hi - 1) * 128, :].rearrange("(b p) c -> p b c", p=128)), w=[vbb], dma=True)
                    pos = []
                    for kvh in range(2):
                        po, pob = PS.get()
                        pos.append((po, pob))
                        b0 = kvh * 64
                        pts = []
                        for kb in kbs:
                            ps_, psb = PS.get()
                            S.op("pe", lambda e, ps_=ps_, kbt=kbt, qb=qb, kb=kb, b0=b0: e.matmul(
                                ps_[:, :], lhsT=kbt[b0:b0 + 64, kb, :], rhs=qb[b0:b0 + 64, :, :].rearrange("p c t -> p (c t)"), start=True, stop=True),
                                r=[kbb, qbb], w=[psb])
                            tb, tbb = TB.get()
                            off = (kb * 2 + kvh) * 512
                            S.op("dve", lambda e, tb=tb, ps_=ps_, off=off: e.scalar_tensor_tensor(
                                out=tb[:, :], in0=ps_[:, :], scalar=0.125, in1=abias[:, off:off + 512], op0=ALU.mult, op1=ALU.add),
                                r=[psb, abiasb], w=[tbb])
                            pt, ptb = PT.get()
                            S.op("act", lambda e, tb=tb, pt=pt: e.activation(out=pt[:, :], in_=tb[:, :], func=AF.Exp), r=[tbb], w=[ptb])
                            pts.append((kb, pt, ptb))
                        for g in range(4):
                            for kb, pt, ptb in pts:
                                S.op("pe", lambda e, po=po, pt=pt, vb=vb, g=g, kb=kb, kvh=kvh, st_=(kb == kbs[0]), sp_=(kb == kbs[-1]): e.matmul(
                                    po[:, g * 65:(g + 1) * 65], lhsT=pt[:, g * 128:(g + 1) * 128], rhs=vb[:, kb, kvh * 65:(kvh + 1) * 65],
                                    start=st_, stop=sp_), r=[ptb, vbb], w=[pob])
                    den, denb = DEN.get()
                    ob, obb = OB.get()
                    for kvh in range(2):
                        po, pob = pos[kvh]
                        S.op("dve", lambda e, den=den, po=po, kvh=kvh: e.tensor_tensor(
                            out=den[:, kvh * 4:(kvh + 1) * 4], in0=po[:, 0:260].rearrange("p (g c) -> p g c", c=65)[:, :, 64],
                            in1=esink[:, kvh * 4:(kvh + 1) * 4], op=ALU.add), r=[pob, esinkb], w=[denb])
                    S.op("dve", lambda e, den=den: e.reciprocal(out=den[:, 8:16], in_=den[:, 0:8]), r=[denb], w=[denb])
                    for kvh in range(2):
                        po, pob = pos[kvh]
                        for g in range(4):
                            h = kvh * 4 + g
                            S.op("dve", lambda e, ob=ob, po=po, den=den, g=g, h=h: e.tensor_scalar(
                                out=ob[:, h * 64:(h + 1) * 64], in0=po[:, g * 65:g * 65 + 64], scalar1=den[:, 8 + h:9 + h], scalar2=None,
                                op0=ALU.mult), r=[pob, denb], w=[obb])
                    S.op("pool", lambda e, ob=ob, i=i: e.dma_start(out=OA.ap()[i * 128:(i + 1) * 128, :], in_=ob[:]), r=[obb], dma=True)
                S.flush()
            if STOP == 2:
                return nc

            for dr in range(2):
                with ExitStack() as c3:
                    dmk, dmkb = Pool(S, c3, "dmk", [128, 5, 128], F32).get()
                    strict4, strict4b = Pool(S, c3, "strict4", [128, 4, 128], F32).get()
                    eye4, eye4b = Pool(S, c3, "eye4", [128, 4, 128], F32).get()
                    cw, cwb = Pool(S, c3, "cw", [128, 12, 5], F32).get()
                    gpar, gparb = Pool(S, c3, "gpar", [128, 8], F32).get()
                    S.op("sp", lambda e: e.dma_start(out=dmk[:], in_=c_dmask.ap()[:, dr * 640:(dr + 1) * 640].rearrange(
                        "p (m i) -> p m i", i=128)), w=[dmkb], dma=True)
                    for h in range(4):
                        S.op("sp", lambda e, h=h: e.dma_start(out=strict4[:, h, :], in_=c_dmask.ap()[:, dr * 640 + 384:dr * 640 + 512]),
                             w=[strict4b], dma=True)
                        S.op("sp", lambda e, h=h: e.dma_start(out=eye4[:, h, :], in_=c_ident.ap()), w=[eye4b], dma=True)
                    for c in range(12):
                        S.op("sp", lambda e, c=c: e.dma_start(out=cw[:, c, :], in_=w_cv.ap()[:, c * 128:(c + 1) * 128].rearrange("j p -> p j"),
                                                              allow_slow_non_contiguous=True), w=[cwb], dma=True)
                    S.op("sp", lambda e: e.dma_start(out=gpar[:, 0:4], in_=bc(w_alog, dr * 4, 4)), w=[gparb], dma=True)
                    S.op("sp", lambda e: e.dma_start(out=gpar[:, 4:8], in_=bc(w_dtb, dr * 4, 4)), w=[gparb], dma=True)
                    S.op("act", lambda e: e.activation(out=gpar[:, 0:4], in_=gpar[:, 0:4], func=AF.Exp), r=[gparb], w=[gparb])
                    S.op("dve", lambda e: e.tensor_scalar(out=gpar[:, 0:4], in0=gpar[:, 0:4], scalar1=-1.0, scalar2=None, op0=ALU.mult),
                         r=[gparb], w=[gparb])
                    U = lambda: dmk[:, 0, :]
                    BONES = lambda: dmk[:, 1, :]
                    MINC = lambda: dmk[:, 2, :]
                    PSA = SubPool(PS.t[0:3])
                    PSB = SubPool(PS.t[3:5])
                    PSC = SubPool(PS.t[5:8])
                    XIN = Pool(S, c3, "xin", [128, 12, 132], F32, n=2)
                    BAI = Pool(S, c3, "bai", [128, 16], F32, n=2)
                    CA = Pool(S, c3, "ca", [128, 12, 128], F32, n=1)
                    SL = Pool(S, c3, "sl", [128, 12, 128], F32, n=1)
                    TM = Pool(S, c3, "tm", [128, 12, 128], F32, n=1)
                    SS = Pool(S, c3, "ss", [128, 16], F32, n=2)
                    JK = Pool(S, c3, "jk", [128, 8, 128], F32, n=1)
                    NRM = Pool(S, c3, "nrm", [128, 8, 128], F32, n=2)
                    QKT = Pool(S, c3, "qkt", [128, 8, 128], BF16, n=2)
                    GU = Pool(S, c3, "gu", [128, 4, 128], F32, n=1)
                    DTMP = Pool(S, c3, "dtmp", [128, 4, 128], F32, n=1)
                    DCY = Pool(S, c3, "dcy", [128, 4, 128], F32, n=1)
                    GS = Pool(S, c3, "gs", [128, 32], F32, n=3)
                    EROW = Pool(S, c3, "erow", [128, 4, 128], F32, n=3)
                    ATT = Pool(S, c3, "att", [128, 4, 128], BF16, n=3)
                    KD = Pool(S, c3, "kd", [128, 4, 128], BF16, n=3)
                    QD = Pool(S, c3, "qd", [128, 4, 128], BF16, n=3)
                    XM = Pool(S, c3, "xm", [128, 4, 128], F32, n=2)
                    VBF = Pool(S, c3, "vbf", [128, 4, 128], BF16, n=2)
                    KG = Pool(S, c3, "kg", [128, 4, 128], BF16, n=2)
                    PK = Pool(S, c3, "pk", [128, 4, 128], BF16, n=2)
                    PKT = Pool(S, c3, "pkt", [128, 4, 128], BF16, n=2)
                    RF = Pool(S, c3, "rf", [128, 4, 128], F32, n=2)
                    RB = Pool(S, c3, "rbb", [128, 4, 128], BF16, n=2)
                    UB = Pool(S, c3, "ub", [128, 4, 128], F32, n=2)
                    WT = Pool(S, c3, "wt", [128, 4, 128], BF16, n=2)
                    VN = Pool(S, c3, "vn", [128, 4, 128], BF16, n=2)
                    OT = Pool(S, c3, "ot", [128, 4, 128], F32, n=2)
                    SF = Pool(S, c3, "sf", [128, 4, 128], F32, n=2)
                    SB_ = Pool(S, c3, "sbs", [128, 4, 128], BF16, n=2)
                    st8 = {}
                    st8["sf"], st8["sfb"] = SF.get()
                    st8["sb"], st8["sbb"] = SB_.get()
                    S.op("pool", lambda e: e.memset(st8["sf"][:], 0.0), w=[st8["sfb"]])
                    S.op("pool", lambda e: e.memset(st8["sb"][:], 0.0), w=[st8["sbb"]])
                    S.flush()

                    def v4(p):
                        return p[:, :].rearrange("p (h d) -> p h d", d=128)

                    def stage_fe(ti):
                        T = {}
                        t0 = ti * 128
                        own = (not rot) or ti < OWNB
                        bai, baib = BAI.get()
                        S.op("sp", lambda e: e.dma_start(out=bai[:], in_=BA.ap()[t0:t0 + 128, :]), w=[baib], dma=True)
                        if dr == 0:
                            xi, xib = XIN.get()
                            a0 = max(t0 - 2, 0)
                            a1 = min(t0 + 130, L)
                            if (a0 != t0 - 2 or a1 != t0 + 130) and not rot:
                                S.op("pool", lambda e: e.memset(xi[:], 0.0), w=[xib])
                            S.op("sp", lambda e: e.dma_start(out=xi[:, :, a0 - (t0 - 2):a1 - (t0 - 2)], in_=DT.ap()[:, :, a0:a1]), w=[xib], dma=True)
                            if rot:
                                if t0 == 0:
                                    S.op("sp", lambda e: e.dma_start(out=xi[:, :, 0:2], in_=DT.ap()[:, :, L - 2:L]), w=[xib], dma=True)
                                if t0 + 128 == L:
                                    S.op("sp", lambda e: e.dma_start(out=xi[:, :, 130:132], in_=DT.ap()[:, :, 0:2]), w=[xib], dma=True)
                                if ti % OWNB == 0:
                                    fi = (ti // OWNB - 1) % 8
                                    S.op("act", lambda e: e.activation(out=xi[:, :, 0:2], in_=xi[:, :, 0:2], func=AF.Copy, scale=wf[:, fi:fi + 1]),
                                         r=[xib, wfb], w=[xib])
                                if ti % OWNB == OWNB - 1:
                                    fi2 = ti // OWNB
                                    S.op("act", lambda e: e.activation(out=xi[:, :, 130:132], in_=xi[:, :, 130:132], func=AF.Copy, scale=wf[:, fi2:fi2 + 1]),
                                         r=[xib, wfb], w=[xib])
                            ca, cab = CA.get()
                            for j in range(5):
                                for c in range(12):
                                    if j == 0:
                                        S.op("act", lambda e, c=c: e.activation(
                                            out=ca[:, c, :], in_=xi[:, c, 0:128], func=AF.Copy, scale=cw[:, c, 0:1]),
                                            r=[xib, cwb], w=[cab])
                                    else:
                                        S.op("dve", lambda e, c=c, j=j: e.scalar_tensor_tensor(
                                            out=ca[:, c, :], in0=xi[:, c, j:j + 128], scalar=cw[:, c, j:j + 1], in1=ca[:, c, :],
                                            op0=ALU.mult, op1=ALU.add), r=[xib, cwb, cab], w=[cab])
                            sl, slb = SL.get()
                            S.op("act", lambda e: e.activation(out=sl[:], in_=ca[:], func=AF.Silu), r=[cab], w=[slb])
                            tm, tmb = TM.get()
                            for q4 in range(3):
                                p, pb = PSA.get()
                                for h in range(4):
                                    S.op("pe", lambda e, p=p, q4=q4, h=h: e.transpose(
                                        out=p[:, h * 128:(h + 1) * 128], in_=sl[:, q4 * 4 + h, :], identity=ident[:]), r=[slb, identb], w=[pb])
                                evac(lambda q4=q4: tm[:, q4 * 4:(q4 + 1) * 4, :], lambda p=p: v4(p), pb, [tmb])
                            ss, ssb = SS.get()
                            jk, jkb = JK.get()
                            S.op("act", lambda e: e.activation(out=jk[:], in_=tm[:, 0:8, :], func=AF.Square), r=[tmb], w=[jkb])
                            S.op("dve", lambda e: e.tensor_reduce(out=ss[:, 0:8], in_=jk[:], axis=AX.X, op=ALU.add), r=[jkb], w=[ssb])
                            S.op("act", lambda e: e.activation(out=ss[:, 8:16], in_=ss[:, 0:8], func=AF.Ln, bias=RMS_EPS, scale=1.0), r=[ssb], w=[ssb])
                            S.op("act", lambda e: e.activation(out=ss[:, 8:16], in_=ss[:, 8:16], func=AF.Exp, scale=-0.5), r=[ssb], w=[ssb])
                            S.op("dve", lambda e: e.tensor_scalar(out=ss[:, 8:12], in0=ss[:, 8:12], scalar1=128.0 ** -0.5, scalar2=None, op0=ALU.mult),
                                 r=[ssb], w=[ssb])
                            nrm, nrmb = NRM.get()
                            for idx in range(8):
                                if idx % 2:
                                    S.op("dve", lambda e, idx=idx: e.tensor_scalar(
                                        out=nrm[:, idx, :], in0=tm[:, idx, :], scalar1=ss[:, 8 + idx:9 + idx], scalar2=None, op0=ALU.mult),
                                        r=[tmb, ssb], w=[nrmb])
                                else:
                                    S.op("act", lambda e, idx=idx: e.activation(
                                        out=nrm[:, idx, :], in_=tm[:, idx, :], func=AF.Copy, scale=ss[:, 8 + idx:9 + idx]),
                                        r=[tmb, ssb], w=[nrmb])
                            vbf, vbfb = VBF.get()
                            S.op("act", lambda e: e.copy(out=vbf[:], in_=tm[:, 8:12, :]), r=[tmb], w=[vbfb])
                            qkt, qktb = QKT.get()
                            for q4 in ((0, 1) if own else (1,)):
                                p, pb = PSA.get()
                                for h in range(4):
                                    S.op("pe", lambda e, p=p, q4=q4, h=h: e.transpose(
                                        out=p[:, h * 128:(h + 1) * 128], in_=nrm[:, q4 * 4 + h, :], identity=ident[:]), r=[nrmb, identb], w=[pb])
                                evac(lambda q4=q4: qkt[:, q4 * 4:(q4 + 1) * 4, :], lambda p=p: v4(p), pb, [qktb])
                            S.op("pool", lambda e: e.dma_start(out=FEK.ap()[ti], in_=nrm[:, 4:8, :].rearrange("p h d -> p (h d)")), r=[nrmb], dma=True)
                            S.op("pool", lambda e: e.dma_start(out=FEV.ap()[ti], in_=vbf[:].rearrange("p h d -> p (h d)")), r=[vbfb], dma=True)
                            if own:
                                S.op("pool", lambda e: e.dma_start(out=FEQ.ap()[ti], in_=qkt[:].rearrange("p h d -> p (h d)")), r=[qktb], dma=True)
                            else:
                                S.op("pool", lambda e: e.dma_start(out=FEQ.ap()[ti, :, 512:1024], in_=qkt[:, 4:8, :].rearrange("p h d -> p (h d)")),
                                     r=[qktb], dma=True)
                        else:
                            nrm, nrmb = NRM.get()
                            vbf, vbfb = VBF.get()
                            qkt, qktb = QKT.get()
                            S.op("sp", lambda e: e.dma_start(out=nrm[:, 4:8, :].rearrange("p h d -> p (h d)"), in_=FEK.ap()[ti]), w=[nrmb], dma=True)
                            S.op("sp", lambda e: e.dma_start(out=vbf[:].rearrange("p h d -> p (h d)"), in_=FEV.ap()[ti]), w=[vbfb], dma=True)
                            if own:
                                S.op("sp", lambda e: e.dma_start(out=qkt[:].rearrange("p h d -> p (h d)"), in_=FEQ.ap()[ti]), w=[qktb], dma=True)
                            else:
                                S.op("sp", lambda e: e.dma_start(out=qkt[:, 4:8, :].rearrange("p h d -> p (h d)"), in_=FEQ.ap()[ti, :, 512:1024]),
                                     w=[qktb], dma=True)
                        gs, gsb = GS.get()
                        S.op("act", lambda e: e.activation(out=gs[:, 0:4], in_=bai[:, dr * 4:dr * 4 + 4], func=AF.Sigmoid), r=[baib], w=[gsb])
                        S.op("dve", lambda e: e.tensor_scalar(out=gs[:, 4:8], in0=gs[:, 0:4], scalar1=-1.0, scalar2=None, op0=ALU.mult), r=[gsb], w=[gsb])
                        S.op("dve", lambda e: e.tensor_tensor(out=gs[:, 8:12], in0=bai[:, 8 + dr * 4:12 + dr * 4], in1=gpar[:, 4:8], op=ALU.add),
                             r=[baib, gparb], w=[gsb])
                        S.op("act", lambda e: e.activation(out=gs[:, 8:12], in_=gs[:, 8:12], func=AF.Exp), r=[gsb], w=[gsb])
                        S.op("act", lambda e: e.activation(out=gs[:, 8:12], in_=gs[:, 8:12], func=AF.Ln, bias=1.0, scale=1.0), r=[gsb], w=[gsb])
                        S.op("dve", lambda e: e.tensor_tensor(out=gs[:, 8:12], in0=gs[:, 8:12], in1=gpar[:, 0:4], op=ALU.mult), r=[gsb, gparb], w=[gsb])
                        p, pb = PSA.get()
                        S.op("pe", lambda e, p=p: e.matmul(p[:, 0:4], lhsT=U(), rhs=gs[:, 8:12], start=True, stop=True), r=[dmkb, gsb], w=[pb])
                        S.op("pe", lambda e, p=p: e.matmul(p[:, 4:8], lhsT=BONES(), rhs=gs[:, 8:12], start=True, stop=True), r=[dmkb, gsb], w=[pb])
                        S.op("dve", lambda e, p=p: e.tensor_copy(out=gs[:, 12:16], in_=p[:, 0:4]), r=[pb], w=[gsb])
                        S.op("dve", lambda e, p=p: e.tensor_scalar(out=gs[:, 16:20], in0=p[:, 0:4], scalar1=-1.0, scalar2=None, op0=ALU.mult), r=[pb], w=[gsb])
                        S.op("dve", lambda e, p=p: e.tensor_tensor(out=gs[:, 24:28], in0=p[:, 4:8], in1=gs[:, 12:16], op=ALU.subtract), r=[pb, gsb], w=[gsb])
                        S.op("act", lambda e: e.activation(out=gs[:, 20:24], in_=gs[:, 12:16], func=AF.Exp), r=[gsb], w=[gsb])
                        S.op("act", lambda e: e.activation(out=gs[:, 24:28], in_=gs[:, 24:28], func=AF.Exp), r=[gsb], w=[gsb])
                        gu, gub = GU.get()
                        for h in range(4):
                            S.op("act", lambda e, h=h: e.activation(
                                out=gu[:, h, :], in_=U(), func=AF.Copy, scale=gs[:, 8 + h:9 + h]), r=[dmkb, gsb], w=[gub])
                        pg, pgb = PSA.get()
                        for h in range(4):
                            S.op("pe", lambda e, h=h: e.matmul(pg[:, h * 128:(h + 1) * 128], lhsT=ones[:], rhs=gu[:, h, :], start=True, stop=True),
                                 r=[onesb, gub], w=[pgb])
                        erow, erowb = EROW.get()
                        S.op("act", lambda e: e.activation(out=erow[:], in_=v4(pg), func=AF.Exp), r=[pgb], w=[erowb])
                        dtmp, dtmpb = DTMP.get()
                        for h in range(4):
                            S.op("dve", lambda e, h=h: e.scalar_tensor_tensor(
                                out=dtmp[:, h, :], in0=pg[:, h * 128:(h + 1) * 128], scalar=gs[:, 16 + h:17 + h], in1=MINC(),
                                op0=ALU.add, op1=ALU.add), r=[pgb, gsb, dmkb], w=[dtmpb])
                        dcy, dcyb = DCY.get()
                        S.op("act", lambda e: e.activation(out=dcy[:], in_=dtmp[:], func=AF.Exp), r=[dtmpb], w=[dcyb])
                        pkk, pkkb = PSA.get()
                        for h in range(4):
                            S.op("pe", lambda e, h=h: e.matmul(pkk[:, h * 128:(h + 1) * 128], lhsT=qkt[:, 4 + h, :], rhs=qkt[:, 4 + h, :],
                                                               start=True, stop=True), r=[qktb], w=[pkkb])
                        if own:
                            pqk, pqkb = PSA.get()
                            for h in range(4):
                                S.op("pe", lambda e, h=h: e.matmul(pqk[:, h * 128:(h + 1) * 128], lhsT=qkt[:, 4 + h, :], rhs=qkt[:, h, :],
                                                                   start=True, stop=True), r=[qktb], w=[pqkb])
                        xm, xmb = XM.get()
                        for h in range(4):
                            S.op("dve", lambda e, h=h: e.scalar_tensor_tensor(
                                out=xm[:, h, :], in0=pkk[:, h * 128:(h + 1) * 128], scalar=gs[:, 4 + h:5 + h], in1=dcy[:, h, :],
                                op0=ALU.mult, op1=ALU.mult), r=[pkkb, gsb, dcyb], w=[xmb])
                        S.op("dve", lambda e: e.tensor_tensor(out=xm[:], in0=xm[:], in1=strict4[:], op=ALU.mult), r=[xmb, strict4b], w=[xmb])
                        att, attb = ATT.get()
                        if own:
                            S.op("dve", lambda e: e.tensor_tensor(out=att[:], in0=v4(pqk), in1=dcy[:], op=ALU.mult), r=[pqkb, dcyb], w=[attb])
                        kg, kgb = KG.get()
                        kd, kdb = KD.get()
                        for h in range(4):
                            S.op("act", lambda e, h=h: e.activation(
                                out=kg[:, h, :], in_=nrm[:, 4 + h, :], func=AF.Copy, scale=gs[:, 20 + h:21 + h]),
                                r=[nrmb, gsb], w=[kgb])
                            S.op("act", lambda e, h=h: e.activation(
                                out=kd[:, h, :], in_=nrm[:, 4 + h, :], func=AF.Copy, scale=gs[:, 24 + h:25 + h]),
                                r=[nrmb, gsb], w=[kdb])
                        qd, qdb = QD.get()
                        if own:
                            S.op("dve", lambda e: e.tensor_tensor(out=qd[:], in0=qkt[:, 0:4, :], in1=erow[:], op=ALU.mult), r=[qktb, erowb], w=[qdb])
                        T.update(own=own, ti=ti, t0=t0, xm=xm, xmb=xmb, vbf=vbf, vbfb=vbfb, kg=kg, kgb=kgb, gs=gs, gsb=gsb, att=att, attb=attb,
                                 kd=kd, kdb=kdb, qd=qd, qdb=qdb, erow=erow, erowb=erowb)
                        return T

                    def stage_inv(T):
                        xm, xmb, gs, gsb = T["xm"], T["xmb"], T["gs"], T["gsb"]
                        vbf, vbfb, kg, kgb = T["vbf"], T["vbfb"], T["kg"], T["kgb"]
                        pk, pkb_ = PK.get()
                        S.op("act", lambda e, pk=pk: e.copy(out=pk[:], in_=xm[:]), r=[xmb], w=[pkb_])
                        ptp, ptpb = PSB.get()
                        for h in range(4):
                            S.op("pe", lambda e, h=h: e.transpose(out=ptp[:, h * 128:(h + 1) * 128], in_=xm[:, h, :], identity=ident[:]),
                                 r=[xmb, identb], w=[ptpb])
                        pkt, pktb = PKT.get()
                        evac(lambda pkt=pkt: pkt[:], lambda: v4(ptp), ptpb, [pktb])
                        rf, rfb = RF.get()
                        rb, rbb = RB.get()
                        S.op("dve", lambda e, rf=rf: e.tensor_tensor(out=rf[:], in0=xm[:], in1=eye4[:], op=ALU.add), r=[xmb, eye4b], w=[rfb])
                        S.op("act", lambda e, rb=rb, rf=rf: e.copy(out=rb[:], in_=rf[:]), r=[rfb], w=[rbb])
                        for rnd in range(5):
                            pa, pab = PSB.get()
                            for h in range(4):
                                S.op("pe", lambda e, pa=pa, pk=pk, pkt=pkt, h=h: e.matmul(pa[:, h * 128:(h + 1) * 128], lhsT=pk[:, h, :], rhs=pkt[:, h, :],
                                                                                         start=True, stop=True), r=[pkb_, pktb], w=[pab])
                            if rnd < 4:
                                pb2, pb2b = PSB.get()
                                for h in range(4):
                                    S.op("pe", lambda e, pb2=pb2, pk=pk, pkt=pkt, h=h: e.matmul(pb2[:, h * 128:(h + 1) * 128], lhsT=pkt[:, h, :],
                                                                                               rhs=pk[:, h, :], start=True, stop=True),
                                         r=[pkb_, pktb], w=[pb2b])
                            pktn, pktnb = PKT.get()
                            S.op("act", lambda e, pktn=pktn, pa=pa: e.copy(out=pktn[:], in_=v4(pa)), r=[pab], w=[pktnb])
                            if rnd < 4:
                                pkn, pknb = PK.get()
                                S.op("dve", lambda e, pkn=pkn, pb2=pb2: e.tensor_copy(out=pkn[:], in_=v4(pb2)), r=[pb2b], w=[pknb])
                                pk, pkb_ = pkn, pknb
                            pkt, pktb = pktn, pktnb
                            pr, prb = PSB.get()
                            for h in range(4):
                                S.op("pe", lambda e, pr=pr, pkt=pkt, rb=rb, h=h: e.matmul(pr[:, h * 128:(h + 1) * 128], lhsT=pkt[:, h, :], rhs=rb[:, h, :],
                                                                                         start=True, stop=True), r=[pktb, rbb], w=[prb])
                            rfn, rfnb = RF.get()
                            rbn, rbnb = RB.get()
                            S.op("dve", lambda e, rbn=rbn, rf=rf, pr=pr: e.tensor_tensor(out=rbn[:], in0=v4(pr), in1=rf[:], op=ALU.add),
                                 r=[prb, rfb], w=[rbnb])
                            if rnd < 4:
                                S.op("dve", lambda e, rfn=rfn, rf=rf, pr=pr: e.tensor_tensor(out=rfn[:], in0=v4(pr), in1=rf[:], op=ALU.add),
                                     r=[prb, rfb], w=[rfnb])
                            rf, rfb, rb, rbb = rfn, rfnb, rbn, rbnb
                        pu, pub = PSB.get()
                        pw, pwb = PSB.get()
                        for h in range(4):
                            S.op("pe", lambda e, h=h, rb=rb: e.matmul(pu[:, h * 128:(h + 1) * 128], lhsT=rb[:, h, :], rhs=vbf[:, h, :], start=True, stop=True),
                                 r=[rbb, vbfb], w=[pub])
                        for h in range(4):
                            S.op("pe", lambda e, h=h, rb=rb: e.matmul(pw[:, h * 128:(h + 1) * 128], lhsT=kg[:, h, :], rhs=rb[:, h, :], start=True, stop=True),
                                 r=[rbb, kgb], w=[pwb])
                        ub, ubb = UB.get()
                        for h in range(4):
                            S.op("dve", lambda e, h=h: e.tensor_scalar(
                                out=ub[:, h, :], in0=pu[:, h * 128:(h + 1) * 128], scalar1=gs[:, h:h + 1], scalar2=None, op0=ALU.mult),
                                r=[pub, gsb], w=[ubb])
                        wt, wtb = WT.get()
                        S.op("act", lambda e: e.copy(out=wt[:], in_=v4(pw)), r=[pwb], w=[wtb])
                        T.update(ub=ub, ubb=ubb, wt=wt, wtb=wtb)

                    def stage_scan(T):
                        gs, gsb, att, attb, kd, kdb, qd, qdb = T["gs"], T["gsb"], T["att"], T["attb"], T["kd"], T["kdb"], T["qd"], T["qdb"]
                        erow, erowb, ub, ubb, wt, wtb, t0 = T["erow"], T["erowb"], T["ub"], T["ubb"], T["wt"], T["wtb"], T["t0"]
                        own, ti = T["own"], T["ti"]
                        if own:
                            ot, otb = OT.get()
                        if rot:
                            fi = None
                            if dr == 0 and ti % OWNB == 0 and ti != OWNB:
                                fi = (ti // OWNB - 1) % 8
                            if dr == 1 and ti % OWNB == OWNB - 1 and ti != NB - 1:
                                fi = ti // OWNB
                            if fi is not None:
                                sf0, sf0b, sb0, sb0b = st8["sf"], st8["sfb"], st8["sb"], st8["sbb"]
                                S.op("dve", lambda e, sf0=sf0, fi=fi: e.tensor_scalar(out=sf0[:], in0=sf0[:], scalar1=wf[:, fi:fi + 1], scalar2=None,
                                                                                      op0=ALU.mult), r=[sf0b, wfb], w=[sf0b])
                                S.op("dve", lambda e, sb0=sb0, fi=fi: e.tensor_scalar(out=sb0[:], in0=sb0[:], scalar1=wf[:, fi:fi + 1], scalar2=None,
                                                                                      op0=ALU.mult), r=[sb0b, wfb], w=[sb0b])
                        for ck in ((0, 1) if dr == 0 else (1, 0)):
                            po_ = ck * 64
                            lastcol = (po_ + 63) if dr == 0 else po_
                            sf, sfb, sb, sbb = st8["sf"], st8["sfb"], st8["sb"], st8["sbb"]
                            pws, pwsb = PSC.get()
                            for h in range(4):
                                S.op("pe", lambda e, pws=pws, sb=sb, h=h: e.matmul(pws[:, h * 128:(h + 1) * 128], lhsT=wt[:, h, :], rhs=sb[:, h, :],
                                                                                   start=True, stop=True), r=[wtb, sbb], w=[pwsb])
                            vn, vnb = VN.get()
                            for h in range(4):
                                S.op("dve", lambda e, vn=vn, pws=pws, h=h, po_=po_: e.scalar_tensor_tensor(
                                    out=vn[po_:po_ + 64, h, :], in0=pws[po_:po_ + 64, h * 128:(h + 1) * 128], scalar=gs[po_:po_ + 64, 4 + h:5 + h],
                                    in1=ub[po_:po_ + 64, h, :], op0=ALU.mult, op1=ALU.add), r=[pwsb, gsb, ubb], w=[vnb])
                            pD, pDb = PSC.get()
                            for h in range(4):
                                S.op("pe", lambda e, pD=pD, vn=vn, h=h, po_=po_: e.matmul(
                                    pD[:, h * 128:(h + 1) * 128], lhsT=kd[po_:po_ + 64, h, :], rhs=vn[po_:po_ + 64, h, :], start=True, stop=True),
                                    r=[kdb, vnb], w=[pDb])
                            if own:
                                pO, pOb = PSC.get()
                                for h in range(4):
                                    S.op("pe", lambda e, pO=pO, sb=sb, h=h: e.matmul(pO[:, h * 128:(h + 1) * 128], lhsT=qd[:, h, :], rhs=sb[:, h, :],
                                                                                     start=True, stop=False), r=[qdb, sbb], w=[pOb])
                                    S.op("pe", lambda e, pO=pO, vn=vn, h=h, po_=po_: e.matmul(
                                        pO[:, h * 128:(h + 1) * 128], lhsT=att[po_:po_ + 64, h, :], rhs=vn[po_:po_ + 64, h, :], start=False, stop=True),
                                        r=[attb, vnb], w=[pOb])
                            sfn, sfnb = SF.get()
                            sbn, sbnb = SB_.get()
                            for h in range(4):
                                S.op("dve", lambda e, sbn=sbn, sf=sf, pD=pD, h=h, lastcol=lastcol: e.scalar_tensor_tensor(
                                    out=sbn[:, h, :], in0=sf[:, h, :], scalar=erow[:, h, lastcol:lastcol + 1], in1=pD[:, h * 128:(h + 1) * 128],
                                    op0=ALU.mult, op1=ALU.add), r=[sfb, erowb, pDb], w=[sbnb])
                            for h in range(4):
                                S.op("dve", lambda e, sfn=sfn, sf=sf, pD=pD, h=h, lastcol=lastcol: e.scalar_tensor_tensor(
                                    out=sfn[:, h, :], in0=sf[:, h, :], scalar=erow[:, h, lastcol:lastcol + 1], in1=pD[:, h * 128:(h + 1) * 128],
                                    op0=ALU.mult, op1=ALU.add), r=[sfb, erowb, pDb], w=[sfnb])
                            if own:
                                S.op("act", lambda e, pO=pO, po_=po_: e.copy(out=ot[po_:po_ + 64, :, :], in_=v4(pO)[po_:po_ + 64]), r=[pOb], w=[otb])
                            st8["sf"], st8["sfb"], st8["sb"], st8["sbb"] = sfn, sfnb, sbn, sbnb
                        if own:
                            S.op("pool", lambda e: e.dma_start(out=ODN[dr].ap()[t0:t0 + 128, :], in_=ot[:].rearrange("p h d -> p (h d)")),
                                 r=[otb], dma=True)

                    if rot and dr == 0:
                        order = list(range(OWNB, NB)) + list(range(OWNB))
                    else:
                        order = list(range(NB)) if dr == 0 else list(range(NB - 1, -1, -1))
                    ctxs = {}
                    for step in range(NB + 2):
                        lists = []
                        if step < NB:
                            S.begin()
                            ctxs[step] = stage_fe(order[step])
                            lists.append(S.end())
                        if 1 <= step < NB + 1:
                            S.begin()
                            stage_inv(ctxs[step - 1])
                            lists.append(S.end())
                        if step >= 2:
                            S.begin()
                            stage_scan(ctxs.pop(step - 2))
                            lists.append(S.end())
                        S.merge(lists)
                    S.flush()
                if STOP == 3:
                    return nc

            with ExitStack() as c5:
                wo, wob = Pool(S, c5, "wo2", [128, NCH, D], BF16).get()
                wm, wmb = Pool(S, c5, "wm", [128, 8, D], BF16).get()
                ng4, ng4b = Pool(S, c5, "ng4", [128, 4, 128], F32).get()
                S.op("sp", lambda e: e.dma_start(out=wo[:], in_=s_f2o.ap().rearrange("(c p) d -> p c d", p=128)), w=[wob], dma=True)
                S.op("sp", lambda e: e.dma_start(out=wm[:], in_=s_wout.ap().rearrange("(k p) c -> p k c", p=128)), w=[wmb], dma=True)
                for h in range(4):
                    S.op("sp", lambda e, h=h: e.dma_start(out=ng4[:, h, :], in_=bc(w_ng, 0, 128)), w=[ng4b], dma=True)
                load_ln(c5, [1, 2])
                B0 = Pool(S, c5, "b0", [128, 4, D], F32, n=1)
                B1 = Pool(S, c5, "b1", [128, 4, D], F32, n=2)
                XTT = Pool(S, c5, "xtt5", [128, 8, 512], BF16, n=1)
                X2TT = Pool(S, c5, "x2tt5", [128, 8, 512], BF16, n=2)
                STB = Pool(S, c5, "st5b", [128, 12], F32, n=2)
                MVB = Pool(S, c5, "mv5b", [128, 8], F32, n=2)
                PSA5 = SubPool(PS.t[0:3])
                PSB5 = SubPool(PS.t[3:8])
                pend5 = [None]
                GT = Pool(S, c5, "gt5", [128, NCH, 512], BF16, n=1)
                WG = Pool(S, c5, "wg5", [128, 8, 2, 256], BF16, n=2)
                SG = Pool(S, c5, "sg5", [128, 512], BF16, n=2)
                ST = Pool(S, c5, "st5", [128, 12], F32, n=2)
                MV = Pool(S, c5, "mv5", [128, 8], F32, n=2)
                OF = Pool(S, c5, "of", [128, 4, 128], F32, n=2)
                OBk = Pool(S, c5, "obk", [128, 4, 128], F32, n=2)
                ZI = Pool(S, c5, "zi", [128, 4, 128], F32, n=2)
                SQ = Pool(S, c5, "sq", [128, 4, 128], F32, n=2)
                RS = Pool(S, c5, "rs", [128, 8], F32, n=2)
                for ti in range(OWN // 512):
                    t0 = ti * 512
                    S.begin()
                    cps[0] = PSA5
                    b0, b0b = B0.get()
                    b1, b1b = B1.get()
                    S.op("sp", lambda e, b0=b0, t0=t0: e.dma_start(
                        out=b0[:], in_=X1.ap()[t0:t0 + 512, :].rearrange("(s p) d -> p s d", p=128)), w=[b0b], dma=True)
                    S.op("sp", lambda e, b1=b1, t0=t0: e.dma_start(
                        out=b1[:, :, 0:512], in_=OA.ap()[t0:t0 + 512, :].rearrange("(s p) d -> p s d", p=128)), w=[b1b], dma=True)
                    for s in range(4):
                        r0 = t0 + s * 128
                        of_, ofb = OF.get()
                        obk, obkb = OBk.get()
                        zi, zib = ZI.get()
                        S.op("sp", lambda e, of_=of_, r0=r0: e.dma_start(out=of_[:].rearrange("p h d -> p (h d)"), in_=ODN[0].ap()[r0:r0 + 128, :]),
                             w=[ofb], dma=True)
                        S.op("sp", lambda e, obk=obk, r0=r0: e.dma_start(out=obk[:].rearrange("p h d -> p (h d)"), in_=ODN[1].ap()[r0:r0 + 128, :]),
                             w=[obkb], dma=True)
                        S.op("sp", lambda e, zi=zi, r0=r0: e.dma_start(out=zi[:].rearrange("p h d -> p (h d)"), in_=ZZ.ap()[r0:r0 + 128, :]),
                             w=[zib], dma=True)
                        S.op("dve", lambda e, of_=of_, obk=obk: e.tensor_tensor(out=of_[:], in0=of_[:], in1=obk[:], op=ALU.add), r=[ofb, obkb], w=[ofb])
                        sq, sqb = SQ.get()
                        rs, rsb = RS.get()
                        S.op("pool", lambda e, sq=sq, of_=of_: e.tensor_tensor(out=sq[:], in0=of_[:], in1=of_[:], op=ALU.mult), r=[ofb], w=[sqb])
                        S.op("dve", lambda e, sq=sq, rs=rs: e.tensor_reduce(out=rs[:, 0:4], in_=sq[:], axis=AX.X, op=ALU.add), r=[sqb], w=[rsb])
                        S.op("act", lambda e, rs=rs: e.activation(out=rs[:, 4:8], in_=rs[:, 0:4], func=AF.Ln, bias=RMS_EPS, scale=1.0 / 128.0),
                             r=[rsb], w=[rsb])
                        S.op("act", lambda e, rs=rs: e.activation(out=rs[:, 4:8], in_=rs[:, 4:8], func=AF.Exp, scale=-0.5), r=[rsb], w=[rsb])
                        S.op("act", lambda e, zi=zi: e.activation(out=zi[:], in_=zi[:], func=AF.Silu), r=[zib], w=[zib])
                        S.op("pool", lambda e, zi=zi: e.tensor_tensor(out=zi[:], in0=zi[:], in1=ng4[:], op=ALU.mult), r=[zib, ng4b], w=[zib])
                        for h in range(4):
                            S.op("dve", lambda e, b1=b1, of_=of_, rs=rs, zi=zi, s=s, h=h: e.scalar_tensor_tensor(
                                out=b1[:, s, 512 + h * 128:512 + (h + 1) * 128], in0=of_[:, h, :], scalar=rs[:, 4 + h:5 + h], in1=zi[:, h, :],
                                op0=ALU.mult, op1=ALU.mult), r=[ofb, rsb, zib], w=[b1b])
                    mT, mTb = XTT.get()
                    transpose_tok(b1, b1b, mT, mTb, 4)
                    S.op("act", lambda e, b0=b0: e.mul(out=b0[:], in_=b0[:], mul=ALPHA), r=[b0b], w=[b0b])
                    for s in range(4):
                        for nh in range(2):
                            po, pob = cps[0].get()
                            for k in range(8):
                                S.op("pe", lambda e, po=po, mT=mT, k=k, s=s, nh=nh: e.matmul(
                                    po[:, :], lhsT=mT[:, k, s * 128:(s + 1) * 128], rhs=wm[:, k, nh * 512:(nh + 1) * 512],
                                    start=(k == 0), stop=(k == 7)), r=[mTb, wmb], w=[pob])
                            S.op("dve", lambda e, po=po, b1=b1, b0=b0, s=s, nh=nh: e.tensor_tensor(
                                out=b1[:, s, nh * 512:(nh + 1) * 512], in0=po[:, :], in1=b0[:, s, nh * 512:(nh + 1) * 512], op=ALU.add),
                                r=[pob, b0b], w=[b1b])
                    layer_norm(b1, b1b, 1, (ST, MV))
                    x2T, x2Tb = X2TT.get()
                    transpose_tok(b1, b1b, x2T, x2Tb, 4)
                    S.op("act", lambda e, b1=b1: e.mul(out=b1[:], in_=b1[:], mul=ALPHA), r=[b1b], w=[b1b])
                    la = S.end()
                    S.begin()
                    cps[0] = PSB5
                    ffn(x2T, x2Tb, b1, b1b, s_f2i, wo, wob, GT, WG, SG)
                    layer_norm(b1, b1b, 2, (STB, MVB))
                    S.op("pool", lambda e, b1=b1, t0=t0: e.dma_start(
                        out=y_d.ap()[t0:t0 + 512, :].rearrange("(s p) d -> p s d", p=128), in_=b1[:]), r=[b1b], dma=True)
                    lb = S.end()
                    S.merge([la] + ([pend5[0]] if pend5[0] else []))
                    pend5[0] = lb
                S.merge([pend5[0]])
                cps[0] = PS
                S.flush()
    return nc


_NC_CACHE = {}


def _run(seqs, in_maps, n_cores):
    key = tuple(seqs)
    if key not in _NC_CACHE:
        _NC_CACHE[key] = build_nc(seqs)
    nc = _NC_CACHE[key]
    return run_bass_kernel_spmd(nc, in_maps, core_ids=list(range(n_cores)))


def _common_inputs(inp):
    f = lambda a: np.ascontiguousarray(np.asarray(a, dtype=np.float32))
    m = {
        "ffn1_w_in": f(inp["ffn1_w_in"][0]), "ffn1_w_out": f(inp["ffn1_w_out"][0]),
        "w_in": f(inp["w_in"][0]), "conv_w": f(inp["conv_w"][0]),
        "attn_sink": f(inp["attn_sink"]).reshape(1, 8),
        "dn_a_log": f(inp["dn_a_log"]).reshape(1, 8), "dn_dt_bias": f(inp["dn_dt_bias"]).reshape(1, 8),
        "dn_norm_gain": f(inp["dn_norm_gain"]).reshape(1, 128),
        "w_out": f(inp["w_out"][0]), "ffn2_w_in": f(inp["ffn2_w_in"][0]), "ffn2_w_out": f(inp["ffn2_w_out"][0]),
        "ln_gain": f(inp["ln_gain"]).reshape(1, 3 * D), "ln_bias": f(inp["ln_bias"]).reshape(1, 3 * D),
    }
    wq = m["w_in"][:, 0:512].reshape(D, 2, 4, 64).transpose(0, 2, 1, 3).reshape(D, 512)
    m["w_in"] = np.ascontiguousarray(np.concatenate([wq, m["w_in"][:, 512:]], axis=1))
    m.update(_consts())
    return m


def kernel(**inputs):
    xp = np.asarray(inputs["x_prompt"], dtype=np.float32)
    xs = np.asarray(inputs["x_sample"], dtype=np.float32)
    common = _common_inputs(inputs)
    seqs = (("s", xs.shape[1], False), ("p", xp.shape[1], True))
    Lp = xp.shape[1]
    sl = Lp // N_CORES
    in_maps = []
    for c in range(N_CORES):
        m = dict(common)
        m["x_s"] = np.ascontiguousarray(xs[c])
        m["x_p"] = np.ascontiguousarray(np.concatenate([xp[0, c * sl:], xp[0, :c * sl]], axis=0))
        m["wflag"] = np.array([[0.0 if (c + s_) % 8 == 7 else 1.0 for s_ in range(8)]], np.float32)
        in_maps.append(m)
    res = _run(seqs, in_maps, N_CORES)
    y_s = np.stack([np.asarray(res.results[c]["y_s"], dtype=np.float32) for c in range(N_CORES)], 0)
    y_p = np.concatenate([np.asarray(res.results[c]["y_p"], dtype=np.float32) for c in range(N_CORES)], 0)[None]
    return (y_p, y_s)
```

```python
from contextlib import ExitStack
import numpy as np
import ml_dtypes
import concourse.bass as bass
import concourse.mybir as mybir
from concourse.bass_utils import run_bass_kernel_spmd

F32 = mybir.dt.float32
BF16 = mybir.dt.bfloat16
AF = mybir.ActivationFunctionType
ALU = mybir.AluOpType
AX = mybir.AxisListType

D = 1024
DFF = 2816
NCH = DFF // 128
PROJ = 2832
ALPHA = 2.0 ** 0.25
LN_EPS = 1e-5
RMS_EPS = 1e-6
NEG = -1.0e6
N_CORES = 8
L_S = 8192
L_P = 16384


class Buf:
    __slots__ = ("lw", "rd", "excl", "swt", "srt")

    def __init__(self):
        self.lw = None
        self.rd = []
        self.excl = False
        self.swt = 0.0
        self.srt = 0.0


DMA_K = {"sp": 8, "pool": 4, "act": 4}
COMPUTE = ("pe", "act", "dve", "pool")


class Sched:
    def __init__(self, nc, ctx):
        self.nc = nc
        self.ops = []
        self.cur = None
        self.eng_t = {}
        self.csem = {e: ctx.enter_context(nc.semaphore("c_" + e)) for e in COMPUTE}
        self.ccnt = {e: 0 for e in COMPUTE}
        self.dsem = {q: [ctx.enter_context(nc.semaphore(f"d_{q}{i}")) for i in range(k)]
                     for q, k in DMA_K.items()}
        self.dcnt = {q: 0 for q in DMA_K}
        self.bufs = []

    def buf(self):
        b = Buf()
        self.bufs.append(b)
        return b

    COST = {"pe": 230.0, "act": 450.0, "dve": 350.0, "pool": 2500.0, "sp": 100.0}

    def op(self, eng, fn, r=(), w=(), dma=False, c=None):
        o = (eng, fn, tuple(r), tuple(w), dma, c)
        if self.cur is None:
            self._place(o)
        else:
            self.cur.append(o)

    def _est(self, o):
        eng, fn, r, w, dma, c = o
        t = self.eng_t.get(eng, 0.0)
        for b in r:
            if b.swt + 200.0 > t:
                t = b.swt + 200.0
        for b in w:
            m = max(b.swt, b.srt) + 200.0
            if m > t:
                t = m
        return t

    def _place(self, o):
        eng, fn, r, w, dma, c = o
        t = self._est(o)
        if dma:
            self.eng_t[eng] = t + 100.0
            fin = t + (c if c is not None else 4000.0)
        else:
            fin = t + (c if c is not None else self.COST[eng])
            self.eng_t[eng] = fin
        for b in r:
            if fin > b.srt:
                b.srt = fin
        for b in w:
            b.swt = fin
            b.srt = 0.0
        self.ops.append((eng, fn, r, w, dma))

    def begin(self):
        self.cur = []

    def end(self):
        c = self.cur
        self.cur = None
        return c

    def merge(self, lists):
        lists = [l for l in lists if l]
        ptr = [0] * len(lists)
        while True:
            best = None
            for k, l in enumerate(lists):
                if ptr[k] < len(l):
                    t = self._est(l[ptr[k]])
                    key = (t, ptr[k] / len(l))
                    if best is None or key < best[0]:
                        best = (key, k)
            if best is None:
                break
            k = best[1]
            self._place(lists[k][ptr[k]])
            ptr[k] += 1

    def flush(self):
        nc = self.nc
        ops = self.ops
        n = len(ops)
        deps = [None] * n
        for i, (eng, fn, r, w, dma) in enumerate(ops):
            d = set()
            for b in r:
                if b.lw is not None:
                    d.add(b.lw)
                if b.excl:
                    for q in b.rd:
                        if ops[q][0] != eng:
                            d.add(q)
            for b in w:
                if b.lw is not None:
                    d.add(b.lw)
                d.update(b.rd)
            d.discard(i)
            for b in r:
                b.rd.append(i)
            for b in w:
                b.lw = i
                b.rd = []
            deps[i] = d
        need_inc = [False] * n
        for i in range(n):
            eng, _, _, _, dma = ops[i]
            keep = []
            for p in deps[i]:
                pe, _, _, _, pdma = ops[p]
                if (not dma) and (not pdma) and pe == eng == "pe":
                    continue
                keep.append(p)
                if not pdma:
                    need_inc[p] = True
            deps[i] = keep
        target = [None] * n
        dma_prev = [None] * n
        per_eng = {e: [] for e in ("pe", "act", "dve", "pool", "sp")}
        for i in range(n):
            eng, _, _, _, dma = ops[i]
            per_eng[eng].append(i)
            if dma:
                j = self.dcnt[eng]
                k = DMA_K[eng]
                self.dcnt[eng] = j + 1
                target[i] = (self.dsem[eng][j % k], 16 * (j // k + 1))
                if j >= k:
                    dma_prev[i] = (self.dsem[eng][j % k], 16 * (j // k))
            elif need_inc[i]:
                self.ccnt[eng] += 1
                target[i] = (self.csem[eng], self.ccnt[eng])
        final = {}
        for i in range(n):
            if target[i] is not None:
                s, v = target[i]
                final[id(s)] = (s, max(v, final.get(id(s), (s, 0))[1]))

        def emit(ename, e):
            waited = {}

            def wait(s, v):
                if waited.get(id(s), 0) >= v:
                    return
                waited[id(s)] = v
                e.wait_ge(s, v)

            for i in per_eng[ename]:
                eng, fn, _, _, dma = ops[i]
                if dma_prev[i] is not None:
                    wait(*dma_prev[i])
                for p in deps[i]:
                    wait(*target[p])
                ins = fn(e)
                if target[i] is not None:
                    s, v = target[i]
                    ins.then_inc(s, 16 if dma else 1)
            for s, v in final.values():
                wait(s, v)

        with nc.Block() as block:
            @block.sync
            def _(e):
                emit("sp", e)

            @block.tensor
            def _(e):
                emit("pe", e)

            @block.scalar
            def _(e):
                emit("act", e)

            @block.vector
            def _(e):
                emit("dve", e)

            @block.gpsimd
            def _(e):
                emit("pool", e)
        self.ops = []
        self.eng_t = {}
        for b in self.bufs:
            b.lw = None
            b.rd = []
            b.swt = 0.0
            b.srt = 0.0


class Pool:
    uid = 0

    def __init__(self, S, ctx, name, shape, dtype, n=1, psum=False):
        nc = S.nc
        self.t = []
        for i in range(n):
            alloc = nc.psum_tensor if psum else nc.sbuf_tensor
            Pool.uid += 1
            h = ctx.enter_context(alloc(f"{name}_{i}_{Pool.uid}", list(shape), dtype))
            bb = S.buf()
            bb.excl = psum
            self.t.append((h, bb))
        self.i = 0
        self.S = S

    def get(self):
        r = self.t[self.i % len(self.t)]
        self.i += 1
        return r


class SubPool:
    def __init__(self, items):
        self.t = list(items)
        self.i = 0

    def get(self):
        r = self.t[self.i % len(self.t)]
        self.i += 1
        return r


def _consts():
    c = {}
    c["ident"] = np.eye(128, dtype=np.float32)
    c["ones"] = np.ones((128, 128), np.float32)
    tk = np.arange(128)[:, None]
    tq = np.arange(128)[None, :]
    ab = np.zeros((128, 3, 2, 4, 128), np.float32)
    for kb in range(3):
        dist = np.abs(tq - tk - (kb - 1) * 128)
        for kvh in range(2):
            for g in range(4):
                h = kvh * 4 + g
                slope = 2.0 ** (-8.0 * (h + 1) / 8.0)
                ab[:, kb, kvh, g, :] = np.where(dist <= 128, -slope * dist, NEG)
    c["abias"] = ab.reshape(128, 3 * 2 * 512)
    a = np.arange(128)
    same = (a[:, None] // 64) == (a[None, :] // 64)
    dm = np.zeros((128, 2, 5, 128), np.float32)
    for d in range(2):
        if d == 0:
            le = a[:, None] <= a[None, :]
            lt = a[:, None] < a[None, :]
        else:
            le = a[:, None] >= a[None, :]
            lt = a[:, None] > a[None, :]
        dm[:, d, 0, :] = (same & le)
        dm[:, d, 1, :] = same
        dm[:, d, 2, :] = np.where(same & le, 0.0, NEG)
        dm[:, d, 3, :] = (same & lt)
        dm[:, d, 4, :] = np.eye(128)
    c["dmask"] = dm.reshape(128, 2 * 5 * 128)
    return c


import os
STOP = int(os.environ.get("KSTOP", "99"))
KSUB = int(os.environ.get("KSUB", "0"))
KDBG = int(os.environ.get("KDBG", "0"))


class _Stop(Exception):
    pass


def build_nc(seqs):
    nc = bass.Bass("TRN2", target_bir_lowering=False)
    try:
        _build(nc, seqs)
    except _Stop:
        pass
    return nc


def _build(nc, seqs):
    ctx = ExitStack()
    with ctx:
        S = Sched(nc, ctx)

        def chk(n):
            if KSUB == n:
                S.flush()
                raise _Stop()

        def dram(name, shape, dt, kind):
            return nc.dram_tensor(name, list(shape), dt, kind=kind)

        xin = {nm: dram("x_" + nm, [L, D], F32, "ExternalInput") for nm, L, rot in seqs}
        yout = {nm: dram("y_" + nm, [(L // 8) if rot else L, D], F32, "ExternalOutput") for nm, L, rot in seqs}
        w_flag = dram("wflag", [1, 8], F32, "ExternalInput")
        w_f1i = dram("ffn1_w_in", [D, 2 * DFF], F32, "ExternalInput")
        w_f1o = dram("ffn1_w_out", [DFF, D], F32, "ExternalInput")
        w_in = dram("w_in", [D, PROJ], F32, "ExternalInput")
        w_cv = dram("conv_w", [5, 1536], F32, "ExternalInput")
        w_sink = dram("attn_sink", [1, 8], F32, "ExternalInput")
        w_alog = dram("dn_a_log", [1, 8], F32, "ExternalInput")
        w_dtb = dram("dn_dt_bias", [1, 8], F32, "ExternalInput")
        w_ng = dram("dn_norm_gain", [1, 128], F32, "ExternalInput")
        w_out = dram("w_out", [D, D], F32, "ExternalInput")
        w_f2i = dram("ffn2_w_in", [D, 2 * DFF], F32, "ExternalInput")
        w_f2o = dram("ffn2_w_out", [DFF, D], F32, "ExternalInput")
        w_lng = dram("ln_gain", [1, 3 * D], F32, "ExternalInput")
        w_lnb = dram("ln_bias", [1, 3 * D], F32, "ExternalInput")
        c_ident = dram("ident", [128, 128], F32, "ExternalInput")
        c_ones = dram("ones", [128, 128], F32, "ExternalInput")
        c_abias = dram("abias", [128, 3072], F32, "ExternalInput")
        c_dmask = dram("dmask", [128, 1280], F32, "ExternalInput")

        LM = max(L for _, L, _r in seqs)
        s_f1i = dram("s_f1i", [11, 128, 4096], BF16, "Internal")
        s_f1o = dram("s_f1o", [DFF, D], BF16, "Internal")
        s_win = dram("s_win", [D, PROJ], BF16, "Internal")
        s_wout = dram("s_wout", [D, D], BF16, "Internal")
        s_f2i = dram("s_f2i", [11, 128, 4096], BF16, "Internal")
        s_f2o = dram("s_f2o", [DFF, D], BF16, "Internal")
        DK = "ExternalOutput" if KDBG else "Internal"
        X1 = dram("X1", [LM, D], F32, DK)
        QT = dram("QT", [128, 4, LM], BF16, "Internal")
        KT = dram("KT", [128, LM], BF16, "Internal")
        VX = dram("VX", [LM, 130], BF16, "Internal")
        DT = dram("DT", [128, 12, LM], F32, "Internal")
        ZZ = dram("ZZ", [LM, 512], F32, "Internal")
        BA = dram("BA", [LM, 16], F32, "Internal")
        OA = dram("OA", [LM, 512], F32, DK)
        ODN = [dram(f"ODN{d}", [LM, 512], F32, DK) for d in range(2)]
        FEK = dram("FEK", [LM // 128, 128, 512], F32, "Internal")
        FEV = dram("FEV", [LM // 128, 128, 512], BF16, "Internal")
        FEQ = dram("FEQ", [LM // 128, 128, 1024], BF16, "Internal")

        def bc(t, off, n):
            return bass.AP(t, off, [[0, 128], [1, n]])

        PS = Pool(S, ctx, "ps", [128, 512], F32, n=8, psum=True)
        cps = [PS]
        ident, identb = Pool(S, ctx, "ident", [128, 128], F32).get()
        ones, onesb = Pool(S, ctx, "ones", [128, 128], F32).get()
        wf, wfb = Pool(S, ctx, "wf", [128, 8], F32).get()
        S.op("sp", lambda e: e.dma_start(out=wf[:], in_=bc(w_flag, 0, 8)), w=[wfb], dma=True)
        lnc = {}

        def load_ln(cx, lis):
            for li in lis:
                g, gb = Pool(S, cx, "lng", [128, D], F32).get()
                b, bb = Pool(S, cx, "lnb", [128, D], F32).get()
                S.op("sp", lambda e, g=g, li=li: e.dma_start(out=g[:], in_=bc(w_lng, li * D, D)), w=[gb], dma=True)
                S.op("sp", lambda e, b=b, li=li: e.dma_start(out=b[:], in_=bc(w_lnb, li * D, D)), w=[bb], dma=True)
                lnc[li] = (g, gb, b, bb)
        S.op("sp", lambda e: e.dma_start(out=ident[:], in_=c_ident.ap()), w=[identb], dma=True)
        S.op("sp", lambda e: e.dma_start(out=ones[:], in_=c_ones.ap()), w=[onesb], dma=True)

        with ExitStack() as c0:
            STG = Pool(S, c0, "stg", [128, 2048], F32, n=3)
            STB = Pool(S, c0, "stb", [128, 2048], BF16, n=3)
            rr = [0]
            def cast_op(a, ab_, b, bb_, cw):
                k = rr[0] % 3
                rr[0] += 1
                if k == 0:
                    S.op("dve", lambda e: e.tensor_copy(out=b[:, 0:cw], in_=a[:, 0:cw]), r=[ab_], w=[bb_])
                elif k == 1:
                    S.op("act", lambda e: e.copy(out=b[:, 0:cw], in_=a[:, 0:cw]), r=[ab_], w=[bb_])
                else:
                    S.op("pool", lambda e: e.tensor_copy(out=b[:, 0:cw], in_=a[:, 0:cw]), r=[ab_], w=[bb_])

            def conv_ffn_in(src, dst):
                d5 = dst.ap().rearrange("j p (k u c) -> j p k u c", k=8, u=2, c=256)
                for k in range(8):
                    for u in range(2):
                        for jj in range(0, 11, 4):
                            ng = min(4, 11 - jj)
                            a, ab_ = STG.get()
                            b, bb_ = STB.get()
                            S.op("sp", lambda e, a=a, k=k, u=u, jj=jj, ng=ng: e.dma_start(
                                out=a[:, 0:ng * 256], in_=src.ap()[k * 128:(k + 1) * 128, u * DFF + jj * 256:u * DFF + (jj + ng) * 256]),
                                w=[ab_], dma=True)
                            cast_op(a, ab_, b, bb_, ng * 256)
                            S.op("pool", lambda e, b=b, k=k, u=u, jj=jj, ng=ng: e.dma_start(
                                out=d5[jj:jj + ng, :, k, u, :].rearrange("j p c -> p j c"),
                                in_=b[:, 0:ng * 256].rearrange("p (j c) -> p j c", c=256)), r=[bb_], dma=True)

            conv_ffn_in(w_f1i, s_f1i)
            conv_ffn_in(w_f2i, s_f2i)
            for src, dst, R, C in ((w_f1o, s_f1o, DFF, D),
                                   (w_in, s_win, D, PROJ), (w_out, s_wout, D, D),
                                   (w_f2o, s_f2o, DFF, D)):
                for r0 in range(0, R, 128):
                    for c0_ in range(0, C, 2048):
                        cw = min(2048, C - c0_)
                        a, ab_ = STG.get()
                        b, bb_ = STB.get()
                        S.op("sp", lambda e, a=a, r0=r0, c0_=c0_, cw=cw, src=src:
                             e.dma_start(out=a[:, 0:cw], in_=src.ap()[r0:r0 + 128, c0_:c0_ + cw]),
                             w=[ab_], dma=True)
                        k = rr[0] % 3
                        rr[0] += 1
                        if k == 0:
                            S.op("dve", lambda e, a=a, b=b, cw=cw: e.tensor_copy(out=b[:, 0:cw], in_=a[:, 0:cw]),
                                 r=[ab_], w=[bb_])
                        elif k == 1:
                            S.op("act", lambda e, a=a, b=b, cw=cw: e.copy(out=b[:, 0:cw], in_=a[:, 0:cw]),
                                 r=[ab_], w=[bb_])
                        else:
                            S.op("pool", lambda e, a=a, b=b, cw=cw: e.tensor_copy(out=b[:, 0:cw], in_=a[:, 0:cw]),
                                 r=[ab_], w=[bb_])
                        S.op("pool", lambda e, b=b, r0=r0, c0_=c0_, cw=cw, dst=dst:
                             e.dma_start(out=dst.ap()[r0:r0 + 128, c0_:c0_ + cw], in_=b[:, 0:cw]),
                             r=[bb_], dma=True)
            S.flush()
        if STOP == 0:
            return nc

        def transpose_tok(src, srcb, dst, dstb, nsub, rot=[0]):
            for k in range(8):
                p, pb = cps[0].get()
                for s in range(nsub):
                    S.op("pe", lambda e, p=p, s=s, k=k: e.transpose(
                        out=p[:, s * 128:(s + 1) * 128], in_=src[:, s, k * 128:(k + 1) * 128], identity=ident[:]),
                        r=[srcb, identb], w=[pb])
                rot[0] += 1
                if rot[0] % 2:
                    S.op("dve", lambda e, p=p, k=k: e.tensor_copy(out=dst[:, k, 0:nsub * 128], in_=p[:, 0:nsub * 128]),
                         r=[pb], w=[dstb])
                else:
                    S.op("act", lambda e, p=p, k=k: e.copy(out=dst[:, k, 0:nsub * 128], in_=p[:, 0:nsub * 128]),
                         r=[pb], w=[dstb])

        def layer_norm(y, yb, li, pools, nsub=4):
            ST, MV = pools
            for s in range(nsub):
                st, stb = ST.get()
                mv, mvb = MV.get()
                S.op("dve", lambda e, st=st, s=s: e.bn_stats(out=st[:, 0:6], in_=y[:, s, 0:512]), r=[yb], w=[stb])
                S.op("dve", lambda e, st=st, s=s: e.bn_stats(out=st[:, 6:12], in_=y[:, s, 512:1024]), r=[yb], w=[stb])
                S.op("dve", lambda e, st=st, mv=mv: e.bn_aggr(out=mv[:, 0:2], in_=st[:, 0:12]), r=[stb], w=[mvb])
                S.op("act", lambda e, mv=mv: e.activation(out=mv[:, 2:3], in_=mv[:, 1:2], func=AF.Ln, bias=LN_EPS, scale=1.0),
                     r=[mvb], w=[mvb])
                S.op("act", lambda e, mv=mv: e.activation(out=mv[:, 3:4], in_=mv[:, 2:3], func=AF.Exp, scale=-0.5),
                     r=[mvb], w=[mvb])
                S.op("dve", lambda e, mv=mv: e.scalar_tensor_tensor(out=mv[:, 4:5], in0=mv[:, 0:1], scalar=-1.0, in1=mv[:, 3:4],
                                                                    op0=ALU.mult, op1=ALU.mult), r=[mvb], w=[mvb])
                S.op("act", lambda e, mv=mv, s=s: e.activation(out=y[:, s, :], in_=y[:, s, :], func=AF.Identity,
                                                              bias=mv[:, 4:5], scale=mv[:, 3:4]), r=[mvb, yb], w=[yb], c=1500.0)
                lg, lgb, lb_, lbb = lnc[li]
                S.op("pool", lambda e, s=s, lg=lg: e.tensor_tensor(out=y[:, s, :], in0=y[:, s, :], in1=lg[:], op=ALU.mult), r=[yb, lgb], w=[yb], c=9400.0)
                S.op("pool", lambda e, s=s, lb_=lb_: e.tensor_tensor(out=y[:, s, :], in0=y[:, s, :], in1=lb_[:], op=ALU.add), r=[yb, lbb], w=[yb], c=9400.0)

        def ffn(xT, xTb, xa, xab, wsc, wo, wob, GT, WG, SG):
            gT, gTb = GT.get()
            for j in range(11):
                wg, wgb = WG.get()
                S.op("sp" if j % 2 == 0 else "act", lambda e, wg=wg, j=j: e.dma_start(
                    out=wg[:].rearrange("p k u c -> p (k u c)"), in_=wsc.ap()[j]), w=[wgb], dma=True)
                for hf in range(2):
                    c = 2 * j + hf
                    pg, pgb = cps[0].get()
                    pu, pub = cps[0].get()
                    for k in range(8):
                        S.op("pe", lambda e, pg=pg, wg=wg, k=k, hf=hf: e.matmul(
                            pg[:, :], lhsT=wg[:, k, 0, hf * 128:(hf + 1) * 128], rhs=xT[:, k, :], start=(k == 0), stop=(k == 7)),
                            r=[wgb, xTb], w=[pgb])
                    for k in range(8):
                        S.op("pe", lambda e, pu=pu, wg=wg, k=k, hf=hf: e.matmul(
                            pu[:, :], lhsT=wg[:, k, 1, hf * 128:(hf + 1) * 128], rhs=xT[:, k, :], start=(k == 0), stop=(k == 7)),
                            r=[wgb, xTb], w=[pub])
                    sg, sgb = SG.get()
                    S.op("act", lambda e, sg=sg, pg=pg: e.activation(out=sg[:, :], in_=pg[:, :], func=AF.Silu), r=[pgb], w=[sgb])
                    S.op("dve", lambda e, sg=sg, pu=pu, c=c: e.tensor_tensor(out=gT[:, c, :], in0=sg[:, :], in1=pu[:, :], op=ALU.mult),
                         r=[sgb, pub], w=[gTb])
            for s in range(4):
                for nh in range(2):
                    po, pob = cps[0].get()
                    for c in range(NCH):
                        S.op("pe", lambda e, po=po, c=c, s=s, nh=nh: e.matmul(
                            po[:, :], lhsT=gT[:, c, s * 128:(s + 1) * 128], rhs=wo[:, c, nh * 512:(nh + 1) * 512],
                            start=(c == 0), stop=(c == NCH - 1)), r=[gTb, wob], w=[pob])
                    S.op("dve", lambda e, po=po, s=s, nh=nh: e.scalar_tensor_tensor(
                        out=xa[:, s, nh * 512:(nh + 1) * 512], in0=po[:, :], scalar=0.5, in1=xa[:, s, nh * 512:(nh + 1) * 512],
                        op0=ALU.mult, op1=ALU.add), r=[pob, xab], w=[xab])

        def v4(p):
            return p[:, :].rearrange("p (h d) -> p h d", d=128)

        evr = [0]

        def evac(dst_fn, p, pb, wbufs, rbufs=()):
            evr[0] += 1
            if evr[0] % 2:
                S.op("dve", lambda e: e.tensor_copy(out=dst_fn(), in_=p()), r=[pb, *rbufs], w=wbufs)
            else:
                S.op("act", lambda e: e.copy(out=dst_fn(), in_=p()), r=[pb, *rbufs], w=wbufs)

        for nm, L, rot in seqs:
            OWN = (L // 8) if rot else L
            OWNB = OWN // 128
            x_d, y_d = xin[nm], yout[nm]
            NT = L // 512
            NB = L // 128
            with ExitStack() as c1:
                wo, wob = Pool(S, c1, "wo1", [128, NCH, D], BF16).get()
                wi, wib = Pool(S, c1, "wi", [128, 8, PROJ], BF16).get()
                S.op("sp", lambda e: e.dma_start(out=wo[:], in_=s_f1o.ap().rearrange("(c p) d -> p c d", p=128)), w=[wob], dma=True)
                S.op("sp", lambda e: e.dma_start(out=wi[:], in_=s_win.ap().rearrange("(k p) c -> p k c", p=128)), w=[wib], dma=True)
                load_ln(c1, [0])
                XT = Pool(S, c1, "xt", [128, 4, D], F32, n=1)
                XTT = Pool(S, c1, "xtt", [128, 8, 512], BF16, n=1)
                X1TT = Pool(S, c1, "x1tt", [128, 8, 512], BF16, n=2)
                PSA1 = SubPool(PS.t[0:4])
                PSB1 = SubPool(PS.t[4:8])
                pend1 = [None]
                GT = Pool(S, c1, "gt", [128, NCH, 512], BF16, n=1)
                WG = Pool(S, c1, "wg", [128, 8, 2, 256], BF16, n=2)
                SG = Pool(S, c1, "sg", [128, 512], BF16, n=2)
                ST = Pool(S, c1, "st", [128, 12], F32, n=2)
                MV = Pool(S, c1, "mv", [128, 8], F32, n=2)
                QTT = Pool(S, c1, "qtt", [128, 4, 512], BF16, n=1)
                KTT = Pool(S, c1, "ktt", [128, 512], BF16, n=1)
                DTT = Pool(S, c1, "dtt", [128, 512], F32, n=3)
                VXT = Pool(S, c1, "vxt", [128, 4, 130], BF16, n=1)
                ZT = Pool(S, c1, "zt", [128, 4, 512], F32, n=1)
                BAT = Pool(S, c1, "bat", [128, 4, 16], F32, n=1)
                for ti in range(NT):
                    t0 = ti * 512
                    S.begin()
                    cps[0] = PSA1
                    xt, xtb = XT.get()
                    S.op("sp", lambda e, xt=xt, t0=t0: e.dma_start(
                        out=xt[:], in_=x_d.ap()[t0:t0 + 512, :].rearrange("(s p) d -> p s d", p=128)), w=[xtb], dma=True)
                    xT, xTb = XTT.get()
                    transpose_tok(xt, xtb, xT, xTb, 4)
                    S.op("act", lambda e, xt=xt: e.mul(out=xt[:], in_=xt[:], mul=ALPHA), r=[xtb], w=[xtb])
                    ffn(xT, xTb, xt, xtb, s_f1i, wo, wob, GT, WG, SG)
                    layer_norm(xt, xtb, 0, (ST, MV))
                    S.op("pool", lambda e, xt=xt, t0=t0: e.dma_start(
                        out=X1.ap()[t0:t0 + 512, :].rearrange("(s p) d -> p s d", p=128), in_=xt[:]), r=[xtb], dma=True)
                    x1T, x1Tb = X1TT.get()
                    transpose_tok(xt, xtb, x1T, x1Tb, 4)
                    la = S.end()
                    S.begin()
                    cps[0] = PSB1
                    qtt, qttb = QTT.get()
                    for c in range(4):
                        p, pb = cps[0].get()
                        for k in range(8):
                            S.op("pe", lambda e, p=p, k=k, c=c, x1T=x1T: e.matmul(
                                p[:, :], lhsT=wi[:, k, c * 128:(c + 1) * 128],
                                rhs=x1T[:, k, :], start=(k == 0), stop=(k == 7)), r=[wib, x1Tb], w=[pb])
                        evac(lambda qtt=qtt, c=c: qtt[:, c, :], lambda p=p: p[:, :], pb, [qttb])
                    S.op("pool", lambda e, qtt=qtt, t0=t0: e.dma_start(out=QT.ap()[:, :, t0:t0 + 512], in_=qtt[:]), r=[qttb], dma=True)
                    ktt, kttb = KTT.get()
                    p, pb = cps[0].get()
                    for k in range(8):
                        S.op("pe", lambda e, p=p, k=k, x1T=x1T: e.matmul(
                            p[:, :], lhsT=wi[:, k, 512:640], rhs=x1T[:, k, :], start=(k == 0), stop=(k == 7)), r=[wib, x1Tb], w=[pb])
                    evac(lambda ktt=ktt: ktt[:, :], lambda p=p: p[:, :], pb, [kttb])
                    S.op("pool", lambda e, ktt=ktt, t0=t0: e.dma_start(out=KT.ap()[:, t0:t0 + 512], in_=ktt[:]), r=[kttb], dma=True)
                    for c in range(12):
                        p, pb = cps[0].get()
                        for k in range(8):
                            S.op("pe", lambda e, p=p, k=k, c=c, x1T=x1T: e.matmul(
                                p[:, :], lhsT=wi[:, k, 768 + c * 128:768 + (c + 1) * 128], rhs=x1T[:, k, :],
                                start=(k == 0), stop=(k == 7)), r=[wib, x1Tb], w=[pb])
                        dtt, dttb = DTT.get()
                        evac(lambda dtt=dtt: dtt[:, :], lambda p=p: p[:, :], pb, [dttb])
                        S.op("sp", lambda e, dtt=dtt, c=c, t0=t0: e.dma_start(out=DT.ap()[:, c, t0:t0 + 512], in_=dtt[:]),
                             r=[dttb], dma=True)
                    vxt, vxtb = VXT.get()
                    zt, ztb = ZT.get()
                    bat, batb = BAT.get()
                    S.op("pool", lambda e, vxt=vxt: e.memset(vxt[:], 1.0), w=[vxtb])
                    for s in range(4):
                        p, pb = cps[0].get()
                        for k in range(8):
                            S.op("pe", lambda e, p=p, k=k, s=s, x1T=x1T: e.matmul(
                                p[:, 0:128], lhsT=x1T[:, k, s * 128:(s + 1) * 128], rhs=wi[:, k, 640:768],
                                start=(k == 0), stop=(k == 7)), r=[wib, x1Tb], w=[pb])
                        evac(lambda vxt=vxt, s=s: vxt[:, s, :].rearrange("p (h c) -> p h c", h=2)[:, :, 0:64],
                             lambda p=p: p[:, 0:128].rearrange("p (h c) -> p h c", h=2), pb, [vxtb])
                        p, pb = cps[0].get()
                        for k in range(8):
                            S.op("pe", lambda e, p=p, k=k, s=s, x1T=x1T: e.matmul(
                                p[:, :], lhsT=x1T[:, k, s * 128:(s + 1) * 128], rhs=wi[:, k, 2304:2816],
                                start=(k == 0), stop=(k == 7)), r=[wib, x1Tb], w=[pb])
                        evac(lambda zt=zt, s=s: zt[:, s, :], lambda p=p: p[:, :], pb, [ztb])
                        p, pb = cps[0].get()
                        for k in range(8):
                            S.op("pe", lambda e, p=p, k=k, s=s, x1T=x1T: e.matmul(
                                p[:, 0:16], lhsT=x1T[:, k, s * 128:(s + 1) * 128], rhs=wi[:, k, 2816:2832],
                                start=(k == 0), stop=(k == 7)), r=[wib, x1Tb], w=[pb])
                        evac(lambda bat=bat, s=s: bat[:, s, :], lambda p=p: p[:, 0:16], pb, [batb])
                    S.op("pool", lambda e, vxt=vxt, t0=t0: e.dma_start(
                        out=VX.ap()[t0:t0 + 512, :].rearrange("(s p) c -> p s c", p=128), in_=vxt[:]), r=[vxtb], dma=True)
                    S.op("pool", lambda e, zt=zt, t0=t0: e.dma_start(
                        out=ZZ.ap()[t0:t0 + 512, :].rearrange("(s p) c -> p s c", p=128), in_=zt[:]), r=[ztb], dma=True)
                    S.op("pool", lambda e, bat=bat, t0=t0: e.dma_start(
                        out=BA.ap()[t0:t0 + 512, :].rearrange("(s p) c -> p s c", p=128), in_=bat[:]), r=[batb], dma=True)
                    lb = S.end()
                    S.merge([la] + ([pend1[0]] if pend1[0] else []))
                    pend1[0] = lb
                S.merge([pend1[0]])
                cps[0] = PS
                S.flush()
            if STOP == 1:
                return nc

            with ExitStack() as c2:
                abias, abiasb = Pool(S, c2, "abias", [128, 3072], F32).get()
                esink, esinkb = Pool(S, c2, "esink", [128, 8], F32).get()
                S.op("sp", lambda e: e.dma_start(out=abias[:], in_=c_abias.ap()), w=[abiasb], dma=True)
                S.op("sp", lambda e: e.dma_start(out=esink[:], in_=bc(w_sink, 0, 8)), w=[esinkb], dma=True)
                S.op("act", lambda e: e.activation(out=esink[:], in_=esink[:], func=AF.Exp), r=[esinkb], w=[esinkb])
                QB = Pool(S, c2, "qb", [128, 4, 128], BF16, n=2)
                KB = Pool(S, c2, "kb", [128, 3, 128], BF16, n=2)
                VB = Pool(S, c2, "vb", [128, 3, 130], BF16, n=2)
                TB = Pool(S, c2, "tb", [128, 512], F32, n=2)
                PT = Pool(S, c2, "pt", [128, 512], BF16, n=6)
                DEN = Pool(S, c2, "den", [128, 16], F32, n=2)
                OB = Pool(S, c2, "ob", [128, 512], F32, n=2)
                for i in range(OWNB):
                    kbs = [kb for kb in range(3) if (rot or 0 <= i + kb - 1 < NB)]
                    lo, hi = kbs[0], kbs[-1] + 1
                    qb, qbb = QB.get()
                    kbt, kbb = KB.get()
                    vb, vbb = VB.get()
                    S.op("sp", lambda e, qb=qb, i=i: e.dma_start(out=qb[:], in_=QT.ap()[:, :, i * 128:(i + 1) * 128]), w=[qbb], dma=True)
                    if rot and (i == 0 or i == OWNB - 1):
                        for kb in range(3):
                            bi = (i + kb - 1) % NB
                            S.op("sp", lambda e, kbt=kbt, kb=kb, bi=bi: e.dma_start(
                                out=kbt[:, kb, :], in_=KT.ap()[:, bi * 128:(bi + 1) * 128]), w=[kbb], dma=True)
                            S.op("sp", lambda e, vb=vb, kb=kb, bi=bi: e.dma_start(
                                out=vb[:, kb, :], in_=VX.ap()[bi * 128:(bi + 1) * 128, :]), w=[vbb], dma=True)
                        hk, fi = (0, 7) if i == 0 else (2, 0)
                        S.op("act", lambda e, vb=vb, hk=hk, fi=fi: e.activation(
                            out=vb[:, hk, :], in_=vb[:, hk, :], func=AF.Copy, scale=wf[:, fi:fi + 1]), r=[vbb, wfb], w=[vbb])
                        lo, hi = 0, 0
                    if hi > lo:
                        S.op("sp", lambda e, kbt=kbt, i=i, lo=lo, hi=hi: e.dma_start(
                            out=kbt[:, lo:hi, :],
                            in_=KT.ap()[:, (i + lo - 1) * 128:(i + hi - 1) * 128].rearrange("p (b t) -> p b t", t=128)), w=[kbb], dma=True)
                        S.op("sp", lambda e, vb=vb, i=i, lo=lo, hi=hi: e.dma_start(
                            out=vb[:, lo:hi, :],
                            in_=VX.ap()[(i + lo - 1) * 128:(i + hi - 1) * 128, :].rearrange("(b p) c -> p b c", p=128)), w=[vbb], dma=True)
                    pos = []
                    for kvh in range(2):
                        po, pob = PS.get()
                        pos.append((po, pob))
                        b0 = kvh * 64
                        pts = []
                        for kb in kbs:
                            ps_, psb = PS.get()
                            S.op("pe", lambda e, ps_=ps_, kbt=kbt, qb=qb, kb=kb, b0=b0: e.matmul(
                                ps_[:, :], lhsT=kbt[b0:b0 + 64, kb, :], rhs=qb[b0:b0 + 64, :, :].rearrange("p c t -> p (c t)"), start=True, stop=True),
                                r=[kbb, qbb], w=[psb])
                            tb, tbb = TB.get()
                            off = (kb * 2 + kvh) * 512
                            S.op("dve", lambda e, tb=tb, ps_=ps_, off=off: e.scalar_tensor_tensor(
                                out=tb[:, :], in0=ps_[:, :], scalar=0.125, in1=abias[:, off:off + 512], op0=ALU.mult, op1=ALU.add),
                                r=[psb, abiasb], w=[tbb])
                            pt, ptb = PT.get()
                            S.op("act", lambda e, tb=tb, pt=pt: e.activation(out=pt[:, :], in_=tb[:, :], func=AF.Exp), r=[tbb], w=[ptb])
                            pts.append((kb, pt, ptb))
                        for g in range(4):
                            for kb, pt, ptb in pts:
                                S.op("pe", lambda e, po=po, pt=pt, vb=vb, g=g, kb=kb, kvh=kvh, st_=(kb == kbs[0]), sp_=(kb == kbs[-1]): e.matmul(
                                    po[:, g * 65:(g + 1) * 65], lhsT=pt[:, g * 128:(g + 1) * 128], rhs=vb[:, kb, kvh * 65:(kvh + 1) * 65],
                                    start=st_, stop=sp_), r=[ptb, vbb], w=[pob])
                    den, denb = DEN.get()
                    ob, obb = OB.get()
                    for kvh in range(2):
                        po, pob = pos[kvh]
                        S.op("dve", lambda e, den=den, po=po, kvh=kvh: e.tensor_tensor(
                            out=den[:, kvh * 4:(kvh + 1) * 4], in0=po[:, 0:260].rearrange("p (g c) -> p g c", c=65)[:, :, 64],
                            in1=esink[:, kvh * 4:(kvh + 1) * 4], op=ALU.add), r=[pob, esinkb], w=[denb])
                    S.op("dve", lambda e, den=den: e.reciprocal(out=den[:, 8:16], in_=den[:, 0:8]), r=[denb], w=[denb])
                    for kvh in range(2):
                        po, pob = pos[kvh]
                        for g in range(4):
                            h = kvh * 4 + g
                            S.op("dve", lambda e, ob=ob, po=po, den=den, g=g, h=h: e.tensor_scalar(
                                out=ob[:, h * 64:(h + 1) * 64], in0=po[:, g * 65:g * 65 + 64], scalar1=den[:, 8 + h:9 + h], scalar2=None,
                                op0=ALU.mult), r=[pob, denb], w=[obb])
                    S.op("pool", lambda e, ob=ob, i=i: e.dma_start(out=OA.ap()[i * 128:(i + 1) * 128, :], in_=ob[:]), r=[obb], dma=True)
                S.flush()
            if STOP == 2:
                return nc

            with ExitStack() as cf:
                cw, cwb = Pool(S, cf, "cw", [128, 12, 5], F32).get()
                for c in range(12):
                    S.op("sp", lambda e, c=c: e.dma_start(out=cw[:, c, :], in_=w_cv.ap()[:, c * 128:(c + 1) * 128].rearrange("j p -> p j"),
                                                          allow_slow_non_contiguous=True), w=[cwb], dma=True)
                XIN = Pool(S, cf, "xin", [128, 12, 132], F32, n=4)
                CA = Pool(S, cf, "ca", [128, 12, 128], F32, n=4)
                SL = Pool(S, cf, "sl", [128, 12, 128], F32, n=4)
                TM = Pool(S, cf, "tm", [128, 12, 128], F32, n=4)
                SS = Pool(S, cf, "ss", [128, 16], F32, n=4)
                JK = Pool(S, cf, "jk", [128, 8, 128], F32, n=4)
                NRM = Pool(S, cf, "nrm", [128, 8, 128], F32, n=4)
                VBF = Pool(S, cf, "vbf", [128, 4, 128], BF16, n=4)
                QKT = Pool(S, cf, "qkt", [128, 8, 128], BF16, n=4)
                PSFS = [(SubPool(PS.t[0:2]), SubPool(PS.t[4:6])), (SubPool(PS.t[2:4]), SubPool(PS.t[6:8]))]
                def fe_tile(ti, PSF1, PSF2):
                    t0 = ti * 128
                    own = (not rot) or ti < OWNB
                    S.begin()
                    xi, xib = XIN.get()
                    a0 = max(t0 - 2, 0)
                    a1 = min(t0 + 130, L)
                    if (a0 != t0 - 2 or a1 != t0 + 130) and not rot:
                        S.op("pool", lambda e: e.memset(xi[:], 0.0), w=[xib])
                    S.op("sp", lambda e: e.dma_start(out=xi[:, :, a0 - (t0 - 2):a1 - (t0 - 2)], in_=DT.ap()[:, :, a0:a1]), w=[xib], dma=True)
                    if rot:
                        if t0 == 0:
                            S.op("sp", lambda e: e.dma_start(out=xi[:, :, 0:2], in_=DT.ap()[:, :, L - 2:L]), w=[xib], dma=True)
                        if t0 + 128 == L:
                            S.op("sp", lambda e: e.dma_start(out=xi[:, :, 130:132], in_=DT.ap()[:, :, 0:2]), w=[xib], dma=True)
                        if ti % OWNB == 0:
                            fi = (ti // OWNB - 1) % 8
                            S.op("act", lambda e: e.activation(out=xi[:, :, 0:2], in_=xi[:, :, 0:2], func=AF.Copy, scale=wf[:, fi:fi + 1]),
                                 r=[xib, wfb], w=[xib])
                        if ti % OWNB == OWNB - 1:
                            fi2 = ti // OWNB
                            S.op("act", lambda e: e.activation(out=xi[:, :, 130:132], in_=xi[:, :, 130:132], func=AF.Copy, scale=wf[:, fi2:fi2 + 1]),
                                 r=[xib, wfb], w=[xib])
                    ca, cab = CA.get()
                    for j in range(5):
                        for c in range(12):
                            if j == 0:
                                S.op("act", lambda e, c=c: e.activation(
                                    out=ca[:, c, :], in_=xi[:, c, 0:128], func=AF.Copy, scale=cw[:, c, 0:1]),
                                    r=[xib, cwb], w=[cab])
                            else:
                                S.op("dve", lambda e, c=c, j=j: e.scalar_tensor_tensor(
                                    out=ca[:, c, :], in0=xi[:, c, j:j + 128], scalar=cw[:, c, j:j + 1], in1=ca[:, c, :],
                                    op0=ALU.mult, op1=ALU.add), r=[xib, cwb, cab], w=[cab])
                    sl, slb = SL.get()
                    S.op("act", lambda e: e.activation(out=sl[:], in_=ca[:], func=AF.Silu), r=[cab], w=[slb])
                    tm, tmb = TM.get()
                    for q4 in range(3):
                        p, pb = PSF1.get()
                        for h in range(4):
                            S.op("pe", lambda e, p=p, q4=q4, h=h: e.transpose(
                                out=p[:, h * 128:(h + 1) * 128], in_=sl[:, q4 * 4 + h, :], identity=ident[:]), r=[slb, identb], w=[pb])
                        evac(lambda q4=q4: tm[:, q4 * 4:(q4 + 1) * 4, :], lambda p=p: v4(p), pb, [tmb])
                    ss, ssb = SS.get()
                    jk, jkb = JK.get()
                    S.op("act", lambda e: e.activation(out=jk[:], in_=tm[:, 0:8, :], func=AF.Square), r=[tmb], w=[jkb])
                    S.op("dve", lambda e: e.tensor_reduce(out=ss[:, 0:8], in_=jk[:], axis=AX.X, op=ALU.add), r=[jkb], w=[ssb])
                    S.op("act", lambda e: e.activation(out=ss[:, 8:16], in_=ss[:, 0:8], func=AF.Ln, bias=RMS_EPS, scale=1.0), r=[ssb], w=[ssb])
                    S.op("act", lambda e: e.activation(out=ss[:, 8:16], in_=ss[:, 8:16], func=AF.Exp, scale=-0.5), r=[ssb], w=[ssb])
                    S.op("dve", lambda e: e.tensor_scalar(out=ss[:, 8:12], in0=ss[:, 8:12], scalar1=128.0 ** -0.5, scalar2=None, op0=ALU.mult),
                         r=[ssb], w=[ssb])
                    l1 = S.end()
                    S.begin()
                    nrm, nrmb = NRM.get()
                    for idx in range(8):
                        if idx % 2:
                            S.op("dve", lambda e, idx=idx: e.tensor_scalar(
                                out=nrm[:, idx, :], in0=tm[:, idx, :], scalar1=ss[:, 8 + idx:9 + idx], scalar2=None, op0=ALU.mult),
                                r=[tmb, ssb], w=[nrmb])
                        else:
                            S.op("act", lambda e, idx=idx: e.activation(
                                out=nrm[:, idx, :], in_=tm[:, idx, :], func=AF.Copy, scale=ss[:, 8 + idx:9 + idx]),
                                r=[tmb, ssb], w=[nrmb])
                    vbf, vbfb = VBF.get()
                    S.op("act", lambda e: e.copy(out=vbf[:], in_=tm[:, 8:12, :]), r=[tmb], w=[vbfb])
                    qkt, qktb = QKT.get()
                    for q4 in ((0, 1) if own else (1,)):
                        p, pb = PSF2.get()
                        for h in range(4):
                            S.op("pe", lambda e, p=p, q4=q4, h=h: e.transpose(
                                out=p[:, h * 128:(h + 1) * 128], in_=nrm[:, q4 * 4 + h, :], identity=ident[:]), r=[nrmb, identb], w=[pb])
                        evac(lambda q4=q4: qkt[:, q4 * 4:(q4 + 1) * 4, :], lambda p=p: v4(p), pb, [qktb])
                    S.op("pool", lambda e: e.dma_start(out=FEK.ap()[ti], in_=nrm[:, 4:8, :].rearrange("p h d -> p (h d)")), r=[nrmb], dma=True)
                    S.op("pool", lambda e: e.dma_start(out=FEV.ap()[ti], in_=vbf[:].rearrange("p h d -> p (h d)")), r=[vbfb], dma=True)
                    if own:
                        S.op("pool", lambda e: e.dma_start(out=FEQ.ap()[ti], in_=qkt[:].rearrange("p h d -> p (h d)")), r=[qktb], dma=True)
                    else:
                        S.op("pool", lambda e: e.dma_start(out=FEQ.ap()[ti, :, 512:1024], in_=qkt[:, 4:8, :].rearrange("p h d -> p (h d)")),
                             r=[qktb], dma=True)
                    l2 = S.end()
                    return l1, l2

                pendf = []
                for tp in range(0, NB, 2):
                    cur1, cur2 = [], []
                    for q_ in range(2):
                        if tp + q_ < NB:
                            l1, l2 = fe_tile(tp + q_, *PSFS[q_])
                            cur1.append(l1)
                            cur2.append(l2)
                    S.merge(cur1 + pendf)
                    pendf = cur2
                S.merge(pendf)
                S.flush()

            for dr in range(2):
                with ExitStack() as c3:
                    dmk, dmkb = Pool(S, c3, "dmk", [128, 5, 128], F32).get()
                    strict4, strict4b = Pool(S, c3, "strict4", [128, 4, 128], F32).get()
                    eye4, eye4b = Pool(S, c3, "eye4", [128, 4, 128], F32).get()
                    cw, cwb = Pool(S, c3, "cw", [128, 12, 5], F32).get()
                    gpar, gparb = Pool(S, c3, "gpar", [128, 8], F32).get()
                    S.op("sp", lambda e: e.dma_start(out=dmk[:], in_=c_dmask.ap()[:, dr * 640:(dr + 1) * 640].rearrange(
                        "p (m i) -> p m i", i=128)), w=[dmkb], dma=True)
                    for h in range(4):
                        S.op("sp", lambda e, h=h: e.dma_start(out=strict4[:, h, :], in_=c_dmask.ap()[:, dr * 640 + 384:dr * 640 + 512]),
                             w=[strict4b], dma=True)
                        S.op("sp", lambda e, h=h: e.dma_start(out=eye4[:, h, :], in_=c_ident.ap()), w=[eye4b], dma=True)
                    for c in range(12):
                        S.op("sp", lambda e, c=c: e.dma_start(out=cw[:, c, :], in_=w_cv.ap()[:, c * 128:(c + 1) * 128].rearrange("j p -> p j"),
                                                              allow_slow_non_contiguous=True), w=[cwb], dma=True)
                    S.op("sp", lambda e: e.dma_start(out=gpar[:, 0:4], in_=bc(w_alog, dr * 4, 4)), w=[gparb], dma=True)
                    S.op("sp", lambda e: e.dma_start(out=gpar[:, 4:8], in_=bc(w_dtb, dr * 4, 4)), w=[gparb], dma=True)
                    S.op("act", lambda e: e.activation(out=gpar[:, 0:4], in_=gpar[:, 0:4], func=AF.Exp), r=[gparb], w=[gparb])
                    S.op("dve", lambda e: e.tensor_scalar(out=gpar[:, 0:4], in0=gpar[:, 0:4], scalar1=-1.0, scalar2=None, op0=ALU.mult),
                         r=[gparb], w=[gparb])
                    U = lambda: dmk[:, 0, :]
                    BONES = lambda: dmk[:, 1, :]
                    MINC = lambda: dmk[:, 2, :]
                    PSA = SubPool(PS.t[0:3])
                    PSB = SubPool(PS.t[3:5])
                    PSC = SubPool(PS.t[5:8])
                    XIN = Pool(S, c3, "xin", [128, 12, 132], F32, n=2)
                    BAI = Pool(S, c3, "bai", [128, 16], F32, n=2)
                    CA = Pool(S, c3, "ca", [128, 12, 128], F32, n=1)
                    SL = Pool(S, c3, "sl", [128, 12, 128], F32, n=1)
                    TM = Pool(S, c3, "tm", [128, 12, 128], F32, n=1)
                    SS = Pool(S, c3, "ss", [128, 16], F32, n=2)
                    JK = Pool(S, c3, "jk", [128, 8, 128], F32, n=1)
                    NRM = Pool(S, c3, "nrm", [128, 8, 128], F32, n=2)
                    QKT = Pool(S, c3, "qkt", [128, 8, 128], BF16, n=2)
                    GU = Pool(S, c3, "gu", [128, 4, 128], F32, n=1)
                    DTMP = Pool(S, c3, "dtmp", [128, 4, 128], F32, n=1)
                    DCY = Pool(S, c3, "dcy", [128, 4, 128], F32, n=1)
                    GS = Pool(S, c3, "gs", [128, 32], F32, n=3)
                    EROW = Pool(S, c3, "erow", [128, 4, 128], F32, n=3)
                    ATT = Pool(S, c3, "att", [128, 4, 128], BF16, n=3)
                    KD = Pool(S, c3, "kd", [128, 4, 128], BF16, n=3)
                    QD = Pool(S, c3, "qd", [128, 4, 128], BF16, n=3)
                    XM = Pool(S, c3, "xm", [128, 4, 128], F32, n=2)
                    VBF = Pool(S, c3, "vbf", [128, 4, 128], BF16, n=2)
                    KG = Pool(S, c3, "kg", [128, 4, 128], BF16, n=2)
                    PK = Pool(S, c3, "pk", [128, 4, 128], BF16, n=2)
                    PKT = Pool(S, c3, "pkt", [128, 4, 128], BF16, n=2)
                    RF = Pool(S, c3, "rf", [128, 4, 128], F32, n=2)
                    RB = Pool(S, c3, "rbb", [128, 4, 128], BF16, n=2)
                    UB = Pool(S, c3, "ub", [128, 4, 128], F32, n=2)
                    WT = Pool(S, c3, "wt", [128, 4, 128], BF16, n=2)
                    VN = Pool(S, c3, "vn", [128, 4, 128], BF16, n=2)
                    OT = Pool(S, c3, "ot", [128, 4, 128], F32, n=2)
                    SF = Pool(S, c3, "sf", [128, 4, 128], F32, n=2)
                    SB_ = Pool(S, c3, "sbs", [128, 4, 128], BF16, n=2)
                    st8 = {}
                    st8["sf"], st8["sfb"] = SF.get()
                    st8["sb"], st8["sbb"] = SB_.get()
                    S.op("pool", lambda e: e.memset(st8["sf"][:], 0.0), w=[st8["sfb"]])
                    S.op("pool", lambda e: e.memset(st8["sb"][:], 0.0), w=[st8["sbb"]])
                    S.flush()

                    def stage_fe(ti):
                        T = {}
                        t0 = ti * 128
                        own = (not rot) or ti < OWNB
                        bai, baib = BAI.get()
                        S.op("sp", lambda e: e.dma_start(out=bai[:], in_=BA.ap()[t0:t0 + 128, :]), w=[baib], dma=True)
                        nrm, nrmb = NRM.get()
                        vbf, vbfb = VBF.get()
                        qkt, qktb = QKT.get()
                        S.op("sp", lambda e: e.dma_start(out=nrm[:, 4:8, :].rearrange("p h d -> p (h d)"), in_=FEK.ap()[ti]), w=[nrmb], dma=True)
                        S.op("sp", lambda e: e.dma_start(out=vbf[:].rearrange("p h d -> p (h d)"), in_=FEV.ap()[ti]), w=[vbfb], dma=True)
                        if own:
                            S.op("sp", lambda e: e.dma_start(out=qkt[:].rearrange("p h d -> p (h d)"), in_=FEQ.ap()[ti]), w=[qktb], dma=True)
                        else:
                            S.op("sp", lambda e: e.dma_start(out=qkt[:, 4:8, :].rearrange("p h d -> p (h d)"), in_=FEQ.ap()[ti, :, 512:1024]),
                                 w=[qktb], dma=True)
                        gs, gsb = GS.get()
                        S.op("act", lambda e: e.activation(out=gs[:, 0:4], in_=bai[:, dr * 4:dr * 4 + 4], func=AF.Sigmoid), r=[baib], w=[gsb])
                        S.op("dve", lambda e: e.tensor_scalar(out=gs[:, 4:8], in0=gs[:, 0:4], scalar1=-1.0, scalar2=None, op0=ALU.mult), r=[gsb], w=[gsb])
                        S.op("dve", lambda e: e.tensor_tensor(out=gs[:, 8:12], in0=bai[:, 8 + dr * 4:12 + dr * 4], in1=gpar[:, 4:8], op=ALU.add),
                             r=[baib, gparb], w=[gsb])
                        S.op("act", lambda e: e.activation(out=gs[:, 8:12], in_=gs[:, 8:12], func=AF.Exp), r=[gsb], w=[gsb])
                        S.op("act", lambda e: e.activation(out=gs[:, 8:12], in_=gs[:, 8:12], func=AF.Ln, bias=1.0, scale=1.0), r=[gsb], w=[gsb])
                        S.op("dve", lambda e: e.tensor_tensor(out=gs[:, 8:12], in0=gs[:, 8:12], in1=gpar[:, 0:4], op=ALU.mult), r=[gsb, gparb], w=[gsb])
                        p, pb = PSA.get()
                        S.op("pe", lambda e, p=p: e.matmul(p[:, 0:4], lhsT=U(), rhs=gs[:, 8:12], start=True, stop=True), r=[dmkb, gsb], w=[pb])
                        S.op("pe", lambda e, p=p: e.matmul(p[:, 4:8], lhsT=BONES(), rhs=gs[:, 8:12], start=True, stop=True), r=[dmkb, gsb], w=[pb])
                        S.op("dve", lambda e, p=p: e.tensor_copy(out=gs[:, 12:16], in_=p[:, 0:4]), r=[pb], w=[gsb])
                        S.op("dve", lambda e, p=p: e.tensor_scalar(out=gs[:, 16:20], in0=p[:, 0:4], scalar1=-1.0, scalar2=None, op0=ALU.mult), r=[pb], w=[gsb])
                        S.op("dve", lambda e, p=p: e.tensor_tensor(out=gs[:, 24:28], in0=p[:, 4:8], in1=gs[:, 12:16], op=ALU.subtract), r=[pb, gsb], w=[gsb])
                        S.op("act", lambda e: e.activation(out=gs[:, 20:24], in_=gs[:, 12:16], func=AF.Exp), r=[gsb], w=[gsb])
                        S.op("act", lambda e: e.activation(out=gs[:, 24:28], in_=gs[:, 24:28], func=AF.Exp), r=[gsb], w=[gsb])
                        gu, gub = GU.get()
                        for h in range(4):
                            S.op("act", lambda e, h=h: e.activation(
                                out=gu[:, h, :], in_=U(), func=AF.Copy, scale=gs[:, 8 + h:9 + h]), r=[dmkb, gsb], w=[gub])
                        pg, pgb = PSA.get()
                        for h in range(4):
                            S.op("pe", lambda e, h=h: e.matmul(pg[:, h * 128:(h + 1) * 128], lhsT=ones[:], rhs=gu[:, h, :], start=True, stop=True),
                                 r=[onesb, gub], w=[pgb])
                        erow, erowb = EROW.get()
                        S.op("act", lambda e: e.activation(out=erow[:], in_=v4(pg), func=AF.Exp), r=[pgb], w=[erowb])
                        dtmp, dtmpb = DTMP.get()
                        for h in range(4):
                            S.op("dve", lambda e, h=h: e.scalar_tensor_tensor(
                                out=dtmp[:, h, :], in0=pg[:, h * 128:(h + 1) * 128], scalar=gs[:, 16 + h:17 + h], in1=MINC(),
                                op0=ALU.add, op1=ALU.add), r=[pgb, gsb, dmkb], w=[dtmpb])
                        dcy, dcyb = DCY.get()
                        S.op("act", lambda e: e.activation(out=dcy[:], in_=dtmp[:], func=AF.Exp), r=[dtmpb], w=[dcyb])
                        pkk, pkkb = PSA.get()
                        for h in range(4):
                            S.op("pe", lambda e, h=h: e.matmul(pkk[:, h * 128:(h + 1) * 128], lhsT=qkt[:, 4 + h, :], rhs=qkt[:, 4 + h, :],
                                                               start=True, stop=True), r=[qktb], w=[pkkb])
                        if own:
                            pqk, pqkb = PSA.get()
                            for h in range(4):
                                S.op("pe", lambda e, h=h: e.matmul(pqk[:, h * 128:(h + 1) * 128], lhsT=qkt[:, 4 + h, :], rhs=qkt[:, h, :],
                                                                   start=True, stop=True), r=[qktb], w=[pqkb])
                        xm, xmb = XM.get()
                        for h in range(4):
                            S.op("dve", lambda e, h=h: e.scalar_tensor_tensor(
                                out=xm[:, h, :], in0=pkk[:, h * 128:(h + 1) * 128], scalar=gs[:, 4 + h:5 + h], in1=dcy[:, h, :],
                                op0=ALU.mult, op1=ALU.mult), r=[pkkb, gsb, dcyb], w=[xmb])
                        S.op("dve", lambda e: e.tensor_tensor(out=xm[:], in0=xm[:], in1=strict4[:], op=ALU.mult), r=[xmb, strict4b], w=[xmb])
                        att, attb = ATT.get()
                        if own:
                            S.op("dve", lambda e: e.tensor_tensor(out=att[:], in0=v4(pqk), in1=dcy[:], op=ALU.mult), r=[pqkb, dcyb], w=[attb])
                        kg, kgb = KG.get()
                        kd, kdb = KD.get()
                        for h in range(4):
                            S.op("act", lambda e, h=h: e.activation(
                                out=kg[:, h, :], in_=nrm[:, 4 + h, :], func=AF.Copy, scale=gs[:, 20 + h:21 + h]),
                                r=[nrmb, gsb], w=[kgb])
                            S.op("act", lambda e, h=h: e.activation(
                                out=kd[:, h, :], in_=nrm[:, 4 + h, :], func=AF.Copy, scale=gs[:, 24 + h:25 + h]),
                                r=[nrmb, gsb], w=[kdb])
                        qd, qdb = QD.get()
                        if own:
                            S.op("dve", lambda e: e.tensor_tensor(out=qd[:], in0=qkt[:, 0:4, :], in1=erow[:], op=ALU.mult), r=[qktb, erowb], w=[qdb])
                        T.update(own=own, ti=ti, t0=t0, xm=xm, xmb=xmb, vbf=vbf, vbfb=vbfb, kg=kg, kgb=kgb, gs=gs, gsb=gsb, att=att, attb=attb,
                                 kd=kd, kdb=kdb, qd=qd, qdb=qdb, erow=erow, erowb=erowb)
                        return T

                    def stage_inv(T):
                        xm, xmb, gs, gsb = T["xm"], T["xmb"], T["gs"], T["gsb"]
                        vbf, vbfb, kg, kgb = T["vbf"], T["vbfb"], T["kg"], T["kgb"]
                        pk, pkb_ = PK.get()
                        S.op("act", lambda e, pk=pk: e.copy(out=pk[:], in_=xm[:]), r=[xmb], w=[pkb_])
                        ptp, ptpb = PSB.get()
                        for h in range(4):
                            S.op("pe", lambda e, h=h: e.transpose(out=ptp[:, h * 128:(h + 1) * 128], in_=xm[:, h, :], identity=ident[:]),
                                 r=[xmb, identb], w=[ptpb])
                        pkt, pktb = PKT.get()
                        evac(lambda pkt=pkt: pkt[:], lambda: v4(ptp), ptpb, [pktb])
                        rf, rfb = RF.get()
                        rb, rbb = RB.get()
                        S.op("dve", lambda e, rf=rf: e.tensor_tensor(out=rf[:], in0=xm[:], in1=eye4[:], op=ALU.add), r=[xmb, eye4b], w=[rfb])
                        S.op("act", lambda e, rb=rb, rf=rf: e.copy(out=rb[:], in_=rf[:]), r=[rfb], w=[rbb])
                        for rnd in range(5):
                            pa, pab = PSB.get()
                            for h in range(4):
                                S.op("pe", lambda e, pa=pa, pk=pk, pkt=pkt, h=h: e.matmul(pa[:, h * 128:(h + 1) * 128], lhsT=pk[:, h, :], rhs=pkt[:, h, :],
                                                                                         start=True, stop=True), r=[pkb_, pktb], w=[pab])
                            if rnd < 4:
                                pb2, pb2b = PSB.get()
                                for h in range(4):
                                    S.op("pe", lambda e, pb2=pb2, pk=pk, pkt=pkt, h=h: e.matmul(pb2[:, h * 128:(h + 1) * 128], lhsT=pkt[:, h, :],
                                                                                               rhs=pk[:, h, :], start=True, stop=True),
                                         r=[pkb_, pktb], w=[pb2b])
                            pktn, pktnb = PKT.get()
                            S.op("act", lambda e, pktn=pktn, pa=pa: e.copy(out=pktn[:], in_=v4(pa)), r=[pab], w=[pktnb])
                            if rnd < 4:
                                pkn, pknb = PK.get()
                                S.op("dve", lambda e, pkn=pkn, pb2=pb2: e.tensor_copy(out=pkn[:], in_=v4(pb2)), r=[pb2b], w=[pknb])
                                pk, pkb_ = pkn, pknb
                            pkt, pktb = pktn, pktnb
                            pr, prb = PSB.get()
                            for h in range(4):
                                S.op("pe", lambda e, pr=pr, pkt=pkt, rb=rb, h=h: e.matmul(pr[:, h * 128:(h + 1) * 128], lhsT=pkt[:, h, :], rhs=rb[:, h, :],
                                                                                         start=True, stop=True), r=[pktb, rbb], w=[prb])
                            rfn, rfnb = RF.get()
                            rbn, rbnb = RB.get()
                            S.op("dve", lambda e, rbn=rbn, rf=rf, pr=pr: e.tensor_tensor(out=rbn[:], in0=v4(pr), in1=rf[:], op=ALU.add),
                                 r=[prb, rfb], w=[rbnb])
                            if rnd < 4:
                                S.op("dve", lambda e, rfn=rfn, rf=rf, pr=pr: e.tensor_tensor(out=rfn[:], in0=v4(pr), in1=rf[:], op=ALU.add),
                                     r=[prb, rfb], w=[rfnb])
                            rf, rfb, rb, rbb = rfn, rfnb, rbn, rbnb
                        pu, pub = PSB.get()
                        pw, pwb = PSB.get()
                        for h in range(4):
                            S.op("pe", lambda e, h=h, rb=rb: e.matmul(pu[:, h * 128:(h + 1) * 128], lhsT=rb[:, h, :], rhs=vbf[:, h, :], start=True, stop=True),
                                 r=[rbb, vbfb], w=[pub])
                        for h in range(4):
                            S.op("pe", lambda e, h=h, rb=rb: e.matmul(pw[:, h * 128:(h + 1) * 128], lhsT=kg[:, h, :], rhs=rb[:, h, :], start=True, stop=True),
                                 r=[rbb, kgb], w=[pwb])
                        ub, ubb = UB.get()
                        for h in range(4):
                            S.op("dve", lambda e, h=h: e.tensor_scalar(
                                out=ub[:, h, :], in0=pu[:, h * 128:(h + 1) * 128], scalar1=gs[:, h:h + 1], scalar2=None, op0=ALU.mult),
                                r=[pub, gsb], w=[ubb])
                        wt, wtb = WT.get()
                        S.op("act", lambda e: e.copy(out=wt[:], in_=v4(pw)), r=[pwb], w=[wtb])
                        T.update(ub=ub, ubb=ubb, wt=wt, wtb=wtb)

                    def stage_scan(T):
                        gs, gsb, att, attb, kd, kdb, qd, qdb = T["gs"], T["gsb"], T["att"], T["attb"], T["kd"], T["kdb"], T["qd"], T["qdb"]
                        erow, erowb, ub, ubb, wt, wtb, t0 = T["erow"], T["erowb"], T["ub"], T["ubb"], T["wt"], T["wtb"], T["t0"]
                        own, ti = T["own"], T["ti"]
                        if own:
                            ot, otb = OT.get()
                        if rot:
                            fi = None
                            if dr == 0 and ti % OWNB == 0 and ti != OWNB:
                                fi = (ti // OWNB - 1) % 8
                            if dr == 1 and ti % OWNB == OWNB - 1 and ti != NB - 1:
                                fi = ti // OWNB
                            if fi is not None:
                                sf0, sf0b, sb0, sb0b = st8["sf"], st8["sfb"], st8["sb"], st8["sbb"]
                                S.op("dve", lambda e, sf0=sf0, fi=fi: e.tensor_scalar(out=sf0[:], in0=sf0[:], scalar1=wf[:, fi:fi + 1], scalar2=None,
                                                                                      op0=ALU.mult), r=[sf0b, wfb], w=[sf0b])
                                S.op("dve", lambda e, sb0=sb0, fi=fi: e.tensor_scalar(out=sb0[:], in0=sb0[:], scalar1=wf[:, fi:fi + 1], scalar2=None,
                                                                                      op0=ALU.mult), r=[sb0b, wfb], w=[sb0b])
                        for ck in ((0, 1) if dr == 0 else (1, 0)):
                            po_ = ck * 64
                            lastcol = (po_ + 63) if dr == 0 else po_
                            sf, sfb, sb, sbb = st8["sf"], st8["sfb"], st8["sb"], st8["sbb"]
                            pws, pwsb = PSC.get()
                            for h in range(4):
                                S.op("pe", lambda e, pws=pws, sb=sb, h=h: e.matmul(pws[:, h * 128:(h + 1) * 128], lhsT=wt[:, h, :], rhs=sb[:, h, :],
                                                                                   start=True, stop=True), r=[wtb, sbb], w=[pwsb])
                            vn, vnb = VN.get()
                            for h in range(4):
                                S.op("dve", lambda e, vn=vn, pws=pws, h=h, po_=po_: e.scalar_tensor_tensor(
                                    out=vn[po_:po_ + 64, h, :], in0=pws[po_:po_ + 64, h * 128:(h + 1) * 128], scalar=gs[po_:po_ + 64, 4 + h:5 + h],
                                    in1=ub[po_:po_ + 64, h, :], op0=ALU.mult, op1=ALU.add), r=[pwsb, gsb, ubb], w=[vnb])
                            pD, pDb = PSC.get()
                            for h in range(4):
                                S.op("pe", lambda e, pD=pD, vn=vn, h=h, po_=po_: e.matmul(
                                    pD[:, h * 128:(h + 1) * 128], lhsT=kd[po_:po_ + 64, h, :], rhs=vn[po_:po_ + 64, h, :], start=True, stop=True),
                                    r=[kdb, vnb], w=[pDb])
                            if own:
                                pO, pOb = PSC.get()
                                for h in range(4):
                                    S.op("pe", lambda e, pO=pO, sb=sb, h=h: e.matmul(pO[:, h * 128:(h + 1) * 128], lhsT=qd[:, h, :], rhs=sb[:, h, :],
                                                                                     start=True, stop=False), r=[qdb, sbb], w=[pOb])
                                    S.op("pe", lambda e, pO=pO, vn=vn, h=h, po_=po_: e.matmul(
                                        pO[:, h * 128:(h + 1) * 128], lhsT=att[po_:po_ + 64, h, :], rhs=vn[po_:po_ + 64, h, :], start=False, stop=True),
                                        r=[attb, vnb], w=[pOb])
                            sfn, sfnb = SF.get()
                            sbn, sbnb = SB_.get()
                            for h in range(4):
                                S.op("dve", lambda e, sbn=sbn, sf=sf, pD=pD, h=h, lastcol=lastcol: e.scalar_tensor_tensor(
                                    out=sbn[:, h, :], in0=sf[:, h, :], scalar=erow[:, h, lastcol:lastcol + 1], in1=pD[:, h * 128:(h + 1) * 128],
                                    op0=ALU.mult, op1=ALU.add), r=[sfb, erowb, pDb], w=[sbnb])
                            for h in range(4):
                                S.op("dve", lambda e, sfn=sfn, sf=sf, pD=pD, h=h, lastcol=lastcol: e.scalar_tensor_tensor(
                                    out=sfn[:, h, :], in0=sf[:, h, :], scalar=erow[:, h, lastcol:lastcol + 1], in1=pD[:, h * 128:(h + 1) * 128],
                                    op0=ALU.mult, op1=ALU.add), r=[sfb, erowb, pDb], w=[sfnb])
                            if own:
                                S.op("act", lambda e, pO=pO, po_=po_: e.copy(out=ot[po_:po_ + 64, :, :], in_=v4(pO)[po_:po_ + 64]), r=[pOb], w=[otb])
                            st8["sf"], st8["sfb"], st8["sb"], st8["sbb"] = sfn, sfnb, sbn, sbnb
                        if own:
                            S.op("pool", lambda e: e.dma_start(out=ODN[dr].ap()[t0:t0 + 128, :], in_=ot[:].rearrange("p h d -> p (h d)")),
                                 r=[otb], dma=True)

                    if rot and dr == 0:
                        order = list(range(OWNB, NB)) + list(range(OWNB))
                    else:
                        order = list(range(NB)) if dr == 0 else list(range(NB - 1, -1, -1))
                    ctxs = {}
                    for step in range(NB + 2):
                        lists = []
                        if step < NB:
                            S.begin()
                            ctxs[step] = stage_fe(order[step])
                            lists.append(S.end())
                        if 1 <= step < NB + 1:
                            S.begin()
                            stage_inv(ctxs[step - 1])
                            lists.append(S.end())
                        if step >= 2:
                            S.begin()
                            stage_scan(ctxs.pop(step - 2))
                            lists.append(S.end())
                        S.merge(lists)
                    S.flush()
                if STOP == 3:
                    return nc

            with ExitStack() as c5:
                wo, wob = Pool(S, c5, "wo2", [128, NCH, D], BF16).get()
                wm, wmb = Pool(S, c5, "wm", [128, 8, D], BF16).get()
                ng4, ng4b = Pool(S, c5, "ng4", [128, 4, 128], F32).get()
                S.op("sp", lambda e: e.dma_start(out=wo[:], in_=s_f2o.ap().rearrange("(c p) d -> p c d", p=128)), w=[wob], dma=True)
                S.op("sp", lambda e: e.dma_start(out=wm[:], in_=s_wout.ap().rearrange("(k p) c -> p k c", p=128)), w=[wmb], dma=True)
                for h in range(4):
                    S.op("sp", lambda e, h=h: e.dma_start(out=ng4[:, h, :], in_=bc(w_ng, 0, 128)), w=[ng4b], dma=True)
                load_ln(c5, [1, 2])
                B0 = Pool(S, c5, "b0", [128, 4, D], F32, n=1)
                B1 = Pool(S, c5, "b1", [128, 4, D], F32, n=2)
                XTT = Pool(S, c5, "xtt5", [128, 8, 512], BF16, n=1)
                X2TT = Pool(S, c5, "x2tt5", [128, 8, 512], BF16, n=2)
                STB = Pool(S, c5, "st5b", [128, 12], F32, n=2)
                MVB = Pool(S, c5, "mv5b", [128, 8], F32, n=2)
                PSA5 = SubPool(PS.t[0:3])
                PSB5 = SubPool(PS.t[3:8])
                pend5 = [None]
                GT = Pool(S, c5, "gt5", [128, NCH, 512], BF16, n=1)
                WG = Pool(S, c5, "wg5", [128, 8, 2, 256], BF16, n=2)
                SG = Pool(S, c5, "sg5", [128, 512], BF16, n=2)
                ST = Pool(S, c5, "st5", [128, 12], F32, n=2)
                MV = Pool(S, c5, "mv5", [128, 8], F32, n=2)
                OF = Pool(S, c5, "of", [128, 4, 128], F32, n=2)
                OBk = Pool(S, c5, "obk", [128, 4, 128], F32, n=2)
                ZI = Pool(S, c5, "zi", [128, 4, 128], F32, n=2)
                SQ = Pool(S, c5, "sq", [128, 4, 128], F32, n=2)
                RS = Pool(S, c5, "rs", [128, 8], F32, n=2)
                for ti in range(OWN // 512):
                    t0 = ti * 512
                    S.begin()
                    cps[0] = PSA5
                    b0, b0b = B0.get()
                    b1, b1b = B1.get()
                    S.op("sp", lambda e, b0=b0, t0=t0: e.dma_start(
                        out=b0[:], in_=X1.ap()[t0:t0 + 512, :].rearrange("(s p) d -> p s d", p=128)), w=[b0b], dma=True)
                    S.op("sp", lambda e, b1=b1, t0=t0: e.dma_start(
                        out=b1[:, :, 0:512], in_=OA.ap()[t0:t0 + 512, :].rearrange("(s p) d -> p s d", p=128)), w=[b1b], dma=True)
                    for s in range(4):
                        r0 = t0 + s * 128
                        of_, ofb = OF.get()
                        obk, obkb = OBk.get()
                        zi, zib = ZI.get()
                        S.op("sp", lambda e, of_=of_, r0=r0: e.dma_start(out=of_[:].rearrange("p h d -> p (h d)"), in_=ODN[0].ap()[r0:r0 + 128, :]),
                             w=[ofb], dma=True)
                        S.op("sp", lambda e, obk=obk, r0=r0: e.dma_start(out=obk[:].rearrange("p h d -> p (h d)"), in_=ODN[1].ap()[r0:r0 + 128, :]),
                             w=[obkb], dma=True)
                        S.op("sp", lambda e, zi=zi, r0=r0: e.dma_start(out=zi[:].rearrange("p h d -> p (h d)"), in_=ZZ.ap()[r0:r0 + 128, :]),
                             w=[zib], dma=True)
                        S.op("dve", lambda e, of_=of_, obk=obk: e.tensor_tensor(out=of_[:], in0=of_[:], in1=obk[:], op=ALU.add), r=[ofb, obkb], w=[ofb])
                        sq, sqb = SQ.get()
                        rs, rsb = RS.get()
                        S.op("pool", lambda e, sq=sq, of_=of_: e.tensor_tensor(out=sq[:], in0=of_[:], in1=of_[:], op=ALU.mult), r=[ofb], w=[sqb])
                        S.op("dve", lambda e, sq=sq, rs=rs: e.tensor_reduce(out=rs[:, 0:4], in_=sq[:], axis=AX.X, op=ALU.add), r=[sqb], w=[rsb])
                        S.op("act", lambda e, rs=rs: e.activation(out=rs[:, 4:8], in_=rs[:, 0:4], func=AF.Ln, bias=RMS_EPS, scale=1.0 / 128.0),
                             r=[rsb], w=[rsb])
                        S.op("act", lambda e, rs=rs: e.activation(out=rs[:, 4:8], in_=rs[:, 4:8], func=AF.Exp, scale=-0.5), r=[rsb], w=[rsb])
                        S.op("act", lambda e, zi=zi: e.activation(out=zi[:], in_=zi[:], func=AF.Silu), r=[zib], w=[zib])
                        S.op("pool", lambda e, zi=zi: e.tensor_tensor(out=zi[:], in0=zi[:], in1=ng4[:], op=ALU.mult), r=[zib, ng4b], w=[zib])
                        for h in range(4):
                            S.op("dve", lambda e, b1=b1, of_=of_, rs=rs, zi=zi, s=s, h=h: e.scalar_tensor_tensor(
                                out=b1[:, s, 512 + h * 128:512 + (h + 1) * 128], in0=of_[:, h, :], scalar=rs[:, 4 + h:5 + h], in1=zi[:, h, :],
                                op0=ALU.mult, op1=ALU.mult), r=[ofb, rsb, zib], w=[b1b])
                    mT, mTb = XTT.get()
                    transpose_tok(b1, b1b, mT, mTb, 4)
                    S.op("act", lambda e, b0=b0: e.mul(out=b0[:], in_=b0[:], mul=ALPHA), r=[b0b], w=[b0b])
                    for s in range(4):
                        for nh in range(2):
                            po, pob = cps[0].get()
                            for k in range(8):
                                S.op("pe", lambda e, po=po, mT=mT, k=k, s=s, nh=nh: e.matmul(
                                    po[:, :], lhsT=mT[:, k, s * 128:(s + 1) * 128], rhs=wm[:, k, nh * 512:(nh + 1) * 512],
                                    start=(k == 0), stop=(k == 7)), r=[mTb, wmb], w=[pob])
                            S.op("dve", lambda e, po=po, b1=b1, b0=b0, s=s, nh=nh: e.tensor_tensor(
                                out=b1[:, s, nh * 512:(nh + 1) * 512], in0=po[:, :], in1=b0[:, s, nh * 512:(nh + 1) * 512], op=ALU.add),
                                r=[pob, b0b], w=[b1b])
                    layer_norm(b1, b1b, 1, (ST, MV))
                    x2T, x2Tb = X2TT.get()
                    transpose_tok(b1, b1b, x2T, x2Tb, 4)
                    S.op("act", lambda e, b1=b1: e.mul(out=b1[:], in_=b1[:], mul=ALPHA), r=[b1b], w=[b1b])
                    la = S.end()
                    S.begin()
                    cps[0] = PSB5
                    ffn(x2T, x2Tb, b1, b1b, s_f2i, wo, wob, GT, WG, SG)
                    layer_norm(b1, b1b, 2, (STB, MVB))
                    S.op("pool", lambda e, b1=b1, t0=t0: e.dma_start(
                        out=y_d.ap()[t0:t0 + 512, :].rearrange("(s p) d -> p s d", p=128), in_=b1[:]), r=[b1b], dma=True)
                    lb = S.end()
                    S.merge([la] + ([pend5[0]] if pend5[0] else []))
                    pend5[0] = lb
                S.merge([pend5[0]])
                cps[0] = PS
                S.flush()
    return nc


_NC_CACHE = {}


def _run(seqs, in_maps, n_cores):
    key = tuple(seqs)
    if key not in _NC_CACHE:
        _NC_CACHE[key] = build_nc(seqs)
    nc = _NC_CACHE[key]
    return run_bass_kernel_spmd(nc, in_maps, core_ids=list(range(n_cores)))


def _common_inputs(inp):
    f = lambda a: np.ascontiguousarray(np.asarray(a, dtype=np.float32))
    m = {
        "ffn1_w_in": f(inp["ffn1_w_in"][0]), "ffn1_w_out": f(inp["ffn1_w_out"][0]),
        "w_in": f(inp["w_in"][0]), "conv_w": f(inp["conv_w"][0]),
        "attn_sink": f(inp["attn_sink"]).reshape(1, 8),
        "dn_a_log": f(inp["dn_a_log"]).reshape(1, 8), "dn_dt_bias": f(inp["dn_dt_bias"]).reshape(1, 8),
        "dn_norm_gain": f(inp["dn_norm_gain"]).reshape(1, 128),
        "w_out": f(inp["w_out"][0]), "ffn2_w_in": f(inp["ffn2_w_in"][0]), "ffn2_w_out": f(inp["ffn2_w_out"][0]),
        "ln_gain": f(inp["ln_gain"]).reshape(1, 3 * D), "ln_bias": f(inp["ln_bias"]).reshape(1, 3 * D),
    }
    wq = m["w_in"][:, 0:512].reshape(D, 2, 4, 64).transpose(0, 2, 1, 3).reshape(D, 512)
    m["w_in"] = np.ascontiguousarray(np.concatenate([wq, m["w_in"][:, 512:]], axis=1))
    m.update(_consts())
    return m


def kernel(**inputs):
    xp = np.asarray(inputs["x_prompt"], dtype=np.float32)
    xs = np.asarray(inputs["x_sample"], dtype=np.float32)
    common = _common_inputs(inputs)
    seqs = (("s", xs.shape[1], False), ("p", xp.shape[1], True))
    Lp = xp.shape[1]
    sl = Lp // N_CORES
    in_maps = []
    for c in range(N_CORES):
        m = dict(common)
        m["x_s"] = np.ascontiguousarray(xs[c])
        m["x_p"] = np.ascontiguousarray(np.concatenate([xp[0, c * sl:], xp[0, :c * sl]], axis=0))
        m["wflag"] = np.array([[0.0 if (c + s_) % 8 == 7 else 1.0 for s_ in range(8)]], np.float32)
        in_maps.append(m)
    res = _run(seqs, in_maps, N_CORES)
    y_s = np.stack([np.asarray(res.results[c]["y_s"], dtype=np.float32) for c in range(N_CORES)], 0)
    y_p = np.concatenate([np.asarray(res.results[c]["y_p"], dtype=np.float32) for c in range(N_CORES)], 0)[None]
    return (y_p, y_s)
```

```python
from contextlib import ExitStack
import numpy as np
import ml_dtypes
import concourse.bass as bass
import concourse.mybir as mybir
from concourse.bass_utils import run_bass_kernel_spmd

F32 = mybir.dt.float32
BF16 = mybir.dt.bfloat16
AF = mybir.ActivationFunctionType
ALU = mybir.AluOpType
AX = mybir.AxisListType

D = 1024
DFF = 2816
NCH = DFF // 128
PROJ = 2832
ALPHA = 2.0 ** 0.25
LN_EPS = 1e-5
RMS_EPS = 1e-6
NEG = -1.0e6
N_CORES = 8
L_S = 8192
L_P = 16384


class Buf:
    __slots__ = ("lw", "rd", "excl", "swt", "srt")

    def __init__(self):
        self.lw = None
        self.rd = []
        self.excl = False
        self.swt = 0.0
        self.srt = 0.0


DMA_K = {"sp": 8, "pool": 4, "act": 4}
COMPUTE = ("pe", "act", "dve", "pool")


class Sched:
    def __init__(self, nc, ctx):
        self.nc = nc
        self.ops = []
        self.cur = None
        self.eng_t = {}
        self.csem = {e: ctx.enter_context(nc.semaphore("c_" + e)) for e in COMPUTE}
        self.ccnt = {e: 0 for e in COMPUTE}
        self.dsem = {q: [ctx.enter_context(nc.semaphore(f"d_{q}{i}")) for i in range(k)]
                     for q, k in DMA_K.items()}
        self.dcnt = {q: 0 for q in DMA_K}
        self.bufs = []

    def buf(self):
        b = Buf()
        self.bufs.append(b)
        return b

    COST = {"pe": 230.0, "act": 450.0, "dve": 350.0, "pool": 2500.0, "sp": 100.0}

    def op(self, eng, fn, r=(), w=(), dma=False, c=None):
        o = (eng, fn, tuple(r), tuple(w), dma, c)
        if self.cur is None:
            self._place(o)
        else:
            self.cur.append(o)

    def _est(self, o):
        eng, fn, r, w, dma, c = o
        t = self.eng_t.get(eng, 0.0)
        for b in r:
            if b.swt + 200.0 > t:
                t = b.swt + 200.0
        for b in w:
            m = max(b.swt, b.srt) + 200.0
            if m > t:
                t = m
        return t

    def _place(self, o):
        eng, fn, r, w, dma, c = o
        t = self._est(o)
        if dma:
            self.eng_t[eng] = t + 100.0
            fin = t + (c if c is not None else 4000.0)
        else:
            fin = t + (c if c is not None else self.COST[eng])
            self.eng_t[eng] = fin
        for b in r:
            if fin > b.srt:
                b.srt = fin
        for b in w:
            b.swt = fin
            b.srt = 0.0
        self.ops.append((eng, fn, r, w, dma))

    def begin(self):
        self.cur = []

    def end(self):
        c = self.cur
        self.cur = None
        return c

    def merge(self, lists):
        lists = [l for l in lists if l]
        ptr = [0] * len(lists)
        while True:
            best = None
            for k, l in enumerate(lists):
                if ptr[k] < len(l):
                    t = self._est(l[ptr[k]])
                    key = (t, ptr[k] / len(l))
                    if best is None or key < best[0]:
                        best = (key, k)
            if best is None:
                break
            k = best[1]
            self._place(lists[k][ptr[k]])
            ptr[k] += 1

    def flush(self):
        nc = self.nc
        ops = self.ops
        n = len(ops)
        deps = [None] * n
        for i, (eng, fn, r, w, dma) in enumerate(ops):
            d = set()
            for b in r:
                if b.lw is not None:
                    d.add(b.lw)
                if b.excl:
                    for q in b.rd:
                        if ops[q][0] != eng:
                            d.add(q)
            for b in w:
                if b.lw is not None:
                    d.add(b.lw)
                d.update(b.rd)
            d.discard(i)
            for b in r:
                b.rd.append(i)
            for b in w:
                b.lw = i
                b.rd = []
            deps[i] = d
        need_inc = [False] * n
        for i in range(n):
            eng, _, _, _, dma = ops[i]
            keep = []
            for p in deps[i]:
                pe, _, _, _, pdma = ops[p]
                if (not dma) and (not pdma) and pe == eng == "pe":
                    continue
                keep.append(p)
                if not pdma:
                    need_inc[p] = True
            deps[i] = keep
        target = [None] * n
        dma_prev = [None] * n
        per_eng = {e: [] for e in ("pe", "act", "dve", "pool", "sp")}
        for i in range(n):
            eng, _, _, _, dma = ops[i]
            per_eng[eng].append(i)
            if dma:
                j = self.dcnt[eng]
                k = DMA_K[eng]
                self.dcnt[eng] = j + 1
                target[i] = (self.dsem[eng][j % k], 16 * (j // k + 1))
                if j >= k:
                    dma_prev[i] = (self.dsem[eng][j % k], 16 * (j // k))
            elif need_inc[i]:
                self.ccnt[eng] += 1
                target[i] = (self.csem[eng], self.ccnt[eng])
        final = {}
        for i in range(n):
            if target[i] is not None:
                s, v = target[i]
                final[id(s)] = (s, max(v, final.get(id(s), (s, 0))[1]))

        def emit(ename, e):
            waited = {}

            def wait(s, v):
                if waited.get(id(s), 0) >= v:
                    return
                waited[id(s)] = v
                e.wait_ge(s, v)

            for i in per_eng[ename]:
                eng, fn, _, _, dma = ops[i]
                if dma_prev[i] is not None:
                    wait(*dma_prev[i])
                for p in deps[i]:
                    wait(*target[p])
                ins = fn(e)
                if target[i] is not None:
                    s, v = target[i]
                    ins.then_inc(s, 16 if dma else 1)
            for s, v in final.values():
                wait(s, v)

        with nc.Block() as block:
            @block.sync
            def _(e):
                emit("sp", e)

            @block.tensor
            def _(e):
                emit("pe", e)

            @block.scalar
            def _(e):
                emit("act", e)

            @block.vector
            def _(e):
                emit("dve", e)

            @block.gpsimd
            def _(e):
                emit("pool", e)
        self.ops = []
        self.eng_t = {}
        for b in self.bufs:
            b.lw = None
            b.rd = []
            b.swt = 0.0
            b.srt = 0.0


class Pool:
    uid = 0

    def __init__(self, S, ctx, name, shape, dtype, n=1, psum=False):
        nc = S.nc
        self.t = []
        for i in range(n):
            alloc = nc.psum_tensor if psum else nc.sbuf_tensor
            Pool.uid += 1
            h = ctx.enter_context(alloc(f"{name}_{i}_{Pool.uid}", list(shape), dtype))
            bb = S.buf()
            bb.excl = psum
            self.t.append((h, bb))
        self.i = 0
        self.S = S

    def get(self):
        r = self.t[self.i % len(self.t)]
        self.i += 1
        return r


class SubPool:
    def __init__(self, items):
        self.t = list(items)
        self.i = 0

    def get(self):
        r = self.t[self.i % len(self.t)]
        self.i += 1
        return r


def _consts():
    c = {}
    c["ident"] = np.eye(128, dtype=np.float32)
    c["ones"] = np.ones((128, 128), np.float32)
    tk = np.arange(128)[:, None]
    tq = np.arange(128)[None, :]
    ab = np.zeros((128, 3, 2, 4, 128), np.float32)
    for kb in range(3):
        dist = np.abs(tq - tk - (kb - 1) * 128)
        for kvh in range(2):
            for g in range(4):
                h = kvh * 4 + g
                slope = 2.0 ** (-8.0 * (h + 1) / 8.0)
                ab[:, kb, kvh, g, :] = np.where(dist <= 128, -slope * dist, NEG)
    c["abias"] = ab.reshape(128, 3 * 2 * 512)
    a = np.arange(128)
    same = (a[:, None] // 64) == (a[None, :] // 64)
    dm = np.zeros((128, 2, 5, 128), np.float32)
    for d in range(2):
        if d == 0:
            le = a[:, None] <= a[None, :]
            lt = a[:, None] < a[None, :]
        else:
            le = a[:, None] >= a[None, :]
            lt = a[:, None] > a[None, :]
        dm[:, d, 0, :] = (same & le)
        dm[:, d, 1, :] = same
        dm[:, d, 2, :] = np.where(same & le, 0.0, NEG)
        dm[:, d, 3, :] = (same & lt)
        dm[:, d, 4, :] = np.eye(128)
    c["dmask"] = dm.reshape(128, 2 * 5 * 128)
    return c


import os
STOP = int(os.environ.get("KSTOP", "99"))
KSUB = int(os.environ.get("KSUB", "0"))
KDBG = int(os.environ.get("KDBG", "0"))


class _Stop(Exception):
    pass


def build_nc(seqs):
    nc = bass.Bass("TRN2", target_bir_lowering=False)
    try:
        _build(nc, seqs)
    except _Stop:
        pass
    return nc


def _build(nc, seqs):
    ctx = ExitStack()
    with ctx:
        S = Sched(nc, ctx)

        def chk(n):
            if KSUB == n:
                S.flush()
                raise _Stop()

        def dram(name, shape, dt, kind):
            return nc.dram_tensor(name, list(shape), dt, kind=kind)

        xin = {nm: dram("x_" + nm, [L, D], F32, "ExternalInput") for nm, L, rot in seqs}
        yout = {nm: dram("y_" + nm, [(L // 8) if rot else L, D], F32, "ExternalOutput") for nm, L, rot in seqs}
        w_flag = dram("wflag", [1, 8], F32, "ExternalInput")
        w_f1i = dram("ffn1_w_in", [D, 2 * DFF], F32, "ExternalInput")
        w_f1o = dram("ffn1_w_out", [DFF, D], F32, "ExternalInput")
        w_in = dram("w_in", [D, PROJ], F32, "ExternalInput")
        w_cv = dram("conv_w", [5, 1536], F32, "ExternalInput")
        w_sink = dram("attn_sink", [1, 8], F32, "ExternalInput")
        w_alog = dram("dn_a_log", [1, 8], F32, "ExternalInput")
        w_dtb = dram("dn_dt_bias", [1, 8], F32, "ExternalInput")
        w_ng = dram("dn_norm_gain", [1, 128], F32, "ExternalInput")
        w_out = dram("w_out", [D, D], F32, "ExternalInput")
        w_f2i = dram("ffn2_w_in", [D, 2 * DFF], F32, "ExternalInput")
        w_f2o = dram("ffn2_w_out", [DFF, D], F32, "ExternalInput")
        w_lng = dram("ln_gain", [1, 3 * D], F32, "ExternalInput")
        w_lnb = dram("ln_bias", [1, 3 * D], F32, "ExternalInput")
        c_ident = dram("ident", [128, 128], F32, "ExternalInput")
        c_ones = dram("ones", [128, 128], F32, "ExternalInput")
        c_abias = dram("abias", [128, 3072], F32, "ExternalInput")
        c_dmask = dram("dmask", [128, 1280], F32, "ExternalInput")

        LM = max(L for _, L, _r in seqs)
        s_f1i = dram("s_f1i", [11, 128, 4096], BF16, "Internal")
        s_f1o = dram("s_f1o", [DFF, D], BF16, "Internal")
        s_win = dram("s_win", [D, PROJ], BF16, "Internal")
        s_wout = dram("s_wout", [D, D], BF16, "Internal")
        s_f2i = dram("s_f2i", [11, 128, 4096], BF16, "Internal")
        s_f2o = dram("s_f2o", [DFF, D], BF16, "Internal")
        DK = "ExternalOutput" if KDBG else "Internal"
        X1 = dram("X1", [LM, D], F32, DK)
        QT = dram("QT", [128, 4, LM], BF16, "Internal")
        KT = dram("KT", [128, LM], BF16, "Internal")
        VX = dram("VX", [LM, 130], BF16, "Internal")
        DT = dram("DT", [128, 12, LM], F32, "Internal")
        ZZ = dram("ZZ", [LM, 512], F32, "Internal")
        BA = dram("BA", [LM, 16], F32, "Internal")
        OA = dram("OA", [LM, 512], F32, DK)
        ODN = [dram(f"ODN{d}", [LM, 512], F32, DK) for d in range(2)]
        FEK = dram("FEK", [LM // 128, 128, 512], F32, "Internal")
        FEV = dram("FEV", [LM // 128, 128, 512], BF16, "Internal")
        FEQ = dram("FEQ", [LM // 128, 128, 1024], BF16, "Internal")

        def bc(t, off, n):
            return bass.AP(t, off, [[0, 128], [1, n]])

        PS = Pool(S, ctx, "ps", [128, 512], F32, n=8, psum=True)
        cps = [PS]
        ident, identb = Pool(S, ctx, "ident", [128, 128], F32).get()
        ones, onesb = Pool(S, ctx, "ones", [128, 128], F32).get()
        wf, wfb = Pool(S, ctx, "wf", [128, 8], F32).get()
        S.op("sp", lambda e: e.dma_start(out=wf[:], in_=bc(w_flag, 0, 8)), w=[wfb], dma=True)
        lnc = {}

        def load_ln(cx, lis):
            for li in lis:
                g, gb = Pool(S, cx, "lng", [128, D], F32).get()
                b, bb = Pool(S, cx, "lnb", [128, D], F32).get()
                S.op("sp", lambda e, g=g, li=li: e.dma_start(out=g[:], in_=bc(w_lng, li * D, D)), w=[gb], dma=True)
                S.op("sp", lambda e, b=b, li=li: e.dma_start(out=b[:], in_=bc(w_lnb, li * D, D)), w=[bb], dma=True)
                lnc[li] = (g, gb, b, bb)
        S.op("sp", lambda e: e.dma_start(out=ident[:], in_=c_ident.ap()), w=[identb], dma=True)
        S.op("sp", lambda e: e.dma_start(out=ones[:], in_=c_ones.ap()), w=[onesb], dma=True)

        with ExitStack() as c0:
            STG = Pool(S, c0, "stg", [128, 2048], F32, n=3)
            STB = Pool(S, c0, "stb", [128, 2048], BF16, n=3)
            rr = [0]
            def cast_op(a, ab_, b, bb_, cw):
                k = rr[0] % 3
                rr[0] += 1
                if k == 0:
                    S.op("dve", lambda e: e.tensor_copy(out=b[:, 0:cw], in_=a[:, 0:cw]), r=[ab_], w=[bb_])
                elif k == 1:
                    S.op("act", lambda e: e.copy(out=b[:, 0:cw], in_=a[:, 0:cw]), r=[ab_], w=[bb_])
                else:
                    S.op("pool", lambda e: e.tensor_copy(out=b[:, 0:cw], in_=a[:, 0:cw]), r=[ab_], w=[bb_])

            def conv_ffn_in(src, dst):
                d5 = dst.ap().rearrange("j p (k u c) -> j p k u c", k=8, u=2, c=256)
                for k in range(8):
                    for u in range(2):
                        for jj in range(0, 11, 4):
                            ng = min(4, 11 - jj)
                            a, ab_ = STG.get()
                            b, bb_ = STB.get()
                            S.op("sp", lambda e, a=a, k=k, u=u, jj=jj, ng=ng: e.dma_start(
                                out=a[:, 0:ng * 256], in_=src.ap()[k * 128:(k + 1) * 128, u * DFF + jj * 256:u * DFF + (jj + ng) * 256]),
                                w=[ab_], dma=True)
                            cast_op(a, ab_, b, bb_, ng * 256)
                            S.op("pool", lambda e, b=b, k=k, u=u, jj=jj, ng=ng: e.dma_start(
                                out=d5[jj:jj + ng, :, k, u, :].rearrange("j p c -> p j c"),
                                in_=b[:, 0:ng * 256].rearrange("p (j c) -> p j c", c=256)), r=[bb_], dma=True)

            conv_ffn_in(w_f1i, s_f1i)
            conv_ffn_in(w_f2i, s_f2i)
            for src, dst, R, C in ((w_f1o, s_f1o, DFF, D),
                                   (w_in, s_win, D, PROJ), (w_out, s_wout, D, D),
                                   (w_f2o, s_f2o, DFF, D)):
                for r0 in range(0, R, 128):
                    for c0_ in range(0, C, 2048):
                        cw = min(2048, C - c0_)
                        a, ab_ = STG.get()
                        b, bb_ = STB.get()
                        S.op("sp", lambda e, a=a, r0=r0, c0_=c0_, cw=cw, src=src:
                             e.dma_start(out=a[:, 0:cw], in_=src.ap()[r0:r0 + 128, c0_:c0_ + cw]),
                             w=[ab_], dma=True)
                        k = rr[0] % 3
                        rr[0] += 1
                        if k == 0:
                            S.op("dve", lambda e, a=a, b=b, cw=cw: e.tensor_copy(out=b[:, 0:cw], in_=a[:, 0:cw]),
                                 r=[ab_], w=[bb_])
                        elif k == 1:
                            S.op("act", lambda e, a=a, b=b, cw=cw: e.copy(out=b[:, 0:cw], in_=a[:, 0:cw]),
                                 r=[ab_], w=[bb_])
                        else:
                            S.op("pool", lambda e, a=a, b=b, cw=cw: e.tensor_copy(out=b[:, 0:cw], in_=a[:, 0:cw]),
                                 r=[ab_], w=[bb_])
                        S.op("pool", lambda e, b=b, r0=r0, c0_=c0_, cw=cw, dst=dst:
                             e.dma_start(out=dst.ap()[r0:r0 + 128, c0_:c0_ + cw], in_=b[:, 0:cw]),
                             r=[bb_], dma=True)
            S.flush()
        if STOP == 0:
            return nc

        def transpose_tok(src, srcb, dst, dstb, nsub, rot=[0]):
            for k in range(8):
                p, pb = cps[0].get()
                for s in range(nsub):
                    S.op("pe", lambda e, p=p, s=s, k=k: e.transpose(
                        out=p[:, s * 128:(s + 1) * 128], in_=src[:, s, k * 128:(k + 1) * 128], identity=ident[:]),
                        r=[srcb, identb], w=[pb])
                rot[0] += 1
                if rot[0] % 2:
                    S.op("dve", lambda e, p=p, k=k: e.tensor_copy(out=dst[:, k, 0:nsub * 128], in_=p[:, 0:nsub * 128]),
                         r=[pb], w=[dstb])
                else:
                    S.op("act", lambda e, p=p, k=k: e.copy(out=dst[:, k, 0:nsub * 128], in_=p[:, 0:nsub * 128]),
                         r=[pb], w=[dstb])

        def layer_norm(y, yb, li, pools, nsub=4):
            ST, MV = pools
            for s in range(nsub):
                st, stb = ST.get()
                mv, mvb = MV.get()
                S.op("dve", lambda e, st=st, s=s: e.bn_stats(out=st[:, 0:6], in_=y[:, s, 0:512]), r=[yb], w=[stb])
                S.op("dve", lambda e, st=st, s=s: e.bn_stats(out=st[:, 6:12], in_=y[:, s, 512:1024]), r=[yb], w=[stb])
                S.op("dve", lambda e, st=st, mv=mv: e.bn_aggr(out=mv[:, 0:2], in_=st[:, 0:12]), r=[stb], w=[mvb])
                S.op("act", lambda e, mv=mv: e.activation(out=mv[:, 2:3], in_=mv[:, 1:2], func=AF.Ln, bias=LN_EPS, scale=1.0),
                     r=[mvb], w=[mvb])
                S.op("act", lambda e, mv=mv: e.activation(out=mv[:, 3:4], in_=mv[:, 2:3], func=AF.Exp, scale=-0.5),
                     r=[mvb], w=[mvb])
                S.op("dve", lambda e, mv=mv: e.scalar_tensor_tensor(out=mv[:, 4:5], in0=mv[:, 0:1], scalar=-1.0, in1=mv[:, 3:4],
                                                                    op0=ALU.mult, op1=ALU.mult), r=[mvb], w=[mvb])
                S.op("act", lambda e, mv=mv, s=s: e.activation(out=y[:, s, :], in_=y[:, s, :], func=AF.Identity,
                                                              bias=mv[:, 4:5], scale=mv[:, 3:4]), r=[mvb, yb], w=[yb], c=1500.0)
                lg, lgb, lb_, lbb = lnc[li]
                S.op("dve", lambda e, s=s, lg=lg: e.tensor_tensor(out=y[:, s, :], in0=y[:, s, :], in1=lg[:], op=ALU.mult), r=[yb, lgb], w=[yb], c=1200.0)
                S.op("dve", lambda e, s=s, lb_=lb_: e.tensor_tensor(out=y[:, s, :], in0=y[:, s, :], in1=lb_[:], op=ALU.add), r=[yb, lbb], w=[yb], c=1200.0)

        def ffn(xT, xTb, xa, xab, wsc, wo, wob, GT, WG, SG):
            gT, gTb = GT.get()
            for j in range(11):
                wg, wgb = WG.get()
                S.op("sp" if j % 2 == 0 else "act", lambda e, wg=wg, j=j: e.dma_start(
                    out=wg[:].rearrange("p k u c -> p (k u c)"), in_=wsc.ap()[j]), w=[wgb], dma=True)
                for hf in range(2):
                    c = 2 * j + hf
                    pg, pgb = cps[0].get()
                    pu, pub = cps[0].get()
                    for k in range(8):
                        S.op("pe", lambda e, pg=pg, wg=wg, k=k, hf=hf: e.matmul(
                            pg[:, :], lhsT=wg[:, k, 0, hf * 128:(hf + 1) * 128], rhs=xT[:, k, :], start=(k == 0), stop=(k == 7)),
                            r=[wgb, xTb], w=[pgb])
                    for k in range(8):
                        S.op("pe", lambda e, pu=pu, wg=wg, k=k, hf=hf: e.matmul(
                            pu[:, :], lhsT=wg[:, k, 1, hf * 128:(hf + 1) * 128], rhs=xT[:, k, :], start=(k == 0), stop=(k == 7)),
                            r=[wgb, xTb], w=[pub])
                    sg, sgb = SG.get()
                    S.op("act", lambda e, sg=sg, pg=pg: e.activation(out=sg[:, :], in_=pg[:, :], func=AF.Silu), r=[pgb], w=[sgb])
                    S.op("dve", lambda e, sg=sg, pu=pu, c=c: e.tensor_tensor(out=gT[:, c, :], in0=sg[:, :], in1=pu[:, :], op=ALU.mult),
                         r=[sgb, pub], w=[gTb])
            for s in range(4):
                for nh in range(2):
                    po, pob = cps[0].get()
                    for c in range(NCH):
                        S.op("pe", lambda e, po=po, c=c, s=s, nh=nh: e.matmul(
                            po[:, :], lhsT=gT[:, c, s * 128:(s + 1) * 128], rhs=wo[:, c, nh * 512:(nh + 1) * 512],
                            start=(c == 0), stop=(c == NCH - 1)), r=[gTb, wob], w=[pob])
                    S.op("dve", lambda e, po=po, s=s, nh=nh: e.scalar_tensor_tensor(
                        out=xa[:, s, nh * 512:(nh + 1) * 512], in0=po[:, :], scalar=0.5, in1=xa[:, s, nh * 512:(nh + 1) * 512],
                        op0=ALU.mult, op1=ALU.add), r=[pob, xab], w=[xab])

        def v4(p):
            return p[:, :].rearrange("p (h d) -> p h d", d=128)

        evr = [0]

        def evac(dst_fn, p, pb, wbufs, rbufs=()):
            evr[0] += 1
            if evr[0] % 2:
                S.op("dve", lambda e: e.tensor_copy(out=dst_fn(), in_=p()), r=[pb, *rbufs], w=wbufs)
            else:
                S.op("act", lambda e: e.copy(out=dst_fn(), in_=p()), r=[pb, *rbufs], w=wbufs)

        for nm, L, rot in seqs:
            OWN = (L // 8) if rot else L
            OWNB = OWN // 128
            x_d, y_d = xin[nm], yout[nm]
            NT = L // 512
            NB = L // 128
            with ExitStack() as c1:
                wo, wob = Pool(S, c1, "wo1", [128, NCH, D], BF16).get()
                wi, wib = Pool(S, c1, "wi", [128, 8, PROJ], BF16).get()
                S.op("sp", lambda e: e.dma_start(out=wo[:], in_=s_f1o.ap().rearrange("(c p) d -> p c d", p=128)), w=[wob], dma=True)
                S.op("sp", lambda e: e.dma_start(out=wi[:], in_=s_win.ap().rearrange("(k p) c -> p k c", p=128)), w=[wib], dma=True)
                load_ln(c1, [0])
                XT = Pool(S, c1, "xt", [128, 4, D], F32, n=1)
                XTT = Pool(S, c1, "xtt", [128, 8, 512], BF16, n=1)
                X1TT = Pool(S, c1, "x1tt", [128, 8, 512], BF16, n=2)
                PSA1 = SubPool(PS.t[0:4])
                PSB1 = SubPool(PS.t[4:8])
                pend1 = [None]
                GT = Pool(S, c1, "gt", [128, NCH, 512], BF16, n=1)
                WG = Pool(S, c1, "wg", [128, 8, 2, 256], BF16, n=2)
                SG = Pool(S, c1, "sg", [128, 512], BF16, n=2)
                ST = Pool(S, c1, "st", [128, 12], F32, n=2)
                MV = Pool(S, c1, "mv", [128, 8], F32, n=2)
                QTT = Pool(S, c1, "qtt", [128, 4, 512], BF16, n=1)
                KTT = Pool(S, c1, "ktt", [128, 512], BF16, n=1)
                DTT = Pool(S, c1, "dtt", [128, 512], F32, n=3)
                VXT = Pool(S, c1, "vxt", [128, 4, 130], BF16, n=1)
                ZT = Pool(S, c1, "zt", [128, 4, 512], F32, n=1)
                BAT = Pool(S, c1, "bat", [128, 4, 16], F32, n=1)
                for ti in range(NT):
                    t0 = ti * 512
                    S.begin()
                    cps[0] = PSA1
                    xt, xtb = XT.get()
                    S.op("sp", lambda e, xt=xt, t0=t0: e.dma_start(
                        out=xt[:], in_=x_d.ap()[t0:t0 + 512, :].rearrange("(s p) d -> p s d", p=128)), w=[xtb], dma=True)
                    xT, xTb = XTT.get()
                    transpose_tok(xt, xtb, xT, xTb, 4)
                    S.op("act", lambda e, xt=xt: e.mul(out=xt[:], in_=xt[:], mul=ALPHA), r=[xtb], w=[xtb])
                    ffn(xT, xTb, xt, xtb, s_f1i, wo, wob, GT, WG, SG)
                    layer_norm(xt, xtb, 0, (ST, MV))
                    S.op("pool", lambda e, xt=xt, t0=t0: e.dma_start(
                        out=X1.ap()[t0:t0 + 512, :].rearrange("(s p) d -> p s d", p=128), in_=xt[:]), r=[xtb], dma=True)
                    x1T, x1Tb = X1TT.get()
                    transpose_tok(xt, xtb, x1T, x1Tb, 4)
                    la = S.end()
                    S.begin()
                    cps[0] = PSB1
                    qtt, qttb = QTT.get()
                    for c in range(4):
                        p, pb = cps[0].get()
                        for k in range(8):
                            S.op("pe", lambda e, p=p, k=k, c=c, x1T=x1T: e.matmul(
                                p[:, :], lhsT=wi[:, k, c * 128:(c + 1) * 128],
                                rhs=x1T[:, k, :], start=(k == 0), stop=(k == 7)), r=[wib, x1Tb], w=[pb])
                        evac(lambda qtt=qtt, c=c: qtt[:, c, :], lambda p=p: p[:, :], pb, [qttb])
                    S.op("pool", lambda e, qtt=qtt, t0=t0: e.dma_start(out=QT.ap()[:, :, t0:t0 + 512], in_=qtt[:]), r=[qttb], dma=True)
                    ktt, kttb = KTT.get()
                    p, pb = cps[0].get()
                    for k in range(8):
                        S.op("pe", lambda e, p=p, k=k, x1T=x1T: e.matmul(
                            p[:, :], lhsT=wi[:, k, 512:640], rhs=x1T[:, k, :], start=(k == 0), stop=(k == 7)), r=[wib, x1Tb], w=[pb])
                    evac(lambda ktt=ktt: ktt[:, :], lambda p=p: p[:, :], pb, [kttb])
                    S.op("pool", lambda e, ktt=ktt, t0=t0: e.dma_start(out=KT.ap()[:, t0:t0 + 512], in_=ktt[:]), r=[kttb], dma=True)
                    for c in range(12):
                        p, pb = cps[0].get()
                        for k in range(8):
                            S.op("pe", lambda e, p=p, k=k, c=c, x1T=x1T: e.matmul(
                                p[:, :], lhsT=wi[:, k, 768 + c * 128:768 + (c + 1) * 128], rhs=x1T[:, k, :],
                                start=(k == 0), stop=(k == 7)), r=[wib, x1Tb], w=[pb])
                        dtt, dttb = DTT.get()
                        evac(lambda dtt=dtt: dtt[:, :], lambda p=p: p[:, :], pb, [dttb])
                        S.op("sp", lambda e, dtt=dtt, c=c, t0=t0: e.dma_start(out=DT.ap()[:, c, t0:t0 + 512], in_=dtt[:]),
                             r=[dttb], dma=True)
                    vxt, vxtb = VXT.get()
                    zt, ztb = ZT.get()
                    bat, batb = BAT.get()
                    S.op("pool", lambda e, vxt=vxt: e.memset(vxt[:], 1.0), w=[vxtb])
                    for s in range(4):
                        p, pb = cps[0].get()
                        for k in range(8):
                            S.op("pe", lambda e, p=p, k=k, s=s, x1T=x1T: e.matmul(
                                p[:, 0:128], lhsT=x1T[:, k, s * 128:(s + 1) * 128], rhs=wi[:, k, 640:768],
                                start=(k == 0), stop=(k == 7)), r=[wib, x1Tb], w=[pb])
                        evac(lambda vxt=vxt, s=s: vxt[:, s, :].rearrange("p (h c) -> p h c", h=2)[:, :, 0:64],
                             lambda p=p: p[:, 0:128].rearrange("p (h c) -> p h c", h=2), pb, [vxtb])
                        p, pb = cps[0].get()
                        for k in range(8):
                            S.op("pe", lambda e, p=p, k=k, s=s, x1T=x1T: e.matmul(
                                p[:, :], lhsT=x1T[:, k, s * 128:(s + 1) * 128], rhs=wi[:, k, 2304:2816],
                                start=(k == 0), stop=(k == 7)), r=[wib, x1Tb], w=[pb])
                        evac(lambda zt=zt, s=s: zt[:, s, :], lambda p=p: p[:, :], pb, [ztb])
                        p, pb = cps[0].get()
                        for k in range(8):
                            S.op("pe", lambda e, p=p, k=k, s=s, x1T=x1T: e.matmul(
                                p[:, 0:16], lhsT=x1T[:, k, s * 128:(s + 1) * 128], rhs=wi[:, k, 2816:2832],
                                start=(k == 0), stop=(k == 7)), r=[wib, x1Tb], w=[pb])
                        evac(lambda bat=bat, s=s: bat[:, s, :], lambda p=p: p[:, 0:16], pb, [batb])
                    S.op("pool", lambda e, vxt=vxt, t0=t0: e.dma_start(
                        out=VX.ap()[t0:t0 + 512, :].rearrange("(s p) c -> p s c", p=128), in_=vxt[:]), r=[vxtb], dma=True)
                    S.op("pool", lambda e, zt=zt, t0=t0: e.dma_start(
                        out=ZZ.ap()[t0:t0 + 512, :].rearrange("(s p) c -> p s c", p=128), in_=zt[:]), r=[ztb], dma=True)
                    S.op("pool", lambda e, bat=bat, t0=t0: e.dma_start(
                        out=BA.ap()[t0:t0 + 512, :].rearrange("(s p) c -> p s c", p=128), in_=bat[:]), r=[batb], dma=True)
                    lb = S.end()
                    S.merge([la] + ([pend1[0]] if pend1[0] else []))
                    pend1[0] = lb
                S.merge([pend1[0]])
                cps[0] = PS
                S.flush()
            if STOP == 1:
                return nc

            with ExitStack() as c2:
                abias, abiasb = Pool(S, c2, "abias", [128, 3072], F32).get()
                esink, esinkb = Pool(S, c2, "esink", [128, 8], F32).get()
                S.op("sp", lambda e: e.dma_start(out=abias[:], in_=c_abias.ap()), w=[abiasb], dma=True)
                S.op("sp", lambda e: e.dma_start(out=esink[:], in_=bc(w_sink, 0, 8)), w=[esinkb], dma=True)
                S.op("act", lambda e: e.activation(out=esink[:], in_=esink[:], func=AF.Exp), r=[esinkb], w=[esinkb])
                QB = Pool(S, c2, "qb", [128, 4, 128], BF16, n=2)
                KB = Pool(S, c2, "kb", [128, 3, 128], BF16, n=2)
                VB = Pool(S, c2, "vb", [128, 3, 130], BF16, n=2)
                TB = Pool(S, c2, "tb", [128, 512], F32, n=2)
                PT = Pool(S, c2, "pt", [128, 512], BF16, n=6)
                DEN = Pool(S, c2, "den", [128, 16], F32, n=2)
                OB = Pool(S, c2, "ob", [128, 512], F32, n=2)
                for i in range(OWNB):
                    kbs = [kb for kb in range(3) if (rot or 0 <= i + kb - 1 < NB)]
                    lo, hi = kbs[0], kbs[-1] + 1
                    qb, qbb = QB.get()
                    kbt, kbb = KB.get()
                    vb, vbb = VB.get()
                    S.op("sp", lambda e, qb=qb, i=i: e.dma_start(out=qb[:], in_=QT.ap()[:, :, i * 128:(i + 1) * 128]), w=[qbb], dma=True)
                    if rot and (i == 0 or i == OWNB - 1):
                        for kb in range(3):
                            bi = (i + kb - 1) % NB
                            S.op("sp", lambda e, kbt=kbt, kb=kb, bi=bi: e.dma_start(
                                out=kbt[:, kb, :], in_=KT.ap()[:, bi * 128:(bi + 1) * 128]), w=[kbb], dma=True)
                            S.op("sp", lambda e, vb=vb, kb=kb, bi=bi: e.dma_start(
                                out=vb[:, kb, :], in_=VX.ap()[bi * 128:(bi + 1) * 128, :]), w=[vbb], dma=True)
                        hk, fi = (0, 7) if i == 0 else (2, 0)
                        S.op("act", lambda e, vb=vb, hk=hk, fi=fi: e.activation(
                            out=vb[:, hk, :], in_=vb[:, hk, :], func=AF.Copy, scale=wf[:, fi:fi + 1]), r=[vbb, wfb], w=[vbb])
                        lo, hi = 0, 0
                    if hi > lo:
                        S.op("sp", lambda e, kbt=kbt, i=i, lo=lo, hi=hi: e.dma_start(
                            out=kbt[:, lo:hi, :],
                            in_=KT.ap()[:, (i + lo - 1) * 128:(i + hi - 1) * 128].rearrange("p (b t) -> p b t", t=128)), w=[kbb], dma=True)
                        S.op("sp", lambda e, vb=vb, i=i, lo=lo, hi=hi: e.dma_start(
                            out=vb[:, lo:hi, :],
                            in_=VX.ap()[(i + lo - 1) * 128:(i + hi - 1) * 128, :].rearrange("(b p) c -> p b c", p=128)), w=[vbb], dma=True)
                    pos = []
                    for kvh in range(2):
                        po, pob = PS.get()
                        pos.append((po, pob))
                        b0 = kvh * 64
                        pts = []
                        for kb in kbs:
                            ps_, psb = PS.get()
                            S.op("pe", lambda e, ps_=ps_, kbt=kbt, qb=qb, kb=kb, b0=b0: e.matmul(
                                ps_[:, :], lhsT=kbt[b0:b0 + 64, kb, :], rhs=qb[b0:b0 + 64, :, :].rearrange("p c t -> p (c t)"), start=True, stop=True),
                                r=[kbb, qbb], w=[psb])
                            tb, tbb = TB.get()
                            off = (kb * 2 + kvh) * 512
                            S.op("dve", lambda e, tb=tb, ps_=ps_, off=off: e.scalar_tensor_tensor(
                                out=tb[:, :], in0=ps_[:, :], scalar=0.125, in1=abias[:, off:off + 512], op0=ALU.mult, op1=ALU.add),
                                r=[psb, abiasb], w=[tbb])
                            pt, ptb = PT.get()
                            S.op("act", lambda e, tb=tb, pt=pt: e.activation(out=pt[:, :], in_=tb[:, :], func=AF.Exp), r=[tbb], w=[ptb])
                            pts.append((kb, pt, ptb))
                        for g in range(4):
                            for kb, pt, ptb in pts:
                                S.op("pe", lambda e, po=po, pt=pt, vb=vb, g=g, kb=kb, kvh=kvh, st_=(kb == kbs[0]), sp_=(kb == kbs[-1]): e.matmul(
                                    po[:, g * 65:(g + 1) * 65], lhsT=pt[:, g * 128:(g + 1) * 128], rhs=vb[:, kb, kvh * 65:(kvh + 1) * 65],
                                    start=st_, stop=sp_), r=[ptb, vbb], w=[pob])
                    den, denb = DEN.get()
                    ob, obb = OB.get()
                    for kvh in range(2):
                        po, pob = pos[kvh]
                        S.op("dve", lambda e, den=den, po=po, kvh=kvh: e.tensor_tensor(
                            out=den[:, kvh * 4:(kvh + 1) * 4], in0=po[:, 0:260].rearrange("p (g c) -> p g c", c=65)[:, :, 64],
                            in1=esink[:, kvh * 4:(kvh + 1) * 4], op=ALU.add), r=[pob, esinkb], w=[denb])
                    S.op("dve", lambda e, den=den: e.reciprocal(out=den[:, 8:16], in_=den[:, 0:8]), r=[denb], w=[denb])
                    for kvh in range(2):
                        po, pob = pos[kvh]
                        for g in range(4):
                            h = kvh * 4 + g
                            S.op("dve", lambda e, ob=ob, po=po, den=den, g=g, h=h: e.tensor_scalar(
                                out=ob[:, h * 64:(h + 1) * 64], in0=po[:, g * 65:g * 65 + 64], scalar1=den[:, 8 + h:9 + h], scalar2=None,
                                op0=ALU.mult), r=[pob, denb], w=[obb])
                    S.op("pool", lambda e, ob=ob, i=i: e.dma_start(out=OA.ap()[i * 128:(i + 1) * 128, :], in_=ob[:]), r=[obb], dma=True)
                S.flush()
            if STOP == 2:
                return nc

            with ExitStack() as cf:
                cw, cwb = Pool(S, cf, "cw", [128, 12, 5], F32).get()
                for c in range(12):
                    S.op("sp", lambda e, c=c: e.dma_start(out=cw[:, c, :], in_=w_cv.ap()[:, c * 128:(c + 1) * 128].rearrange("j p -> p j"),
                                                          allow_slow_non_contiguous=True), w=[cwb], dma=True)
                XIN = Pool(S, cf, "xin", [128, 12, 132], F32, n=4)
                CA = Pool(S, cf, "ca", [128, 12, 128], F32, n=4)
                SL = Pool(S, cf, "sl", [128, 12, 128], F32, n=4)
                TM = Pool(S, cf, "tm", [128, 12, 128], F32, n=4)
                SS = Pool(S, cf, "ss", [128, 16], F32, n=4)
                JK = Pool(S, cf, "jk", [128, 8, 128], F32, n=4)
                NRM = Pool(S, cf, "nrm", [128, 8, 128], F32, n=4)
                VBF = Pool(S, cf, "vbf", [128, 4, 128], BF16, n=4)
                QKT = Pool(S, cf, "qkt", [128, 8, 128], BF16, n=4)
                PSFS = [(SubPool(PS.t[0:2]), SubPool(PS.t[4:6])), (SubPool(PS.t[2:4]), SubPool(PS.t[6:8]))]
                def fe_tile(ti, PSF1, PSF2):
                    t0 = ti * 128
                    own = (not rot) or ti < OWNB
                    S.begin()
                    xi, xib = XIN.get()
                    a0 = max(t0 - 2, 0)
                    a1 = min(t0 + 130, L)
                    if (a0 != t0 - 2 or a1 != t0 + 130) and not rot:
                        S.op("pool", lambda e: e.memset(xi[:], 0.0), w=[xib])
                    S.op("sp", lambda e: e.dma_start(out=xi[:, :, a0 - (t0 - 2):a1 - (t0 - 2)], in_=DT.ap()[:, :, a0:a1]), w=[xib], dma=True)
                    if rot:
                        if t0 == 0:
                            S.op("sp", lambda e: e.dma_start(out=xi[:, :, 0:2], in_=DT.ap()[:, :, L - 2:L]), w=[xib], dma=True)
                        if t0 + 128 == L:
                            S.op("sp", lambda e: e.dma_start(out=xi[:, :, 130:132], in_=DT.ap()[:, :, 0:2]), w=[xib], dma=True)
                        if ti % OWNB == 0:
                            fi = (ti // OWNB - 1) % 8
                            S.op("act", lambda e: e.activation(out=xi[:, :, 0:2], in_=xi[:, :, 0:2], func=AF.Copy, scale=wf[:, fi:fi + 1]),
                                 r=[xib, wfb], w=[xib])
                        if ti % OWNB == OWNB - 1:
                            fi2 = ti // OWNB
                            S.op("act", lambda e: e.activation(out=xi[:, :, 130:132], in_=xi[:, :, 130:132], func=AF.Copy, scale=wf[:, fi2:fi2 + 1]),
                                 r=[xib, wfb], w=[xib])
                    ca, cab = CA.get()
                    for j in range(5):
                        for c in range(12):
                            if j == 0:
                                S.op("act", lambda e, c=c: e.activation(
                                    out=ca[:, c, :], in_=xi[:, c, 0:128], func=AF.Copy, scale=cw[:, c, 0:1]),
                                    r=[xib, cwb], w=[cab])
                            else:
                                S.op("dve", lambda e, c=c, j=j: e.scalar_tensor_tensor(
                                    out=ca[:, c, :], in0=xi[:, c, j:j + 128], scalar=cw[:, c, j:j + 1], in1=ca[:, c, :],
                                    op0=ALU.mult, op1=ALU.add), r=[xib, cwb, cab], w=[cab])
                    sl, slb = SL.get()
                    S.op("act", lambda e: e.activation(out=sl[:], in_=ca[:], func=AF.Silu), r=[cab], w=[slb])
                    tm, tmb = TM.get()
                    for q4 in range(3):
                        p, pb = PSF1.get()
                        for h in range(4):
                            S.op("pe", lambda e, p=p, q4=q4, h=h: e.transpose(
                                out=p[:, h * 128:(h + 1) * 128], in_=sl[:, q4 * 4 + h, :], identity=ident[:]), r=[slb, identb], w=[pb])
                        evac(lambda q4=q4: tm[:, q4 * 4:(q4 + 1) * 4, :], lambda p=p: v4(p), pb, [tmb])
                    ss, ssb = SS.get()
                    jk, jkb = JK.get()
                    S.op("act", lambda e: e.activation(out=jk[:], in_=tm[:, 0:8, :], func=AF.Square), r=[tmb], w=[jkb])
                    S.op("dve", lambda e: e.tensor_reduce(out=ss[:, 0:8], in_=jk[:], axis=AX.X, op=ALU.add), r=[jkb], w=[ssb])
                    S.op("act", lambda e: e.activation(out=ss[:, 8:16], in_=ss[:, 0:8], func=AF.Ln, bias=RMS_EPS, scale=1.0), r=[ssb], w=[ssb])
                    S.op("act", lambda e: e.activation(out=ss[:, 8:16], in_=ss[:, 8:16], func=AF.Exp, scale=-0.5), r=[ssb], w=[ssb])
                    S.op("dve", lambda e: e.tensor_scalar(out=ss[:, 8:12], in0=ss[:, 8:12], scalar1=128.0 ** -0.5, scalar2=None, op0=ALU.mult),
                         r=[ssb], w=[ssb])
                    l1 = S.end()
                    S.begin()
                    nrm, nrmb = NRM.get()
                    for idx in range(8):
                        if idx % 2:
                            S.op("dve", lambda e, idx=idx: e.tensor_scalar(
                                out=nrm[:, idx, :], in0=tm[:, idx, :], scalar1=ss[:, 8 + idx:9 + idx], scalar2=None, op0=ALU.mult),
                                r=[tmb, ssb], w=[nrmb])
                        else:
                            S.op("act", lambda e, idx=idx: e.activation(
                                out=nrm[:, idx, :], in_=tm[:, idx, :], func=AF.Copy, scale=ss[:, 8 + idx:9 + idx]),
                                r=[tmb, ssb], w=[nrmb])
                    vbf, vbfb = VBF.get()
                    S.op("act", lambda e: e.copy(out=vbf[:], in_=tm[:, 8:12, :]), r=[tmb], w=[vbfb])
                    qkt, qktb = QKT.get()
                    for q4 in ((0, 1) if own else (1,)):
                        p, pb = PSF2.get()
                        for h in range(4):
                            S.op("pe", lambda e, p=p, q4=q4, h=h: e.transpose(
                                out=p[:, h * 128:(h + 1) * 128], in_=nrm[:, q4 * 4 + h, :], identity=ident[:]), r=[nrmb, identb], w=[pb])
                        evac(lambda q4=q4: qkt[:, q4 * 4:(q4 + 1) * 4, :], lambda p=p: v4(p), pb, [qktb])
                    S.op("pool", lambda e: e.dma_start(out=FEK.ap()[ti], in_=nrm[:, 4:8, :].rearrange("p h d -> p (h d)")), r=[nrmb], dma=True)
                    S.op("pool", lambda e: e.dma_start(out=FEV.ap()[ti], in_=vbf[:].rearrange("p h d -> p (h d)")), r=[vbfb], dma=True)
                    if own:
                        S.op("pool", lambda e: e.dma_start(out=FEQ.ap()[ti], in_=qkt[:].rearrange("p h d -> p (h d)")), r=[qktb], dma=True)
                    else:
                        S.op("pool", lambda e: e.dma_start(out=FEQ.ap()[ti, :, 512:1024], in_=qkt[:, 4:8, :].rearrange("p h d -> p (h d)")),
                             r=[qktb], dma=True)
                    l2 = S.end()
                    return l1, l2

                pendf = []
                for tp in range(0, NB, 2):
                    cur1, cur2 = [], []
                    for q_ in range(2):
                        if tp + q_ < NB:
                            l1, l2 = fe_tile(tp + q_, *PSFS[q_])
                            cur1.append(l1)
                            cur2.append(l2)
                    S.merge(cur1 + pendf)
                    pendf = cur2
                S.merge(pendf)
                S.flush()

            for dr in range(2):
                with ExitStack() as c3:
                    dmk, dmkb = Pool(S, c3, "dmk", [128, 5, 128], F32).get()
                    strict4, strict4b = Pool(S, c3, "strict4", [128, 4, 128], F32).get()
                    eye4, eye4b = Pool(S, c3, "eye4", [128, 4, 128], F32).get()
                    cw, cwb = Pool(S, c3, "cw", [128, 12, 5], F32).get()
                    gpar, gparb = Pool(S, c3, "gpar", [128, 8], F32).get()
                    S.op("sp", lambda e: e.dma_start(out=dmk[:], in_=c_dmask.ap()[:, dr * 640:(dr + 1) * 640].rearrange(
                        "p (m i) -> p m i", i=128)), w=[dmkb], dma=True)
                    for h in range(4):
                        S.op("sp", lambda e, h=h: e.dma_start(out=strict4[:, h, :], in_=c_dmask.ap()[:, dr * 640 + 384:dr * 640 + 512]),
                             w=[strict4b], dma=True)
                        S.op("sp", lambda e, h=h: e.dma_start(out=eye4[:, h, :], in_=c_ident.ap()), w=[eye4b], dma=True)
                    for c in range(12):
                        S.op("sp", lambda e, c=c: e.dma_start(out=cw[:, c, :], in_=w_cv.ap()[:, c * 128:(c + 1) * 128].rearrange("j p -> p j"),
                                                              allow_slow_non_contiguous=True), w=[cwb], dma=True)
                    S.op("sp", lambda e: e.dma_start(out=gpar[:, 0:4], in_=bc(w_alog, dr * 4, 4)), w=[gparb], dma=True)
                    S.op("sp", lambda e: e.dma_start(out=gpar[:, 4:8], in_=bc(w_dtb, dr * 4, 4)), w=[gparb], dma=True)
                    S.op("act", lambda e: e.activation(out=gpar[:, 0:4], in_=gpar[:, 0:4], func=AF.Exp), r=[gparb], w=[gparb])
                    S.op("dve", lambda e: e.tensor_scalar(out=gpar[:, 0:4], in0=gpar[:, 0:4], scalar1=-1.0, scalar2=None, op0=ALU.mult),
                         r=[gparb], w=[gparb])
                    U = lambda: dmk[:, 0, :]
                    BONES = lambda: dmk[:, 1, :]
                    MINC = lambda: dmk[:, 2, :]
                    PSA = SubPool(PS.t[0:3])
                    PSB = SubPool(PS.t[3:5])
                    PSC = SubPool(PS.t[5:8])
                    XIN = Pool(S, c3, "xin", [128, 12, 132], F32, n=2)
                    BAI = Pool(S, c3, "bai", [128, 16], F32, n=2)
                    CA = Pool(S, c3, "ca", [128, 12, 128], F32, n=1)
                    SL = Pool(S, c3, "sl", [128, 12, 128], F32, n=1)
                    TM = Pool(S, c3, "tm", [128, 12, 128], F32, n=1)
                    SS = Pool(S, c3, "ss", [128, 16], F32, n=2)
                    JK = Pool(S, c3, "jk", [128, 8, 128], F32, n=1)
                    NRM = Pool(S, c3, "nrm", [128, 8, 128], F32, n=2)
                    QKT = Pool(S, c3, "qkt", [128, 8, 128], BF16, n=2)
                    GU = Pool(S, c3, "gu", [128, 4, 128], F32, n=1)
                    DTMP = Pool(S, c3, "dtmp", [128, 4, 128], F32, n=1)
                    DCY = Pool(S, c3, "dcy", [128, 4, 128], F32, n=1)
                    GS = Pool(S, c3, "gs", [128, 32], F32, n=3)
                    EROW = Pool(S, c3, "erow", [128, 4, 128], F32, n=3)
                    ATT = Pool(S, c3, "att", [128, 4, 128], BF16, n=3)
                    KD = Pool(S, c3, "kd", [128, 4, 128], BF16, n=3)
                    QD = Pool(S, c3, "qd", [128, 4, 128], BF16, n=3)
                    XM = Pool(S, c3, "xm", [128, 4, 128], F32, n=2)
                    VBF = Pool(S, c3, "vbf", [128, 4, 128], BF16, n=2)
                    KG = Pool(S, c3, "kg", [128, 4, 128], BF16, n=2)
                    PK = Pool(S, c3, "pk", [128, 4, 128], BF16, n=2)
                    PKT = Pool(S, c3, "pkt", [128, 4, 128], BF16, n=2)
                    RF = Pool(S, c3, "rf", [128, 4, 128], F32, n=2)
                    RB = Pool(S, c3, "rbb", [128, 4, 128], BF16, n=2)
                    UB = Pool(S, c3, "ub", [128, 4, 128], F32, n=2)
                    WT = Pool(S, c3, "wt", [128, 4, 128], BF16, n=2)
                    VN = Pool(S, c3, "vn", [128, 4, 128], BF16, n=2)
                    OT = Pool(S, c3, "ot", [128, 4, 128], F32, n=2)
                    SF = Pool(S, c3, "sf", [128, 4, 128], F32, n=2)
                    SB_ = Pool(S, c3, "sbs", [128, 4, 128], BF16, n=2)
                    st8 = {}
                    st8["sf"], st8["sfb"] = SF.get()
                    st8["sb"], st8["sbb"] = SB_.get()
                    S.op("pool", lambda e: e.memset(st8["sf"][:], 0.0), w=[st8["sfb"]])
                    S.op("pool", lambda e: e.memset(st8["sb"][:], 0.0), w=[st8["sbb"]])
                    S.flush()

                    def stage_fe(ti):
                        T = {}
                        t0 = ti * 128
                        own = (not rot) or ti < OWNB
                        bai, baib = BAI.get()
                        S.op("sp", lambda e: e.dma_start(out=bai[:], in_=BA.ap()[t0:t0 + 128, :]), w=[baib], dma=True)
                        nrm, nrmb = NRM.get()
                        vbf, vbfb = VBF.get()
                        qkt, qktb = QKT.get()
                        S.op("sp", lambda e: e.dma_start(out=nrm[:, 4:8, :].rearrange("p h d -> p (h d)"), in_=FEK.ap()[ti]), w=[nrmb], dma=True)
                        S.op("sp", lambda e: e.dma_start(out=vbf[:].rearrange("p h d -> p (h d)"), in_=FEV.ap()[ti]), w=[vbfb], dma=True)
                        if own:
                            S.op("sp", lambda e: e.dma_start(out=qkt[:].rearrange("p h d -> p (h d)"), in_=FEQ.ap()[ti]), w=[qktb], dma=True)
                        else:
                            S.op("sp", lambda e: e.dma_start(out=qkt[:, 4:8, :].rearrange("p h d -> p (h d)"), in_=FEQ.ap()[ti, :, 512:1024]),
                                 w=[qktb], dma=True)
                        gs, gsb = GS.get()
                        S.op("act", lambda e: e.activation(out=gs[:, 0:4], in_=bai[:, dr * 4:dr * 4 + 4], func=AF.Sigmoid), r=[baib], w=[gsb])
                        S.op("dve", lambda e: e.tensor_scalar(out=gs[:, 4:8], in0=gs[:, 0:4], scalar1=-1.0, scalar2=None, op0=ALU.mult), r=[gsb], w=[gsb])
                        S.op("dve", lambda e: e.tensor_tensor(out=gs[:, 8:12], in0=bai[:, 8 + dr * 4:12 + dr * 4], in1=gpar[:, 4:8], op=ALU.add),
                             r=[baib, gparb], w=[gsb])
                        S.op("act", lambda e: e.activation(out=gs[:, 8:12], in_=gs[:, 8:12], func=AF.Exp), r=[gsb], w=[gsb])
                        S.op("act", lambda e: e.activation(out=gs[:, 8:12], in_=gs[:, 8:12], func=AF.Ln, bias=1.0, scale=1.0), r=[gsb], w=[gsb])
                        S.op("dve", lambda e: e.tensor_tensor(out=gs[:, 8:12], in0=gs[:, 8:12], in1=gpar[:, 0:4], op=ALU.mult), r=[gsb, gparb], w=[gsb])
                        p, pb = PSA.get()
                        S.op("pe", lambda e, p=p: e.matmul(p[:, 0:4], lhsT=U(), rhs=gs[:, 8:12], start=True, stop=True), r=[dmkb, gsb], w=[pb])
                        S.op("pe", lambda e, p=p: e.matmul(p[:, 4:8], lhsT=BONES(), rhs=gs[:, 8:12], start=True, stop=True), r=[dmkb, gsb], w=[pb])
                        S.op("dve", lambda e, p=p: e.tensor_copy(out=gs[:, 12:16], in_=p[:, 0:4]), r=[pb], w=[gsb])
                        S.op("dve", lambda e, p=p: e.tensor_scalar(out=gs[:, 16:20], in0=p[:, 0:4], scalar1=-1.0, scalar2=None, op0=ALU.mult), r=[pb], w=[gsb])
                        S.op("dve", lambda e, p=p: e.tensor_tensor(out=gs[:, 24:28], in0=p[:, 4:8], in1=gs[:, 12:16], op=ALU.subtract), r=[pb, gsb], w=[gsb])
                        S.op("act", lambda e: e.activation(out=gs[:, 20:24], in_=gs[:, 12:16], func=AF.Exp), r=[gsb], w=[gsb])
                        S.op("act", lambda e: e.activation(out=gs[:, 24:28], in_=gs[:, 24:28], func=AF.Exp), r=[gsb], w=[gsb])
                        gu, gub = GU.get()
                        for h in range(4):
                            S.op("act", lambda e, h=h: e.activation(
                                out=gu[:, h, :], in_=U(), func=AF.Copy, scale=gs[:, 8 + h:9 + h]), r=[dmkb, gsb], w=[gub])
                        pg, pgb = PSA.get()
                        for h in range(4):
                            S.op("pe", lambda e, h=h: e.matmul(pg[:, h * 128:(h + 1) * 128], lhsT=ones[:], rhs=gu[:, h, :], start=True, stop=True),
                                 r=[onesb, gub], w=[pgb])
                        erow, erowb = EROW.get()
                        S.op("act", lambda e: e.activation(out=erow[:], in_=v4(pg), func=AF.Exp), r=[pgb], w=[erowb])
                        dtmp, dtmpb = DTMP.get()
                        for h in range(4):
                            S.op("dve", lambda e, h=h: e.scalar_tensor_tensor(
                                out=dtmp[:, h, :], in0=pg[:, h * 128:(h + 1) * 128], scalar=gs[:, 16 + h:17 + h], in1=MINC(),
                                op0=ALU.add, op1=ALU.add), r=[pgb, gsb, dmkb], w=[dtmpb])
                        dcy, dcyb = DCY.get()
                        S.op("act", lambda e: e.activation(out=dcy[:], in_=dtmp[:], func=AF.Exp), r=[dtmpb], w=[dcyb])
                        pkk, pkkb = PSA.get()
                        for h in range(4):
                            S.op("pe", lambda e, h=h: e.matmul(pkk[:, h * 128:(h + 1) * 128], lhsT=qkt[:, 4 + h, :], rhs=qkt[:, 4 + h, :],
                                                               start=True, stop=True), r=[qktb], w=[pkkb])
                        if own:
                            pqk, pqkb = PSA.get()
                            for h in range(4):
                                S.op("pe", lambda e, h=h: e.matmul(pqk[:, h * 128:(h + 1) * 128], lhsT=qkt[:, 4 + h, :], rhs=qkt[:, h, :],
                                                                   start=True, stop=True), r=[qktb], w=[pqkb])
                        xm, xmb = XM.get()
                        for h in range(4):
                            S.op("dve", lambda e, h=h: e.scalar_tensor_tensor(
                                out=xm[:, h, :], in0=pkk[:, h * 128:(h + 1) * 128], scalar=gs[:, 4 + h:5 + h], in1=dcy[:, h, :],
                                op0=ALU.mult, op1=ALU.mult), r=[pkkb, gsb, dcyb], w=[xmb])
                        S.op("dve", lambda e: e.tensor_tensor(out=xm[:], in0=xm[:], in1=strict4[:], op=ALU.mult), r=[xmb, strict4b], w=[xmb])
                        att, attb = ATT.get()
                        if own:
                            S.op("dve", lambda e: e.tensor_tensor(out=att[:], in0=v4(pqk), in1=dcy[:], op=ALU.mult), r=[pqkb, dcyb], w=[attb])
                        kg, kgb = KG.get()
                        kd, kdb = KD.get()
                        for h in range(4):
                            S.op("act", lambda e, h=h: e.activation(
                                out=kg[:, h, :], in_=nrm[:, 4 + h, :], func=AF.Copy, scale=gs[:, 20 + h:21 + h]),
                                r=[nrmb, gsb], w=[kgb])
                            S.op("act", lambda e, h=h: e.activation(
                                out=kd[:, h, :], in_=nrm[:, 4 + h, :], func=AF.Copy, scale=gs[:, 24 + h:25 + h]),
                                r=[nrmb, gsb], w=[kdb])
                        qd, qdb = QD.get()
                        if own:
                            S.op("dve", lambda e: e.tensor_tensor(out=qd[:], in0=qkt[:, 0:4, :], in1=erow[:], op=ALU.mult), r=[qktb, erowb], w=[qdb])
                        T.update(own=own, ti=ti, t0=t0, xm=xm, xmb=xmb, vbf=vbf, vbfb=vbfb, kg=kg, kgb=kgb, gs=gs, gsb=gsb, att=att, attb=attb,
                                 kd=kd, kdb=kdb, qd=qd, qdb=qdb, erow=erow, erowb=erowb)
                        return T

                    def stage_inv(T):
                        xm, xmb, gs, gsb = T["xm"], T["xmb"], T["gs"], T["gsb"]
                        vbf, vbfb, kg, kgb = T["vbf"], T["vbfb"], T["kg"], T["kgb"]
                        pk, pkb_ = PK.get()
                        S.op("act", lambda e, pk=pk: e.copy(out=pk[:], in_=xm[:]), r=[xmb], w=[pkb_])
                        ptp, ptpb = PSB.get()
                        for h in range(4):
                            S.op("pe", lambda e, h=h: e.transpose(out=ptp[:, h * 128:(h + 1) * 128], in_=xm[:, h, :], identity=ident[:]),
                                 r=[xmb, identb], w=[ptpb])
                        pkt, pktb = PKT.get()
                        evac(lambda pkt=pkt: pkt[:], lambda: v4(ptp), ptpb, [pktb])
                        rf, rfb = RF.get()
                        rb, rbb = RB.get()
                        S.op("dve", lambda e, rf=rf: e.tensor_tensor(out=rf[:], in0=xm[:], in1=eye4[:], op=ALU.add), r=[xmb, eye4b], w=[rfb])
                        S.op("act", lambda e, rb=rb, rf=rf: e.copy(out=rb[:], in_=rf[:]), r=[rfb], w=[rbb])
                        for rnd in range(5):
                            pa, pab = PSB.get()
                            for h in range(4):
                                S.op("pe", lambda e, pa=pa, pk=pk, pkt=pkt, h=h: e.matmul(pa[:, h * 128:(h + 1) * 128], lhsT=pk[:, h, :], rhs=pkt[:, h, :],
                                                                                         start=True, stop=True), r=[pkb_, pktb], w=[pab])
                            if rnd < 4:
                                pb2, pb2b = PSB.get()
                                for h in range(4):
                                    S.op("pe", lambda e, pb2=pb2, pk=pk, pkt=pkt, h=h: e.matmul(pb2[:, h * 128:(h + 1) * 128], lhsT=pkt[:, h, :],
                                                                                               rhs=pk[:, h, :], start=True, stop=True),
                                         r=[pkb_, pktb], w=[pb2b])
                            pktn, pktnb = PKT.get()
                            S.op("act", lambda e, pktn=pktn, pa=pa: e.copy(out=pktn[:], in_=v4(pa)), r=[pab], w=[pktnb])
                            if rnd < 4:
                                pkn, pknb = PK.get()
                                S.op("dve", lambda e, pkn=pkn, pb2=pb2: e.tensor_copy(out=pkn[:], in_=v4(pb2)), r=[pb2b], w=[pknb])
                                pk, pkb_ = pkn, pknb
                            pkt, pktb = pktn, pktnb
                            pr, prb = PSB.get()
                            for h in range(4):
                                S.op("pe", lambda e, pr=pr, pkt=pkt, rb=rb, h=h: e.matmul(pr[:, h * 128:(h + 1) * 128], lhsT=pkt[:, h, :], rhs=rb[:, h, :],
                                                                                         start=True, stop=True), r=[pktb, rbb], w=[prb])
                            rfn, rfnb = RF.get()
                            rbn, rbnb = RB.get()
                            S.op("dve", lambda e, rbn=rbn, rf=rf, pr=pr: e.tensor_tensor(out=rbn[:], in0=v4(pr), in1=rf[:], op=ALU.add),
                                 r=[prb, rfb], w=[rbnb])
                            if rnd < 4:
                                S.op("dve", lambda e, rfn=rfn, rf=rf, pr=pr: e.tensor_tensor(out=rfn[:], in0=v4(pr), in1=rf[:], op=ALU.add),
                                     r=[prb, rfb], w=[rfnb])
                            rf, rfb, rb, rbb = rfn, rfnb, rbn, rbnb
                        pu, pub = PSB.get()
                        pw, pwb = PSB.get()
                        for h in range(4):
                            S.op("pe", lambda e, h=h, rb=rb: e.matmul(pu[:, h * 128:(h + 1) * 128], lhsT=rb[:, h, :], rhs=vbf[:, h, :], start=True, stop=True),
                                 r=[rbb, vbfb], w=[pub])
                        for h in range(4):
                            S.op("pe", lambda e, h=h, rb=rb: e.matmul(pw[:, h * 128:(h + 1) * 128], lhsT=kg[:, h, :], rhs=rb[:, h, :], start=True, stop=True),
                                 r=[rbb, kgb], w=[pwb])
                        ub, ubb = UB.get()
                        for h in range(4):
                            S.op("dve", lambda e, h=h: e.tensor_scalar(
                                out=ub[:, h, :], in0=pu[:, h * 128:(h + 1) * 128], scalar1=gs[:, h:h + 1], scalar2=None, op0=ALU.mult),
                                r=[pub, gsb], w=[ubb])
                        wt, wtb = WT.get()
                        S.op("act", lambda e: e.copy(out=wt[:], in_=v4(pw)), r=[pwb], w=[wtb])
                        T.update(ub=ub, ubb=ubb, wt=wt, wtb=wtb)

                    def stage_scan(T):
                        gs, gsb, att, attb, kd, kdb, qd, qdb = T["gs"], T["gsb"], T["att"], T["attb"], T["kd"], T["kdb"], T["qd"], T["qdb"]
                        erow, erowb, ub, ubb, wt, wtb, t0 = T["erow"], T["erowb"], T["ub"], T["ubb"], T["wt"], T["wtb"], T["t0"]
                        own, ti = T["own"], T["ti"]
                        if own:
                            ot, otb = OT.get()
                        if rot:
                            fi = None
                            if dr == 0 and ti % OWNB == 0 and ti != OWNB:
                                fi = (ti // OWNB - 1) % 8
                            if dr == 1 and ti % OWNB == OWNB - 1 and ti != NB - 1:
                                fi = ti // OWNB
                            if fi is not None:
                                sf0, sf0b, sb0, sb0b = st8["sf"], st8["sfb"], st8["sb"], st8["sbb"]
                                S.op("dve", lambda e, sf0=sf0, fi=fi: e.tensor_scalar(out=sf0[:], in0=sf0[:], scalar1=wf[:, fi:fi + 1], scalar2=None,
                                                                                      op0=ALU.mult), r=[sf0b, wfb], w=[sf0b])
                                S.op("dve", lambda e, sb0=sb0, fi=fi: e.tensor_scalar(out=sb0[:], in0=sb0[:], scalar1=wf[:, fi:fi + 1], scalar2=None,
                                                                                      op0=ALU.mult), r=[sb0b, wfb], w=[sb0b])
                        for ck in ((0, 1) if dr == 0 else (1, 0)):
                            po_ = ck * 64
                            lastcol = (po_ + 63) if dr == 0 else po_
                            sf, sfb, sb, sbb = st8["sf"], st8["sfb"], st8["sb"], st8["sbb"]
                            pws, pwsb = PSC.get()
                            for h in range(4):
                                S.op("pe", lambda e, pws=pws, sb=sb, h=h: e.matmul(pws[:, h * 128:(h + 1) * 128], lhsT=wt[:, h, :], rhs=sb[:, h, :],
                                                                                   start=True, stop=True), r=[wtb, sbb], w=[pwsb])
                            vn, vnb = VN.get()
                            for h in range(4):
                                S.op("dve", lambda e, vn=vn, pws=pws, h=h, po_=po_: e.scalar_tensor_tensor(
                                    out=vn[po_:po_ + 64, h, :], in0=pws[po_:po_ + 64, h * 128:(h + 1) * 128], scalar=gs[po_:po_ + 64, 4 + h:5 + h],
                                    in1=ub[po_:po_ + 64, h, :], op0=ALU.mult, op1=ALU.add), r=[pwsb, gsb, ubb], w=[vnb])
                            pD, pDb = PSC.get()
                            for h in range(4):
                                S.op("pe", lambda e, pD=pD, vn=vn, h=h, po_=po_: e.matmul(
                                    pD[:, h * 128:(h + 1) * 128], lhsT=kd[po_:po_ + 64, h, :], rhs=vn[po_:po_ + 64, h, :], start=True, stop=True),
                                    r=[kdb, vnb], w=[pDb])
                            if own:
                                pO, pOb = PSC.get()
                                for h in range(4):
                                    S.op("pe", lambda e, pO=pO, sb=sb, h=h: e.matmul(pO[:, h * 128:(h + 1) * 128], lhsT=qd[:, h, :], rhs=sb[:, h, :],
                                                                                     start=True, stop=False), r=[qdb, sbb], w=[pOb])
                                    S.op("pe", lambda e, pO=pO, vn=vn, h=h, po_=po_: e.matmul(
                                        pO[:, h * 128:(h + 1) * 128], lhsT=att[po_:po_ + 64, h, :], rhs=vn[po_:po_ + 64, h, :], start=False, stop=True),
                                        r=[attb, vnb], w=[pOb])
                            sfn, sfnb = SF.get()
                            sbn, sbnb = SB_.get()
                            for h in range(4):
                                S.op("dve", lambda e, sbn=sbn, sf=sf, pD=pD, h=h, lastcol=lastcol: e.scalar_tensor_tensor(
                                    out=sbn[:, h, :], in0=sf[:, h, :], scalar=erow[:, h, lastcol:lastcol + 1], in1=pD[:, h * 128:(h + 1) * 128],
                                    op0=ALU.mult, op1=ALU.add), r=[sfb, erowb, pDb], w=[sbnb])
                            for h in range(4):
                                S.op("dve", lambda e, sfn=sfn, sf=sf, pD=pD, h=h, lastcol=lastcol: e.scalar_tensor_tensor(
                                    out=sfn[:, h, :], in0=sf[:, h, :], scalar=erow[:, h, lastcol:lastcol + 1], in1=pD[:, h * 128:(h + 1) * 128],
                                    op0=ALU.mult, op1=ALU.add), r=[sfb, erowb, pDb], w=[sfnb])
                            if own:
                                S.op("act", lambda e, pO=pO, po_=po_: e.copy(out=ot[po_:po_ + 64, :, :], in_=v4(pO)[po_:po_ + 64]), r=[pOb], w=[otb])
                            st8["sf"], st8["sfb"], st8["sb"], st8["sbb"] = sfn, sfnb, sbn, sbnb
                        if own:
                            S.op("pool", lambda e: e.dma_start(out=ODN[dr].ap()[t0:t0 + 128, :], in_=ot[:].rearrange("p h d -> p (h d)")),
                                 r=[otb], dma=True)

                    if rot and dr == 0:
                        order = list(range(OWNB, NB)) + list(range(OWNB))
                    else:
                        order = list(range(NB)) if dr == 0 else list(range(NB - 1, -1, -1))
                    ctxs = {}
                    for step in range(NB + 2):
                        lists = []
                        if step < NB:
                            S.begin()
                            ctxs[step] = stage_fe(order[step])
                            lists.append(S.end())
                        if 1 <= step < NB + 1:
                            S.begin()
                            stage_inv(ctxs[step - 1])
                            lists.append(S.end())
                        if step >= 2:
                            S.begin()
                            stage_scan(ctxs.pop(step - 2))
                            lists.append(S.end())
                        S.merge(lists)
                    S.flush()
                if STOP == 3:
                    return nc

            with ExitStack() as c5:
                wo, wob = Pool(S, c5, "wo2", [128, NCH, D], BF16).get()
                wm, wmb = Pool(S, c5, "wm", [128, 8, D], BF16).get()
                ng4, ng4b = Pool(S, c5, "ng4", [128, 4, 128], F32).get()
                S.op("sp", lambda e: e.dma_start(out=wo[:], in_=s_f2o.ap().rearrange("(c p) d -> p c d", p=128)), w=[wob], dma=True)
                S.op("sp", lambda e: e.dma_start(out=wm[:], in_=s_wout.ap().rearrange("(k p) c -> p k c", p=128)), w=[wmb], dma=True)
                for h in range(4):
                    S.op("sp", lambda e, h=h: e.dma_start(out=ng4[:, h, :], in_=bc(w_ng, 0, 128)), w=[ng4b], dma=True)
                load_ln(c5, [1, 2])
                B0 = Pool(S, c5, "b0", [128, 4, D], F32, n=1)
                B1 = Pool(S, c5, "b1", [128, 4, D], F32, n=2)
                XTT = Pool(S, c5, "xtt5", [128, 8, 512], BF16, n=1)
                X2TT = Pool(S, c5, "x2tt5", [128, 8, 512], BF16, n=2)
                STB = Pool(S, c5, "st5b", [128, 12], F32, n=2)
                MVB = Pool(S, c5, "mv5b", [128, 8], F32, n=2)
                PSA5 = SubPool(PS.t[0:3])
                PSB5 = SubPool(PS.t[3:8])
                pend5 = [None]
                GT = Pool(S, c5, "gt5", [128, NCH, 512], BF16, n=1)
                WG = Pool(S, c5, "wg5", [128, 8, 2, 256], BF16, n=2)
                SG = Pool(S, c5, "sg5", [128, 512], BF16, n=2)
                ST = Pool(S, c5, "st5", [128, 12], F32, n=2)
                MV = Pool(S, c5, "mv5", [128, 8], F32, n=2)
                OF = Pool(S, c5, "of", [128, 4, 128], F32, n=2)
                OBk = Pool(S, c5, "obk", [128, 4, 128], F32, n=2)
                ZI = Pool(S, c5, "zi", [128, 4, 128], F32, n=2)
                SQ = Pool(S, c5, "sq", [128, 4, 128], F32, n=2)
                RS = Pool(S, c5, "rs", [128, 8], F32, n=2)
                for ti in range(OWN // 512):
                    t0 = ti * 512
                    S.begin()
                    cps[0] = PSA5
                    b0, b0b = B0.get()
                    b1, b1b = B1.get()
                    S.op("sp", lambda e, b0=b0, t0=t0: e.dma_start(
                        out=b0[:], in_=X1.ap()[t0:t0 + 512, :].rearrange("(s p) d -> p s d", p=128)), w=[b0b], dma=True)
                    S.op("sp", lambda e, b1=b1, t0=t0: e.dma_start(
                        out=b1[:, :, 0:512], in_=OA.ap()[t0:t0 + 512, :].rearrange("(s p) d -> p s d", p=128)), w=[b1b], dma=True)
                    for s in range(4):
                        r0 = t0 + s * 128
                        of_, ofb = OF.get()
                        obk, obkb = OBk.get()
                        zi, zib = ZI.get()
                        S.op("sp", lambda e, of_=of_, r0=r0: e.dma_start(out=of_[:].rearrange("p h d -> p (h d)"), in_=ODN[0].ap()[r0:r0 + 128, :]),
                             w=[ofb], dma=True)
                        S.op("sp", lambda e, obk=obk, r0=r0: e.dma_start(out=obk[:].rearrange("p h d -> p (h d)"), in_=ODN[1].ap()[r0:r0 + 128, :]),
                             w=[obkb], dma=True)
                        S.op("sp", lambda e, zi=zi, r0=r0: e.dma_start(out=zi[:].rearrange("p h d -> p (h d)"), in_=ZZ.ap()[r0:r0 + 128, :]),
                             w=[zib], dma=True)
                        S.op("dve", lambda e, of_=of_, obk=obk: e.tensor_tensor(out=of_[:], in0=of_[:], in1=obk[:], op=ALU.add), r=[ofb, obkb], w=[ofb])
                        sq, sqb = SQ.get()
                        rs, rsb = RS.get()
                        S.op("pool", lambda e, sq=sq, of_=of_: e.tensor_tensor(out=sq[:], in0=of_[:], in1=of_[:], op=ALU.mult), r=[ofb], w=[sqb])
                        S.op("dve", lambda e, sq=sq, rs=rs: e.tensor_reduce(out=rs[:, 0:4], in_=sq[:], axis=AX.X, op=ALU.add), r=[sqb], w=[rsb])
                        S.op("act", lambda e, rs=rs: e.activation(out=rs[:, 4:8], in_=rs[:, 0:4], func=AF.Ln, bias=RMS_EPS, scale=1.0 / 128.0),
                             r=[rsb], w=[rsb])
                        S.op("act", lambda e, rs=rs: e.activation(out=rs[:, 4:8], in_=rs[:, 4:8], func=AF.Exp, scale=-0.5), r=[rsb], w=[rsb])
                        S.op("act", lambda e, zi=zi: e.activation(out=zi[:], in_=zi[:], func=AF.Silu), r=[zib], w=[zib])
                        S.op("pool", lambda e, zi=zi: e.tensor_tensor(out=zi[:], in0=zi[:], in1=ng4[:], op=ALU.mult), r=[zib, ng4b], w=[zib])
                        for h in range(4):
                            S.op("dve", lambda e, b1=b1, of_=of_, rs=rs, zi=zi, s=s, h=h: e.scalar_tensor_tensor(
                                out=b1[:, s, 512 + h * 128:512 + (h + 1) * 128], in0=of_[:, h, :], scalar=rs[:, 4 + h:5 + h], in1=zi[:, h, :],
                                op0=ALU.mult, op1=ALU.mult), r=[ofb, rsb, zib], w=[b1b])
                    mT, mTb = XTT.get()
                    transpose_tok(b1, b1b, mT, mTb, 4)
                    S.op("act", lambda e, b0=b0: e.mul(out=b0[:], in_=b0[:], mul=ALPHA), r=[b0b], w=[b0b])
                    for s in range(4):
                        for nh in range(2):
                            po, pob = cps[0].get()
                            for k in range(8):
                                S.op("pe", lambda e, po=po, mT=mT, k=k, s=s, nh=nh: e.matmul(
                                    po[:, :], lhsT=mT[:, k, s * 128:(s + 1) * 128], rhs=wm[:, k, nh * 512:(nh + 1) * 512],
                                    start=(k == 0), stop=(k == 7)), r=[mTb, wmb], w=[pob])
                            S.op("dve", lambda e, po=po, b1=b1, b0=b0, s=s, nh=nh: e.tensor_tensor(
                                out=b1[:, s, nh * 512:(nh + 1) * 512], in0=po[:, :], in1=b0[:, s, nh * 512:(nh + 1) * 512], op=ALU.add),
                                r=[pob, b0b], w=[b1b])
                    layer_norm(b1, b1b, 1, (ST, MV))
                    x2T, x2Tb = X2TT.get()
                    transpose_tok(b1, b1b, x2T, x2Tb, 4)
                    S.op("act", lambda e, b1=b1: e.mul(out=b1[:], in_=b1[:], mul=ALPHA), r=[b1b], w=[b1b])
                    la = S.end()
                    S.begin()
                    cps[0] = PSB5
                    ffn(x2T, x2Tb, b1, b1b, s_f2i, wo, wob, GT, WG, SG)
                    layer_norm(b1, b1b, 2, (STB, MVB))
                    S.op("pool", lambda e, b1=b1, t0=t0: e.dma_start(
                        out=y_d.ap()[t0:t0 + 512, :].rearrange("(s p) d -> p s d", p=128), in_=b1[:]), r=[b1b], dma=True)
                    lb = S.end()
                    S.merge([la] + ([pend5[0]] if pend5[0] else []))
                    pend5[0] = lb
                S.merge([pend5[0]])
                cps[0] = PS
                S.flush()
    return nc


_NC_CACHE = {}


def _run(seqs, in_maps, n_cores):
    key = tuple(seqs)
    if key not in _NC_CACHE:
        _NC_CACHE[key] = build_nc(seqs)
    nc = _NC_CACHE[key]
    return run_bass_kernel_spmd(nc, in_maps, core_ids=list(range(n_cores)))


def _common_inputs(inp):
    f = lambda a: np.ascontiguousarray(np.asarray(a, dtype=np.float32))
    m = {
        "ffn1_w_in": f(inp["ffn1_w_in"][0]), "ffn1_w_out": f(inp["ffn1_w_out"][0]),
        "w_in": f(inp["w_in"][0]), "conv_w": f(inp["conv_w"][0]),
        "attn_sink": f(inp["attn_sink"]).reshape(1, 8),
        "dn_a_log": f(inp["dn_a_log"]).reshape(1, 8), "dn_dt_bias": f(inp["dn_dt_bias"]).reshape(1, 8),
        "dn_norm_gain": f(inp["dn_norm_gain"]).reshape(1, 128),
        "w_out": f(inp["w_out"][0]), "ffn2_w_in": f(inp["ffn2_w_in"][0]), "ffn2_w_out": f(inp["ffn2_w_out"][0]),
        "ln_gain": f(inp["ln_gain"]).reshape(1, 3 * D), "ln_bias": f(inp["ln_bias"]).reshape(1, 3 * D),
    }
    wq = m["w_in"][:, 0:512].reshape(D, 2, 4, 64).transpose(0, 2, 1, 3).reshape(D, 512)
    m["w_in"] = np.ascontiguousarray(np.concatenate([wq, m["w_in"][:, 512:]], axis=1))
    m.update(_consts())
    return m


def kernel(**inputs):
    xp = np.asarray(inputs["x_prompt"], dtype=np.float32)
    xs = np.asarray(inputs["x_sample"], dtype=np.float32)
    common = _common_inputs(inputs)
    seqs = (("s", xs.shape[1], False), ("p", xp.shape[1], True))
    Lp = xp.shape[1]
    sl = Lp // N_CORES
    in_maps = []
    for c in range(N_CORES):
        m = dict(common)
        m["x_s"] = np.ascontiguousarray(xs[c])
        m["x_p"] = np.ascontiguousarray(np.concatenate([xp[0, c * sl:], xp[0, :c * sl]], axis=0))
        m["wflag"] = np.array([[0.0 if (c + s_) % 8 == 7 else 1.0 for s_ in range(8)]], np.float32)
        in_maps.append(m)
    res = _run(seqs, in_maps, N_CORES)
    y_s = np.stack([np.asarray(res.results[c]["y_s"], dtype=np.float32) for c in range(N_CORES)], 0)
    y_p = np.concatenate([np.asarray(res.results[c]["y_p"], dtype=np.float32) for c in range(N_CORES)], 0)[None]
    return (y_p, y_s)
```

```python
from contextlib import ExitStack
import numpy as np
import ml_dtypes
import concourse.bass as bass
import concourse.mybir as mybir
from concourse.bass_utils import run_bass_kernel_spmd

F32 = mybir.dt.float32
BF16 = mybir.dt.bfloat16
AF = mybir.ActivationFunctionType
ALU = mybir.AluOpType
AX = mybir.AxisListType

D = 1024
DFF = 2816
NCH = DFF // 128
PROJ = 2832
ALPHA = 2.0 ** 0.25
LN_EPS = 1e-5
RMS_EPS = 1e-6
NEG = -1.0e6
N_CORES = 8
L_S = 8192
L_P = 16384


class Buf:
    __slots__ = ("lw", "rd", "excl", "swt", "srt")

    def __init__(self):
        self.lw = None
        self.rd = []
        self.excl = False
        self.swt = 0.0
        self.srt = 0.0


DMA_K = {"sp": 8, "pool": 4, "act": 4}
COMPUTE = ("pe", "act", "dve", "pool")


class Sched:
    def __init__(self, nc, ctx):
        self.nc = nc
        self.ops = []
        self.cur = None
        self.eng_t = {}
        self.csem = {e: ctx.enter_context(nc.semaphore("c_" + e)) for e in COMPUTE}
        self.ccnt = {e: 0 for e in COMPUTE}
        self.dsem = {q: [ctx.enter_context(nc.semaphore(f"d_{q}{i}")) for i in range(k)]
                     for q, k in DMA_K.items()}
        self.dcnt = {q: 0 for q in DMA_K}
        self.bufs = []

    def buf(self):
        b = Buf()
        self.bufs.append(b)
        return b

    COST = {"pe": 230.0, "act": 450.0, "dve": 350.0, "pool": 2500.0, "sp": 100.0}

    def op(self, eng, fn, r=(), w=(), dma=False, c=None):
        o = (eng, fn, tuple(r), tuple(w), dma, c)
        if self.cur is None:
            self._place(o)
        else:
            self.cur.append(o)

    def _est(self, o):
        eng, fn, r, w, dma, c = o
        t = self.eng_t.get(eng, 0.0)
        for b in r:
            if b.swt + 200.0 > t:
                t = b.swt + 200.0
        for b in w:
            m = max(b.swt, b.srt) + 200.0
            if m > t:
                t = m
        return t

    def _place(self, o):
        eng, fn, r, w, dma, c = o
        t = self._est(o)
        if dma:
            self.eng_t[eng] = t + 100.0
            fin = t + (c if c is not None else 4000.0)
        else:
            fin = t + (c if c is not None else self.COST[eng])
            self.eng_t[eng] = fin
        for b in r:
            if fin > b.srt:
                b.srt = fin
        for b in w:
            b.swt = fin
            b.srt = 0.0
        self.ops.append((eng, fn, r, w, dma))

    def begin(self):
        self.cur = []

    def end(self):
        c = self.cur
        self.cur = None
        return c

    def merge(self, lists):
        lists = [l for l in lists if l]
        ptr = [0] * len(lists)
        while True:
            best = None
            for k, l in enumerate(lists):
                if ptr[k] < len(l):
                    t = self._est(l[ptr[k]])
                    key = (t, ptr[k] / len(l))
                    if best is None or key < best[0]:
                        best = (key, k)
            if best is None:
                break
            k = best[1]
            self._place(lists[k][ptr[k]])
            ptr[k] += 1

    def flush(self):
        nc = self.nc
        ops = self.ops
        n = len(ops)
        deps = [None] * n
        for i, (eng, fn, r, w, dma) in enumerate(ops):
            d = set()
            for b in r:
                if b.lw is not None:
                    d.add(b.lw)
                if b.excl:
                    for q in b.rd:
                        if ops[q][0] != eng:
                            d.add(q)
            for b in w:
                if b.lw is not None:
                    d.add(b.lw)
                d.update(b.rd)
            d.discard(i)
            for b in r:
                b.rd.append(i)
            for b in w:
                b.lw = i
                b.rd = []
            deps[i] = d
        need_inc = [False] * n
        for i in range(n):
            eng, _, _, _, dma = ops[i]
            keep = []
            for p in deps[i]:
                pe, _, _, _, pdma = ops[p]
                if (not dma) and (not pdma) and pe == eng == "pe":
                    continue
                keep.append(p)
                if not pdma:
                    need_inc[p] = True
            deps[i] = keep
        target = [None] * n
        dma_prev = [None] * n
        per_eng = {e: [] for e in ("pe", "act", "dve", "pool", "sp")}
        for i in range(n):
            eng, _, _, _, dma = ops[i]
            per_eng[eng].append(i)
            if dma:
                j = self.dcnt[eng]
                k = DMA_K[eng]
                self.dcnt[eng] = j + 1
                target[i] = (self.dsem[eng][j % k], 16 * (j // k + 1))
                if j >= k:
                    dma_prev[i] = (self.dsem[eng][j % k], 16 * (j // k))
            elif need_inc[i]:
                self.ccnt[eng] += 1
                target[i] = (self.csem[eng], self.ccnt[eng])
        final = {}
        for i in range(n):
            if target[i] is not None:
                s, v = target[i]
                final[id(s)] = (s, max(v, final.get(id(s), (s, 0))[1]))

        def emit(ename, e):
            waited = {}

            def wait(s, v):
                if waited.get(id(s), 0) >= v:
                    return
                waited[id(s)] = v
                e.wait_ge(s, v)

            for i in per_eng[ename]:
                eng, fn, _, _, dma = ops[i]
                if dma_prev[i] is not None:
                    wait(*dma_prev[i])
                for p in deps[i]:
                    wait(*target[p])
                ins = fn(e)
                if target[i] is not None:
                    s, v = target[i]
                    ins.then_inc(s, 16 if dma else 1)
            for s, v in final.values():
                wait(s, v)

        with nc.Block() as block:
            @block.sync
            def _(e):
                emit("sp", e)

            @block.tensor
            def _(e):
                emit("pe", e)

            @block.scalar
            def _(e):
                emit("act", e)

            @block.vector
            def _(e):
                emit("dve", e)

            @block.gpsimd
            def _(e):
                emit("pool", e)
        self.ops = []
        self.eng_t = {}
        for b in self.bufs:
            b.lw = None
            b.rd = []
            b.swt = 0.0
            b.srt = 0.0


class Pool:
    uid = 0

    def __init__(self, S, ctx, name, shape, dtype, n=1, psum=False):
        nc = S.nc
        self.t = []
        for i in range(n):
            alloc = nc.psum_tensor if psum else nc.sbuf_tensor
            Pool.uid += 1
            h = ctx.enter_context(alloc(f"{name}_{i}_{Pool.uid}", list(shape), dtype))
            bb = S.buf()
            bb.excl = psum
            self.t.append((h, bb))
        self.i = 0
        self.S = S

    def get(self):
        r = self.t[self.i % len(self.t)]
        self.i += 1
        return r


class SubPool:
    def __init__(self, items):
        self.t = list(items)
        self.i = 0

    def get(self):
        r = self.t[self.i % len(self.t)]
        self.i += 1
        return r


def _consts():
    c = {}
    c["ident"] = np.eye(128, dtype=np.float32)
    c["ones"] = np.ones((128, 128), np.float32)
    tk = np.arange(128)[:, None]
    tq = np.arange(128)[None, :]
    ab = np.zeros((128, 3, 2, 4, 128), np.float32)
    for kb in range(3):
        dist = np.abs(tq - tk - (kb - 1) * 128)
        for kvh in range(2):
            for g in range(4):
                h = kvh * 4 + g
                slope = 2.0 ** (-8.0 * (h + 1) / 8.0)
                ab[:, kb, kvh, g, :] = np.where(dist <= 128, -slope * dist, NEG)
    c["abias"] = ab.reshape(128, 3 * 2 * 512)
    a = np.arange(128)
    same = (a[:, None] // 64) == (a[None, :] // 64)
    dm = np.zeros((128, 2, 5, 128), np.float32)
    for d in range(2):
        if d == 0:
            le = a[:, None] <= a[None, :]
            lt = a[:, None] < a[None, :]
        else:
            le = a[:, None] >= a[None, :]
            lt = a[:, None] > a[None, :]
        dm[:, d, 0, :] = (same & le)
        dm[:, d, 1, :] = same
        dm[:, d, 2, :] = np.where(same & le, 0.0, NEG)
        dm[:, d, 3, :] = (same & lt)
        dm[:, d, 4, :] = np.eye(128)
    c["dmask"] = dm.reshape(128, 2 * 5 * 128)
    return c


import os
STOP = int(os.environ.get("KSTOP", "99"))
KSUB = int(os.environ.get("KSUB", "0"))
KDBG = int(os.environ.get("KDBG", "0"))


class _Stop(Exception):
    pass


def build_nc(seqs):
    nc = bass.Bass("TRN2", target_bir_lowering=False)
    try:
        _build(nc, seqs)
    except _Stop:
        pass
    return nc


def _build(nc, seqs):
    ctx = ExitStack()
    with ctx:
        S = Sched(nc, ctx)

        def chk(n):
            if KSUB == n:
                S.flush()
                raise _Stop()

        def dram(name, shape, dt, kind):
            return nc.dram_tensor(name, list(shape), dt, kind=kind)

        xin = {nm: dram("x_" + nm, [L, D], F32, "ExternalInput") for nm, L, rot in seqs}
        yout = {nm: dram("y_" + nm, [(L // 8) if rot else L, D], F32, "ExternalOutput") for nm, L, rot in seqs}
        w_flag = dram("wflag", [1, 8], F32, "ExternalInput")
        w_f1i = dram("ffn1_w_in", [D, 2 * DFF], F32, "ExternalInput")
        w_f1o = dram("ffn1_w_out", [DFF, D], F32, "ExternalInput")
        w_in = dram("w_in", [D, PROJ], F32, "ExternalInput")
        w_cv = dram("conv_w", [5, 1536], F32, "ExternalInput")
        w_sink = dram("attn_sink", [1, 8], F32, "ExternalInput")
        w_alog = dram("dn_a_log", [1, 8], F32, "ExternalInput")
        w_dtb = dram("dn_dt_bias", [1, 8], F32, "ExternalInput")
        w_ng = dram("dn_norm_gain", [1, 128], F32, "ExternalInput")
        w_out = dram("w_out", [D, D], F32, "ExternalInput")
        w_f2i = dram("ffn2_w_in", [D, 2 * DFF], F32, "ExternalInput")
        w_f2o = dram("ffn2_w_out", [DFF, D], F32, "ExternalInput")
        w_lng = dram("ln_gain", [1, 3 * D], F32, "ExternalInput")
        w_lnb = dram("ln_bias", [1, 3 * D], F32, "ExternalInput")
        c_ident = dram("ident", [128, 128], F32, "ExternalInput")
        c_ones = dram("ones", [128, 128], F32, "ExternalInput")
        c_abias = dram("abias", [128, 3072], F32, "ExternalInput")
        c_dmask = dram("dmask", [128, 1280], F32, "ExternalInput")

        LM = max(L for _, L, _r in seqs)
        s_f1i = dram("s_f1i", [11, 128, 4096], BF16, "Internal")
        s_f1o = dram("s_f1o", [DFF, D], BF16, "Internal")
        s_win = dram("s_win", [D, PROJ], BF16, "Internal")
        s_wout = dram("s_wout", [D, D], BF16, "Internal")
        s_f2i = dram("s_f2i", [11, 128, 4096], BF16, "Internal")
        s_f2o = dram("s_f2o", [DFF, D], BF16, "Internal")
        DK = "ExternalOutput" if KDBG else "Internal"
        X1 = dram("X1", [LM, D], F32, DK)
        QT = dram("QT", [128, 4, LM], BF16, "Internal")
        KT = dram("KT", [128, LM], BF16, "Internal")
        VX = dram("VX", [LM, 130], BF16, "Internal")
        DT = dram("DT", [128, 12, LM], F32, "Internal")
        ZZ = dram("ZZ", [LM, 512], F32, "Internal")
        BA = dram("BA", [LM, 16], F32, "Internal")
        OA = dram("OA", [LM, 512], F32, DK)
        ODN = [dram(f"ODN{d}", [LM, 512], F32, DK) for d in range(2)]
        FEK = dram("FEK", [LM // 128, 128, 512], F32, "Internal")
        FEV = dram("FEV", [LM // 128, 128, 512], BF16, "Internal")
        FEQ = dram("FEQ", [LM // 128, 128, 1024], BF16, "Internal")

        def bc(t, off, n):
            return bass.AP(t, off, [[0, 128], [1, n]])

        PS = Pool(S, ctx, "ps", [128, 512], F32, n=8, psum=True)
        cps = [PS]
        ident, identb = Pool(S, ctx, "ident", [128, 128], F32).get()
        ones, onesb = Pool(S, ctx, "ones", [128, 128], F32).get()
        wf, wfb = Pool(S, ctx, "wf", [128, 8], F32).get()
        S.op("sp", lambda e: e.dma_start(out=wf[:], in_=bc(w_flag, 0, 8)), w=[wfb], dma=True)
        lnc = {}

        def load_ln(cx, lis):
            for li in lis:
                g, gb = Pool(S, cx, "lng", [128, D], F32).get()
                b, bb = Pool(S, cx, "lnb", [128, D], F32).get()
                S.op("sp", lambda e, g=g, li=li: e.dma_start(out=g[:], in_=bc(w_lng, li * D, D)), w=[gb], dma=True)
                S.op("sp", lambda e, b=b, li=li: e.dma_start(out=b[:], in_=bc(w_lnb, li * D, D)), w=[bb], dma=True)
                lnc[li] = (g, gb, b, bb)
        S.op("sp", lambda e: e.dma_start(out=ident[:], in_=c_ident.ap()), w=[identb], dma=True)
        S.op("sp", lambda e: e.dma_start(out=ones[:], in_=c_ones.ap()), w=[onesb], dma=True)

        with ExitStack() as c0:
            STG = Pool(S, c0, "stg", [128, 2048], F32, n=3)
            STB = Pool(S, c0, "stb", [128, 2048], BF16, n=3)
            rr = [0]
            def cast_op(a, ab_, b, bb_, cw):
                k = rr[0] % 3
                rr[0] += 1
                if k == 0:
                    S.op("dve", lambda e: e.tensor_copy(out=b[:, 0:cw], in_=a[:, 0:cw]), r=[ab_], w=[bb_])
                elif k == 1:
                    S.op("act", lambda e: e.copy(out=b[:, 0:cw], in_=a[:, 0:cw]), r=[ab_], w=[bb_])
                else:
                    S.op("pool", lambda e: e.tensor_copy(out=b[:, 0:cw], in_=a[:, 0:cw]), r=[ab_], w=[bb_])

            def conv_ffn_in(src, dst):
                d5 = dst.ap().rearrange("j p (k u c) -> j p k u c", k=8, u=2, c=256)
                for k in range(8):
                    for u in range(2):
                        for jj in range(0, 11, 4):
                            ng = min(4, 11 - jj)
                            a, ab_ = STG.get()
                            b, bb_ = STB.get()
                            S.op("sp", lambda e, a=a, k=k, u=u, jj=jj, ng=ng: e.dma_start(
                                out=a[:, 0:ng * 256], in_=src.ap()[k * 128:(k + 1) * 128, u * DFF + jj * 256:u * DFF + (jj + ng) * 256]),
                                w=[ab_], dma=True)
                            cast_op(a, ab_, b, bb_, ng * 256)
                            S.op("pool", lambda e, b=b, k=k, u=u, jj=jj, ng=ng: e.dma_start(
                                out=d5[jj:jj + ng, :, k, u, :].rearrange("j p c -> p j c"),
                                in_=b[:, 0:ng * 256].rearrange("p (j c) -> p j c", c=256)), r=[bb_], dma=True)

            conv_ffn_in(w_f1i, s_f1i)
            conv_ffn_in(w_f2i, s_f2i)
            for src, dst, R, C in ((w_f1o, s_f1o, DFF, D),
                                   (w_in, s_win, D, PROJ), (w_out, s_wout, D, D),
                                   (w_f2o, s_f2o, DFF, D)):
                for r0 in range(0, R, 128):
                    for c0_ in range(0, C, 2048):
                        cw = min(2048, C - c0_)
                        a, ab_ = STG.get()
                        b, bb_ = STB.get()
                        S.op("sp", lambda e, a=a, r0=r0, c0_=c0_, cw=cw, src=src:
                             e.dma_start(out=a[:, 0:cw], in_=src.ap()[r0:r0 + 128, c0_:c0_ + cw]),
                             w=[ab_], dma=True)
                        k = rr[0] % 3
                        rr[0] += 1
                        if k == 0:
                            S.op("dve", lambda e, a=a, b=b, cw=cw: e.tensor_copy(out=b[:, 0:cw], in_=a[:, 0:cw]),
                                 r=[ab_], w=[bb_])
                        elif k == 1:
                            S.op("act", lambda e, a=a, b=b, cw=cw: e.copy(out=b[:, 0:cw], in_=a[:, 0:cw]),
                                 r=[ab_], w=[bb_])
                        else:
                            S.op("pool", lambda e, a=a, b=b, cw=cw: e.tensor_copy(out=b[:, 0:cw], in_=a[:, 0:cw]),
                                 r=[ab_], w=[bb_])
                        S.op("pool", lambda e, b=b, r0=r0, c0_=c0_, cw=cw, dst=dst:
                             e.dma_start(out=dst.ap()[r0:r0 + 128, c0_:c0_ + cw], in_=b[:, 0:cw]),
                             r=[bb_], dma=True)
            S.flush()
        if STOP == 0:
            return nc

        def transpose_tok(src, srcb, dst, dstb, nsub, rot=[0]):
            for k in range(8):
                p, pb = cps[0].get()
                for s in range(nsub):
                    S.op("pe", lambda e, p=p, s=s, k=k: e.transpose(
                        out=p[:, s * 128:(s + 1) * 128], in_=src[:, s, k * 128:(k + 1) * 128], identity=ident[:]),
                        r=[srcb, identb], w=[pb])
                rot[0] += 1
                if rot[0] % 2:
                    S.op("dve", lambda e, p=p, k=k: e.tensor_copy(out=dst[:, k, 0:nsub * 128], in_=p[:, 0:nsub * 128]),
                         r=[pb], w=[dstb])
                else:
                    S.op("act", lambda e, p=p, k=k: e.copy(out=dst[:, k, 0:nsub * 128], in_=p[:, 0:nsub * 128]),
                         r=[pb], w=[dstb])

        def layer_norm(y, yb, li, pools, nsub=4):
            ST, MV = pools
            for s in range(nsub):
                st, stb = ST.get()
                mv, mvb = MV.get()
                S.op("dve", lambda e, st=st, s=s: e.bn_stats(out=st[:, 0:6], in_=y[:, s, 0:512]), r=[yb], w=[stb])
                S.op("dve", lambda e, st=st, s=s: e.bn_stats(out=st[:, 6:12], in_=y[:, s, 512:1024]), r=[yb], w=[stb])
                S.op("dve", lambda e, st=st, mv=mv: e.bn_aggr(out=mv[:, 0:2], in_=st[:, 0:12]), r=[stb], w=[mvb])
                S.op("act", lambda e, mv=mv: e.activation(out=mv[:, 2:3], in_=mv[:, 1:2], func=AF.Ln, bias=LN_EPS, scale=1.0),
                     r=[mvb], w=[mvb])
                S.op("act", lambda e, mv=mv: e.activation(out=mv[:, 3:4], in_=mv[:, 2:3], func=AF.Exp, scale=-0.5),
                     r=[mvb], w=[mvb])
                S.op("dve", lambda e, mv=mv: e.scalar_tensor_tensor(out=mv[:, 4:5], in0=mv[:, 0:1], scalar=-1.0, in1=mv[:, 3:4],
                                                                    op0=ALU.mult, op1=ALU.mult), r=[mvb], w=[mvb])
                S.op("act", lambda e, mv=mv, s=s: e.activation(out=y[:, s, :], in_=y[:, s, :], func=AF.Identity,
                                                              bias=mv[:, 4:5], scale=mv[:, 3:4]), r=[mvb, yb], w=[yb], c=1500.0)
                lg, lgb, lb_, lbb = lnc[li]
                S.op("dve", lambda e, s=s, lg=lg: e.tensor_tensor(out=y[:, s, :], in0=y[:, s, :], in1=lg[:], op=ALU.mult), r=[yb, lgb], w=[yb], c=1200.0)
                S.op("dve", lambda e, s=s, lb_=lb_: e.tensor_tensor(out=y[:, s, :], in0=y[:, s, :], in1=lb_[:], op=ALU.add), r=[yb, lbb], w=[yb], c=1200.0)

        def ffn(xT, xTb, xa, xab, wsc, wo, wob, GT, WG, SG):
            gT, gTb = GT.get()
            for j in range(11):
                wg, wgb = WG.get()
                S.op("sp" if j % 2 == 0 else "act", lambda e, wg=wg, j=j: e.dma_start(
                    out=wg[:].rearrange("p k u c -> p (k u c)"), in_=wsc.ap()[j]), w=[wgb], dma=True)
                for hf in range(2):
                    c = 2 * j + hf
                    pg, pgb = cps[0].get()
                    pu, pub = cps[0].get()
                    for k in range(8):
                        S.op("pe", lambda e, pg=pg, wg=wg, k=k, hf=hf: e.matmul(
                            pg[:, :], lhsT=wg[:, k, 0, hf * 128:(hf + 1) * 128], rhs=xT[:, k, :], start=(k == 0), stop=(k == 7)),
                            r=[wgb, xTb], w=[pgb])
                    for k in range(8):
                        S.op("pe", lambda e, pu=pu, wg=wg, k=k, hf=hf: e.matmul(
                            pu[:, :], lhsT=wg[:, k, 1, hf * 128:(hf + 1) * 128], rhs=xT[:, k, :], start=(k == 0), stop=(k == 7)),
                            r=[wgb, xTb], w=[pub])
                    sg, sgb = SG.get()
                    S.op("act", lambda e, sg=sg, pg=pg: e.activation(out=sg[:, :], in_=pg[:, :], func=AF.Silu), r=[pgb], w=[sgb])
                    S.op("dve", lambda e, sg=sg, pu=pu, c=c: e.tensor_tensor(out=gT[:, c, :], in0=sg[:, :], in1=pu[:, :], op=ALU.mult),
                         r=[sgb, pub], w=[gTb])
            for s in range(4):
                for nh in range(2):
                    po, pob = cps[0].get()
                    for c in range(NCH):
                        S.op("pe", lambda e, po=po, c=c, s=s, nh=nh: e.matmul(
                            po[:, :], lhsT=gT[:, c, s * 128:(s + 1) * 128], rhs=wo[:, c, nh * 512:(nh + 1) * 512],
                            start=(c == 0), stop=(c == NCH - 1)), r=[gTb, wob], w=[pob])
                    S.op("dve", lambda e, po=po, s=s, nh=nh: e.scalar_tensor_tensor(
                        out=xa[:, s, nh * 512:(nh + 1) * 512], in0=po[:, :], scalar=0.5, in1=xa[:, s, nh * 512:(nh + 1) * 512],
                        op0=ALU.mult, op1=ALU.add), r=[pob, xab], w=[xab])

        def v4(p):
            return p[:, :].rearrange("p (h d) -> p h d", d=128)

        evr = [0]

        def evac(dst_fn, p, pb, wbufs, rbufs=()):
            evr[0] += 1
            if evr[0] % 2:
                S.op("dve", lambda e: e.tensor_copy(out=dst_fn(), in_=p()), r=[pb, *rbufs], w=wbufs)
            else:
                S.op("act", lambda e: e.copy(out=dst_fn(), in_=p()), r=[pb, *rbufs], w=wbufs)

        for nm, L, rot in seqs:
            OWN = (L // 8) if rot else L
            OWNB = OWN // 128
            x_d, y_d = xin[nm], yout[nm]
            NT = L // 512
            NB = L // 128
            with ExitStack() as c1:
                wo, wob = Pool(S, c1, "wo1", [128, NCH, D], BF16).get()
                wi, wib = Pool(S, c1, "wi", [128, 8, PROJ], BF16).get()
                S.op("sp", lambda e: e.dma_start(out=wo[:], in_=s_f1o.ap().rearrange("(c p) d -> p c d", p=128)), w=[wob], dma=True)
                S.op("sp", lambda e: e.dma_start(out=wi[:], in_=s_win.ap().rearrange("(k p) c -> p k c", p=128)), w=[wib], dma=True)
                load_ln(c1, [0])
                XT = Pool(S, c1, "xt", [128, 4, D], F32, n=1)
                XTT = Pool(S, c1, "xtt", [128, 8, 512], BF16, n=1)
                X1TT = Pool(S, c1, "x1tt", [128, 8, 512], BF16, n=2)
                PSA1 = SubPool(PS.t[0:4])
                PSB1 = SubPool(PS.t[4:8])
                pend1 = [None]
                GT = Pool(S, c1, "gt", [128, NCH, 512], BF16, n=1)
                WG = Pool(S, c1, "wg", [128, 8, 2, 256], BF16, n=2)
                SG = Pool(S, c1, "sg", [128, 512], BF16, n=2)
                ST = Pool(S, c1, "st", [128, 12], F32, n=2)
                MV = Pool(S, c1, "mv", [128, 8], F32, n=2)
                QTT = Pool(S, c1, "qtt", [128, 4, 512], BF16, n=1)
                KTT = Pool(S, c1, "ktt", [128, 512], BF16, n=1)
                DTT = Pool(S, c1, "dtt", [128, 512], F32, n=3)
                VXT = Pool(S, c1, "vxt", [128, 4, 130], BF16, n=1)
                ZT = Pool(S, c1, "zt", [128, 4, 512], F32, n=1)
                BAT = Pool(S, c1, "bat", [128, 4, 16], F32, n=1)
                for ti in range(NT):
                    t0 = ti * 512
                    S.begin()
                    cps[0] = PSA1
                    xt, xtb = XT.get()
                    S.op("sp", lambda e, xt=xt, t0=t0: e.dma_start(
                        out=xt[:], in_=x_d.ap()[t0:t0 + 512, :].rearrange("(s p) d -> p s d", p=128)), w=[xtb], dma=True)
                    xT, xTb = XTT.get()
                    transpose_tok(xt, xtb, xT, xTb, 4)
                    S.op("act", lambda e, xt=xt: e.mul(out=xt[:], in_=xt[:], mul=ALPHA), r=[xtb], w=[xtb])
                    ffn(xT, xTb, xt, xtb, s_f1i, wo, wob, GT, WG, SG)
                    layer_norm(xt, xtb, 0, (ST, MV))
                    S.op("pool", lambda e, xt=xt, t0=t0: e.dma_start(
                        out=X1.ap()[t0:t0 + 512, :].rearrange("(s p) d -> p s d", p=128), in_=xt[:]), r=[xtb], dma=True)
                    x1T, x1Tb = X1TT.get()
                    transpose_tok(xt, xtb, x1T, x1Tb, 4)
                    la = S.end()
                    S.begin()
                    cps[0] = PSB1
                    qtt, qttb = QTT.get()
                    for c in range(4):
                        p, pb = cps[0].get()
                        for k in range(8):
                            S.op("pe", lambda e, p=p, k=k, c=c, x1T=x1T: e.matmul(
                                p[:, :], lhsT=wi[:, k, c * 128:(c + 1) * 128],
                                rhs=x1T[:, k, :], start=(k == 0), stop=(k == 7)), r=[wib, x1Tb], w=[pb])
                        evac(lambda qtt=qtt, c=c: qtt[:, c, :], lambda p=p: p[:, :], pb, [qttb])
                    S.op("pool", lambda e, qtt=qtt, t0=t0: e.dma_start(out=QT.ap()[:, :, t0:t0 + 512], in_=qtt[:]), r=[qttb], dma=True)
                    ktt, kttb = KTT.get()
                    p, pb = cps[0].get()
                    for k in range(8):
                        S.op("pe", lambda e, p=p, k=k, x1T=x1T: e.matmul(
                            p[:, :], lhsT=wi[:, k, 512:640], rhs=x1T[:, k, :], start=(k == 0), stop=(k == 7)), r=[wib, x1Tb], w=[pb])
                    evac(lambda ktt=ktt: ktt[:, :], lambda p=p: p[:, :], pb, [kttb])
                    S.op("pool", lambda e, ktt=ktt, t0=t0: e.dma_start(out=KT.ap()[:, t0:t0 + 512], in_=ktt[:]), r=[kttb], dma=True)
                    for c in range(12):
                        p, pb = cps[0].get()
                        for k in range(8):
                            S.op("pe", lambda e, p=p, k=k, c=c, x1T=x1T: e.matmul(
                                p[:, :], lhsT=wi[:, k, 768 + c * 128:768 + (c + 1) * 128], rhs=x1T[:, k, :],
                                start=(k == 0), stop=(k == 7)), r=[wib, x1Tb], w=[pb])
                        dtt, dttb = DTT.get()
                        evac(lambda dtt=dtt: dtt[:, :], lambda p=p: p[:, :], pb, [dttb])
                        S.op("sp", lambda e, dtt=dtt, c=c, t0=t0: e.dma_start(out=DT.ap()[:, c, t0:t0 + 512], in_=dtt[:]),
                             r=[dttb], dma=True)
                    vxt, vxtb = VXT.get()
                    zt, ztb = ZT.get()
                    bat, batb = BAT.get()
                    S.op("pool", lambda e, vxt=vxt: e.memset(vxt[:], 1.0), w=[vxtb])
                    for s in range(4):
                        p, pb = cps[0].get()
                        for k in range(8):
                            S.op("pe", lambda e, p=p, k=k, s=s, x1T=x1T: e.matmul(
                                p[:, 0:128], lhsT=x1T[:, k, s * 128:(s + 1) * 128], rhs=wi[:, k, 640:768],
                                start=(k == 0), stop=(k == 7)), r=[wib, x1Tb], w=[pb])
                        evac(lambda vxt=vxt, s=s: vxt[:, s, :].rearrange("p (h c) -> p h c", h=2)[:, :, 0:64],
                             lambda p=p: p[:, 0:128].rearrange("p (h c) -> p h c", h=2), pb, [vxtb])
                        p, pb = cps[0].get()
                        for k in range(8):
                            S.op("pe", lambda e, p=p, k=k, s=s, x1T=x1T: e.matmul(
                                p[:, :], lhsT=x1T[:, k, s * 128:(s + 1) * 128], rhs=wi[:, k, 2304:2816],
                                start=(k == 0), stop=(k == 7)), r=[wib, x1Tb], w=[pb])
                        evac(lambda zt=zt, s=s: zt[:, s, :], lambda p=p: p[:, :], pb, [ztb])
                        p, pb = cps[0].get()
                        for k in range(8):
                            S.op("pe", lambda e, p=p, k=k, s=s, x1T=x1T: e.matmul(
                                p[:, 0:16], lhsT=x1T[:, k, s * 128:(s + 1) * 128], rhs=wi[:, k, 2816:2832],
                                start=(k == 0), stop=(k == 7)), r=[wib, x1Tb], w=[pb])
                        evac(lambda bat=bat, s=s: bat[:, s, :], lambda p=p: p[:, 0:16], pb, [batb])
                    S.op("pool", lambda e, vxt=vxt, t0=t0: e.dma_start(
                        out=VX.ap()[t0:t0 + 512, :].rearrange("(s p) c -> p s c", p=128), in_=vxt[:]), r=[vxtb], dma=True)
                    S.op("pool", lambda e, zt=zt, t0=t0: e.dma_start(
                        out=ZZ.ap()[t0:t0 + 512, :].rearrange("(s p) c -> p s c", p=128), in_=zt[:]), r=[ztb], dma=True)
                    S.op("pool", lambda e, bat=bat, t0=t0: e.dma_start(
                        out=BA.ap()[t0:t0 + 512, :].rearrange("(s p) c -> p s c", p=128), in_=bat[:]), r=[batb], dma=True)
                    lb = S.end()
                    S.merge([la] + ([pend1[0]] if pend1[0] else []))
                    pend1[0] = lb
                S.merge([pend1[0]])
                cps[0] = PS
                S.flush()
            if STOP == 1:
                return nc

            with ExitStack() as c2:
                abias, abiasb = Pool(S, c2, "abias", [128, 3072], F32).get()
                esink, esinkb = Pool(S, c2, "esink", [128, 8], F32).get()
                S.op("sp", lambda e: e.dma_start(out=abias[:], in_=c_abias.ap()), w=[abiasb], dma=True)
                S.op("sp", lambda e: e.dma_start(out=esink[:], in_=bc(w_sink, 0, 8)), w=[esinkb], dma=True)
                S.op("act", lambda e: e.activation(out=esink[:], in_=esink[:], func=AF.Exp), r=[esinkb], w=[esinkb])
                QB = Pool(S, c2, "qb", [128, 4, 128], BF16, n=2)
                KB = Pool(S, c2, "kb", [128, 3, 128], BF16, n=2)
                VB = Pool(S, c2, "vb", [128, 3, 130], BF16, n=2)
                TB = Pool(S, c2, "tb", [128, 512], F32, n=2)
                PT = Pool(S, c2, "pt", [128, 512], BF16, n=6)
                DEN = Pool(S, c2, "den", [128, 16], F32, n=2)
                OB = Pool(S, c2, "ob", [128, 512], F32, n=2)
                for i in range(OWNB):
                    kbs = [kb for kb in range(3) if (rot or 0 <= i + kb - 1 < NB)]
                    lo, hi = kbs[0], kbs[-1] + 1
                    qb, qbb = QB.get()
                    kbt, kbb = KB.get()
                    vb, vbb = VB.get()
                    S.op("sp", lambda e, qb=qb, i=i: e.dma_start(out=qb[:], in_=QT.ap()[:, :, i * 128:(i + 1) * 128]), w=[qbb], dma=True)
                    if rot and (i == 0 or i == OWNB - 1):
                        for kb in range(3):
                            bi = (i + kb - 1) % NB
                            S.op("sp", lambda e, kbt=kbt, kb=kb, bi=bi: e.dma_start(
                                out=kbt[:, kb, :], in_=KT.ap()[:, bi * 128:(bi + 1) * 128]), w=[kbb], dma=True)
                            S.op("sp", lambda e, vb=vb, kb=kb, bi=bi: e.dma_start(
                                out=vb[:, kb, :], in_=VX.ap()[bi * 128:(bi + 1) * 128, :]), w=[vbb], dma=True)
                        hk, fi = (0, 7) if i == 0 else (2, 0)
                        S.op("act", lambda e, vb=vb, hk=hk, fi=fi: e.activation(
                            out=vb[:, hk, :], in_=vb[:, hk, :], func=AF.Copy, scale=wf[:, fi:fi + 1]), r=[vbb, wfb], w=[vbb])
                        lo, hi = 0, 0
                    if hi > lo:
                        S.op("sp", lambda e, kbt=kbt, i=i, lo=lo, hi=hi: e.dma_start(
                            out=kbt[:, lo:hi, :],
                            in_=KT.ap()[:, (i + lo - 1) * 128:(i + hi - 1) * 128].rearrange("p (b t) -> p b t", t=128)), w=[kbb], dma=True)
                        S.op("sp", lambda e, vb=vb, i=i, lo=lo, hi=hi: e.dma_start(
                            out=vb[:, lo:hi, :],
                            in_=VX.ap()[(i + lo - 1) * 128:(i + hi - 1) * 128, :].rearrange("(b p) c -> p b c", p=128)), w=[vbb], dma=True)
                    pos = []
                    for kvh in range(2):
                        po, pob = PS.get()
                        pos.append((po, pob))
                        b0 = kvh * 64
                        pts = []
                        for kb in kbs:
                            ps_, psb = PS.get()
                            S.op("pe", lambda e, ps_=ps_, kbt=kbt, qb=qb, kb=kb, b0=b0: e.matmul(
                                ps_[:, :], lhsT=kbt[b0:b0 + 64, kb, :], rhs=qb[b0:b0 + 64, :, :].rearrange("p c t -> p (c t)"), start=True, stop=True),
                                r=[kbb, qbb], w=[psb])
                            tb, tbb = TB.get()
                            off = (kb * 2 + kvh) * 512
                            S.op("dve", lambda e, tb=tb, ps_=ps_, off=off: e.scalar_tensor_tensor(
                                out=tb[:, :], in0=ps_[:, :], scalar=0.125, in1=abias[:, off:off + 512], op0=ALU.mult, op1=ALU.add),
                                r=[psb, abiasb], w=[tbb])
                            pt, ptb = PT.get()
                            S.op("act", lambda e, tb=tb, pt=pt: e.activation(out=pt[:, :], in_=tb[:, :], func=AF.Exp), r=[tbb], w=[ptb])
                            pts.append((kb, pt, ptb))
                        for g in range(4):
                            for kb, pt, ptb in pts:
                                S.op("pe", lambda e, po=po, pt=pt, vb=vb, g=g, kb=kb, kvh=kvh, st_=(kb == kbs[0]), sp_=(kb == kbs[-1]): e.matmul(
                                    po[:, g * 65:(g + 1) * 65], lhsT=pt[:, g * 128:(g + 1) * 128], rhs=vb[:, kb, kvh * 65:(kvh + 1) * 65],
                                    start=st_, stop=sp_), r=[ptb, vbb], w=[pob])
                    den, denb = DEN.get()
                    ob, obb = OB.get()
                    for kvh in range(2):
                        po, pob = pos[kvh]
                        S.op("dve", lambda e, den=den, po=po, kvh=kvh: e.tensor_tensor(
                            out=den[:, kvh * 4:(kvh + 1) * 4], in0=po[:, 0:260].rearrange("p (g c) -> p g c", c=65)[:, :, 64],
                            in1=esink[:, kvh * 4:(kvh + 1) * 4], op=ALU.add), r=[pob, esinkb], w=[denb])
                    S.op("dve", lambda e, den=den: e.reciprocal(out=den[:, 8:16], in_=den[:, 0:8]), r=[denb], w=[denb])
                    for kvh in range(2):
                        po, pob = pos[kvh]
                        for g in range(4):
                            h = kvh * 4 + g
                            S.op("dve", lambda e, ob=ob, po=po, den=den, g=g, h=h: e.tensor_scalar(
                                out=ob[:, h * 64:(h + 1) * 64], in0=po[:, g * 65:g * 65 + 64], scalar1=den[:, 8 + h:9 + h], scalar2=None,
                                op0=ALU.mult), r=[pob, denb], w=[obb])
                    S.op("pool", lambda e, ob=ob, i=i: e.dma_start(out=OA.ap()[i * 128:(i + 1) * 128, :], in_=ob[:]), r=[obb], dma=True)
                S.flush()
            if STOP == 2:
                return nc

            with ExitStack() as cf:
                cw, cwb = Pool(S, cf, "cw", [128, 12, 5], F32).get()
                for c in range(12):
                    S.op("sp", lambda e, c=c: e.dma_start(out=cw[:, c, :], in_=w_cv.ap()[:, c * 128:(c + 1) * 128].rearrange("j p -> p j"),
                                                          allow_slow_non_contiguous=True), w=[cwb], dma=True)
                XIN = Pool(S, cf, "xin", [128, 12, 132], F32, n=4)
                CA = Pool(S, cf, "ca", [128, 12, 128], F32, n=4)
                SL = Pool(S, cf, "sl", [128, 12, 128], F32, n=4)
                TM = Pool(S, cf, "tm", [128, 12, 128], F32, n=4)
                SS = Pool(S, cf, "ss", [128, 16], F32, n=4)
                JK = Pool(S, cf, "jk", [128, 8, 128], F32, n=4)
                NRM = Pool(S, cf, "nrm", [128, 8, 128], F32, n=4)
                VBF = Pool(S, cf, "vbf", [128, 4, 128], BF16, n=4)
                QKT = Pool(S, cf, "qkt", [128, 8, 128], BF16, n=4)
                PSFS = [(SubPool(PS.t[0:2]), SubPool(PS.t[4:6])), (SubPool(PS.t[2:4]), SubPool(PS.t[6:8]))]
                def fe_tile(ti, PSF1, PSF2):
                    t0 = ti * 128
                    own = (not rot) or ti < OWNB
                    S.begin()
                    xi, xib = XIN.get()
                    a0 = max(t0 - 2, 0)
                    a1 = min(t0 + 130, L)
                    if (a0 != t0 - 2 or a1 != t0 + 130) and not rot:
                        S.op("pool", lambda e: e.memset(xi[:], 0.0), w=[xib])
                    S.op("sp", lambda e: e.dma_start(out=xi[:, :, a0 - (t0 - 2):a1 - (t0 - 2)], in_=DT.ap()[:, :, a0:a1]), w=[xib], dma=True)
                    if rot:
                        if t0 == 0:
                            S.op("sp", lambda e: e.dma_start(out=xi[:, :, 0:2], in_=DT.ap()[:, :, L - 2:L]), w=[xib], dma=True)
                        if t0 + 128 == L:
                            S.op("sp", lambda e: e.dma_start(out=xi[:, :, 130:132], in_=DT.ap()[:, :, 0:2]), w=[xib], dma=True)
                        if ti % OWNB == 0:
                            fi = (ti // OWNB - 1) % 8
                            S.op("act", lambda e: e.activation(out=xi[:, :, 0:2], in_=xi[:, :, 0:2], func=AF.Copy, scale=wf[:, fi:fi + 1]),
                                 r=[xib, wfb], w=[xib])
                        if ti % OWNB == OWNB - 1:
                            fi2 = ti // OWNB
                            S.op("act", lambda e: e.activation(out=xi[:, :, 130:132], in_=xi[:, :, 130:132], func=AF.Copy, scale=wf[:, fi2:fi2 + 1]),
                                 r=[xib, wfb], w=[xib])
                    ca, cab = CA.get()
                    for j in range(5):
                        for c in range(12):
                            if j == 0:
                                S.op("act", lambda e, c=c: e.activation(
                                    out=ca[:, c, :], in_=xi[:, c, 0:128], func=AF.Copy, scale=cw[:, c, 0:1]),
                                    r=[xib, cwb], w=[cab])
                            else:
                                S.op("dve", lambda e, c=c, j=j: e.scalar_tensor_tensor(
                                    out=ca[:, c, :], in0=xi[:, c, j:j + 128], scalar=cw[:, c, j:j + 1], in1=ca[:, c, :],
                                    op0=ALU.mult, op1=ALU.add), r=[xib, cwb, cab], w=[cab])
                    sl, slb = SL.get()
                    S.op("act", lambda e: e.activation(out=sl[:], in_=ca[:], func=AF.Silu), r=[cab], w=[slb])
                    tm, tmb = TM.get()
                    for q4 in range(3):
                        p, pb = PSF1.get()
                        for h in range(4):
                            S.op("pe", lambda e, p=p, q4=q4, h=h: e.transpose(
                                out=p[:, h * 128:(h + 1) * 128], in_=sl[:, q4 * 4 + h, :], identity=ident[:]), r=[slb, identb], w=[pb])
                        evac(lambda q4=q4: tm[:, q4 * 4:(q4 + 1) * 4, :], lambda p=p: v4(p), pb, [tmb])
                    ss, ssb = SS.get()
                    jk, jkb = JK.get()
                    S.op("act", lambda e: e.activation(out=jk[:], in_=tm[:, 0:8, :], func=AF.Square), r=[tmb], w=[jkb])
                    S.op("dve", lambda e: e.tensor_reduce(out=ss[:, 0:8], in_=jk[:], axis=AX.X, op=ALU.add), r=[jkb], w=[ssb])
                    S.op("act", lambda e: e.activation(out=ss[:, 8:16], in_=ss[:, 0:8], func=AF.Ln, bias=RMS_EPS, scale=1.0), r=[ssb], w=[ssb])
                    S.op("act", lambda e: e.activation(out=ss[:, 8:16], in_=ss[:, 8:16], func=AF.Exp, scale=-0.5), r=[ssb], w=[ssb])
                    S.op("dve", lambda e: e.tensor_scalar(out=ss[:, 8:12], in0=ss[:, 8:12], scalar1=128.0 ** -0.5, scalar2=None, op0=ALU.mult),
                         r=[ssb], w=[ssb])
                    l1 = S.end()
                    S.begin()
                    nrm, nrmb = NRM.get()
                    for idx in range(8):
                        if idx % 2:
                            S.op("dve", lambda e, idx=idx: e.tensor_scalar(
                                out=nrm[:, idx, :], in0=tm[:, idx, :], scalar1=ss[:, 8 + idx:9 + idx], scalar2=None, op0=ALU.mult),
                                r=[tmb, ssb], w=[nrmb])
                        else:
                            S.op("act", lambda e, idx=idx: e.activation(
                                out=nrm[:, idx, :], in_=tm[:, idx, :], func=AF.Copy, scale=ss[:, 8 + idx:9 + idx]),
                                r=[tmb, ssb], w=[nrmb])
                    vbf, vbfb = VBF.get()
                    S.op("act", lambda e: e.copy(out=vbf[:], in_=tm[:, 8:12, :]), r=[tmb], w=[vbfb])
                    qkt, qktb = QKT.get()
                    for q4 in ((0, 1) if own else (1,)):
                        p, pb = PSF2.get()
                        for h in range(4):
                            S.op("pe", lambda e, p=p, q4=q4, h=h: e.transpose(
                                out=p[:, h * 128:(h + 1) * 128], in_=nrm[:, q4 * 4 + h, :], identity=ident[:]), r=[nrmb, identb], w=[pb])
                        evac(lambda q4=q4: qkt[:, q4 * 4:(q4 + 1) * 4, :], lambda p=p: v4(p), pb, [qktb])
                    S.op("pool", lambda e: e.dma_start(out=FEK.ap()[ti], in_=nrm[:, 4:8, :].rearrange("p h d -> p (h d)")), r=[nrmb], dma=True)
                    S.op("pool", lambda e: e.dma_start(out=FEV.ap()[ti], in_=vbf[:].rearrange("p h d -> p (h d)")), r=[vbfb], dma=True)
                    if own:
                        S.op("pool", lambda e: e.dma_start(out=FEQ.ap()[ti], in_=qkt[:].rearrange("p h d -> p (h d)")), r=[qktb], dma=True)
                    else:
                        S.op("pool", lambda e: e.dma_start(out=FEQ.ap()[ti, :, 512:1024], in_=qkt[:, 4:8, :].rearrange("p h d -> p (h d)")),
                             r=[qktb], dma=True)
                    l2 = S.end()
                    return l1, l2

                pendf = []
                for tp in range(0, NB, 2):
                    cur1, cur2 = [], []
                    for q_ in range(2):
                        if tp + q_ < NB:
                            l1, l2 = fe_tile(tp + q_, *PSFS[q_])
                            cur1.append(l1)
                            cur2.append(l2)
                    S.merge(cur1 + pendf)
                    pendf = cur2
                S.merge(pendf)
                S.flush()

            def make_dir(dr, c3, PSA, PSB, PSC):
                dmk, dmkb = Pool(S, c3, "dmk", [128, 5, 128], F32).get()
                strict4, strict4b = Pool(S, c3, "strict4", [128, 4, 128], F32).get()
                eye4, eye4b = Pool(S, c3, "eye4", [128, 4, 128], F32).get()
                gpar, gparb = Pool(S, c3, "gpar", [128, 8], F32).get()
                S.op("sp", lambda e: e.dma_start(out=dmk[:], in_=c_dmask.ap()[:, dr * 640:(dr + 1) * 640].rearrange(
                    "p (m i) -> p m i", i=128)), w=[dmkb], dma=True)
                for h in range(4):
                    S.op("sp", lambda e, h=h: e.dma_start(out=strict4[:, h, :], in_=c_dmask.ap()[:, dr * 640 + 384:dr * 640 + 512]),
                         w=[strict4b], dma=True)
                    S.op("sp", lambda e, h=h: e.dma_start(out=eye4[:, h, :], in_=c_ident.ap()), w=[eye4b], dma=True)
                S.op("sp", lambda e: e.dma_start(out=gpar[:, 0:4], in_=bc(w_alog, dr * 4, 4)), w=[gparb], dma=True)
                S.op("sp", lambda e: e.dma_start(out=gpar[:, 4:8], in_=bc(w_dtb, dr * 4, 4)), w=[gparb], dma=True)
                S.op("act", lambda e: e.activation(out=gpar[:, 0:4], in_=gpar[:, 0:4], func=AF.Exp), r=[gparb], w=[gparb])
                S.op("dve", lambda e: e.tensor_scalar(out=gpar[:, 0:4], in0=gpar[:, 0:4], scalar1=-1.0, scalar2=None, op0=ALU.mult),
                     r=[gparb], w=[gparb])
                U = lambda: dmk[:, 0, :]
                BONES = lambda: dmk[:, 1, :]
                MINC = lambda: dmk[:, 2, :]
                BAI = Pool(S, c3, "bai", [128, 16], F32, n=2)
                NRM = Pool(S, c3, "nrm", [128, 8, 128], F32, n=2)
                QKT = Pool(S, c3, "qkt", [128, 8, 128], BF16, n=2)
                GU = Pool(S, c3, "gu", [128, 4, 128], F32, n=1)
                DTMP = Pool(S, c3, "dtmp", [128, 4, 128], F32, n=1)
                DCY = Pool(S, c3, "dcy", [128, 4, 128], F32, n=1)
                GS = Pool(S, c3, "gs", [128, 32], F32, n=3)
                EROW = Pool(S, c3, "erow", [128, 4, 128], F32, n=3)
                ATT = Pool(S, c3, "att", [128, 4, 128], BF16, n=3)
                KD = Pool(S, c3, "kd", [128, 4, 128], BF16, n=3)
                QD = Pool(S, c3, "qd", [128, 4, 128], BF16, n=3)
                XM = Pool(S, c3, "xm", [128, 4, 128], F32, n=2)
                VBF = Pool(S, c3, "vbf", [128, 4, 128], BF16, n=2)
                KG = Pool(S, c3, "kg", [128, 4, 128], BF16, n=2)
                PK = Pool(S, c3, "pk", [128, 4, 128], BF16, n=2)
                PKT = Pool(S, c3, "pkt", [128, 4, 128], BF16, n=2)
                RF = Pool(S, c3, "rf", [128, 4, 128], F32, n=2)
                RB = Pool(S, c3, "rbb", [128, 4, 128], BF16, n=2)
                UB = Pool(S, c3, "ub", [128, 4, 128], F32, n=2)
                WT = Pool(S, c3, "wt", [128, 4, 128], BF16, n=2)
                VN = Pool(S, c3, "vn", [128, 4, 128], BF16, n=2)
                OT = Pool(S, c3, "ot", [128, 4, 128], F32, n=2)
                SF = Pool(S, c3, "sf", [128, 4, 128], F32, n=2)
                SB_ = Pool(S, c3, "sbs", [128, 4, 128], BF16, n=2)
                st8 = {}
                st8["sf"], st8["sfb"] = SF.get()
                st8["sb"], st8["sbb"] = SB_.get()
                S.op("pool", lambda e: e.memset(st8["sf"][:], 0.0), w=[st8["sfb"]])
                S.op("pool", lambda e: e.memset(st8["sb"][:], 0.0), w=[st8["sbb"]])

                def stage_fe(ti):
                    T = {}
                    t0 = ti * 128
                    own = (not rot) or ti < OWNB
                    bai, baib = BAI.get()
                    S.op("sp", lambda e: e.dma_start(out=bai[:], in_=BA.ap()[t0:t0 + 128, :]), w=[baib], dma=True)
                    nrm, nrmb = NRM.get()
                    vbf, vbfb = VBF.get()
                    qkt, qktb = QKT.get()
                    S.op("sp", lambda e: e.dma_start(out=nrm[:, 4:8, :].rearrange("p h d -> p (h d)"), in_=FEK.ap()[ti]), w=[nrmb], dma=True)
                    S.op("sp", lambda e: e.dma_start(out=vbf[:].rearrange("p h d -> p (h d)"), in_=FEV.ap()[ti]), w=[vbfb], dma=True)
                    if own:
                        S.op("sp", lambda e: e.dma_start(out=qkt[:].rearrange("p h d -> p (h d)"), in_=FEQ.ap()[ti]), w=[qktb], dma=True)
                    else:
                        S.op("sp", lambda e: e.dma_start(out=qkt[:, 4:8, :].rearrange("p h d -> p (h d)"), in_=FEQ.ap()[ti, :, 512:1024]),
                             w=[qktb], dma=True)
                    gs, gsb = GS.get()
                    S.op("act", lambda e: e.activation(out=gs[:, 0:4], in_=bai[:, dr * 4:dr * 4 + 4], func=AF.Sigmoid), r=[baib], w=[gsb])
                    S.op("dve", lambda e: e.tensor_scalar(out=gs[:, 4:8], in0=gs[:, 0:4], scalar1=-1.0, scalar2=None, op0=ALU.mult), r=[gsb], w=[gsb])
                    S.op("dve", lambda e: e.tensor_tensor(out=gs[:, 8:12], in0=bai[:, 8 + dr * 4:12 + dr * 4], in1=gpar[:, 4:8], op=ALU.add),
                         r=[baib, gparb], w=[gsb])
                    S.op("act", lambda e: e.activation(out=gs[:, 8:12], in_=gs[:, 8:12], func=AF.Exp), r=[gsb], w=[gsb])
                    S.op("act", lambda e: e.activation(out=gs[:, 8:12], in_=gs[:, 8:12], func=AF.Ln, bias=1.0, scale=1.0), r=[gsb], w=[gsb])
                    S.op("dve", lambda e: e.tensor_tensor(out=gs[:, 8:12], in0=gs[:, 8:12], in1=gpar[:, 0:4], op=ALU.mult), r=[gsb, gparb], w=[gsb])
                    p, pb = PSA.get()
                    S.op("pe", lambda e, p=p: e.matmul(p[:, 0:4], lhsT=U(), rhs=gs[:, 8:12], start=True, stop=True), r=[dmkb, gsb], w=[pb])
                    S.op("pe", lambda e, p=p: e.matmul(p[:, 4:8], lhsT=BONES(), rhs=gs[:, 8:12], start=True, stop=True), r=[dmkb, gsb], w=[pb])
                    S.op("dve", lambda e, p=p: e.tensor_copy(out=gs[:, 12:16], in_=p[:, 0:4]), r=[pb], w=[gsb])
                    S.op("dve", lambda e, p=p: e.tensor_scalar(out=gs[:, 16:20], in0=p[:, 0:4], scalar1=-1.0, scalar2=None, op0=ALU.mult), r=[pb], w=[gsb])
                    S.op("dve", lambda e, p=p: e.tensor_tensor(out=gs[:, 24:28], in0=p[:, 4:8], in1=gs[:, 12:16], op=ALU.subtract), r=[pb, gsb], w=[gsb])
                    S.op("act", lambda e: e.activation(out=gs[:, 20:24], in_=gs[:, 12:16], func=AF.Exp), r=[gsb], w=[gsb])
                    S.op("act", lambda e: e.activation(out=gs[:, 24:28], in_=gs[:, 24:28], func=AF.Exp), r=[gsb], w=[gsb])
                    gu, gub = GU.get()
                    for h in range(4):
                        S.op("act", lambda e, h=h: e.activation(
                            out=gu[:, h, :], in_=U(), func=AF.Copy, scale=gs[:, 8 + h:9 + h]), r=[dmkb, gsb], w=[gub])
                    pg, pgb = PSA.get()
                    for h in range(4):
                        S.op("pe", lambda e, h=h: e.matmul(pg[:, h * 128:(h + 1) * 128], lhsT=ones[:], rhs=gu[:, h, :], start=True, stop=True),
                             r=[onesb, gub], w=[pgb])
                    erow, erowb = EROW.get()
                    S.op("act", lambda e: e.activation(out=erow[:], in_=v4(pg), func=AF.Exp), r=[pgb], w=[erowb])
                    dtmp, dtmpb = DTMP.get()
                    for h in range(4):
                        S.op("dve", lambda e, h=h: e.scalar_tensor_tensor(
                            out=dtmp[:, h, :], in0=pg[:, h * 128:(h + 1) * 128], scalar=gs[:, 16 + h:17 + h], in1=MINC(),
                            op0=ALU.add, op1=ALU.add), r=[pgb, gsb, dmkb], w=[dtmpb])
                    dcy, dcyb = DCY.get()
                    S.op("act", lambda e: e.activation(out=dcy[:], in_=dtmp[:], func=AF.Exp), r=[dtmpb], w=[dcyb])
                    pkk, pkkb = PSA.get()
                    for h in range(4):
                        S.op("pe", lambda e, h=h: e.matmul(pkk[:, h * 128:(h + 1) * 128], lhsT=qkt[:, 4 + h, :], rhs=qkt[:, 4 + h, :],
                                                           start=True, stop=True), r=[qktb], w=[pkkb])
                    xm, xmb = XM.get()
                    for h in range(4):
                        S.op("dve", lambda e, h=h: e.scalar_tensor_tensor(
                            out=xm[:, h, :], in0=pkk[:, h * 128:(h + 1) * 128], scalar=gs[:, 4 + h:5 + h], in1=dcy[:, h, :],
                            op0=ALU.mult, op1=ALU.mult), r=[pkkb, gsb, dcyb], w=[xmb])
                    S.op("dve", lambda e: e.tensor_tensor(out=xm[:], in0=xm[:], in1=strict4[:], op=ALU.mult), r=[xmb, strict4b], w=[xmb])
                    if own:
                        pqk, pqkb = PSA.get()
                        for h in range(4):
                            S.op("pe", lambda e, h=h: e.matmul(pqk[:, h * 128:(h + 1) * 128], lhsT=qkt[:, 4 + h, :], rhs=qkt[:, h, :],
                                                               start=True, stop=True), r=[qktb], w=[pqkb])
                    att, attb = ATT.get()
                    if own:
                        S.op("dve", lambda e: e.tensor_tensor(out=att[:], in0=v4(pqk), in1=dcy[:], op=ALU.mult), r=[pqkb, dcyb], w=[attb])
                    kg, kgb = KG.get()
                    kd, kdb = KD.get()
                    for h in range(4):
                        S.op("act", lambda e, h=h: e.activation(
                            out=kg[:, h, :], in_=nrm[:, 4 + h, :], func=AF.Copy, scale=gs[:, 20 + h:21 + h]),
                            r=[nrmb, gsb], w=[kgb])
                        S.op("act", lambda e, h=h: e.activation(
                            out=kd[:, h, :], in_=nrm[:, 4 + h, :], func=AF.Copy, scale=gs[:, 24 + h:25 + h]),
                            r=[nrmb, gsb], w=[kdb])
                    qd, qdb = QD.get()
                    if own:
                        S.op("dve", lambda e: e.tensor_tensor(out=qd[:], in0=qkt[:, 0:4, :], in1=erow[:], op=ALU.mult), r=[qktb, erowb], w=[qdb])
                    T.update(own=own, ti=ti, t0=t0, xm=xm, xmb=xmb, vbf=vbf, vbfb=vbfb, kg=kg, kgb=kgb, gs=gs, gsb=gsb, att=att, attb=attb,
                             kd=kd, kdb=kdb, qd=qd, qdb=qdb, erow=erow, erowb=erowb)
                    return T

                def stage_inv(T):
                    xm, xmb, gs, gsb = T["xm"], T["xmb"], T["gs"], T["gsb"]
                    vbf, vbfb, kg, kgb = T["vbf"], T["vbfb"], T["kg"], T["kgb"]
                    pk, pkb_ = PK.get()
                    S.op("act", lambda e, pk=pk: e.copy(out=pk[:], in_=xm[:]), r=[xmb], w=[pkb_])
                    ptp, ptpb = PSB.get()
                    for h in range(4):
                        S.op("pe", lambda e, h=h: e.transpose(out=ptp[:, h * 128:(h + 1) * 128], in_=xm[:, h, :], identity=ident[:]),
                             r=[xmb, identb], w=[ptpb])
                    pkt, pktb = PKT.get()
                    evac(lambda pkt=pkt: pkt[:], lambda: v4(ptp), ptpb, [pktb])
                    rf, rfb = RF.get()
                    rb, rbb = RB.get()
                    S.op("dve", lambda e, rf=rf: e.tensor_tensor(out=rf[:], in0=xm[:], in1=eye4[:], op=ALU.add), r=[xmb, eye4b], w=[rfb])
                    S.op("act", lambda e, rb=rb, rf=rf: e.copy(out=rb[:], in_=rf[:]), r=[rfb], w=[rbb])
                    for rnd in range(5):
                        pa, pab = PSB.get()
                        for h in range(4):
                            S.op("pe", lambda e, pa=pa, pk=pk, pkt=pkt, h=h: e.matmul(pa[:, h * 128:(h + 1) * 128], lhsT=pk[:, h, :], rhs=pkt[:, h, :],
                                                                                     start=True, stop=True), r=[pkb_, pktb], w=[pab])
                        pktn, pktnb = PKT.get()
                        S.op("act", lambda e, pktn=pktn, pa=pa: e.copy(out=pktn[:], in_=v4(pa)), r=[pab], w=[pktnb])
                        if rnd < 4:
                            pb2, pb2b = PSB.get()
                            for h in range(4):
                                S.op("pe", lambda e, pb2=pb2, pk=pk, pkt=pkt, h=h: e.matmul(pb2[:, h * 128:(h + 1) * 128], lhsT=pkt[:, h, :],
                                                                                           rhs=pk[:, h, :], start=True, stop=True),
                                     r=[pkb_, pktb], w=[pb2b])
                        if rnd < 4:
                            pkn, pknb = PK.get()
                            S.op("dve", lambda e, pkn=pkn, pb2=pb2: e.tensor_copy(out=pkn[:], in_=v4(pb2)), r=[pb2b], w=[pknb])
                            pk, pkb_ = pkn, pknb
                        pkt, pktb = pktn, pktnb
                        pr, prb = PSB.get()
                        for h in range(4):
                            S.op("pe", lambda e, pr=pr, pkt=pkt, rb=rb, h=h: e.matmul(pr[:, h * 128:(h + 1) * 128], lhsT=pkt[:, h, :], rhs=rb[:, h, :],
                                                                                     start=True, stop=True), r=[pktb, rbb], w=[prb])
                        rfn, rfnb = RF.get()
                        rbn, rbnb = RB.get()
                        S.op("dve", lambda e, rbn=rbn, rf=rf, pr=pr: e.tensor_tensor(out=rbn[:], in0=v4(pr), in1=rf[:], op=ALU.add),
                             r=[prb, rfb], w=[rbnb])
                        if rnd < 4:
                            S.op("dve", lambda e, rfn=rfn, rf=rf, pr=pr: e.tensor_tensor(out=rfn[:], in0=v4(pr), in1=rf[:], op=ALU.add),
                                 r=[prb, rfb], w=[rfnb])
                        rf, rfb, rb, rbb = rfn, rfnb, rbn, rbnb
                    pu, pub = PSB.get()
                    for h in range(4):
                        S.op("pe", lambda e, h=h, rb=rb: e.matmul(pu[:, h * 128:(h + 1) * 128], lhsT=rb[:, h, :], rhs=vbf[:, h, :], start=True, stop=True),
                             r=[rbb, vbfb], w=[pub])
                    ub, ubb = UB.get()
                    for h in range(4):
                        S.op("dve", lambda e, h=h: e.tensor_scalar(
                            out=ub[:, h, :], in0=pu[:, h * 128:(h + 1) * 128], scalar1=gs[:, h:h + 1], scalar2=None, op0=ALU.mult),
                            r=[pub, gsb], w=[ubb])
                    pw, pwb = PSB.get()
                    for h in range(4):
                        S.op("pe", lambda e, h=h, rb=rb: e.matmul(pw[:, h * 128:(h + 1) * 128], lhsT=kg[:, h, :], rhs=rb[:, h, :], start=True, stop=True),
                             r=[rbb, kgb], w=[pwb])
                    wt, wtb = WT.get()
                    S.op("act", lambda e: e.copy(out=wt[:], in_=v4(pw)), r=[pwb], w=[wtb])
                    T.update(ub=ub, ubb=ubb, wt=wt, wtb=wtb)

                def stage_scan(T):
                    gs, gsb, att, attb, kd, kdb, qd, qdb = T["gs"], T["gsb"], T["att"], T["attb"], T["kd"], T["kdb"], T["qd"], T["qdb"]
                    erow, erowb, ub, ubb, wt, wtb, t0 = T["erow"], T["erowb"], T["ub"], T["ubb"], T["wt"], T["wtb"], T["t0"]
                    own, ti = T["own"], T["ti"]
                    if own:
                        ot, otb = OT.get()
                    if rot:
                        fi = None
                        if dr == 0 and ti % OWNB == 0 and ti != OWNB:
                            fi = (ti // OWNB - 1) % 8
                        if dr == 1 and ti % OWNB == OWNB - 1 and ti != NB - 1:
                            fi = ti // OWNB
                        if fi is not None:
                            sf0, sf0b, sb0, sb0b = st8["sf"], st8["sfb"], st8["sb"], st8["sbb"]
                            S.op("dve", lambda e, sf0=sf0, fi=fi: e.tensor_scalar(out=sf0[:], in0=sf0[:], scalar1=wf[:, fi:fi + 1], scalar2=None,
                                                                                  op0=ALU.mult), r=[sf0b, wfb], w=[sf0b])
                            S.op("dve", lambda e, sb0=sb0, fi=fi: e.tensor_scalar(out=sb0[:], in0=sb0[:], scalar1=wf[:, fi:fi + 1], scalar2=None,
                                                                                  op0=ALU.mult), r=[sb0b, wfb], w=[sb0b])
                    for ck in ((0, 1) if dr == 0 else (1, 0)):
                        po_ = ck * 64
                        lastcol = (po_ + 63) if dr == 0 else po_
                        sf, sfb, sb, sbb = st8["sf"], st8["sfb"], st8["sb"], st8["sbb"]
                        pws, pwsb = PSC.get()
                        for h in range(4):
                            S.op("pe", lambda e, pws=pws, sb=sb, h=h: e.matmul(pws[:, h * 128:(h + 1) * 128], lhsT=wt[:, h, :], rhs=sb[:, h, :],
                                                                               start=True, stop=True), r=[wtb, sbb], w=[pwsb])
                        vn, vnb = VN.get()
                        for h in range(4):
                            S.op("dve", lambda e, vn=vn, pws=pws, h=h, po_=po_: e.scalar_tensor_tensor(
                                out=vn[po_:po_ + 64, h, :], in0=pws[po_:po_ + 64, h * 128:(h + 1) * 128], scalar=gs[po_:po_ + 64, 4 + h:5 + h],
                                in1=ub[po_:po_ + 64, h, :], op0=ALU.mult, op1=ALU.add), r=[pwsb, gsb, ubb], w=[vnb])
                        pD, pDb = PSC.get()
                        for h in range(4):
                            S.op("pe", lambda e, pD=pD, vn=vn, h=h, po_=po_: e.matmul(
                                pD[:, h * 128:(h + 1) * 128], lhsT=kd[po_:po_ + 64, h, :], rhs=vn[po_:po_ + 64, h, :], start=True, stop=True),
                                r=[kdb, vnb], w=[pDb])
                        if own:
                            pO, pOb = PSC.get()
                            for h in range(4):
                                S.op("pe", lambda e, pO=pO, sb=sb, h=h: e.matmul(pO[:, h * 128:(h + 1) * 128], lhsT=qd[:, h, :], rhs=sb[:, h, :],
                                                                                 start=True, stop=False), r=[qdb, sbb], w=[pOb])
                                S.op("pe", lambda e, pO=pO, vn=vn, h=h, po_=po_: e.matmul(
                                    pO[:, h * 128:(h + 1) * 128], lhsT=att[po_:po_ + 64, h, :], rhs=vn[po_:po_ + 64, h, :], start=False, stop=True),
                                    r=[attb, vnb], w=[pOb])
                        sfn, sfnb = SF.get()
                        sbn, sbnb = SB_.get()
                        for h in range(4):
                            S.op("dve", lambda e, sbn=sbn, sf=sf, pD=pD, h=h, lastcol=lastcol: e.scalar_tensor_tensor(
                                out=sbn[:, h, :], in0=sf[:, h, :], scalar=erow[:, h, lastcol:lastcol + 1], in1=pD[:, h * 128:(h + 1) * 128],
                                op0=ALU.mult, op1=ALU.add), r=[sfb, erowb, pDb], w=[sbnb])
                        for h in range(4):
                            S.op("dve", lambda e, sfn=sfn, sf=sf, pD=pD, h=h, lastcol=lastcol: e.scalar_tensor_tensor(
                                out=sfn[:, h, :], in0=sf[:, h, :], scalar=erow[:, h, lastcol:lastcol + 1], in1=pD[:, h * 128:(h + 1) * 128],
                                op0=ALU.mult, op1=ALU.add), r=[sfb, erowb, pDb], w=[sfnb])
                        if own:
                            S.op("act", lambda e, pO=pO, po_=po_: e.copy(out=ot[po_:po_ + 64, :, :], in_=v4(pO)[po_:po_ + 64]), r=[pOb], w=[otb])
                        st8["sf"], st8["sfb"], st8["sb"], st8["sbb"] = sfn, sfnb, sbn, sbnb
                    if own:
                        S.op("pool", lambda e: e.dma_start(out=ODN[dr].ap()[t0:t0 + 128, :], in_=ot[:].rearrange("p h d -> p (h d)")),
                             r=[otb], dma=True)

                if rot and dr == 0:
                    order = list(range(OWNB, NB)) + list(range(OWNB))
                else:
                    order = list(range(NB)) if dr == 0 else list(range(NB - 1, -1, -1))
                return order, stage_fe, stage_inv, stage_scan

            with ExitStack() as c3:
                dirs = [make_dir(0, c3, SubPool(PS.t[0:1]), SubPool(PS.t[1:2]), SubPool(PS.t[2:4])),
                        make_dir(1, c3, SubPool(PS.t[4:5]), SubPool(PS.t[5:6]), SubPool(PS.t[6:8]))]
                S.flush()
                ctxs = [{}, {}]
                for step in range(NB + 2):
                    lists = []
                    for d_ in range(2):
                        order, sfe, sinv, sscan = dirs[d_]
                        if step < NB:
                            S.begin()
                            ctxs[d_][step] = sfe(order[step])
                            lists.append(S.end())
                        if 1 <= step < NB + 1:
                            S.begin()
                            sinv(ctxs[d_][step - 1])
                            lists.append(S.end())
                        if step >= 2:
                            S.begin()
                            sscan(ctxs[d_].pop(step - 2))
                            lists.append(S.end())
                    S.merge(lists)
                S.flush()
            if STOP == 3:
                return nc

            with ExitStack() as c5:
                wo, wob = Pool(S, c5, "wo2", [128, NCH, D], BF16).get()
                wm, wmb = Pool(S, c5, "wm", [128, 8, D], BF16).get()
                ng4, ng4b = Pool(S, c5, "ng4", [128, 4, 128], F32).get()
                S.op("sp", lambda e: e.dma_start(out=wo[:], in_=s_f2o.ap().rearrange("(c p) d -> p c d", p=128)), w=[wob], dma=True)
                S.op("sp", lambda e: e.dma_start(out=wm[:], in_=s_wout.ap().rearrange("(k p) c -> p k c", p=128)), w=[wmb], dma=True)
                for h in range(4):
                    S.op("sp", lambda e, h=h: e.dma_start(out=ng4[:, h, :], in_=bc(w_ng, 0, 128)), w=[ng4b], dma=True)
                load_ln(c5, [1, 2])
                B0 = Pool(S, c5, "b0", [128, 4, D], F32, n=1)
                B1 = Pool(S, c5, "b1", [128, 4, D], F32, n=2)
                XTT = Pool(S, c5, "xtt5", [128, 8, 512], BF16, n=1)
                X2TT = Pool(S, c5, "x2tt5", [128, 8, 512], BF16, n=2)
                STB = Pool(S, c5, "st5b", [128, 12], F32, n=2)
                MVB = Pool(S, c5, "mv5b", [128, 8], F32, n=2)
                PSA5 = SubPool(PS.t[0:3])
                PSB5 = SubPool(PS.t[3:8])
                pend5 = [None]
                GT = Pool(S, c5, "gt5", [128, NCH, 512], BF16, n=1)
                WG = Pool(S, c5, "wg5", [128, 8, 2, 256], BF16, n=2)
                SG = Pool(S, c5, "sg5", [128, 512], BF16, n=2)
                ST = Pool(S, c5, "st5", [128, 12], F32, n=2)
                MV = Pool(S, c5, "mv5", [128, 8], F32, n=2)
                OF = Pool(S, c5, "of", [128, 4, 128], F32, n=2)
                OBk = Pool(S, c5, "obk", [128, 4, 128], F32, n=2)
                ZI = Pool(S, c5, "zi", [128, 4, 128], F32, n=2)
                SQ = Pool(S, c5, "sq", [128, 4, 128], F32, n=2)
                RS = Pool(S, c5, "rs", [128, 8], F32, n=2)
                for ti in range(OWN // 512):
                    t0 = ti * 512
                    S.begin()
                    cps[0] = PSA5
                    b0, b0b = B0.get()
                    b1, b1b = B1.get()
                    S.op("sp", lambda e, b0=b0, t0=t0: e.dma_start(
                        out=b0[:], in_=X1.ap()[t0:t0 + 512, :].rearrange("(s p) d -> p s d", p=128)), w=[b0b], dma=True)
                    S.op("sp", lambda e, b1=b1, t0=t0: e.dma_start(
                        out=b1[:, :, 0:512], in_=OA.ap()[t0:t0 + 512, :].rearrange("(s p) d -> p s d", p=128)), w=[b1b], dma=True)
                    for s in range(4):
                        r0 = t0 + s * 128
                        of_, ofb = OF.get()
                        obk, obkb = OBk.get()
                        zi, zib = ZI.get()
                        S.op("sp", lambda e, of_=of_, r0=r0: e.dma_start(out=of_[:].rearrange("p h d -> p (h d)"), in_=ODN[0].ap()[r0:r0 + 128, :]),
                             w=[ofb], dma=True)
                        S.op("sp", lambda e, obk=obk, r0=r0: e.dma_start(out=obk[:].rearrange("p h d -> p (h d)"), in_=ODN[1].ap()[r0:r0 + 128, :]),
                             w=[obkb], dma=True)
                        S.op("sp", lambda e, zi=zi, r0=r0: e.dma_start(out=zi[:].rearrange("p h d -> p (h d)"), in_=ZZ.ap()[r0:r0 + 128, :]),
                             w=[zib], dma=True)
                        S.op("dve", lambda e, of_=of_, obk=obk: e.tensor_tensor(out=of_[:], in0=of_[:], in1=obk[:], op=ALU.add), r=[ofb, obkb], w=[ofb])
                        sq, sqb = SQ.get()
                        rs, rsb = RS.get()
                        S.op("pool", lambda e, sq=sq, of_=of_: e.tensor_tensor(out=sq[:], in0=of_[:], in1=of_[:], op=ALU.mult), r=[ofb], w=[sqb])
                        S.op("dve", lambda e, sq=sq, rs=rs: e.tensor_reduce(out=rs[:, 0:4], in_=sq[:], axis=AX.X, op=ALU.add), r=[sqb], w=[rsb])
                        S.op("act", lambda e, rs=rs: e.activation(out=rs[:, 4:8], in_=rs[:, 0:4], func=AF.Ln, bias=RMS_EPS, scale=1.0 / 128.0),
                             r=[rsb], w=[rsb])
                        S.op("act", lambda e, rs=rs: e.activation(out=rs[:, 4:8], in_=rs[:, 4:8], func=AF.Exp, scale=-0.5), r=[rsb], w=[rsb])
                        S.op("act", lambda e, zi=zi: e.activation(out=zi[:], in_=zi[:], func=AF.Silu), r=[zib], w=[zib])
                        S.op("pool", lambda e, zi=zi: e.tensor_tensor(out=zi[:], in0=zi[:], in1=ng4[:], op=ALU.mult), r=[zib, ng4b], w=[zib])
                        for h in range(4):
                            S.op("dve", lambda e, b1=b1, of_=of_, rs=rs, zi=zi, s=s, h=h: e.scalar_tensor_tensor(
                                out=b1[:, s, 512 + h * 128:512 + (h + 1) * 128], in0=of_[:, h, :], scalar=rs[:, 4 + h:5 + h], in1=zi[:, h, :],
                                op0=ALU.mult, op1=ALU.mult), r=[ofb, rsb, zib], w=[b1b])
                    mT, mTb = XTT.get()
                    transpose_tok(b1, b1b, mT, mTb, 4)
                    S.op("act", lambda e, b0=b0: e.mul(out=b0[:], in_=b0[:], mul=ALPHA), r=[b0b], w=[b0b])
                    for s in range(4):
                        for nh in range(2):
                            po, pob = cps[0].get()
                            for k in range(8):
                                S.op("pe", lambda e, po=po, mT=mT, k=k, s=s, nh=nh: e.matmul(
                                    po[:, :], lhsT=mT[:, k, s * 128:(s + 1) * 128], rhs=wm[:, k, nh * 512:(nh + 1) * 512],
                                    start=(k == 0), stop=(k == 7)), r=[mTb, wmb], w=[pob])
                            S.op("dve", lambda e, po=po, b1=b1, b0=b0, s=s, nh=nh: e.tensor_tensor(
                                out=b1[:, s, nh * 512:(nh + 1) * 512], in0=po[:, :], in1=b0[:, s, nh * 512:(nh + 1) * 512], op=ALU.add),
                                r=[pob, b0b], w=[b1b])
                    layer_norm(b1, b1b, 1, (ST, MV))
                    x2T, x2Tb = X2TT.get()
                    transpose_tok(b1, b1b, x2T, x2Tb, 4)
                    S.op("act", lambda e, b1=b1: e.mul(out=b1[:], in_=b1[:], mul=ALPHA), r=[b1b], w=[b1b])
                    la = S.end()
                    S.begin()
                    cps[0] = PSB5
                    ffn(x2T, x2Tb, b1, b1b, s_f2i, wo, wob, GT, WG, SG)
                    layer_norm(b1, b1b, 2, (STB, MVB))
                    S.op("pool", lambda e, b1=b1, t0=t0: e.dma_start(
                        out=y_d.ap()[t0:t0 + 512, :].rearrange("(s p) d -> p s d", p=128), in_=b1[:]), r=[b1b], dma=True)
                    lb = S.end()
                    S.merge([la] + ([pend5[0]] if pend5[0] else []))
                    pend5[0] = lb
                S.merge([pend5[0]])
                cps[0] = PS
                S.flush()
    return nc


_NC_CACHE = {}


def _run(seqs, in_maps, n_cores):
    key = tuple(seqs)
    if key not in _NC_CACHE:
        _NC_CACHE[key] = build_nc(seqs)
    nc = _NC_CACHE[key]
    return run_bass_kernel_spmd(nc, in_maps, core_ids=list(range(n_cores)))


def _common_inputs(inp):
    f = lambda a: np.ascontiguousarray(np.asarray(a, dtype=np.float32))
    m = {
        "ffn1_w_in": f(inp["ffn1_w_in"][0]), "ffn1_w_out": f(inp["ffn1_w_out"][0]),
        "w_in": f(inp["w_in"][0]), "conv_w": f(inp["conv_w"][0]),
        "attn_sink": f(inp["attn_sink"]).reshape(1, 8),
        "dn_a_log": f(inp["dn_a_log"]).reshape(1, 8), "dn_dt_bias": f(inp["dn_dt_bias"]).reshape(1, 8),
        "dn_norm_gain": f(inp["dn_norm_gain"]).reshape(1, 128),
        "w_out": f(inp["w_out"][0]), "ffn2_w_in": f(inp["ffn2_w_in"][0]), "ffn2_w_out": f(inp["ffn2_w_out"][0]),
        "ln_gain": f(inp["ln_gain"]).reshape(1, 3 * D), "ln_bias": f(inp["ln_bias"]).reshape(1, 3 * D),
    }
    wq = m["w_in"][:, 0:512].reshape(D, 2, 4, 64).transpose(0, 2, 1, 3).reshape(D, 512)
    m["w_in"] = np.ascontiguousarray(np.concatenate([wq, m["w_in"][:, 512:]], axis=1))
    m.update(_consts())
    return m


def kernel(**inputs):
    xp = np.asarray(inputs["x_prompt"], dtype=np.float32)
    xs = np.asarray(inputs["x_sample"], dtype=np.float32)
    common = _common_inputs(inputs)
    seqs = (("s", xs.shape[1], False), ("p", xp.shape[1], True))
    Lp = xp.shape[1]
    sl = Lp // N_CORES
    in_maps = []
    for c in range(N_CORES):
        m = dict(common)
        m["x_s"] = np.ascontiguousarray(xs[c])
        m["x_p"] = np.ascontiguousarray(np.concatenate([xp[0, c * sl:], xp[0, :c * sl]], axis=0))
        m["wflag"] = np.array([[0.0 if (c + s_) % 8 == 7 else 1.0 for s_ in range(8)]], np.float32)
        in_maps.append(m)
    res = _run(seqs, in_maps, N_CORES)
    y_s = np.stack([np.asarray(res.results[c]["y_s"], dtype=np.float32) for c in range(N_CORES)], 0)
    y_p = np.concatenate([np.asarray(res.results[c]["y_p"], dtype=np.float32) for c in range(N_CORES)], 0)[None]
    return (y_p, y_s)
```

```python
from contextlib import ExitStack
import numpy as np
import ml_dtypes
import concourse.bass as bass
import concourse.mybir as mybir
from concourse.bass_utils import run_bass_kernel_spmd

F32 = mybir.dt.float32
BF16 = mybir.dt.bfloat16
AF = mybir.ActivationFunctionType
ALU = mybir.AluOpType
AX = mybir.AxisListType

D = 1024
DFF = 2816
NCH = DFF // 128
PROJ = 2832
ALPHA = 2.0 ** 0.25
LN_EPS = 1e-5
RMS_EPS = 1e-6
NEG = -1.0e6
N_CORES = 8
L_S = 8192
L_P = 16384


class Buf:
    __slots__ = ("lw", "rd", "excl", "swt", "srt")

    def __init__(self):
        self.lw = None
        self.rd = []
        self.excl = False
        self.swt = 0.0
        self.srt = 0.0


DMA_K = {"sp": 8, "pool": 4, "act": 4}
COMPUTE = ("pe", "act", "dve", "pool")


class Sched:
    def __init__(self, nc, ctx):
        self.nc = nc
        self.ops = []
        self.cur = None
        self.eng_t = {}
        self.csem = {e: ctx.enter_context(nc.semaphore("c_" + e)) for e in COMPUTE}
        self.ccnt = {e: 0 for e in COMPUTE}
        self.dsem = {q: [ctx.enter_context(nc.semaphore(f"d_{q}{i}")) for i in range(k)]
                     for q, k in DMA_K.items()}
        self.dcnt = {q: 0 for q in DMA_K}
        self.bufs = []

    def buf(self):
        b = Buf()
        self.bufs.append(b)
        return b

    COST = {"pe": 230.0, "act": 450.0, "dve": 350.0, "pool": 2500.0, "sp": 100.0}

    def op(self, eng, fn, r=(), w=(), dma=False, c=None):
        o = (eng, fn, tuple(r), tuple(w), dma, c)
        if self.cur is None:
            self._place(o)
        else:
            self.cur.append(o)

    def _est(self, o):
        eng, fn, r, w, dma, c = o
        t = self.eng_t.get(eng, 0.0)
        for b in r:
            if b.swt + 200.0 > t:
                t = b.swt + 200.0
        for b in w:
            m = max(b.swt, b.srt) + 200.0
            if m > t:
                t = m
        return t

    def _place(self, o):
        eng, fn, r, w, dma, c = o
        t = self._est(o)
        if dma:
            self.eng_t[eng] = t + 100.0
            fin = t + (c if c is not None else 4000.0)
        else:
            fin = t + (c if c is not None else self.COST[eng])
            self.eng_t[eng] = fin
        for b in r:
            if fin > b.srt:
                b.srt = fin
        for b in w:
            b.swt = fin
            b.srt = 0.0
        self.ops.append((eng, fn, r, w, dma))

    def begin(self):
        self.cur = []

    def end(self):
        c = self.cur
        self.cur = None
        return c

    def merge(self, lists):
        lists = [l for l in lists if l]
        ptr = [0] * len(lists)
        while True:
            best = None
            for k, l in enumerate(lists):
                if ptr[k] < len(l):
                    t = self._est(l[ptr[k]])
                    key = (t, ptr[k] / len(l))
                    if best is None or key < best[0]:
                        best = (key, k)
            if best is None:
                break
            k = best[1]
            self._place(lists[k][ptr[k]])
            ptr[k] += 1

    def flush(self):
        nc = self.nc
        ops = self.ops
        n = len(ops)
        deps = [None] * n
        for i, (eng, fn, r, w, dma) in enumerate(ops):
            d = set()
            for b in r:
                if b.lw is not None:
                    d.add(b.lw)
                if b.excl:
                    for q in b.rd:
                        if ops[q][0] != eng:
                            d.add(q)
            for b in w:
                if b.lw is not None:
                    d.add(b.lw)
                d.update(b.rd)
            d.discard(i)
            for b in r:
                b.rd.append(i)
            for b in w:
                b.lw = i
                b.rd = []
            deps[i] = d
        need_inc = [False] * n
        for i in range(n):
            eng, _, _, _, dma = ops[i]
            keep = []
            for p in deps[i]:
                pe, _, _, _, pdma = ops[p]
                if (not dma) and (not pdma) and pe == eng == "pe":
                    continue
                keep.append(p)
                if not pdma:
                    need_inc[p] = True
            deps[i] = keep
        target = [None] * n
        dma_prev = [None] * n
        per_eng = {e: [] for e in ("pe", "act", "dve", "pool", "sp")}
        for i in range(n):
            eng, _, _, _, dma = ops[i]
            per_eng[eng].append(i)
            if dma:
                j = self.dcnt[eng]
                k = DMA_K[eng]
                self.dcnt[eng] = j + 1
                target[i] = (self.dsem[eng][j % k], 16 * (j // k + 1))
                if j >= k:
                    dma_prev[i] = (self.dsem[eng][j % k], 16 * (j // k))
            elif need_inc[i]:
                self.ccnt[eng] += 1
                target[i] = (self.csem[eng], self.ccnt[eng])
        final = {}
        for i in range(n):
            if target[i] is not None:
                s, v = target[i]
                final[id(s)] = (s, max(v, final.get(id(s), (s, 0))[1]))

        def emit(ename, e):
            waited = {}

            def wait(s, v):
                if waited.get(id(s), 0) >= v:
                    return
                waited[id(s)] = v
                e.wait_ge(s, v)

            for i in per_eng[ename]:
                eng, fn, _, _, dma = ops[i]
                if dma_prev[i] is not None:
                    wait(*dma_prev[i])
                for p in deps[i]:
                    wait(*target[p])
                ins = fn(e)
                if target[i] is not None:
                    s, v = target[i]
                    ins.then_inc(s, 16 if dma else 1)
            for s, v in final.values():
                wait(s, v)

        with nc.Block() as block:
            @block.sync
            def _(e):
                emit("sp", e)

            @block.tensor
            def _(e):
                emit("pe", e)

            @block.scalar
            def _(e):
                emit("act", e)

            @block.vector
            def _(e):
                emit("dve", e)

            @block.gpsimd
            def _(e):
                emit("pool", e)
        self.ops = []
        self.eng_t = {}
        for b in self.bufs:
            b.lw = None
            b.rd = []
            b.swt = 0.0
            b.srt = 0.0


class Pool:
    uid = 0

    def __init__(self, S, ctx, name, shape, dtype, n=1, psum=False):
        nc = S.nc
        self.t = []
        for i in range(n):
            alloc = nc.psum_tensor if psum else nc.sbuf_tensor
            Pool.uid += 1
            h = ctx.enter_context(alloc(f"{name}_{i}_{Pool.uid}", list(shape), dtype))
            bb = S.buf()
            bb.excl = psum
            self.t.append((h, bb))
        self.i = 0
        self.S = S

    def get(self):
        r = self.t[self.i % len(self.t)]
        self.i += 1
        return r


class SubPool:
    def __init__(self, items):
        self.t = list(items)
        self.i = 0

    def get(self):
        r = self.t[self.i % len(self.t)]
        self.i += 1
        return r


def _consts():
    c = {}
    c["ident"] = np.eye(128, dtype=np.float32)
    c["ones"] = np.ones((128, 128), np.float32)
    tk = np.arange(128)[:, None]
    tq = np.arange(128)[None, :]
    ab = np.zeros((128, 3, 2, 4, 128), np.float32)
    for kb in range(3):
        dist = np.abs(tq - tk - (kb - 1) * 128)
        for kvh in range(2):
            for g in range(4):
                h = kvh * 4 + g
                slope = 2.0 ** (-8.0 * (h + 1) / 8.0)
                ab[:, kb, kvh, g, :] = np.where(dist <= 128, -slope * dist, NEG)
    c["abias"] = ab.reshape(128, 3 * 2 * 512)
    a = np.arange(128)
    same = (a[:, None] // 64) == (a[None, :] // 64)
    dm = np.zeros((128, 2, 5, 128), np.float32)
    for d in range(2):
        if d == 0:
            le = a[:, None] <= a[None, :]
            lt = a[:, None] < a[None, :]
        else:
            le = a[:, None] >= a[None, :]
            lt = a[:, None] > a[None, :]
        dm[:, d, 0, :] = (same & le)
        dm[:, d, 1, :] = same
        dm[:, d, 2, :] = np.where(same & le, 0.0, NEG)
        dm[:, d, 3, :] = (same & lt)
        dm[:, d, 4, :] = np.eye(128)
    c["dmask"] = dm.reshape(128, 2 * 5 * 128)
    return c


import os
STOP = int(os.environ.get("KSTOP", "99"))
KSUB = int(os.environ.get("KSUB", "0"))
KDBG = int(os.environ.get("KDBG", "0"))


class _Stop(Exception):
    pass


def build_nc(seqs):
    nc = bass.Bass("TRN2", target_bir_lowering=False)
    try:
        _build(nc, seqs)
    except _Stop:
        pass
    return nc


def _build(nc, seqs):
    ctx = ExitStack()
    with ctx:
        S = Sched(nc, ctx)

        def chk(n):
            if KSUB == n:
                S.flush()
                raise _Stop()

        def dram(name, shape, dt, kind):
            return nc.dram_tensor(name, list(shape), dt, kind=kind)

        xin = {nm: dram("x_" + nm, [L, D], F32, "ExternalInput") for nm, L, rot in seqs}
        yout = {nm: dram("y_" + nm, [(L // 8) if rot else L, D], F32, "ExternalOutput") for nm, L, rot in seqs}
        w_flag = dram("wflag", [1, 8], F32, "ExternalInput")
        w_f1i = dram("ffn1_w_in", [D, 2 * DFF], F32, "ExternalInput")
        w_f1o = dram("ffn1_w_out", [DFF, D], F32, "ExternalInput")
        w_in = dram("w_in", [D, PROJ], F32, "ExternalInput")
        w_cv = dram("conv_w", [5, 1536], F32, "ExternalInput")
        w_sink = dram("attn_sink", [1, 8], F32, "ExternalInput")
        w_alog = dram("dn_a_log", [1, 8], F32, "ExternalInput")
        w_dtb = dram("dn_dt_bias", [1, 8], F32, "ExternalInput")
        w_ng = dram("dn_norm_gain", [1, 128], F32, "ExternalInput")
        w_out = dram("w_out", [D, D], F32, "ExternalInput")
        w_f2i = dram("ffn2_w_in", [D, 2 * DFF], F32, "ExternalInput")
        w_f2o = dram("ffn2_w_out", [DFF, D], F32, "ExternalInput")
        w_lng = dram("ln_gain", [1, 3 * D], F32, "ExternalInput")
        w_lnb = dram("ln_bias", [1, 3 * D], F32, "ExternalInput")
        c_ident = dram("ident", [128, 128], F32, "ExternalInput")
        c_ones = dram("ones", [128, 128], F32, "ExternalInput")
        c_abias = dram("abias", [128, 3072], F32, "ExternalInput")
        c_dmask = dram("dmask", [128, 1280], F32, "ExternalInput")

        LM = max(L for _, L, _r in seqs)
        s_f1i = dram("s_f1i", [11, 128, 4096], BF16, "Internal")
        s_f1o = dram("s_f1o", [DFF, D], BF16, "Internal")
        s_win = dram("s_win", [D, PROJ], BF16, "Internal")
        s_wout = dram("s_wout", [D, D], BF16, "Internal")
        s_f2i = dram("s_f2i", [11, 128, 4096], BF16, "Internal")
        s_f2o = dram("s_f2o", [DFF, D], BF16, "Internal")
        DK = "ExternalOutput" if KDBG else "Internal"
        X1 = dram("X1", [LM, D], F32, DK)
        QT = dram("QT", [128, 4, LM], BF16, "Internal")
        KT = dram("KT", [128, LM], BF16, "Internal")
        VX = dram("VX", [LM, 130], BF16, "Internal")
        DT = dram("DT", [128, 12, LM], F32, "Internal")
        ZZ = dram("ZZ", [LM, 512], F32, "Internal")
        BA = dram("BA", [LM, 16], F32, "Internal")
        OA = dram("OA", [LM, 512], F32, DK)
        ODN = [dram(f"ODN{d}", [LM, 512], F32, DK) for d in range(2)]
        FEK = dram("FEK", [LM // 128, 128, 512], F32, "Internal")
        FEV = dram("FEV", [LM // 128, 128, 512], BF16, "Internal")
        FEQ = dram("FEQ", [LM // 128, 128, 1024], BF16, "Internal")

        def bc(t, off, n):
            return bass.AP(t, off, [[0, 128], [1, n]])

        PS = Pool(S, ctx, "ps", [128, 512], F32, n=8, psum=True)
        cps = [PS]
        ident, identb = Pool(S, ctx, "ident", [128, 128], F32).get()
        ones, onesb = Pool(S, ctx, "ones", [128, 128], F32).get()
        wf, wfb = Pool(S, ctx, "wf", [128, 8], F32).get()
        S.op("sp", lambda e: e.dma_start(out=wf[:], in_=bc(w_flag, 0, 8)), w=[wfb], dma=True)
        lnc = {}

        def load_ln(cx, lis):
            for li in lis:
                g, gb = Pool(S, cx, "lng", [128, D], F32).get()
                b, bb = Pool(S, cx, "lnb", [128, D], F32).get()
                S.op("sp", lambda e, g=g, li=li: e.dma_start(out=g[:], in_=bc(w_lng, li * D, D)), w=[gb], dma=True)
                S.op("sp", lambda e, b=b, li=li: e.dma_start(out=b[:], in_=bc(w_lnb, li * D, D)), w=[bb], dma=True)
                lnc[li] = (g, gb, b, bb)
        S.op("sp", lambda e: e.dma_start(out=ident[:], in_=c_ident.ap()), w=[identb], dma=True)
        S.op("sp", lambda e: e.dma_start(out=ones[:], in_=c_ones.ap()), w=[onesb], dma=True)

        with ExitStack() as c0:
            STG = Pool(S, c0, "stg", [128, 2048], F32, n=3)
            STB = Pool(S, c0, "stb", [128, 2048], BF16, n=3)
            rr = [0]
            def cast_op(a, ab_, b, bb_, cw):
                k = rr[0] % 2
                rr[0] += 1
                if k == 0:
                    S.op("dve", lambda e: e.tensor_copy(out=b[:, 0:cw], in_=a[:, 0:cw]), r=[ab_], w=[bb_])
                elif k == 1:
                    S.op("act", lambda e: e.copy(out=b[:, 0:cw], in_=a[:, 0:cw]), r=[ab_], w=[bb_])
                else:
                    S.op("pool", lambda e: e.tensor_copy(out=b[:, 0:cw], in_=a[:, 0:cw]), r=[ab_], w=[bb_])

            def conv_ffn_in(src, dst):
                d5 = dst.ap().rearrange("j p (k u c) -> j p k u c", k=8, u=2, c=256)
                for k in range(8):
                    for u in range(2):
                        for jj in range(0, 11, 4):
                            ng = min(4, 11 - jj)
                            a, ab_ = STG.get()
                            b, bb_ = STB.get()
                            S.op("sp", lambda e, a=a, k=k, u=u, jj=jj, ng=ng: e.dma_start(
                                out=a[:, 0:ng * 256], in_=src.ap()[k * 128:(k + 1) * 128, u * DFF + jj * 256:u * DFF + (jj + ng) * 256]),
                                w=[ab_], dma=True)
                            cast_op(a, ab_, b, bb_, ng * 256)
                            S.op("pool", lambda e, b=b, k=k, u=u, jj=jj, ng=ng: e.dma_start(
                                out=d5[jj:jj + ng, :, k, u, :].rearrange("j p c -> p j c"),
                                in_=b[:, 0:ng * 256].rearrange("p (j c) -> p j c", c=256)), r=[bb_], dma=True)

            conv_ffn_in(w_f1i, s_f1i)
            conv_ffn_in(w_f2i, s_f2i)
            for src, dst, R, C in ((w_f1o, s_f1o, DFF, D),
                                   (w_in, s_win, D, PROJ), (w_out, s_wout, D, D),
                                   (w_f2o, s_f2o, DFF, D)):
                for r0 in range(0, R, 128):
                    for c0_ in range(0, C, 2048):
                        cw = min(2048, C - c0_)
                        a, ab_ = STG.get()
                        b, bb_ = STB.get()
                        S.op("sp", lambda e, a=a, r0=r0, c0_=c0_, cw=cw, src=src:
                             e.dma_start(out=a[:, 0:cw], in_=src.ap()[r0:r0 + 128, c0_:c0_ + cw]),
                             w=[ab_], dma=True)
                        k = rr[0] % 2
                        rr[0] += 1
                        if k == 0:
                            S.op("dve", lambda e, a=a, b=b, cw=cw: e.tensor_copy(out=b[:, 0:cw], in_=a[:, 0:cw]),
                                 r=[ab_], w=[bb_])
                        elif k == 1:
                            S.op("act", lambda e, a=a, b=b, cw=cw: e.copy(out=b[:, 0:cw], in_=a[:, 0:cw]),
                                 r=[ab_], w=[bb_])
                        else:
                            S.op("pool", lambda e, a=a, b=b, cw=cw: e.tensor_copy(out=b[:, 0:cw], in_=a[:, 0:cw]),
                                 r=[ab_], w=[bb_])
                        S.op("pool", lambda e, b=b, r0=r0, c0_=c0_, cw=cw, dst=dst:
                             e.dma_start(out=dst.ap()[r0:r0 + 128, c0_:c0_ + cw], in_=b[:, 0:cw]),
                             r=[bb_], dma=True)
            S.flush()
        if STOP == 0:
            return nc

        def transpose_tok(src, srcb, dst, dstb, nsub, rot=[0]):
            for k in range(8):
                p, pb = cps[0].get()
                for s in range(nsub):
                    S.op("pe", lambda e, p=p, s=s, k=k: e.transpose(
                        out=p[:, s * 128:(s + 1) * 128], in_=src[:, s, k * 128:(k + 1) * 128], identity=ident[:]),
                        r=[srcb, identb], w=[pb])
                rot[0] += 1
                if rot[0] % 2:
                    S.op("dve", lambda e, p=p, k=k: e.tensor_copy(out=dst[:, k, 0:nsub * 128], in_=p[:, 0:nsub * 128]),
                         r=[pb], w=[dstb])
                else:
                    S.op("act", lambda e, p=p, k=k: e.copy(out=dst[:, k, 0:nsub * 128], in_=p[:, 0:nsub * 128]),
                         r=[pb], w=[dstb])

        def layer_norm(y, yb, li, pools, nsub=4):
            ST, MV = pools
            for s in range(nsub):
                st, stb = ST.get()
                mv, mvb = MV.get()
                S.op("dve", lambda e, st=st, s=s: e.bn_stats(out=st[:, 0:6], in_=y[:, s, 0:512]), r=[yb], w=[stb])
                S.op("dve", lambda e, st=st, s=s: e.bn_stats(out=st[:, 6:12], in_=y[:, s, 512:1024]), r=[yb], w=[stb])
                S.op("dve", lambda e, st=st, mv=mv: e.bn_aggr(out=mv[:, 0:2], in_=st[:, 0:12]), r=[stb], w=[mvb])
                S.op("act", lambda e, mv=mv: e.activation(out=mv[:, 2:3], in_=mv[:, 1:2], func=AF.Ln, bias=LN_EPS, scale=1.0),
                     r=[mvb], w=[mvb])
                S.op("act", lambda e, mv=mv: e.activation(out=mv[:, 3:4], in_=mv[:, 2:3], func=AF.Exp, scale=-0.5),
                     r=[mvb], w=[mvb])
                S.op("dve", lambda e, mv=mv: e.scalar_tensor_tensor(out=mv[:, 4:5], in0=mv[:, 0:1], scalar=-1.0, in1=mv[:, 3:4],
                                                                    op0=ALU.mult, op1=ALU.mult), r=[mvb], w=[mvb])
                S.op("act", lambda e, mv=mv, s=s: e.activation(out=y[:, s, :], in_=y[:, s, :], func=AF.Identity,
                                                              bias=mv[:, 4:5], scale=mv[:, 3:4]), r=[mvb, yb], w=[yb], c=1500.0)
                lg, lgb, lb_, lbb = lnc[li]
                S.op("dve", lambda e, s=s, lg=lg: e.tensor_tensor(out=y[:, s, :], in0=y[:, s, :], in1=lg[:], op=ALU.mult), r=[yb, lgb], w=[yb], c=1200.0)
                S.op("dve", lambda e, s=s, lb_=lb_: e.tensor_tensor(out=y[:, s, :], in0=y[:, s, :], in1=lb_[:], op=ALU.add), r=[yb, lbb], w=[yb], c=1200.0)

        def ffn(xT, xTb, xa, xab, wsc, wo, wob, GT, WG, SG):
            gT, gTb = GT.get()
            for j in range(11):
                wg, wgb = WG.get()
                S.op("sp" if j % 2 == 0 else "act", lambda e, wg=wg, j=j: e.dma_start(
                    out=wg[:].rearrange("p k u c -> p (k u c)"), in_=wsc.ap()[j]), w=[wgb], dma=True)
                for hf in range(2):
                    c = 2 * j + hf
                    pg, pgb = cps[0].get()
                    pu, pub = cps[0].get()
                    for k in range(8):
                        S.op("pe", lambda e, pg=pg, wg=wg, k=k, hf=hf: e.matmul(
                            pg[:, :], lhsT=wg[:, k, 0, hf * 128:(hf + 1) * 128], rhs=xT[:, k, :], start=(k == 0), stop=(k == 7)),
                            r=[wgb, xTb], w=[pgb])
                    for k in range(8):
                        S.op("pe", lambda e, pu=pu, wg=wg, k=k, hf=hf: e.matmul(
                            pu[:, :], lhsT=wg[:, k, 1, hf * 128:(hf + 1) * 128], rhs=xT[:, k, :], start=(k == 0), stop=(k == 7)),
                            r=[wgb, xTb], w=[pub])
                    sg, sgb = SG.get()
                    S.op("act", lambda e, sg=sg, pg=pg: e.activation(out=sg[:, :], in_=pg[:, :], func=AF.Silu), r=[pgb], w=[sgb])
                    S.op("dve", lambda e, sg=sg, pu=pu, c=c: e.tensor_tensor(out=gT[:, c, :], in0=sg[:, :], in1=pu[:, :], op=ALU.mult),
                         r=[sgb, pub], w=[gTb])
            for s in range(4):
                for nh in range(2):
                    po, pob = cps[0].get()
                    for c in range(NCH):
                        S.op("pe", lambda e, po=po, c=c, s=s, nh=nh: e.matmul(
                            po[:, :], lhsT=gT[:, c, s * 128:(s + 1) * 128], rhs=wo[:, c, nh * 512:(nh + 1) * 512],
                            start=(c == 0), stop=(c == NCH - 1)), r=[gTb, wob], w=[pob])
                    S.op("dve", lambda e, po=po, s=s, nh=nh: e.scalar_tensor_tensor(
                        out=xa[:, s, nh * 512:(nh + 1) * 512], in0=po[:, :], scalar=0.5, in1=xa[:, s, nh * 512:(nh + 1) * 512],
                        op0=ALU.mult, op1=ALU.add), r=[pob, xab], w=[xab])

        def v4(p):
            return p[:, :].rearrange("p (h d) -> p h d", d=128)

        evr = [0]

        def evac(dst_fn, p, pb, wbufs, rbufs=()):
            evr[0] += 1
            if evr[0] % 2:
                S.op("dve", lambda e: e.tensor_copy(out=dst_fn(), in_=p()), r=[pb, *rbufs], w=wbufs)
            else:
                S.op("act", lambda e: e.copy(out=dst_fn(), in_=p()), r=[pb, *rbufs], w=wbufs)

        for nm, L, rot in seqs:
            OWN = (L // 8) if rot else L
            OWNB = OWN // 128
            x_d, y_d = xin[nm], yout[nm]
            NT = L // 512
            NB = L // 128
            with ExitStack() as c1:
                wo, wob = Pool(S, c1, "wo1", [128, NCH, D], BF16).get()
                wi, wib = Pool(S, c1, "wi", [128, 8, PROJ], BF16).get()
                S.op("sp", lambda e: e.dma_start(out=wo[:], in_=s_f1o.ap().rearrange("(c p) d -> p c d", p=128)), w=[wob], dma=True)
                S.op("sp", lambda e: e.dma_start(out=wi[:], in_=s_win.ap().rearrange("(k p) c -> p k c", p=128)), w=[wib], dma=True)
                load_ln(c1, [0])
                XT = Pool(S, c1, "xt", [128, 4, D], F32, n=1)
                XTT = Pool(S, c1, "xtt", [128, 8, 512], BF16, n=1)
                X1TT = Pool(S, c1, "x1tt", [128, 8, 512], BF16, n=2)
                PSA1 = SubPool(PS.t[0:4])
                PSB1 = SubPool(PS.t[4:8])
                pend1 = [None]
                GT = Pool(S, c1, "gt", [128, NCH, 512], BF16, n=1)
                WG = Pool(S, c1, "wg", [128, 8, 2, 256], BF16, n=3)
                SG = Pool(S, c1, "sg", [128, 512], BF16, n=2)
                ST = Pool(S, c1, "st", [128, 12], F32, n=2)
                MV = Pool(S, c1, "mv", [128, 8], F32, n=2)
                QTT = Pool(S, c1, "qtt", [128, 4, 512], BF16, n=1)
                KTT = Pool(S, c1, "ktt", [128, 512], BF16, n=1)
                DTT = Pool(S, c1, "dtt", [128, 512], F32, n=3)
                VXT = Pool(S, c1, "vxt", [128, 4, 130], BF16, n=1)
                ZT = Pool(S, c1, "zt", [128, 4, 512], F32, n=1)
                BAT = Pool(S, c1, "bat", [128, 4, 16], F32, n=1)
                for ti in range(NT):
                    t0 = ti * 512
                    S.begin()
                    cps[0] = PSA1
                    xt, xtb = XT.get()
                    S.op("sp", lambda e, xt=xt, t0=t0: e.dma_start(
                        out=xt[:], in_=x_d.ap()[t0:t0 + 512, :].rearrange("(s p) d -> p s d", p=128)), w=[xtb], dma=True)
                    xT, xTb = XTT.get()
                    transpose_tok(xt, xtb, xT, xTb, 4)
                    S.op("act", lambda e, xt=xt: e.mul(out=xt[:], in_=xt[:], mul=ALPHA), r=[xtb], w=[xtb])
                    ffn(xT, xTb, xt, xtb, s_f1i, wo, wob, GT, WG, SG)
                    layer_norm(xt, xtb, 0, (ST, MV))
                    S.op("pool", lambda e, xt=xt, t0=t0: e.dma_start(
                        out=X1.ap()[t0:t0 + 512, :].rearrange("(s p) d -> p s d", p=128), in_=xt[:]), r=[xtb], dma=True)
                    x1T, x1Tb = X1TT.get()
                    transpose_tok(xt, xtb, x1T, x1Tb, 4)
                    la = S.end()
                    S.begin()
                    cps[0] = PSB1
                    qtt, qttb = QTT.get()
                    for c in range(4):
                        p, pb = cps[0].get()
                        for k in range(8):
                            S.op("pe", lambda e, p=p, k=k, c=c, x1T=x1T: e.matmul(
                                p[:, :], lhsT=wi[:, k, c * 128:(c + 1) * 128],
                                rhs=x1T[:, k, :], start=(k == 0), stop=(k == 7)), r=[wib, x1Tb], w=[pb])
                        evac(lambda qtt=qtt, c=c: qtt[:, c, :], lambda p=p: p[:, :], pb, [qttb])
                    S.op("pool", lambda e, qtt=qtt, t0=t0: e.dma_start(out=QT.ap()[:, :, t0:t0 + 512], in_=qtt[:]), r=[qttb], dma=True)
                    ktt, kttb = KTT.get()
                    p, pb = cps[0].get()
                    for k in range(8):
                        S.op("pe", lambda e, p=p, k=k, x1T=x1T: e.matmul(
                            p[:, :], lhsT=wi[:, k, 512:640], rhs=x1T[:, k, :], start=(k == 0), stop=(k == 7)), r=[wib, x1Tb], w=[pb])
                    evac(lambda ktt=ktt: ktt[:, :], lambda p=p: p[:, :], pb, [kttb])
                    S.op("pool", lambda e, ktt=ktt, t0=t0: e.dma_start(out=KT.ap()[:, t0:t0 + 512], in_=ktt[:]), r=[kttb], dma=True)
                    for c in range(12):
                        p, pb = cps[0].get()
                        for k in range(8):
                            S.op("pe", lambda e, p=p, k=k, c=c, x1T=x1T: e.matmul(
                                p[:, :], lhsT=wi[:, k, 768 + c * 128:768 + (c + 1) * 128], rhs=x1T[:, k, :],
                                start=(k == 0), stop=(k == 7)), r=[wib, x1Tb], w=[pb])
                        dtt, dttb = DTT.get()
                        evac(lambda dtt=dtt: dtt[:, :], lambda p=p: p[:, :], pb, [dttb])
                        S.op("sp", lambda e, dtt=dtt, c=c, t0=t0: e.dma_start(out=DT.ap()[:, c, t0:t0 + 512], in_=dtt[:]),
                             r=[dttb], dma=True)
                    vxt, vxtb = VXT.get()
                    zt, ztb = ZT.get()
                    bat, batb = BAT.get()
                    S.op("pool", lambda e, vxt=vxt: e.memset(vxt[:], 1.0), w=[vxtb])
                    for s in range(4):
                        p, pb = cps[0].get()
                        for k in range(8):
                            S.op("pe", lambda e, p=p, k=k, s=s, x1T=x1T: e.matmul(
                                p[:, 0:128], lhsT=x1T[:, k, s * 128:(s + 1) * 128], rhs=wi[:, k, 640:768],
                                start=(k == 0), stop=(k == 7)), r=[wib, x1Tb], w=[pb])
                        evac(lambda vxt=vxt, s=s: vxt[:, s, :].rearrange("p (h c) -> p h c", h=2)[:, :, 0:64],
                             lambda p=p: p[:, 0:128].rearrange("p (h c) -> p h c", h=2), pb, [vxtb])
                        p, pb = cps[0].get()
                        for k in range(8):
                            S.op("pe", lambda e, p=p, k=k, s=s, x1T=x1T: e.matmul(
                                p[:, :], lhsT=x1T[:, k, s * 128:(s + 1) * 128], rhs=wi[:, k, 2304:2816],
                                start=(k == 0), stop=(k == 7)), r=[wib, x1Tb], w=[pb])
                        evac(lambda zt=zt, s=s: zt[:, s, :], lambda p=p: p[:, :], pb, [ztb])
                        p, pb = cps[0].get()
                        for k in range(8):
                            S.op("pe", lambda e, p=p, k=k, s=s, x1T=x1T: e.matmul(
                                p[:, 0:16], lhsT=x1T[:, k, s * 128:(s + 1) * 128], rhs=wi[:, k, 2816:2832],
                                start=(k == 0), stop=(k == 7)), r=[wib, x1Tb], w=[pb])
                        evac(lambda bat=bat, s=s: bat[:, s, :], lambda p=p: p[:, 0:16], pb, [batb])
                    S.op("pool", lambda e, vxt=vxt, t0=t0: e.dma_start(
                        out=VX.ap()[t0:t0 + 512, :].rearrange("(s p) c -> p s c", p=128), in_=vxt[:]), r=[vxtb], dma=True)
                    S.op("pool", lambda e, zt=zt, t0=t0: e.dma_start(
                        out=ZZ.ap()[t0:t0 + 512, :].rearrange("(s p) c -> p s c", p=128), in_=zt[:]), r=[ztb], dma=True)
                    S.op("pool", lambda e, bat=bat, t0=t0: e.dma_start(
                        out=BA.ap()[t0:t0 + 512, :].rearrange("(s p) c -> p s c", p=128), in_=bat[:]), r=[batb], dma=True)
                    lb = S.end()
                    S.merge([la] + ([pend1[0]] if pend1[0] else []))
                    pend1[0] = lb
                S.merge([pend1[0]])
                cps[0] = PS
                S.flush()
            if STOP == 1:
                return nc

            with ExitStack() as c2:
                abias, abiasb = Pool(S, c2, "abias", [128, 3072], F32).get()
                esink, esinkb = Pool(S, c2, "esink", [128, 8], F32).get()
                S.op("sp", lambda e: e.dma_start(out=abias[:], in_=c_abias.ap()), w=[abiasb], dma=True)
                S.op("sp", lambda e: e.dma_start(out=esink[:], in_=bc(w_sink, 0, 8)), w=[esinkb], dma=True)
                S.op("act", lambda e: e.activation(out=esink[:], in_=esink[:], func=AF.Exp), r=[esinkb], w=[esinkb])
                QB = Pool(S, c2, "qb", [128, 4, 128], BF16, n=2)
                KB = Pool(S, c2, "kb", [128, 3, 128], BF16, n=2)
                VB = Pool(S, c2, "vb", [128, 3, 130], BF16, n=2)
                TB = Pool(S, c2, "tb", [128, 512], F32, n=2)
                PT = Pool(S, c2, "pt", [128, 512], BF16, n=6)
                DEN = Pool(S, c2, "den", [128, 16], F32, n=2)
                OB = Pool(S, c2, "ob", [128, 512], F32, n=2)
                for i in range(OWNB):
                    kbs = [kb for kb in range(3) if (rot or 0 <= i + kb - 1 < NB)]
                    lo, hi = kbs[0], kbs[-1] + 1
                    qb, qbb = QB.get()
                    kbt, kbb = KB.get()
                    vb, vbb = VB.get()
                    S.op("sp", lambda e, qb=qb, i=i: e.dma_start(out=qb[:], in_=QT.ap()[:, :, i * 128:(i + 1) * 128]), w=[qbb], dma=True)
                    if rot and (i == 0 or i == OWNB - 1):
                        for kb in range(3):
                            bi = (i + kb - 1) % NB
                            S.op("sp", lambda e, kbt=kbt, kb=kb, bi=bi: e.dma_start(
                                out=kbt[:, kb, :], in_=KT.ap()[:, bi * 128:(bi + 1) * 128]), w=[kbb], dma=True)
                            S.op("sp", lambda e, vb=vb, kb=kb, bi=bi: e.dma_start(
                                out=vb[:, kb, :], in_=VX.ap()[bi * 128:(bi + 1) * 128, :]), w=[vbb], dma=True)
                        hk, fi = (0, 7) if i == 0 else (2, 0)
                        S.op("act", lambda e, vb=vb, hk=hk, fi=fi: e.activation(
                            out=vb[:, hk, :], in_=vb[:, hk, :], func=AF.Copy, scale=wf[:, fi:fi + 1]), r=[vbb, wfb], w=[vbb])
                        lo, hi = 0, 0
                    if hi > lo:
                        S.op("sp", lambda e, kbt=kbt, i=i, lo=lo, hi=hi: e.dma_start(
                            out=kbt[:, lo:hi, :],
                            in_=KT.ap()[:, (i + lo - 1) * 128:(i + hi - 1) * 128].rearrange("p (b t) -> p b t", t=128)), w=[kbb], dma=True)
                        S.op("sp", lambda e, vb=vb, i=i, lo=lo, hi=hi: e.dma_start(
                            out=vb[:, lo:hi, :],
                            in_=VX.ap()[(i + lo - 1) * 128:(i + hi - 1) * 128, :].rearrange("(b p) c -> p b c", p=128)), w=[vbb], dma=True)
                    pos = []
                    for kvh in range(2):
                        po, pob = PS.get()
                        pos.append((po, pob))
                        b0 = kvh * 64
                        pts = []
                        for kb in kbs:
                            ps_, psb = PS.get()
                            S.op("pe", lambda e, ps_=ps_, kbt=kbt, qb=qb, kb=kb, b0=b0: e.matmul(
                                ps_[:, :], lhsT=kbt[b0:b0 + 64, kb, :], rhs=qb[b0:b0 + 64, :, :].rearrange("p c t -> p (c t)"), start=True, stop=True),
                                r=[kbb, qbb], w=[psb])
                            tb, tbb = TB.get()
                            off = (kb * 2 + kvh) * 512
                            S.op("dve", lambda e, tb=tb, ps_=ps_, off=off: e.scalar_tensor_tensor(
                                out=tb[:, :], in0=ps_[:, :], scalar=0.125, in1=abias[:, off:off + 512], op0=ALU.mult, op1=ALU.add),
                                r=[psb, abiasb], w=[tbb])
                            pt, ptb = PT.get()
                            S.op("act", lambda e, tb=tb, pt=pt: e.activation(out=pt[:, :], in_=tb[:, :], func=AF.Exp), r=[tbb], w=[ptb])
                            pts.append((kb, pt, ptb))
                        for g in range(4):
                            for kb, pt, ptb in pts:
                                S.op("pe", lambda e, po=po, pt=pt, vb=vb, g=g, kb=kb, kvh=kvh, st_=(kb == kbs[0]), sp_=(kb == kbs[-1]): e.matmul(
                                    po[:, g * 65:(g + 1) * 65], lhsT=pt[:, g * 128:(g + 1) * 128], rhs=vb[:, kb, kvh * 65:(kvh + 1) * 65],
                                    start=st_, stop=sp_), r=[ptb, vbb], w=[pob])
                    den, denb = DEN.get()
                    ob, obb = OB.get()
                    for kvh in range(2):
                        po, pob = pos[kvh]
                        S.op("dve", lambda e, den=den, po=po, kvh=kvh: e.tensor_tensor(
                            out=den[:, kvh * 4:(kvh + 1) * 4], in0=po[:, 0:260].rearrange("p (g c) -> p g c", c=65)[:, :, 64],
                            in1=esink[:, kvh * 4:(kvh + 1) * 4], op=ALU.add), r=[pob, esinkb], w=[denb])
                    S.op("dve", lambda e, den=den: e.reciprocal(out=den[:, 8:16], in_=den[:, 0:8]), r=[denb], w=[denb])
                    for kvh in range(2):
                        po, pob = pos[kvh]
                        for g in range(4):
                            h = kvh * 4 + g
                            S.op("dve", lambda e, ob=ob, po=po, den=den, g=g, h=h: e.tensor_scalar(
                                out=ob[:, h * 64:(h + 1) * 64], in0=po[:, g * 65:g * 65 + 64], scalar1=den[:, 8 + h:9 + h], scalar2=None,
                                op0=ALU.mult), r=[pob, denb], w=[obb])
                    S.op("pool", lambda e, ob=ob, i=i: e.dma_start(out=OA.ap()[i * 128:(i + 1) * 128, :], in_=ob[:]), r=[obb], dma=True)
                S.flush()
            if STOP == 2:
                return nc

            with ExitStack() as cf:
                cw, cwb = Pool(S, cf, "cw", [128, 12, 5], F32).get()
                for c in range(12):
                    S.op("sp", lambda e, c=c: e.dma_start(out=cw[:, c, :], in_=w_cv.ap()[:, c * 128:(c + 1) * 128].rearrange("j p -> p j"),
                                                          allow_slow_non_contiguous=True), w=[cwb], dma=True)
                XIN = Pool(S, cf, "xin", [128, 12, 132], F32, n=4)
                CA = Pool(S, cf, "ca", [128, 12, 128], F32, n=4)
                SL = Pool(S, cf, "sl", [128, 12, 128], F32, n=4)
                TM = Pool(S, cf, "tm", [128, 12, 128], F32, n=4)
                SS = Pool(S, cf, "ss", [128, 16], F32, n=4)
                JK = Pool(S, cf, "jk", [128, 8, 128], F32, n=4)
                NRM = Pool(S, cf, "nrm", [128, 8, 128], F32, n=4)
                VBF = Pool(S, cf, "vbf", [128, 4, 128], BF16, n=4)
                QKT = Pool(S, cf, "qkt", [128, 8, 128], BF16, n=4)
                PSFS = [(SubPool(PS.t[0:2]), SubPool(PS.t[4:6])), (SubPool(PS.t[2:4]), SubPool(PS.t[6:8]))]
                def fe_tile(ti, PSF1, PSF2):
                    t0 = ti * 128
                    own = (not rot) or ti < OWNB
                    S.begin()
                    xi, xib = XIN.get()
                    a0 = max(t0 - 2, 0)
                    a1 = min(t0 + 130, L)
                    if (a0 != t0 - 2 or a1 != t0 + 130) and not rot:
                        S.op("pool", lambda e: e.memset(xi[:], 0.0), w=[xib])
                    S.op("sp", lambda e: e.dma_start(out=xi[:, :, a0 - (t0 - 2):a1 - (t0 - 2)], in_=DT.ap()[:, :, a0:a1]), w=[xib], dma=True)
                    if rot:
                        if t0 == 0:
                            S.op("sp", lambda e: e.dma_start(out=xi[:, :, 0:2], in_=DT.ap()[:, :, L - 2:L]), w=[xib], dma=True)
                        if t0 + 128 == L:
                            S.op("sp", lambda e: e.dma_start(out=xi[:, :, 130:132], in_=DT.ap()[:, :, 0:2]), w=[xib], dma=True)
                        if ti % OWNB == 0:
                            fi = (ti // OWNB - 1) % 8
                            S.op("act", lambda e: e.activation(out=xi[:, :, 0:2], in_=xi[:, :, 0:2], func=AF.Copy, scale=wf[:, fi:fi + 1]),
                                 r=[xib, wfb], w=[xib])
                        if ti % OWNB == OWNB - 1:
                            fi2 = ti // OWNB
                            S.op("act", lambda e: e.activation(out=xi[:, :, 130:132], in_=xi[:, :, 130:132], func=AF.Copy, scale=wf[:, fi2:fi2 + 1]),
                                 r=[xib, wfb], w=[xib])
                    ca, cab = CA.get()
                    for j in range(5):
                        for c in range(12):
                            if j == 0:
                                S.op("act", lambda e, c=c: e.activation(
                                    out=ca[:, c, :], in_=xi[:, c, 0:128], func=AF.Copy, scale=cw[:, c, 0:1]),
                                    r=[xib, cwb], w=[cab])
                            else:
                                S.op("dve", lambda e, c=c, j=j: e.scalar_tensor_tensor(
                                    out=ca[:, c, :], in0=xi[:, c, j:j + 128], scalar=cw[:, c, j:j + 1], in1=ca[:, c, :],
                                    op0=ALU.mult, op1=ALU.add), r=[xib, cwb, cab], w=[cab])
                    sl, slb = SL.get()
                    S.op("act", lambda e: e.activation(out=sl[:], in_=ca[:], func=AF.Silu), r=[cab], w=[slb])
                    tm, tmb = TM.get()
                    for q4 in range(3):
                        p, pb = PSF1.get()
                        for h in range(4):
                            S.op("pe", lambda e, p=p, q4=q4, h=h: e.transpose(
                                out=p[:, h * 128:(h + 1) * 128], in_=sl[:, q4 * 4 + h, :], identity=ident[:]), r=[slb, identb], w=[pb])
                        evac(lambda q4=q4: tm[:, q4 * 4:(q4 + 1) * 4, :], lambda p=p: v4(p), pb, [tmb])
                    ss, ssb = SS.get()
                    jk, jkb = JK.get()
                    S.op("act", lambda e: e.activation(out=jk[:], in_=tm[:, 0:8, :], func=AF.Square), r=[tmb], w=[jkb])
                    S.op("dve", lambda e: e.tensor_reduce(out=ss[:, 0:8], in_=jk[:], axis=AX.X, op=ALU.add), r=[jkb], w=[ssb])
                    S.op("act", lambda e: e.activation(out=ss[:, 8:16], in_=ss[:, 0:8], func=AF.Ln, bias=RMS_EPS, scale=1.0), r=[ssb], w=[ssb])
                    S.op("act", lambda e: e.activation(out=ss[:, 8:16], in_=ss[:, 8:16], func=AF.Exp, scale=-0.5), r=[ssb], w=[ssb])
                    S.op("dve", lambda e: e.tensor_scalar(out=ss[:, 8:12], in0=ss[:, 8:12], scalar1=128.0 ** -0.5, scalar2=None, op0=ALU.mult),
                         r=[ssb], w=[ssb])
                    l1 = S.end()
                    S.begin()
                    nrm, nrmb = NRM.get()
                    for idx in range(8):
                        if idx % 2:
                            S.op("dve", lambda e, idx=idx: e.tensor_scalar(
                                out=nrm[:, idx, :], in0=tm[:, idx, :], scalar1=ss[:, 8 + idx:9 + idx], scalar2=None, op0=ALU.mult),
                                r=[tmb, ssb], w=[nrmb])
                        else:
                            S.op("act", lambda e, idx=idx: e.activation(
                                out=nrm[:, idx, :], in_=tm[:, idx, :], func=AF.Copy, scale=ss[:, 8 + idx:9 + idx]),
                                r=[tmb, ssb], w=[nrmb])
                    vbf, vbfb = VBF.get()
                    S.op("act", lambda e: e.copy(out=vbf[:], in_=tm[:, 8:12, :]), r=[tmb], w=[vbfb])
                    qkt, qktb = QKT.get()
                    for q4 in ((0, 1) if own else (1,)):
                        p, pb = PSF2.get()
                        for h in range(4):
                            S.op("pe", lambda e, p=p, q4=q4, h=h: e.transpose(
                                out=p[:, h * 128:(h + 1) * 128], in_=nrm[:, q4 * 4 + h, :], identity=ident[:]), r=[nrmb, identb], w=[pb])
                        evac(lambda q4=q4: qkt[:, q4 * 4:(q4 + 1) * 4, :], lambda p=p: v4(p), pb, [qktb])
                    S.op("pool", lambda e: e.dma_start(out=FEK.ap()[ti], in_=nrm[:, 4:8, :].rearrange("p h d -> p (h d)")), r=[nrmb], dma=True)
                    S.op("pool", lambda e: e.dma_start(out=FEV.ap()[ti], in_=vbf[:].rearrange("p h d -> p (h d)")), r=[vbfb], dma=True)
                    if own:
                        S.op("pool", lambda e: e.dma_start(out=FEQ.ap()[ti], in_=qkt[:].rearrange("p h d -> p (h d)")), r=[qktb], dma=True)
                    else:
                        S.op("pool", lambda e: e.dma_start(out=FEQ.ap()[ti, :, 512:1024], in_=qkt[:, 4:8, :].rearrange("p h d -> p (h d)")),
                             r=[qktb], dma=True)
                    l2 = S.end()
                    return l1, l2

                pendf = []
                for tp in range(0, NB, 2):
                    cur1, cur2 = [], []
                    for q_ in range(2):
                        if tp + q_ < NB:
                            l1, l2 = fe_tile(tp + q_, *PSFS[q_])
                            cur1.append(l1)
                            cur2.append(l2)
                    S.merge(cur1 + pendf)
                    pendf = cur2
                S.merge(pendf)
                S.flush()

            def make_dir(dr, c3, PSA, PSB, PSC):
                dmk, dmkb = Pool(S, c3, "dmk", [128, 5, 128], F32).get()
                strict4, strict4b = Pool(S, c3, "strict4", [128, 4, 128], F32).get()
                eye4, eye4b = Pool(S, c3, "eye4", [128, 4, 128], F32).get()
                gpar, gparb = Pool(S, c3, "gpar", [128, 8], F32).get()
                S.op("sp", lambda e: e.dma_start(out=dmk[:], in_=c_dmask.ap()[:, dr * 640:(dr + 1) * 640].rearrange(
                    "p (m i) -> p m i", i=128)), w=[dmkb], dma=True)
                for h in range(4):
                    S.op("sp", lambda e, h=h: e.dma_start(out=strict4[:, h, :], in_=c_dmask.ap()[:, dr * 640 + 384:dr * 640 + 512]),
                         w=[strict4b], dma=True)
                    S.op("sp", lambda e, h=h: e.dma_start(out=eye4[:, h, :], in_=c_ident.ap()), w=[eye4b], dma=True)
                S.op("sp", lambda e: e.dma_start(out=gpar[:, 0:4], in_=bc(w_alog, dr * 4, 4)), w=[gparb], dma=True)
                S.op("sp", lambda e: e.dma_start(out=gpar[:, 4:8], in_=bc(w_dtb, dr * 4, 4)), w=[gparb], dma=True)
                S.op("act", lambda e: e.activation(out=gpar[:, 0:4], in_=gpar[:, 0:4], func=AF.Exp), r=[gparb], w=[gparb])
                S.op("dve", lambda e: e.tensor_scalar(out=gpar[:, 0:4], in0=gpar[:, 0:4], scalar1=-1.0, scalar2=None, op0=ALU.mult),
                     r=[gparb], w=[gparb])
                U = lambda: dmk[:, 0, :]
                BONES = lambda: dmk[:, 1, :]
                MINC = lambda: dmk[:, 2, :]
                BAI = Pool(S, c3, "bai", [128, 16], F32, n=2)
                NRM = Pool(S, c3, "nrm", [128, 8, 128], F32, n=2)
                QKT = Pool(S, c3, "qkt", [128, 8, 128], BF16, n=2)
                GU = Pool(S, c3, "gu", [128, 4, 128], F32, n=1)
                DTMP = Pool(S, c3, "dtmp", [128, 4, 128], F32, n=1)
                DCY = Pool(S, c3, "dcy", [128, 4, 128], F32, n=1)
                GS = Pool(S, c3, "gs", [128, 32], F32, n=3)
                EROW = Pool(S, c3, "erow", [128, 4, 128], F32, n=3)
                ATT = Pool(S, c3, "att", [128, 4, 128], BF16, n=3)
                KD = Pool(S, c3, "kd", [128, 4, 128], BF16, n=3)
                QD = Pool(S, c3, "qd", [128, 4, 128], BF16, n=3)
                XM = Pool(S, c3, "xm", [128, 4, 128], F32, n=2)
                VBF = Pool(S, c3, "vbf", [128, 4, 128], BF16, n=2)
                KG = Pool(S, c3, "kg", [128, 4, 128], BF16, n=2)
                PK = Pool(S, c3, "pk", [128, 4, 128], BF16, n=2)
                PKT = Pool(S, c3, "pkt", [128, 4, 128], BF16, n=2)
                RF = Pool(S, c3, "rf", [128, 4, 128], F32, n=2)
                RB = Pool(S, c3, "rbb", [128, 4, 128], BF16, n=2)
                UB = Pool(S, c3, "ub", [128, 4, 128], F32, n=2)
                WT = Pool(S, c3, "wt", [128, 4, 128], BF16, n=2)
                VN = Pool(S, c3, "vn", [128, 4, 128], BF16, n=2)
                OT = Pool(S, c3, "ot", [128, 4, 128], F32, n=2)
                SF = Pool(S, c3, "sf", [128, 4, 128], F32, n=2)
                SB_ = Pool(S, c3, "sbs", [128, 4, 128], BF16, n=2)
                st8 = {}
                st8["sf"], st8["sfb"] = SF.get()
                st8["sb"], st8["sbb"] = SB_.get()
                S.op("pool", lambda e: e.memset(st8["sf"][:], 0.0), w=[st8["sfb"]])
                S.op("pool", lambda e: e.memset(st8["sb"][:], 0.0), w=[st8["sbb"]])

                def stage_fe(ti):
                    T = {}
                    t0 = ti * 128
                    own = (not rot) or ti < OWNB
                    bai, baib = BAI.get()
                    S.op("sp", lambda e: e.dma_start(out=bai[:], in_=BA.ap()[t0:t0 + 128, :]), w=[baib], dma=True)
                    nrm, nrmb = NRM.get()
                    vbf, vbfb = VBF.get()
                    qkt, qktb = QKT.get()
                    S.op("sp", lambda e: e.dma_start(out=nrm[:, 4:8, :].rearrange("p h d -> p (h d)"), in_=FEK.ap()[ti]), w=[nrmb], dma=True)
                    S.op("sp", lambda e: e.dma_start(out=vbf[:].rearrange("p h d -> p (h d)"), in_=FEV.ap()[ti]), w=[vbfb], dma=True)
                    if own:
                        S.op("sp", lambda e: e.dma_start(out=qkt[:].rearrange("p h d -> p (h d)"), in_=FEQ.ap()[ti]), w=[qktb], dma=True)
                    else:
                        S.op("sp", lambda e: e.dma_start(out=qkt[:, 4:8, :].rearrange("p h d -> p (h d)"), in_=FEQ.ap()[ti, :, 512:1024]),
                             w=[qktb], dma=True)
                    gs, gsb = GS.get()
                    S.op("act", lambda e: e.activation(out=gs[:, 0:4], in_=bai[:, dr * 4:dr * 4 + 4], func=AF.Sigmoid), r=[baib], w=[gsb])
                    S.op("dve", lambda e: e.tensor_scalar(out=gs[:, 4:8], in0=gs[:, 0:4], scalar1=-1.0, scalar2=None, op0=ALU.mult), r=[gsb], w=[gsb])
                    S.op("dve", lambda e: e.tensor_tensor(out=gs[:, 8:12], in0=bai[:, 8 + dr * 4:12 + dr * 4], in1=gpar[:, 4:8], op=ALU.add),
                         r=[baib, gparb], w=[gsb])
                    S.op("act", lambda e: e.activation(out=gs[:, 8:12], in_=gs[:, 8:12], func=AF.Exp), r=[gsb], w=[gsb])
                    S.op("act", lambda e: e.activation(out=gs[:, 8:12], in_=gs[:, 8:12], func=AF.Ln, bias=1.0, scale=1.0), r=[gsb], w=[gsb])
                    S.op("dve", lambda e: e.tensor_tensor(out=gs[:, 8:12], in0=gs[:, 8:12], in1=gpar[:, 0:4], op=ALU.mult), r=[gsb, gparb], w=[gsb])
                    p, pb = PSA.get()
                    S.op("pe", lambda e, p=p: e.matmul(p[:, 0:4], lhsT=U(), rhs=gs[:, 8:12], start=True, stop=True), r=[dmkb, gsb], w=[pb])
                    S.op("pe", lambda e, p=p: e.matmul(p[:, 4:8], lhsT=BONES(), rhs=gs[:, 8:12], start=True, stop=True), r=[dmkb, gsb], w=[pb])
                    S.op("dve", lambda e, p=p: e.tensor_copy(out=gs[:, 12:16], in_=p[:, 0:4]), r=[pb], w=[gsb])
                    S.op("dve", lambda e, p=p: e.tensor_scalar(out=gs[:, 16:20], in0=p[:, 0:4], scalar1=-1.0, scalar2=None, op0=ALU.mult), r=[pb], w=[gsb])
                    S.op("dve", lambda e, p=p: e.tensor_tensor(out=gs[:, 24:28], in0=p[:, 4:8], in1=gs[:, 12:16], op=ALU.subtract), r=[pb, gsb], w=[gsb])
                    S.op("act", lambda e: e.activation(out=gs[:, 20:24], in_=gs[:, 12:16], func=AF.Exp), r=[gsb], w=[gsb])
                    S.op("act", lambda e: e.activation(out=gs[:, 24:28], in_=gs[:, 24:28], func=AF.Exp), r=[gsb], w=[gsb])
                    gu, gub = GU.get()
                    for h in range(4):
                        S.op("act", lambda e, h=h: e.activation(
                            out=gu[:, h, :], in_=U(), func=AF.Copy, scale=gs[:, 8 + h:9 + h]), r=[dmkb, gsb], w=[gub])
                    pg, pgb = PSA.get()
                    for h in range(4):
                        S.op("pe", lambda e, h=h: e.matmul(pg[:, h * 128:(h + 1) * 128], lhsT=ones[:], rhs=gu[:, h, :], start=True, stop=True),
                             r=[onesb, gub], w=[pgb])
                    erow, erowb = EROW.get()
                    S.op("act", lambda e: e.activation(out=erow[:], in_=v4(pg), func=AF.Exp), r=[pgb], w=[erowb])
                    dtmp, dtmpb = DTMP.get()
                    for h in range(4):
                        S.op("dve", lambda e, h=h: e.scalar_tensor_tensor(
                            out=dtmp[:, h, :], in0=pg[:, h * 128:(h + 1) * 128], scalar=gs[:, 16 + h:17 + h], in1=MINC(),
                            op0=ALU.add, op1=ALU.add), r=[pgb, gsb, dmkb], w=[dtmpb])
                    dcy, dcyb = DCY.get()
                    S.op("act", lambda e: e.activation(out=dcy[:], in_=dtmp[:], func=AF.Exp), r=[dtmpb], w=[dcyb])
                    pkk, pkkb = PSA.get()
                    for h in range(4):
                        S.op("pe", lambda e, h=h: e.matmul(pkk[:, h * 128:(h + 1) * 128], lhsT=qkt[:, 4 + h, :], rhs=qkt[:, 4 + h, :],
                                                           start=True, stop=True), r=[qktb], w=[pkkb])
                    xm, xmb = XM.get()
                    for h in range(4):
                        S.op("dve", lambda e, h=h: e.scalar_tensor_tensor(
                            out=xm[:, h, :], in0=pkk[:, h * 128:(h + 1) * 128], scalar=gs[:, 4 + h:5 + h], in1=dcy[:, h, :],
                            op0=ALU.mult, op1=ALU.mult), r=[pkkb, gsb, dcyb], w=[xmb])
                    S.op("dve", lambda e: e.tensor_tensor(out=xm[:], in0=xm[:], in1=strict4[:], op=ALU.mult), r=[xmb, strict4b], w=[xmb])
                    if own:
                        pqk, pqkb = PSA.get()
                        for h in range(4):
                            S.op("pe", lambda e, h=h: e.matmul(pqk[:, h * 128:(h + 1) * 128], lhsT=qkt[:, 4 + h, :], rhs=qkt[:, h, :],
                                                               start=True, stop=True), r=[qktb], w=[pqkb])
                    att, attb = ATT.get()
                    if own:
                        S.op("dve", lambda e: e.tensor_tensor(out=att[:], in0=v4(pqk), in1=dcy[:], op=ALU.mult), r=[pqkb, dcyb], w=[attb])
                    kg, kgb = KG.get()
                    kd, kdb = KD.get()
                    for h in range(4):
                        S.op("act", lambda e, h=h: e.activation(
                            out=kg[:, h, :], in_=nrm[:, 4 + h, :], func=AF.Copy, scale=gs[:, 20 + h:21 + h]),
                            r=[nrmb, gsb], w=[kgb])
                        S.op("act", lambda e, h=h: e.activation(
                            out=kd[:, h, :], in_=nrm[:, 4 + h, :], func=AF.Copy, scale=gs[:, 24 + h:25 + h]),
                            r=[nrmb, gsb], w=[kdb])
                    qd, qdb = QD.get()
                    if own:
                        S.op("dve", lambda e: e.tensor_tensor(out=qd[:], in0=qkt[:, 0:4, :], in1=erow[:], op=ALU.mult), r=[qktb, erowb], w=[qdb])
                    T.update(own=own, ti=ti, t0=t0, xm=xm, xmb=xmb, vbf=vbf, vbfb=vbfb, kg=kg, kgb=kgb, gs=gs, gsb=gsb, att=att, attb=attb,
                             kd=kd, kdb=kdb, qd=qd, qdb=qdb, erow=erow, erowb=erowb)
                    return T

                def stage_inv(T):
                    xm, xmb, gs, gsb = T["xm"], T["xmb"], T["gs"], T["gsb"]
                    vbf, vbfb, kg, kgb = T["vbf"], T["vbfb"], T["kg"], T["kgb"]
                    pk, pkb_ = PK.get()
                    S.op("act", lambda e, pk=pk: e.copy(out=pk[:], in_=xm[:]), r=[xmb], w=[pkb_])
                    ptp, ptpb = PSB.get()
                    for h in range(4):
                        S.op("pe", lambda e, h=h: e.transpose(out=ptp[:, h * 128:(h + 1) * 128], in_=xm[:, h, :], identity=ident[:]),
                             r=[xmb, identb], w=[ptpb])
                    pkt, pktb = PKT.get()
                    evac(lambda pkt=pkt: pkt[:], lambda: v4(ptp), ptpb, [pktb])
                    rf, rfb = RF.get()
                    rb, rbb = RB.get()
                    S.op("dve", lambda e, rf=rf: e.tensor_tensor(out=rf[:], in0=xm[:], in1=eye4[:], op=ALU.add), r=[xmb, eye4b], w=[rfb])
                    S.op("act", lambda e, rb=rb, rf=rf: e.copy(out=rb[:], in_=rf[:]), r=[rfb], w=[rbb])
                    for rnd in range(5):
                        pa, pab = PSB.get()
                        for h in range(4):
                            S.op("pe", lambda e, pa=pa, pk=pk, pkt=pkt, h=h: e.matmul(pa[:, h * 128:(h + 1) * 128], lhsT=pk[:, h, :], rhs=pkt[:, h, :],
                                                                                     start=True, stop=True), r=[pkb_, pktb], w=[pab])
                        pktn, pktnb = PKT.get()
                        S.op("act", lambda e, pktn=pktn, pa=pa: e.copy(out=pktn[:], in_=v4(pa)), r=[pab], w=[pktnb])
                        if rnd < 4:
                            pb2, pb2b = PSB.get()
                            for h in range(4):
                                S.op("pe", lambda e, pb2=pb2, pk=pk, pkt=pkt, h=h: e.matmul(pb2[:, h * 128:(h + 1) * 128], lhsT=pkt[:, h, :],
                                                                                           rhs=pk[:, h, :], start=True, stop=True),
                                     r=[pkb_, pktb], w=[pb2b])
                        if rnd < 4:
                            pkn, pknb = PK.get()
                            S.op("dve", lambda e, pkn=pkn, pb2=pb2: e.tensor_copy(out=pkn[:], in_=v4(pb2)), r=[pb2b], w=[pknb])
                            pk, pkb_ = pkn, pknb
                        pkt, pktb = pktn, pktnb
                        pr, prb = PSB.get()
                        for h in range(4):
                            S.op("pe", lambda e, pr=pr, pkt=pkt, rb=rb, h=h: e.matmul(pr[:, h * 128:(h + 1) * 128], lhsT=pkt[:, h, :], rhs=rb[:, h, :],
                                                                                     start=True, stop=True), r=[pktb, rbb], w=[prb])
                        rfn, rfnb = RF.get()
                        rbn, rbnb = RB.get()
                        S.op("dve", lambda e, rbn=rbn, rf=rf, pr=pr: e.tensor_tensor(out=rbn[:], in0=v4(pr), in1=rf[:], op=ALU.add),
                             r=[prb, rfb], w=[rbnb])
                        if rnd < 4:
                            S.op("dve", lambda e, rfn=rfn, rf=rf, pr=pr: e.tensor_tensor(out=rfn[:], in0=v4(pr), in1=rf[:], op=ALU.add),
                                 r=[prb, rfb], w=[rfnb])
                        rf, rfb, rb, rbb = rfn, rfnb, rbn, rbnb
                    pu, pub = PSB.get()
                    for h in range(4):
                        S.op("pe", lambda e, h=h, rb=rb: e.matmul(pu[:, h * 128:(h + 1) * 128], lhsT=rb[:, h, :], rhs=vbf[:, h, :], start=True, stop=True),
                             r=[rbb, vbfb], w=[pub])
                    ub, ubb = UB.get()
                    for h in range(4):
                        S.op("dve", lambda e, h=h: e.tensor_scalar(
                            out=ub[:, h, :], in0=pu[:, h * 128:(h + 1) * 128], scalar1=gs[:, h:h + 1], scalar2=None, op0=ALU.mult),
                            r=[pub, gsb], w=[ubb])
                    pw, pwb = PSB.get()
                    for h in range(4):
                        S.op("pe", lambda e, h=h, rb=rb: e.matmul(pw[:, h * 128:(h + 1) * 128], lhsT=kg[:, h, :], rhs=rb[:, h, :], start=True, stop=True),
                             r=[rbb, kgb], w=[pwb])
                    wt, wtb = WT.get()
                    S.op("act", lambda e: e.copy(out=wt[:], in_=v4(pw)), r=[pwb], w=[wtb])
                    T.update(ub=ub, ubb=ubb, wt=wt, wtb=wtb)

                def stage_scan(T):
                    gs, gsb, att, attb, kd, kdb, qd, qdb = T["gs"], T["gsb"], T["att"], T["attb"], T["kd"], T["kdb"], T["qd"], T["qdb"]
                    erow, erowb, ub, ubb, wt, wtb, t0 = T["erow"], T["erowb"], T["ub"], T["ubb"], T["wt"], T["wtb"], T["t0"]
                    own, ti = T["own"], T["ti"]
                    if own:
                        ot, otb = OT.get()
                    if rot:
                        fi = None
                        if dr == 0 and ti % OWNB == 0 and ti != OWNB:
                            fi = (ti // OWNB - 1) % 8
                        if dr == 1 and ti % OWNB == OWNB - 1 and ti != NB - 1:
                            fi = ti // OWNB
                        if fi is not None:
                            sf0, sf0b, sb0, sb0b = st8["sf"], st8["sfb"], st8["sb"], st8["sbb"]
                            S.op("dve", lambda e, sf0=sf0, fi=fi: e.tensor_scalar(out=sf0[:], in0=sf0[:], scalar1=wf[:, fi:fi + 1], scalar2=None,
                                                                                  op0=ALU.mult), r=[sf0b, wfb], w=[sf0b])
                            S.op("dve", lambda e, sb0=sb0, fi=fi: e.tensor_scalar(out=sb0[:], in0=sb0[:], scalar1=wf[:, fi:fi + 1], scalar2=None,
                                                                                  op0=ALU.mult), r=[sb0b, wfb], w=[sb0b])
                    for ck in ((0, 1) if dr == 0 else (1, 0)):
                        po_ = ck * 64
                        lastcol = (po_ + 63) if dr == 0 else po_
                        sf, sfb, sb, sbb = st8["sf"], st8["sfb"], st8["sb"], st8["sbb"]
                        pws, pwsb = PSC.get()
                        for h in range(4):
                            S.op("pe", lambda e, pws=pws, sb=sb, h=h: e.matmul(pws[:, h * 128:(h + 1) * 128], lhsT=wt[:, h, :], rhs=sb[:, h, :],
                                                                               start=True, stop=True), r=[wtb, sbb], w=[pwsb])
                        vn, vnb = VN.get()
                        for h in range(4):
                            S.op("dve", lambda e, vn=vn, pws=pws, h=h, po_=po_: e.scalar_tensor_tensor(
                                out=vn[po_:po_ + 64, h, :], in0=pws[po_:po_ + 64, h * 128:(h + 1) * 128], scalar=gs[po_:po_ + 64, 4 + h:5 + h],
                                in1=ub[po_:po_ + 64, h, :], op0=ALU.mult, op1=ALU.add), r=[pwsb, gsb, ubb], w=[vnb])
                        pD, pDb = PSC.get()
                        for h in range(4):
                            S.op("pe", lambda e, pD=pD, vn=vn, h=h, po_=po_: e.matmul(
                                pD[:, h * 128:(h + 1) * 128], lhsT=kd[po_:po_ + 64, h, :], rhs=vn[po_:po_ + 64, h, :], start=True, stop=True),
                                r=[kdb, vnb], w=[pDb])
                        if own:
                            pO, pOb = PSC.get()
                            for h in range(4):
                                S.op("pe", lambda e, pO=pO, sb=sb, h=h: e.matmul(pO[:, h * 128:(h + 1) * 128], lhsT=qd[:, h, :], rhs=sb[:, h, :],
                                                                                 start=True, stop=False), r=[qdb, sbb], w=[pOb])
                                S.op("pe", lambda e, pO=pO, vn=vn, h=h, po_=po_: e.matmul(
                                    pO[:, h * 128:(h + 1) * 128], lhsT=att[po_:po_ + 64, h, :], rhs=vn[po_:po_ + 64, h, :], start=False, stop=True),
                                    r=[attb, vnb], w=[pOb])
                        sfn, sfnb = SF.get()
                        sbn, sbnb = SB_.get()
                        for h in range(4):
                            S.op("dve", lambda e, sbn=sbn, sf=sf, pD=pD, h=h, lastcol=lastcol: e.scalar_tensor_tensor(
                                out=sbn[:, h, :], in0=sf[:, h, :], scalar=erow[:, h, lastcol:lastcol + 1], in1=pD[:, h * 128:(h + 1) * 128],
                                op0=ALU.mult, op1=ALU.add), r=[sfb, erowb, pDb], w=[sbnb])
                        for h in range(4):
                            S.op("dve", lambda e, sfn=sfn, sf=sf, pD=pD, h=h, lastcol=lastcol: e.scalar_tensor_tensor(
                                out=sfn[:, h, :], in0=sf[:, h, :], scalar=erow[:, h, lastcol:lastcol + 1], in1=pD[:, h * 128:(h + 1) * 128],
                                op0=ALU.mult, op1=ALU.add), r=[sfb, erowb, pDb], w=[sfnb])
                        if own:
                            S.op("act", lambda e, pO=pO, po_=po_: e.copy(out=ot[po_:po_ + 64, :, :], in_=v4(pO)[po_:po_ + 64]), r=[pOb], w=[otb])
                        st8["sf"], st8["sfb"], st8["sb"], st8["sbb"] = sfn, sfnb, sbn, sbnb
                    if own:
                        S.op("pool", lambda e: e.dma_start(out=ODN[dr].ap()[t0:t0 + 128, :], in_=ot[:].rearrange("p h d -> p (h d)")),
                             r=[otb], dma=True)

                if rot and dr == 0:
                    order = list(range(OWNB, NB)) + list(range(OWNB))
                else:
                    order = list(range(NB)) if dr == 0 else list(range(NB - 1, -1, -1))
                return order, stage_fe, stage_inv, stage_scan

            with ExitStack() as c3:
                dirs = [make_dir(0, c3, SubPool(PS.t[0:1]), SubPool(PS.t[1:2]), SubPool(PS.t[2:4])),
                        make_dir(1, c3, SubPool(PS.t[4:5]), SubPool(PS.t[5:6]), SubPool(PS.t[6:8]))]
                S.flush()
                ctxs = [{}, {}]
                for step in range(NB + 2):
                    lists = []
                    for d_ in range(2):
                        order, sfe, sinv, sscan = dirs[d_]
                        if step < NB:
                            S.begin()
                            ctxs[d_][step] = sfe(order[step])
                            lists.append(S.end())
                        if 1 <= step < NB + 1:
                            S.begin()
                            sinv(ctxs[d_][step - 1])
                            lists.append(S.end())
                        if step >= 2:
                            S.begin()
                            sscan(ctxs[d_].pop(step - 2))
                            lists.append(S.end())
                    S.merge(lists)
                S.flush()
            if STOP == 3:
                return nc

            with ExitStack() as c5:
                wo, wob = Pool(S, c5, "wo2", [128, NCH, D], BF16).get()
                wm, wmb = Pool(S, c5, "wm", [128, 8, D], BF16).get()
                ng4, ng4b = Pool(S, c5, "ng4", [128, 4, 128], F32).get()
                S.op("sp", lambda e: e.dma_start(out=wo[:], in_=s_f2o.ap().rearrange("(c p) d -> p c d", p=128)), w=[wob], dma=True)
                S.op("sp", lambda e: e.dma_start(out=wm[:], in_=s_wout.ap().rearrange("(k p) c -> p k c", p=128)), w=[wmb], dma=True)
                for h in range(4):
                    S.op("sp", lambda e, h=h: e.dma_start(out=ng4[:, h, :], in_=bc(w_ng, 0, 128)), w=[ng4b], dma=True)
                load_ln(c5, [1, 2])
                B0 = Pool(S, c5, "b0", [128, 4, D], F32, n=1)
                B1 = Pool(S, c5, "b1", [128, 4, D], F32, n=2)
                XTT = Pool(S, c5, "xtt5", [128, 8, 512], BF16, n=1)
                X2TT = Pool(S, c5, "x2tt5", [128, 8, 512], BF16, n=2)
                STB = Pool(S, c5, "st5b", [128, 12], F32, n=2)
                MVB = Pool(S, c5, "mv5b", [128, 8], F32, n=2)
                PSA5 = SubPool(PS.t[0:3])
                PSB5 = SubPool(PS.t[3:8])
                pend5 = [None]
                GT = Pool(S, c5, "gt5", [128, NCH, 512], BF16, n=1)
                WG = Pool(S, c5, "wg5", [128, 8, 2, 256], BF16, n=2)
                SG = Pool(S, c5, "sg5", [128, 512], BF16, n=2)
                ST = Pool(S, c5, "st5", [128, 12], F32, n=2)
                MV = Pool(S, c5, "mv5", [128, 8], F32, n=2)
                OF = Pool(S, c5, "of", [128, 4, 128], F32, n=2)
                OBk = Pool(S, c5, "obk", [128, 4, 128], F32, n=2)
                ZI = Pool(S, c5, "zi", [128, 4, 128], F32, n=2)
                SQ = Pool(S, c5, "sq", [128, 4, 128], F32, n=2)
                RS = Pool(S, c5, "rs", [128, 8], F32, n=2)
                for ti in range(OWN // 512):
                    t0 = ti * 512
                    S.begin()
                    cps[0] = PSA5
                    b0, b0b = B0.get()
                    b1, b1b = B1.get()
                    S.op("sp", lambda e, b0=b0, t0=t0: e.dma_start(
                        out=b0[:], in_=X1.ap()[t0:t0 + 512, :].rearrange("(s p) d -> p s d", p=128)), w=[b0b], dma=True)
                    S.op("sp", lambda e, b1=b1, t0=t0: e.dma_start(
                        out=b1[:, :, 0:512], in_=OA.ap()[t0:t0 + 512, :].rearrange("(s p) d -> p s d", p=128)), w=[b1b], dma=True)
                    for s in range(4):
                        r0 = t0 + s * 128
                        of_, ofb = OF.get()
                        obk, obkb = OBk.get()
                        zi, zib = ZI.get()
                        S.op("sp", lambda e, of_=of_, r0=r0: e.dma_start(out=of_[:].rearrange("p h d -> p (h d)"), in_=ODN[0].ap()[r0:r0 + 128, :]),
                             w=[ofb], dma=True)
                        S.op("sp", lambda e, obk=obk, r0=r0: e.dma_start(out=obk[:].rearrange("p h d -> p (h d)"), in_=ODN[1].ap()[r0:r0 + 128, :]),
                             w=[obkb], dma=True)
                        S.op("sp", lambda e, zi=zi, r0=r0: e.dma_start(out=zi[:].rearrange("p h d -> p (h d)"), in_=ZZ.ap()[r0:r0 + 128, :]),
                             w=[zib], dma=True)
                        S.op("dve", lambda e, of_=of_, obk=obk: e.tensor_tensor(out=of_[:], in0=of_[:], in1=obk[:], op=ALU.add), r=[ofb, obkb], w=[ofb])
                        sq, sqb = SQ.get()
                        rs, rsb = RS.get()
                        S.op("pool", lambda e, sq=sq, of_=of_: e.tensor_tensor(out=sq[:], in0=of_[:], in1=of_[:], op=ALU.mult), r=[ofb], w=[sqb])
                        S.op("dve", lambda e, sq=sq, rs=rs: e.tensor_reduce(out=rs[:, 0:4], in_=sq[:], axis=AX.X, op=ALU.add), r=[sqb], w=[rsb])
                        S.op("act", lambda e, rs=rs: e.activation(out=rs[:, 4:8], in_=rs[:, 0:4], func=AF.Ln, bias=RMS_EPS, scale=1.0 / 128.0),
                             r=[rsb], w=[rsb])
                        S.op("act", lambda e, rs=rs: e.activation(out=rs[:, 4:8], in_=rs[:, 4:8], func=AF.Exp, scale=-0.5), r=[rsb], w=[rsb])
                        S.op("act", lambda e, zi=zi: e.activation(out=zi[:], in_=zi[:], func=AF.Silu), r=[zib], w=[zib])
                        S.op("pool", lambda e, zi=zi: e.tensor_tensor(out=zi[:], in0=zi[:], in1=ng4[:], op=ALU.mult), r=[zib, ng4b], w=[zib])
                        for h in range(4):
                            S.op("dve", lambda e, b1=b1, of_=of_, rs=rs, zi=zi, s=s, h=h: e.scalar_tensor_tensor(
                                out=b1[:, s, 512 + h * 128:512 + (h + 1) * 128], in0=of_[:, h, :], scalar=rs[:, 4 + h:5 + h], in1=zi[:, h, :],
                                op0=ALU.mult, op1=ALU.mult), r=[ofb, rsb, zib], w=[b1b])
                    mT, mTb = XTT.get()
                    transpose_tok(b1, b1b, mT, mTb, 4)
                    S.op("act", lambda e, b0=b0: e.mul(out=b0[:], in_=b0[:], mul=ALPHA), r=[b0b], w=[b0b])
                    for s in range(4):
                        for nh in range(2):
                            po, pob = cps[0].get()
                            for k in range(8):
                                S.op("pe", lambda e, po=po, mT=mT, k=k, s=s, nh=nh: e.matmul(
                                    po[:, :], lhsT=mT[:, k, s * 128:(s + 1) * 128], rhs=wm[:, k, nh * 512:(nh + 1) * 512],
                                    start=(k == 0), stop=(k == 7)), r=[mTb, wmb], w=[pob])
                            S.op("dve", lambda e, po=po, b1=b1, b0=b0, s=s, nh=nh: e.tensor_tensor(
                                out=b1[:, s, nh * 512:(nh + 1) * 512], in0=po[:, :], in1=b0[:, s, nh * 512:(nh + 1) * 512], op=ALU.add),
                                r=[pob, b0b], w=[b1b])
                    layer_norm(b1, b1b, 1, (ST, MV))
                    x2T, x2Tb = X2TT.get()
                    transpose_tok(b1, b1b, x2T, x2Tb, 4)
                    S.op("act", lambda e, b1=b1: e.mul(out=b1[:], in_=b1[:], mul=ALPHA), r=[b1b], w=[b1b])
                    la = S.end()
                    S.begin()
                    cps[0] = PSB5
                    ffn(x2T, x2Tb, b1, b1b, s_f2i, wo, wob, GT, WG, SG)
                    layer_norm(b1, b1b, 2, (STB, MVB))
                    S.op("pool", lambda e, b1=b1, t0=t0: e.dma_start(
                        out=y_d.ap()[t0:t0 + 512, :].rearrange("(s p) d -> p s d", p=128), in_=b1[:]), r=[b1b], dma=True)
                    lb = S.end()
                    S.merge([la] + ([pend5[0]] if pend5[0] else []))
                    pend5[0] = lb
                S.merge([pend5[0]])
                cps[0] = PS
                S.flush()
    return nc


_NC_CACHE = {}


def _run(seqs, in_maps, n_cores):
    key = tuple(seqs)
    if key not in _NC_CACHE:
        _NC_CACHE[key] = build_nc(seqs)
    nc = _NC_CACHE[key]
    return run_bass_kernel_spmd(nc, in_maps, core_ids=list(range(n_cores)))


def _common_inputs(inp):
    f = lambda a: np.ascontiguousarray(np.asarray(a, dtype=np.float32))
    m = {
        "ffn1_w_in": f(inp["ffn1_w_in"][0]), "ffn1_w_out": f(inp["ffn1_w_out"][0]),
        "w_in": f(inp["w_in"][0]), "conv_w": f(inp["conv_w"][0]),
        "attn_sink": f(inp["attn_sink"]).reshape(1, 8),
        "dn_a_log": f(inp["dn_a_log"]).reshape(1, 8), "dn_dt_bias": f(inp["dn_dt_bias"]).reshape(1, 8),
        "dn_norm_gain": f(inp["dn_norm_gain"]).reshape(1, 128),
        "w_out": f(inp["w_out"][0]), "ffn2_w_in": f(inp["ffn2_w_in"][0]), "ffn2_w_out": f(inp["ffn2_w_out"][0]),
        "ln_gain": f(inp["ln_gain"]).reshape(1, 3 * D), "ln_bias": f(inp["ln_bias"]).reshape(1, 3 * D),
    }
    wq = m["w_in"][:, 0:512].reshape(D, 2, 4, 64).transpose(0, 2, 1, 3).reshape(D, 512)
    m["w_in"] = np.ascontiguousarray(np.concatenate([wq, m["w_in"][:, 512:]], axis=1))
    m.update(_consts())
    return m


def kernel(**inputs):
    xp = np.asarray(inputs["x_prompt"], dtype=np.float32)
    xs = np.asarray(inputs["x_sample"], dtype=np.float32)
    common = _common_inputs(inputs)
    seqs = (("s", xs.shape[1], False), ("p", xp.shape[1], True))
    Lp = xp.shape[1]
    sl = Lp // N_CORES
    in_maps = []
    for c in range(N_CORES):
        m = dict(common)
        m["x_s"] = np.ascontiguousarray(xs[c])
        m["x_p"] = np.ascontiguousarray(np.concatenate([xp[0, c * sl:], xp[0, :c * sl]], axis=0))
        m["wflag"] = np.array([[0.0 if (c + s_) % 8 == 7 else 1.0 for s_ in range(8)]], np.float32)
        in_maps.append(m)
    res = _run(seqs, in_maps, N_CORES)
    y_s = np.stack([np.asarray(res.results[c]["y_s"], dtype=np.float32) for c in range(N_CORES)], 0)
    y_p = np.concatenate([np.asarray(res.results[c]["y_p"], dtype=np.float32) for c in range(N_CORES)], 0)[None]
    return (y_p, y_s)
```
